# Optimizing a Trainium2 kernel written in Bass

```python
import jax, jax.numpy as jnp
from jax import lax
import numpy as np

D_MODEL = 2048
BATCH = 8
SEQ = 2048
DEPTH = 1

GLA_HEADS = 8
GLA_DK = 128
GLA_DV = 128
GLA_LOW_RANK = 16
GLA_GATE_NORMALIZER = 16.0
GDN_HEADS = 8
GDN_DK = 128
GDN_DV = 128
SHORT_CONV = 3
D_FF = 5632
FFN_CONV = 3
CHUNK = 64
NORM_EPS = 1e-6

GLA_KW = GLA_HEADS * GLA_DK
GLA_VW = GLA_HEADS * GLA_DV
GDN_KW = GDN_HEADS * GDN_DK
GDN_VW = GDN_HEADS * GDN_DV
GDN_QKV = 2 * GDN_KW + GDN_VW
IN_SIZES = (
    GLA_KW, GLA_KW, GLA_VW, GLA_VW,
    GLA_LOW_RANK, GLA_LOW_RANK,
    GDN_QKV, GDN_VW,
    GDN_HEADS, GDN_HEADS, GDN_HEADS, GDN_HEADS,
    D_MODEL, D_MODEL,
)
N_IN = sum(IN_SIZES)

kernel_name = "hybrid_gla_gdn_convffn_encoder"


def rms_norm(x, g):
    xf = x.astype(jnp.float32)
    y = xf * lax.rsqrt(jnp.mean(xf * xf, axis=-1, keepdims=True) + NORM_EPS) * g.astype(jnp.float32)
    return y.astype(x.dtype)


def depthwise_conv(x, w):
    k = w.shape[0]
    return lax.conv_general_dilated(
        x, w[:, None, :].astype(x.dtype), window_strides=(1,),
        padding=[(k // 2, k // 2)], dimension_numbers=('NWC', 'WIO', 'NWC'),
        feature_group_count=x.shape[-1])


def to_heads(t, n_heads):
    b, s, _ = t.shape
    return t.astype(jnp.float32).reshape(b, s, n_heads, -1).transpose(0, 2, 1, 3)


def flip_seq(t):
    return jnp.flip(t, axis=2)


def l2_norm(t):
    return t * lax.rsqrt(jnp.sum(t * t, axis=-1, keepdims=True) + NORM_EPS)


def gated_head_norm(o, z, g):
    o = o * lax.rsqrt(jnp.mean(o * o, axis=-1, keepdims=True) + NORM_EPS) * g.astype(jnp.float32)
    b, h, s, d = o.shape
    o = o.transpose(0, 2, 1, 3).reshape(b, s, h * d)
    return o * jax.nn.silu(z.astype(jnp.float32))


def gla_scan(q, k, v, log_g):
    b, h, s, dk = q.shape
    dv = v.shape[-1]
    n = s // CHUNK
    q, k, v, log_g = (t.reshape(b, h, n, CHUNK, t.shape[-1]) for t in (q, k, v, log_g))
    G = jnp.cumsum(log_g, axis=3)
    G_last = G[:, :, :, -1:, :]
    q_dec = q * jnp.exp(G)
    k_inv = k * jnp.exp(-G)
    k_tail = k * jnp.exp(G_last - G)
    lower = jnp.tril(jnp.ones((CHUNK, CHUNK), dtype=bool))
    scores = jnp.where(lower, jnp.einsum('bhnid,bhnjd->bhnij', q_dec, k_inv), 0.0)
    o_intra = jnp.einsum('bhnij,bhnjv->bhniv', scores, v)
    chunk_state = jnp.einsum('bhncd,bhncv->bhndv', k_tail, v)
    chunk_decay = jnp.exp(G_last[:, :, :, 0, :])

    def step(state, inp):
        decay, upd = inp
        return state * decay[..., None] + upd, state

    _, s_in = lax.scan(step, jnp.zeros((b, h, dk, dv), q.dtype),
                       (jnp.moveaxis(chunk_decay, 2, 0), jnp.moveaxis(chunk_state, 2, 0)))
    o_inter = jnp.einsum('bhncd,nbhdv->bhncv', q_dec, s_in)
    return (o_intra + o_inter).reshape(b, h, s, dv)


def gdn_scan(q, k, v, g, beta):
    b, h, s, dk = q.shape
    dv = v.shape[-1]
    n = s // CHUNK
    q, k, v = (t.reshape(b, h, n, CHUNK, t.shape[-1]) for t in (q, k, v))
    g, beta = (t.reshape(b, h, n, CHUNK) for t in (g, beta))
    G = jnp.cumsum(g, axis=-1)
    lower_incl = jnp.tril(jnp.ones((CHUNK, CHUNK), dtype=bool))
    lower_strict = jnp.tril(jnp.ones((CHUNK, CHUNK), dtype=bool), k=-1)
    decay = jnp.exp(jnp.where(lower_incl, G[..., :, None] - G[..., None, :], -jnp.inf))
    k_beta = k * beta[..., None]
    v_beta = v * beta[..., None]
    L = jnp.where(lower_strict, jnp.einsum('bhnid,bhnjd->bhnij', k_beta, k), 0.0) * decay
    system = L + jnp.eye(CHUNK, dtype=L.dtype)
    rhs = jnp.concatenate([k_beta * jnp.exp(G)[..., None], v_beta], axis=-1)
    sol = lax.linalg.triangular_solve(system, rhs, left_side=True, lower=True, unit_diagonal=True)
    w, u = sol[..., :dk], sol[..., dk:]
    attn = jnp.einsum('bhnid,bhnjd->bhnij', q, k) * decay
    q_dec = q * jnp.exp(G)[..., None]
    k_tail = k * jnp.exp(G[..., -1:] - G)[..., None]
    chunk_decay = jnp.exp(G[..., -1])

    def step(state, inp):
        w_c, u_c, q_c, k_c, a_c, d_c = inp
        v_new = u_c - jnp.einsum('bhcd,bhdv->bhcv', w_c, state)
        o_c = jnp.einsum('bhcd,bhdv->bhcv', q_c, state) + jnp.einsum('bhij,bhjv->bhiv', a_c, v_new)
        state = state * d_c[..., None, None] + jnp.einsum('bhcd,bhcv->bhdv', k_c, v_new)
        return state, o_c

    xs = tuple(jnp.moveaxis(t, 2, 0) for t in (w, u, q_dec, k_tail, attn, chunk_decay))
    _, o = lax.scan(step, jnp.zeros((b, h, dk, dv), q.dtype), xs)
    return jnp.moveaxis(o, 0, 2).reshape(b, h, s, dv)


def gla_mixer(q, k, v, gate, lr_f, lr_b, dw_f, db_f, dw_b, db_b, norm_g):
    q = to_heads(q, GLA_HEADS) * (GLA_DK ** -0.5)
    k = to_heads(k, GLA_HEADS)
    v = to_heads(v, GLA_HEADS)
    lg_f = to_heads(jax.nn.log_sigmoid(lr_f.astype(jnp.float32) @ dw_f.astype(jnp.float32)
                                       + db_f.astype(jnp.float32)) / GLA_GATE_NORMALIZER, GLA_HEADS)
    lg_b = to_heads(jax.nn.log_sigmoid(lr_b.astype(jnp.float32) @ dw_b.astype(jnp.float32)
                                       + db_b.astype(jnp.float32)) / GLA_GATE_NORMALIZER, GLA_HEADS)
    o = gla_scan(q, k, v, lg_f) + flip_seq(gla_scan(flip_seq(q), flip_seq(k), flip_seq(v), flip_seq(lg_b)))
    return gated_head_norm(o, gate, norm_g)


def gdn_decay(a, a_log, dt_bias):
    g = -jnp.exp(a_log.astype(jnp.float32)) * jax.nn.softplus(a.astype(jnp.float32) + dt_bias.astype(jnp.float32))
    return g.transpose(0, 2, 1)


def gdn_mixer(qkv, z, a_f, a_b, beta_f, beta_b, conv_w, a_log_f, dt_f, a_log_b, dt_b, norm_g):
    qkv = jax.nn.silu(depthwise_conv(qkv.astype(jnp.float32), conv_w))
    q, k, v = jnp.split(qkv, [GDN_KW, 2 * GDN_KW], axis=-1)
    q = l2_norm(to_heads(q, GDN_HEADS)) * (GDN_DK ** -0.5)
    k = l2_norm(to_heads(k, GDN_HEADS))
    v = to_heads(v, GDN_HEADS)
    g_f = gdn_decay(a_f, a_log_f, dt_f)
    g_b = gdn_decay(a_b, a_log_b, dt_b)
    bt_f = jax.nn.sigmoid(beta_f.astype(jnp.float32)).transpose(0, 2, 1)
    bt_b = jax.nn.sigmoid(beta_b.astype(jnp.float32)).transpose(0, 2, 1)
    o = gdn_scan(q, k, v, g_f, bt_f) + flip_seq(
        gdn_scan(flip_seq(q), flip_seq(k), flip_seq(v), flip_seq(g_b), flip_seq(bt_b)))
    return gated_head_norm(o, z, norm_g)


def conv_ffn(h, w_up, conv_w, conv_b, w_down):
    u = depthwise_conv(h @ w_up, conv_w) + conv_b.astype(h.dtype)
    gate, val = jnp.split(u, 2, axis=-1)
    return (jax.nn.silu(gate) * val) @ w_down


def setup_inputs(seed: int = 0) -> dict:
    key = jax.random.key(seed)
    ks = iter(jax.random.split(key, 32))

    def normal(shape, scale):
        return jax.random.normal(next(ks), shape, jnp.float32) * scale

    def gain(shape):
        return 1.0 + normal(shape, 0.02)

    def a_log():
        return jnp.log(jax.random.uniform(next(ks), (DEPTH, GDN_HEADS), jnp.float32, 1.0, 16.0))

    def dt_bias():
        dt = jnp.exp(jax.random.uniform(next(ks), (DEPTH, GDN_HEADS), jnp.float32,
                                        float(np.log(1e-3)), float(np.log(1e-1))))
        return dt + jnp.log(-jnp.expm1(-dt))

    L = DEPTH
    return {
        "x": normal((BATCH, SEQ, D_MODEL), 1.0),
        "norm1_g": gain((L, D_MODEL)),
        "w_in": normal((L, D_MODEL, N_IN), D_MODEL ** -0.5),
        "gla_decay_w_f": normal((L, GLA_LOW_RANK, GLA_KW), GLA_LOW_RANK ** -0.5),
        "gla_decay_b_f": normal((L, GLA_KW), 0.1),
        "gla_decay_w_b": normal((L, GLA_LOW_RANK, GLA_KW), GLA_LOW_RANK ** -0.5),
        "gla_decay_b_b": normal((L, GLA_KW), 0.1),
        "gla_norm_g": gain((L, GLA_DV)),
        "gdn_conv_w": normal((L, SHORT_CONV, GDN_QKV), SHORT_CONV ** -0.5),
        "gdn_a_log_f": a_log(),
        "gdn_dt_bias_f": dt_bias(),
        "gdn_a_log_b": a_log(),
        "gdn_dt_bias_b": dt_bias(),
        "gdn_norm_g": gain((L, GDN_DV)),
        "w_branch_gla": normal((L, GLA_VW, D_MODEL), GLA_VW ** -0.5),
        "w_branch_gdn": normal((L, GDN_VW, D_MODEL), GDN_VW ** -0.5),
        "w_out": normal((L, D_MODEL, D_MODEL), D_MODEL ** -0.5),
        "norm2_g": gain((L, D_MODEL)),
        "w_up": normal((L, D_MODEL, 2 * D_FF), D_MODEL ** -0.5),
        "ffn_conv_w": normal((L, FFN_CONV, 2 * D_FF), FFN_CONV ** -0.5),
        "ffn_conv_b": normal((L, 2 * D_FF), 0.02),
        "w_down": normal((L, D_FF, D_MODEL), D_FF ** -0.5),
        "final_norm_g": gain((D_MODEL,)),
    }


def reference(x, norm1_g, w_in, gla_decay_w_f, gla_decay_b_f, gla_decay_w_b, gla_decay_b_b,
              gla_norm_g, gdn_conv_w, gdn_a_log_f, gdn_dt_bias_f, gdn_a_log_b, gdn_dt_bias_b,
              gdn_norm_g, w_branch_gla, w_branch_gdn, w_out, norm2_g, w_up, ffn_conv_w,
              ffn_conv_b, w_down, final_norm_g):
    split_points = []
    acc = 0
    for size in IN_SIZES[:-1]:
        acc += size
        split_points.append(acc)
    for l in range(DEPTH):
        h = rms_norm(x, norm1_g[l])
        proj = h @ w_in[l]
        (gla_q, gla_k, gla_v, gla_gate, gla_lr_f, gla_lr_b, gdn_qkv, gdn_z,
         gdn_a_f, gdn_a_b, gdn_beta_f, gdn_beta_b, gate_gla, gate_gdn) = jnp.split(proj, split_points, axis=-1)
        y_gla = gla_mixer(gla_q, gla_k, gla_v, gla_gate, gla_lr_f, gla_lr_b,
                          gla_decay_w_f[l], gla_decay_b_f[l], gla_decay_w_b[l], gla_decay_b_b[l],
                          gla_norm_g[l]).astype(x.dtype)
        y_gdn = gdn_mixer(gdn_qkv, gdn_z, gdn_a_f, gdn_a_b, gdn_beta_f, gdn_beta_b, gdn_conv_w[l],
                          gdn_a_log_f[l], gdn_dt_bias_f[l], gdn_a_log_b[l], gdn_dt_bias_b[l],
                          gdn_norm_g[l]).astype(x.dtype)
        merged = (jax.nn.sigmoid(gate_gla) * (y_gla @ w_branch_gla[l])
                  + jax.nn.sigmoid(gate_gdn) * (y_gdn @ w_branch_gdn[l]))
        x = x + merged @ w_out[l]
        h = rms_norm(x, norm2_g[l])
        x = x + conv_ffn(h, w_up[l], ffn_conv_w[l], ffn_conv_b[l], w_down[l])
    return rms_norm(x, final_norm_g)
```

```python
import contextlib
import numpy as np
import ml_dtypes
import concourse.bass as bass
import concourse.mybir as mybir
from concourse.bass_utils import run_bass_kernel_spmd

F32 = mybir.dt.float32
BF16 = mybir.dt.bfloat16
AF = mybir.ActivationFunctionType
ALU = mybir.AluOpType

T = 2048
D = 2048
KC = 16
NT = 16
N_IN = 12352
D_FF = 5632
EPS = 1e-6

C_GQ, C_GK, C_GV, C_GG = 0, 1024, 2048, 3072
C_LR = 4096
C_DQ, C_DK, C_DV = 4128, 5152, 6176
C_DZ = 7200
C_AB = 8224
C_BG = 8256
C_BD = 10304


class Buf:
    __slots__ = ("w", "r", "name", "excl")

    def __init__(self, name="", excl=False):
        self.w = None
        self.r = {}
        self.name = name
        self.excl = excl


class DSem:
    __slots__ = ("handle", "count")

    def __init__(self, handle):
        self.handle = handle
        self.count = 0


class Sched:
    ENG = ["pe", "act", "dve", "pool", "sp"]

    def __init__(self, nc, es, n_dsem=24):
        self.nc = nc
        self.sem = {e: es.enter_context(nc.semaphore("s_" + e)) for e in self.ENG}
        self.cnt = {e: 0 for e in self.ENG}
        self.seen = {e: {} for e in self.ENG}
        self.prog = {e: [] for e in self.ENG}
        self.dsems = [DSem(es.enter_context(nc.semaphore("d%d" % i))) for i in range(n_dsem)]
        self.dnext = 0

    def _waits(self, eng, reads, writes, skip_same):
        deps = {}

        def add(k, v):
            if skip_same and k == eng:
                return
            if deps.get(k, 0) < v:
                deps[k] = v

        for b in reads:
            if b.w is not None:
                add(*b.w)
        for b in writes:
            if b.w is not None:
                add(*b.w)
            for k, v in b.r.items():
                add(k, v)
        out = []
        seen = self.seen[eng]
        for k, v in deps.items():
            if seen.get(k, 0) < v:
                seen[k] = v
                out.append((k.handle if isinstance(k, DSem) else self.sem[k], v))
        return out

    def _mark(self, d, reads, writes):
        k, v = d
        for b in reads:
            if b.r.get(k, 0) < v:
                b.r[k] = v
        for b in writes:
            b.w = d
            b.r = {}

    def op(self, eng, fn, reads=(), writes=(), skip_same=False):
        if any(b.excl for b in reads):
            writes = list(writes) + [b for b in reads if b.excl]
            reads = [b for b in reads if not b.excl]
        waits = self._waits(eng, reads, writes, skip_same)
        self.cnt[eng] += 1
        self.prog[eng].append((waits, fn, self.sem[eng], 1))
        self._mark((eng, self.cnt[eng]), reads, writes)

    def barrier(self):
        for eng in self.ENG:
            waits = []
            seen = self.seen[eng]
            for k in self.ENG:
                if k != eng and seen.get(k, 0) < self.cnt[k]:
                    seen[k] = self.cnt[k]
                    waits.append((self.sem[k], self.cnt[k]))
            for ds in self.dsems:
                if ds.count > 0 and seen.get(ds, 0) < ds.count:
                    seen[ds] = ds.count
                    waits.append((ds.handle, ds.count))
            if waits:
                self.prog[eng].append((waits, None, None, 0))

    def dma(self, q, out, in_, reads=(), writes=(), **kw):
        ds = self.dsems[self.dnext]
        self.dnext = (self.dnext + 1) % len(self.dsems)
        waits = self._waits(q, reads, writes, False)
        if ds.count > 0 and self.seen[q].get(ds, 0) < ds.count:
            self.seen[q][ds] = ds.count
            waits.append((ds.handle, ds.count))
        ds.count += 16
        self.prog[q].append((waits, (lambda e, o=out, i=in_, kw=kw: e.dma_start(out=o, in_=i, **kw)), ds.handle, 16))
        self._mark((ds, ds.count), reads, writes)

    def finish(self):
        waits = []
        for ds in self.dsems:
            if ds.count > 0:
                waits.append((ds.handle, ds.count))
        self.prog["sp"].append((waits, None, None, 0))

    def emit(self):
        nc = self.nc
        prog = self.prog

        def replay(name, e):
            for waits, fn, sem, inc in prog[name]:
                for s, v in waits:
                    e.wait_ge(s, v)
                if fn is not None:
                    ins = fn(e)
                    ins.then_inc(sem, inc)

        with nc.Block() as block:
            @block.tensor
            def _(e):
                replay("pe", e)

            @block.scalar
            def _(e):
                replay("act", e)

            @block.vector
            def _(e):
                replay("dve", e)

            @block.gpsimd
            def _(e):
                replay("pool", e)

            @block.sync
            def _(e):
                replay("sp", e)


def build_nc(debug=None):
    debug = debug or {}
    nc = bass.Bass("TRN2", target_bir_lowering=False)
    es = contextlib.ExitStack()
    with es:
        _build(nc, es, debug)
    return nc


def _dram_in(nc, name, shape, dt=F32):
    return nc.dram_tensor(name, list(shape), dt, kind="ExternalInput").ap()


class Ctx:
    pass


def _build(nc, es, debug):
    S = Sched(nc, es)
    C = Ctx()
    C.nc, C.S, C.debug = nc, S, debug
    stop_after = debug.get("stop_after", [[None], None])[0][0]

    C.x = _dram_in(nc, "x", [T, D])
    C.norm1_g = _dram_in(nc, "norm1_g", [1, D])
    C.w_in = _dram_in(nc, "w_in", [D, N_IN])
    C.ident_bf_d = _dram_in(nc, "ident_bf", [128, 128], BF16)
    C.cst_d = _dram_in(nc, "cst", [128, NCST])
    C.gla_dw = [_dram_in(nc, "gla_decay_w_f", [16, 1024]), _dram_in(nc, "gla_decay_w_b", [16, 1024])]
    C.gla_db = [_dram_in(nc, "gla_decay_b_f", [1, 1024]), _dram_in(nc, "gla_decay_b_b", [1, 1024])]
    C.gla_norm_g = _dram_in(nc, "gla_norm_g", [1, 128])
    C.gdn_norm_g = _dram_in(nc, "gdn_norm_g", [1, 128])
    C.w_branch_gla = _dram_in(nc, "w_branch_gla", [1024, D])
    C.w_branch_gdn = _dram_in(nc, "w_branch_gdn", [1024, D])
    C.w_out = _dram_in(nc, "w_out", [D, D])
    C.norm2_g = _dram_in(nc, "norm2_g", [1, D])
    C.w_up = _dram_in(nc, "w_up", [D, 2 * D_FF])
    C.w_down = _dram_in(nc, "w_down", [D_FF, D])
    C.ffn_cw_d = _dram_in(nc, "ffn_cw", [128, 88 * 4])
    C.final_norm_g = _dram_in(nc, "final_norm_g", [1, D])
    C.cst2_d = _dram_in(nc, "cst2", [128, NC2])
    C.cstb_d = _dram_in(nc, "cstb", [128, NCB], BF16)
    C.gdn_cw_d = _dram_in(nc, "gdn_cw", [128, 72])
    C.gdn_hp_d = _dram_in(nc, "gdn_hp", [16, 2])
    C.out = nc.dram_tensor("out", [T, D], F32, kind="ExternalOutput").ap()
    C.dbg_out = {}
    for name, (shape, dt) in debug.items():
        if dt is None:
            continue
        C.dbg_out[name] = nc.dram_tensor("dbg_" + name, list(shape), dt, kind="ExternalOutput").ap()
    C.scr = nc.dram_tensor("scr_proj", [N_IN, T], F32).ap()
    C.y_scr = nc.dram_tensor("scr_y", [2048, T], BF16).ap()
    C.h2_scr = nc.dram_tensor("scr_h2", [D, T], BF16).ap()
    C.x1_scr = nc.dram_tensor("scr_x1", [T, D], F32).ap()
    C.x2_scr = nc.dram_tensor("scr_x2", [T, D], F32).ap()

    C.psum = [es.enter_context(nc.psum_tensor("ps%d" % i, [128, 512], F32)) for i in range(8)]
    C.psum_b = [[Buf("ps%d" % i, excl=True)] * 4 for i in range(8)]
    C.ident_bf = es.enter_context(nc.sbuf_tensor("ident_bf_sb", [128, 128], BF16))
    C.ident_bf_b = Buf("ident_bf")
    C.cst = es.enter_context(nc.sbuf_tensor("cst_sb", [128, NCST], F32))
    C.cst_b = Buf("cst")
    S.dma("sp", C.ident_bf[:, :], C.ident_bf_d[:, :], writes=[C.ident_bf_b])
    S.dma("sp", C.cst[:, :], C.cst_d[:, :], writes=[C.cst_b])

    phase_proj(C)
    S.barrier()
    if stop_after != "proj":
        if debug.get("gla_heads", [[8], None])[0][0] > 0:
            phase_gla(C)
            S.barrier()
        if stop_after != "gla":
            if debug.get("gdn_heads", [[8], None])[0][0] > 0:
                phase_gdn(C)
                S.barrier()
            if stop_after != "gdn":
                phase_branch(C)
                S.barrier()
                if stop_after != "branch":
                    phase_ffn(C)
                    S.barrier()
                    phase_final(C)
                    S.barrier()

    if "proj" in C.dbg_out:
        S.dma("sp", C.dbg_out["proj"][:, :], C.scr[0:C.dbg_out["proj"].shape[0], :])
    if "y" in C.dbg_out:
        S.dma("sp", C.dbg_out["y"][:, :], C.y_scr[0:C.dbg_out["y"].shape[0], :])
    if "y2" in C.dbg_out:
        S.dma("sp", C.dbg_out["y2"][:, :], C.y_scr[1024:1024 + C.dbg_out["y2"].shape[0], :])
    for nm, ap_ in (("x1", C.x1_scr), ("x2", C.x2_scr)):
        if nm in C.dbg_out:
            S.dma("sp", C.dbg_out[nm][:, :], ap_[:, :])
    if "h2" in C.dbg_out:
        S.dma("sp", C.dbg_out["h2"][:, :], C.h2_scr[:, :])
    if stop_after is not None:
        S.dma("sp", C.out[0:128, :], C.x[0:128, :])
    S.finish()
    S.emit()


def phase_proj(C):
    nc, S, debug = C.nc, C.S, C.debug
    psum, psum_b = C.psum, C.psum_b
    with contextlib.ExitStack() as pes:
        def sb(name, shape, dt):
            return pes.enter_context(nc.sbuf_tensor(name, list(shape), dt))

        hT = sb("hT", [128, KC, T], BF16)
        hT_b = [Buf("hT%d" % t) for t in range(NT)]
        g1b = sb("g1b", [128, D], F32)
        g1b_b = Buf("g1b")
        S.dma("sp", g1b[:, :], C.norm1_g.partition_broadcast(128), writes=[g1b_b])

        xt = [sb("xt%d" % i, [128, D], F32) for i in range(2)]
        xt_b = [Buf("xt%d" % i) for i in range(2)]
        junk = sb("junk", [128, D], BF16)
        junk_b = Buf("junk")
        hb = [sb("hb%d" % i, [128, D], BF16) for i in range(2)]
        hb_b = [Buf("hb%d" % i) for i in range(2)]
        stat = sb("stat", [128, 4 * NT], F32)
        stat_b = [Buf("stat%d" % i) for i in range(NT)]
        ident_bf, ident_bf_b = C.ident_bf, C.ident_bf_b
        pcnt = 0
        for t in range(NT):
            i = t % 2
            S.dma("sp", xt[i][:, :], C.x[t * 128:(t + 1) * 128, :], writes=[xt_b[i]])
            ss = stat[:, 4 * t:4 * t + 1]
            lnv = stat[:, 4 * t + 1:4 * t + 2]
            rstd = stat[:, 4 * t + 2:4 * t + 3]
            S.op("act", lambda e, i=i, ss=ss: e.activation(out=junk[:, :], in_=xt[i][:, :], func=AF.Square, accum_out=ss),
                 reads=[xt_b[i]], writes=[junk_b, stat_b[t]])
            S.op("act", lambda e, ss=ss, lnv=lnv: e.activation(out=lnv, in_=ss, func=AF.Ln, scale=1.0 / D, bias=EPS),
                 reads=[stat_b[t]], writes=[stat_b[t]])
            S.op("act", lambda e, rstd=rstd, lnv=lnv: e.activation(out=rstd, in_=lnv, func=AF.Exp, scale=-0.5),
                 reads=[stat_b[t]], writes=[stat_b[t]])
            S.op("dve", lambda e, i=i, rstd=rstd: e.scalar_tensor_tensor(
                out=hb[i][:, :], in0=xt[i][:, :], scalar=rstd, in1=g1b[:, :], op0=ALU.mult, op1=ALU.mult),
                reads=[xt_b[i], stat_b[t], g1b_b], writes=[hb_b[i]])
            for g in range(4):
                pi = 4 + (pcnt % 4)
                pcnt += 1
                pt = psum[pi].bitcast(BF16)
                for q in range(4):
                    kc = g * 4 + q
                    S.op("pe", lambda e, pt=pt, q=q, i=i, kc=kc: e.transpose(
                        out=pt[:, q * 128:(q + 1) * 128], in_=hb[i][:, kc * 128:(kc + 1) * 128], identity=ident_bf[:, :]),
                        reads=[hb_b[i], ident_bf_b], writes=[psum_b[pi][q]], skip_same=True)
                if g % 2 == 0:
                    S.op("act", lambda e, pt=pt, g=g, t=t: e.activation(
                        out=hT[:, g * 4:(g + 1) * 4, t * 128:(t + 1) * 128],
                        in_=pt[:, 0:512].rearrange("p (a b) -> p a b", a=4), func=AF.Copy),
                        reads=psum_b[pi], writes=[hT_b[t]])
                else:
                    S.op("dve", lambda e, pt=pt, g=g, t=t: e.tensor_copy(
                        out=hT[:, g * 4:(g + 1) * 4, t * 128:(t + 1) * 128],
                        in_=pt[:, 0:512].rearrange("p (a b) -> p a b", a=4)),
                        reads=psum_b[pi], writes=[hT_b[t]])

        scr = C.scr
        w_in_r = C.w_in.rearrange("(kc p) n -> p kc n", p=128)
        units = []
        for c0 in range(0, 4096, 128):
            units.append((c0, 128))
        units.append((C_LR, 32))
        for c0 in range(C_DQ, C_AB, 128):
            units.append((c0, 128))
        units.append((C_AB, 32))
        for c0 in range(C_BG, N_IN, 128):
            units.append((c0, 128))
        if "nunits" in debug:
            units = units[:debug["nunits"][0][0]]
        NW = 8
        wring = [sb("wr%d" % i, [128, KC, 128], BF16) for i in range(NW)]
        wring_b = [Buf("wr%d" % i) for i in range(NW)]
        NSTG = 3
        stg = [sb("stg%d" % i, [128, T], F32) for i in range(NSTG)]
        stg_b = [Buf("stg%d" % i) for i in range(NSTG)]
        ecnt = 0
        for u, (c0, ncol) in enumerate(units):
            wt, wb = wring[u % NW], wring_b[u % NW]
            S.dma("pool", wt[:, :, 0:ncol], w_in_r[:, :, c0:c0 + ncol], writes=[wb])
            st, stb = stg[u % NSTG], stg_b[u % NSTG]
            for blk in range(4):
                pi = (4 * u + blk) % 8
                for kc in range(KC):
                    S.op("pe", lambda e, pi=pi, wt=wt, kc=kc, blk=blk, ncol=ncol: e.matmul(
                        out=psum[pi][0:ncol, :], lhsT=wt[:, kc, 0:ncol], rhs=hT[:, kc, blk * 512:(blk + 1) * 512],
                        start=(kc == 0), stop=(kc == KC - 1)),
                        reads=[wb] + hT_b[4 * blk:4 * blk + 4], writes=psum_b[pi], skip_same=True)
                if ecnt % 2 == 0:
                    S.op("act", lambda e, pi=pi, st=st, blk=blk, ncol=ncol: e.activation(
                        out=st[0:ncol, blk * 512:(blk + 1) * 512], in_=psum[pi][0:ncol, :], func=AF.Copy),
                        reads=psum_b[pi], writes=[stb])
                else:
                    S.op("dve", lambda e, pi=pi, st=st, blk=blk, ncol=ncol: e.tensor_copy(
                        out=st[0:ncol, blk * 512:(blk + 1) * 512], in_=psum[pi][0:ncol, :]),
                        reads=psum_b[pi], writes=[stb])
                ecnt += 1
            S.dma("sp", scr[c0:c0 + ncol, :], st[0:ncol, :], reads=[stb])


def phase_gla(C):
    nc, S, debug = C.nc, C.S, C.debug
    psum, psum_b = C.psum, C.psum_b
    cst, cst_b = C.cst, C.cst_b
    scr = C.scr
    nheads = debug.get("gla_heads", [[8], None])[0][0]
    with contextlib.ExitStack() as pes:
        def sb(name, shape, dt):
            return pes.enter_context(nc.sbuf_tensor(name, list(shape), dt))

        def B(name):
            return Buf(name)

        lr = sb("lr", [49, T], F32)
        lr_b = B("lr")
        dw = sb("dw", [49, 1024], F32)
        dw_b = B("dw")
        gn = sb("gn", [128, 1], F32)
        gn_b = B("gn")
        S.op("pool", lambda e: e.memset(lr[:, :], 1.0), writes=[lr_b])
        for d in range(2):
            S.dma("sp", lr[32 * d:32 * d + 16, :], scr[C_LR + 16 * d:C_LR + 16 * d + 16, :], writes=[lr_b])
            S.dma("sp", dw[32 * d:32 * d + 16, :], C.gla_dw[d][:, :], writes=[dw_b])
            S.dma("sp", dw[32 * d + 16:32 * d + 17, :], C.gla_db[d][:, :], writes=[dw_b])
        S.dma("sp", gn[:, :], C.gla_norm_g.rearrange("o d -> d o"), writes=[gn_b])

        NB = 2
        qbf = [sb("qbf%d" % i, [128, T], BF16) for i in range(NB)]
        kbf = [sb("kbf%d" % i, [128, T], BF16) for i in range(NB)]
        vbf = [sb("vbf%d" % i, [128, T], BF16) for i in range(NB)]
        g32 = [sb("g32%d" % i, [128, T], F32) for i in range(NB)]
        qbf_b = [B("qbf") for i in range(NB)]
        kbf_b = [B("kbf") for i in range(NB)]
        vbf_b = [B("vbf") for i in range(NB)]
        g32_b = [B("g32") for i in range(NB)]
        vtm = sb("vtm", [128, NT, 128], BF16)
        vtm_b = B("vtm")
        sg = sb("sg", [128, T], BF16)
        sg_b = B("sg")
        L = sb("L", [128, NT, 128], F32)
        L_b = B("L")
        EG = sb("EG", [128, T], BF16)
        EG_b = B("EG")
        EGi = sb("EGi", [128, T], BF16)
        EGi_b = B("EGi")
        EKT = sb("EKT", [128, NT, 128], BF16)
        EKT_b = B("EKT")
        dcl = sb("dcl", [128, 2, NT], F32)
        dcl_b = [B("dcl0"), B("dcl1")]
        qd = [sb("qd%d" % d, [128, T], BF16) for d in range(2)]
        qd_b = [B("qd") for d in range(2)]
        ki = [sb("ki%d" % d, [128, T], BF16) for d in range(2)]
        ki_b = [B("ki") for d in range(2)]
        kt = [sb("kt%d" % d, [128, NT, 128], BF16) for d in range(2)]
        kt_b = [B("kt") for d in range(2)]
        CS = sb("CS", [128, NT, 128], F32)
        CS_b = B("CS")
        Sst = [sb("Sst%d" % d, [128, NT, 128], F32) for d in range(2)]
        Sst_b = [B("Sst") for d in range(2)]
        Sbf = [sb("Sbf%d" % d, [128, NT, 128], BF16) for d in range(2)]
        Sbf_b = [B("Sbf") for d in range(2)]
        tE = [sb("tE%d" % i, [128, 512], F32) for i in range(2)]
        tE_b = [B("tE") for i in range(2)]
        sc1 = sb("sc1", [128, 512], F32)
        sc1_b = B("sc1")
        sc2 = sb("sc2", [128, 512], F32)
        sc2_b = B("sc2")
        scT = sb("scT", [128, 512], BF16)
        scT_b = B("scT")
        sq = sb("sq", [128, 512], F32)
        sq_b = B("sq")
        rs = sb("rs", [128, 512], F32)
        rs_b = B("rs")
        tt = sb("tt", [128, 512], F32)
        tt_b = B("tt")
        yb = [sb("yb%d" % i, [128, T], BF16) for i in range(2)]
        yb_b = [B("yb") for i in range(2)]

        ident_bf, ident_bf_b = C.ident_bf, C.ident_bf_b
        TRI = [cst[:, 0:128], cst[:, 128:256]]
        STRI = [cst[:, 256:384], cst[:, 384:512]]
        ONES = cst[:, 512:640]
        MASK = [cst[:, 640:1152], cst[:, 1152:1664]]
        pc = [0]
        tec = [0]

        def bank():
            pi = 4 + (pc[0] % 4)
            pc[0] += 1
            return pi

        def load_head(h):
            i = h % NB
            S.dma("pool", qbf[i][:, :], scr[C_GQ + h * 128:C_GQ + (h + 1) * 128, :], writes=[qbf_b[i]], max_dma_last_dim=4096)
            S.dma("pool", kbf[i][:, :], scr[C_GK + h * 128:C_GK + (h + 1) * 128, :], writes=[kbf_b[i]], max_dma_last_dim=4096)
            S.dma("pool", vbf[i][:, :], scr[C_GV + h * 128:C_GV + (h + 1) * 128, :], writes=[vbf_b[i]], max_dma_last_dim=4096)
            S.dma("sp", g32[i][:, :], scr[C_GG + h * 128:C_GG + (h + 1) * 128, :], writes=[g32_b[i]])

        load_head(0)
        for h in range(nheads):
            i = h % NB
            if h + 1 < nheads:
                load_head(h + 1)
            for g in range(4):
                pi = bank()
                pt = psum[pi].bitcast(BF16)
                for q in range(4):
                    c = 4 * g + q
                    S.op("pe", lambda e, pt=pt, q=q, c=c, i=i: e.transpose(
                        out=pt[:, q * 128:(q + 1) * 128], in_=vbf[i][:, c * 128:(c + 1) * 128], identity=ident_bf[:, :]),
                        reads=[vbf_b[i], ident_bf_b], writes=[psum_b[pi][q]], skip_same=True)
                S.op("act", lambda e, pt=pt, g=g: e.activation(
                    out=vtm[:, 4 * g:4 * g + 4, :], in_=pt[:, 0:512].rearrange("p (a b) -> p a b", a=4), func=AF.Copy),
                    reads=psum_b[pi], writes=[vtm_b])
            for blk in range(4):
                sl = slice(blk * 512, (blk + 1) * 512)
                j = tec[0] % 2
                tec[0] += 1
                S.op("act", lambda e, j=j, sl=sl, i=i: e.activation(out=tE[j][:, :], in_=g32[i][:, sl], func=AF.Exp, scale=-1.0),
                     reads=[g32_b[i]], writes=[tE_b[j]])
                S.op("act", lambda e, j=j: e.activation(out=tE[j][:, :], in_=tE[j][:, :], func=AF.Ln, bias=1.0),
                     reads=[tE_b[j]], writes=[tE_b[j]])
                S.op("act", lambda e, j=j: e.activation(out=tE[j][:, :], in_=tE[j][:, :], func=AF.Exp, scale=-1.0),
                     reads=[tE_b[j]], writes=[tE_b[j]])
                S.op("dve", lambda e, j=j, sl=sl, i=i: e.tensor_tensor(out=sg[:, sl], in0=g32[i][:, sl], in1=tE[j][:, :], op=ALU.mult),
                     reads=[g32_b[i], tE_b[j]], writes=[sg_b])
            for d in range(2):
                p0 = 32 * d
                for g in range(4):
                    pi = bank()
                    for q in range(4):
                        c = 4 * g + q
                        S.op("pe", lambda e, pi=pi, q=q, c=c, p0=p0, h=h: e.matmul(
                            out=psum[pi][:, q * 128:(q + 1) * 128], lhsT=lr[p0:p0 + 17, c * 128:(c + 1) * 128],
                            rhs=dw[p0:p0 + 17, h * 128:(h + 1) * 128], start=True, stop=True),
                            reads=[lr_b, dw_b], writes=[psum_b[pi][q]], skip_same=True)
                    j = tec[0] % 2
                    tec[0] += 1
                    S.op("act", lambda e, j=j, pi=pi: e.activation(out=tE[j][:, :], in_=psum[pi][:, :], func=AF.Exp, scale=-1.0),
                         reads=psum_b[pi], writes=[tE_b[j]])
                    S.op("act", lambda e, j=j, g=g: e.activation(
                        out=L[:, 4 * g:4 * g + 4, :], in_=tE[j][:, :].rearrange("p (a b) -> p a b", a=4), func=AF.Ln, bias=1.0),
                        reads=[tE_b[j]], writes=[L_b])
                for g in range(4):
                    pi = bank()
                    sl = slice(g * 512, (g + 1) * 512)
                    for q in range(4):
                        c = 4 * g + q
                        S.op("pe", lambda e, pi=pi, q=q, c=c, d=d: e.matmul(
                            out=psum[pi][:, q * 128:(q + 1) * 128], lhsT=L[:, c, :], rhs=TRI[d], start=True, stop=True),
                            reads=[L_b, cst_b], writes=[psum_b[pi][q]], skip_same=True)
                    S.op("act", lambda e, pi=pi, sl=sl: e.activation(out=EG[:, sl], in_=psum[pi][:, :], func=AF.Exp),
                         reads=psum_b[pi], writes=[EG_b])
                    S.op("act", lambda e, pi=pi, sl=sl: e.activation(out=EGi[:, sl], in_=psum[pi][:, :], func=AF.Exp, scale=-1.0),
                         reads=psum_b[pi], writes=[EGi_b])
                    col = 127 if d == 0 else 0
                    S.op("act", lambda e, pi=pi, g=g, d=d, col=col: e.activation(
                        out=dcl[:, d, 4 * g:4 * g + 4], in_=psum[pi][:, :].rearrange("p (a b) -> p a b", a=4)[:, :, col], func=AF.Exp),
                        reads=psum_b[pi], writes=[dcl_b[d]])
                for g in range(4):
                    pi = bank()
                    for q in range(4):
                        c = 4 * g + q
                        S.op("pe", lambda e, pi=pi, q=q, c=c, d=d: e.matmul(
                            out=psum[pi][:, q * 128:(q + 1) * 128], lhsT=STRI[d], rhs=L[:, c, :], start=True, stop=True),
                            reads=[L_b, cst_b], writes=[psum_b[pi][q]], skip_same=True)
                    S.op("act", lambda e, pi=pi, g=g: e.activation(
                        out=EKT[:, 4 * g:4 * g + 4, :], in_=psum[pi][:, :].rearrange("p (a b) -> p a b", a=4), func=AF.Exp),
                        reads=psum_b[pi], writes=[EKT_b])
                S.op("dve", lambda e, d=d, i=i: e.scalar_tensor_tensor(
                    out=qd[d][:, :], in0=qbf[i][:, :], scalar=float(128 ** -0.5), in1=EG[:, :], op0=ALU.mult, op1=ALU.mult),
                    reads=[qbf_b[i], EG_b], writes=[qd_b[d]])
                S.op("dve", lambda e, d=d, i=i: e.tensor_tensor(out=ki[d][:, :], in0=kbf[i][:, :], in1=EGi[:, :], op=ALU.mult),
                     reads=[kbf_b[i], EGi_b], writes=[ki_b[d]])
                for g in range(4):
                    pi = bank()
                    pt = psum[pi].bitcast(BF16)
                    for q in range(4):
                        c = 4 * g + q
                        S.op("pe", lambda e, pt=pt, q=q, c=c, i=i: e.transpose(
                            out=pt[:, q * 128:(q + 1) * 128], in_=kbf[i][:, c * 128:(c + 1) * 128], identity=ident_bf[:, :]),
                            reads=[kbf_b[i], ident_bf_b], writes=[psum_b[pi][q]], skip_same=True)
                    S.op("dve", lambda e, pt=pt, g=g, d=d: e.tensor_tensor(
                        out=kt[d][:, 4 * g:4 * g + 4, :], in0=pt[:, 0:512].rearrange("p (a b) -> p a b", a=4),
                        in1=EKT[:, 4 * g:4 * g + 4, :], op=ALU.mult),
                        reads=psum_b[pi] + [EKT_b], writes=[kt_b[d]])
                for g in range(4):
                    pi = bank()
                    for q in range(4):
                        c = 4 * g + q
                        S.op("pe", lambda e, pi=pi, q=q, c=c, d=d: e.matmul(
                            out=psum[pi][:, q * 128:(q + 1) * 128], lhsT=kt[d][:, c, :], rhs=vtm[:, c, :], start=True, stop=True),
                            reads=[kt_b[d], vtm_b], writes=[psum_b[pi][q]], skip_same=True)
                    S.op("act", lambda e, pi=pi, g=g: e.activation(
                        out=CS[:, 4 * g:4 * g + 4, :], in_=psum[pi][:, :].rearrange("p (a b) -> p a b", a=4), func=AF.Copy),
                        reads=psum_b[pi], writes=[CS_b])
                if d == 0:
                    S.op("pool", lambda e: e.memset(Sst[0][:, 0, :], 0.0), writes=[Sst_b[0]])
                    for c in range(1, NT):
                        S.op("dve", lambda e, c=c: e.scalar_tensor_tensor(
                            out=Sst[0][:, c, :], in0=Sst[0][:, c - 1, :], scalar=dcl[:, 0, c - 1:c], in1=CS[:, c - 1, :],
                            op0=ALU.mult, op1=ALU.add),
                            reads=[Sst_b[0], dcl_b[0], CS_b], writes=[Sst_b[0]])
                else:
                    S.op("pool", lambda e: e.memset(Sst[1][:, NT - 1, :], 0.0), writes=[Sst_b[1]])
                    for c in range(NT - 2, -1, -1):
                        S.op("dve", lambda e, c=c: e.scalar_tensor_tensor(
                            out=Sst[1][:, c, :], in0=Sst[1][:, c + 1, :], scalar=dcl[:, 1, c + 1:c + 2], in1=CS[:, c + 1, :],
                            op0=ALU.mult, op1=ALU.add),
                            reads=[Sst_b[1], dcl_b[1], CS_b], writes=[Sst_b[1]])
                S.op("act", lambda e, d=d: e.activation(out=Sbf[d][:, :, :], in_=Sst[d][:, :, :], func=AF.Copy),
                     reads=[Sst_b[d]], writes=[Sbf_b[d]])
            yi = h % 2
            for g in range(4):
                sl = slice(g * 512, (g + 1) * 512)
                pA, pB, pO, pN = 4, 5, 6, 7
                for q in range(4):
                    c = 4 * g + q
                    cs_ = slice(c * 128, (c + 1) * 128)
                    S.op("pe", lambda e, q=q, cs_=cs_: e.matmul(
                        out=psum[pA][:, q * 128:(q + 1) * 128], lhsT=ki[0][:, cs_], rhs=qd[0][:, cs_], start=True, stop=True),
                        reads=[ki_b[0], qd_b[0]], writes=[psum_b[pA][q]], skip_same=True)
                    S.op("pe", lambda e, q=q, cs_=cs_: e.matmul(
                        out=psum[pB][:, q * 128:(q + 1) * 128], lhsT=ki[1][:, cs_], rhs=qd[1][:, cs_], start=True, stop=True),
                        reads=[ki_b[1], qd_b[1]], writes=[psum_b[pB][q]], skip_same=True)
                S.op("dve", lambda e: e.tensor_tensor(out=sc1[:, :], in0=psum[pA][:, :], in1=MASK[0], op=ALU.mult),
                     reads=psum_b[pA] + [cst_b], writes=[sc1_b])
                S.op("dve", lambda e: e.tensor_tensor(out=sc2[:, :], in0=psum[pB][:, :], in1=MASK[1], op=ALU.mult),
                     reads=psum_b[pB] + [cst_b], writes=[sc2_b])
                S.op("pool", lambda e: e.tensor_tensor(out=scT[:, :], in0=sc1[:, :], in1=sc2[:, :], op=ALU.add),
                     reads=[sc1_b, sc2_b], writes=[scT_b])
                for q in range(4):
                    c = 4 * g + q
                    cs_ = slice(c * 128, (c + 1) * 128)
                    osl = slice(q * 128, (q + 1) * 128)
                    last_f = (c == 0)
                    has_f = c >= 1
                    has_b = c <= NT - 2
                    S.op("pe", lambda e, osl=osl, c=c, has_f=has_f, has_b=has_b: e.matmul(
                        out=psum[pO][:, osl], lhsT=vtm[:, c, :], rhs=scT[:, osl], start=True, stop=not (has_f or has_b)),
                        reads=[vtm_b, scT_b], writes=[psum_b[pO][q]], skip_same=True)
                    if has_f:
                        S.op("pe", lambda e, osl=osl, c=c, cs_=cs_, has_b=has_b: e.matmul(
                            out=psum[pO][:, osl], lhsT=Sbf[0][:, c, :], rhs=qd[0][:, cs_], start=False, stop=not has_b),
                            reads=[Sbf_b[0], qd_b[0]], writes=[psum_b[pO][q]], skip_same=True)
                    if has_b:
                        S.op("pe", lambda e, osl=osl, c=c, cs_=cs_: e.matmul(
                            out=psum[pO][:, osl], lhsT=Sbf[1][:, c, :], rhs=qd[1][:, cs_], start=False, stop=True),
                            reads=[Sbf_b[1], qd_b[1]], writes=[psum_b[pO][q]], skip_same=True)
                S.op("act", lambda e: e.activation(out=sq[:, :], in_=psum[pO][:, :], func=AF.Square),
                     reads=psum_b[pO], writes=[sq_b])
                S.op("pe", lambda e: e.matmul(out=psum[pN][:, :], lhsT=ONES, rhs=sq[:, :], start=True, stop=True),
                     reads=[sq_b, cst_b], writes=psum_b[pN], skip_same=True)
                S.op("act", lambda e: e.activation(out=rs[:, :], in_=psum[pN][:, :], func=AF.Ln, scale=1.0 / 128, bias=EPS),
                     reads=psum_b[pN], writes=[rs_b])
                S.op("act", lambda e: e.activation(out=rs[:, :], in_=rs[:, :], func=AF.Exp, scale=-0.5),
                     reads=[rs_b], writes=[rs_b])
                S.op("dve", lambda e: e.scalar_tensor_tensor(
                    out=tt[:, :], in0=psum[pO][:, :], scalar=gn[:, 0:1], in1=rs[:, :], op0=ALU.mult, op1=ALU.mult),
                    reads=psum_b[pO] + [gn_b, rs_b], writes=[tt_b])
                S.op("dve", lambda e, sl=sl, yi=yi: e.tensor_tensor(out=yb[yi][:, sl], in0=tt[:, :], in1=sg[:, sl], op=ALU.mult),
                     reads=[tt_b, sg_b], writes=[yb_b[yi]])
            S.dma("sp", C.y_scr[h * 128:(h + 1) * 128, :], yb[yi][:, :], reads=[yb_b[yi]])


NCST = 1664

def phase_gdn(C):
    nc, S, debug = C.nc, C.S, C.debug
    psum, psum_b = C.psum, C.psum_b
    cst, cst_b = C.cst, C.cst_b
    scr = C.scr
    nheads = debug.get("gdn_heads", [[8], None])[0][0]
    with contextlib.ExitStack() as pes:
        def sb(name, shape, dt):
            return pes.enter_context(nc.sbuf_tensor(name, list(shape), dt))

        def B(name):
            return Buf(name)

        ident_bf, ident_bf_b = C.ident_bf, C.ident_bf_b
        ONES = cst[:, 512:640]
        c2 = sb("c2", [128, NC2], F32)
        c2_b = B("c2")
        S.dma("sp", c2[:, :], C.cst2_d[:, :], writes=[c2_b])
        U = [c2[:, 0:128], c2[:, 128:256]]
        SU = [c2[:, 256:384], c2[:, 384:512]]
        PEN = [c2[:, 512:640], c2[:, 640:768]]
        IDF = c2[:, 768:896]
        cb = sb("cb", [128, NCB], BF16)
        cb_b = B("cb")
        S.dma("sp", cb[:, :], C.cstb_d[:, :], writes=[cb_b])

        def lmask(d, l):
            o = (d * 7 + (l - 1)) * 128
            return cb[:, o:o + 128]

        cw = sb("cw", [128, 24 * 3], F32)
        cw_b = B("cw")
        S.dma("sp", cw[:, :], C.gdn_cw_d[:, :], writes=[cw_b])
        gn = sb("gn2", [128, 1], F32)
        gn_b = B("gn2")
        S.dma("sp", gn[:, :], C.gdn_norm_g.rearrange("o d -> d o"), writes=[gn_b])
        hp = sb("hp", [16, 4], F32)
        hp_b = B("hp")
        S.dma("sp", hp[:, 0:2], C.gdn_hp_d[:, :], writes=[hp_b])

        gb48 = sb("gb48", [48, T], F32)
        gb48_b = B("gb48")
        S.op("pool", lambda e: e.memset(gb48[:, :], 0.0), writes=[gb48_b])
        S.dma("sp", gb48[0:16, :], scr[C_AB:C_AB + 16, :], writes=[gb48_b])
        S.dma("sp", gb48[32:48, :], scr[C_AB + 16:C_AB + 32, :], writes=[gb48_b])
        S.op("act", lambda e: e.activation(out=hp[:, 2:3], in_=hp[:, 0:1], func=AF.Exp), reads=[hp_b], writes=[hp_b])
        S.op("dve", lambda e: e.tensor_scalar(out=hp[:, 2:3], in0=hp[:, 2:3], scalar1=-1.0, scalar2=None, op0=ALU.mult),
             reads=[hp_b], writes=[hp_b])
        S.op("act", lambda e: e.activation(out=gb48[0:16, :], in_=gb48[0:16, :], func=AF.Exp, bias=hp[:, 1:2]),
             reads=[gb48_b, hp_b], writes=[gb48_b])
        S.op("act", lambda e: e.activation(out=gb48[0:16, :], in_=gb48[0:16, :], func=AF.Ln, bias=1.0),
             reads=[gb48_b], writes=[gb48_b])
        S.op("dve", lambda e: e.tensor_scalar(out=gb48[0:16, :], in0=gb48[0:16, :], scalar1=hp[:, 2:3], scalar2=None, op0=ALU.mult),
             reads=[gb48_b, hp_b], writes=[gb48_b])
        S.op("act", lambda e: e.activation(out=gb48[32:48, :], in_=gb48[32:48, :], func=AF.Exp, scale=-1.0),
             reads=[gb48_b], writes=[gb48_b])
        S.op("act", lambda e: e.activation(out=gb48[32:48, :], in_=gb48[32:48, :], func=AF.Ln, bias=1.0),
             reads=[gb48_b], writes=[gb48_b])
        S.op("act", lambda e: e.activation(out=gb48[32:48, :], in_=gb48[32:48, :], func=AF.Exp, scale=-1.0),
             reads=[gb48_b], writes=[gb48_b])
        gtm = sb("gtm", [128, NT, 48], F32)
        gtm_b = B("gtm")
        for g in range(4):
            pi = 2 + g
            for q in range(4):
                t = 4 * g + q
                S.op("pe", lambda e, pi=pi, q=q, t=t: e.transpose(
                    out=psum[pi][:, q * 48:(q + 1) * 48], in_=gb48[0:48, t * 128:(t + 1) * 128], identity=IDF[0:48, 0:48]),
                    reads=[gb48_b, c2_b], writes=[psum_b[pi][0]], skip_same=True)
            S.op("act", lambda e, pi=pi, g=g: e.activation(
                out=gtm[:, 4 * g:4 * g + 4, :], in_=psum[pi][:, 0:192].rearrange("p (a b) -> p a b", a=4), func=AF.Copy),
                reads=[psum_b[pi][0]], writes=[gtm_b])
        egtm = sb("egtm", [128, NT, 16], F32)
        egtm_b = B("egtm")
        ektm = sb("ektm", [128, NT, 16], F32)
        ektm_b = B("ektm")
        for (mats, dst, dst_b, pi) in ((U, egtm, egtm_b, 6), (SU, ektm, ektm_b, 7)):
            for t in range(NT):
                for d in range(2):
                    S.op("pe", lambda e, pi=pi, t=t, d=d, mats=mats: e.matmul(
                        out=psum[pi][:, t * 16 + d * 8:t * 16 + d * 8 + 8], lhsT=mats[d], rhs=gtm[:, t, d * 8:d * 8 + 8],
                        start=True, stop=True),
                        reads=[gtm_b, c2_b], writes=[psum_b[pi][0]], skip_same=True)
            S.op("act", lambda e, pi=pi, dst=dst: e.activation(
                out=dst[:, :, :], in_=psum[pi][:, 0:256].rearrange("p (a b) -> p a b", a=NT), func=AF.Exp),
                reads=[psum_b[pi][0]], writes=[dst_b])

        pin = [sb("pin%d" % i, [128, T + 2], F32) for i in range(2)]
        pin_b = [B("pin") for i in range(2)]
        for i in range(2):
            S.op("pool", lambda e, i=i: e.memset(pin[i][:, :], 0.0), writes=[pin_b[i]])
        c1 = sb("c1", [128, T], F32)
        c1_b = B("c1")
        tB = sb("tB", [128, T], F32)
        tB_b = B("tB")
        qT = sb("qT", [128, T], BF16)
        qT_b = B("qT")
        kT = sb("kT", [128, T], BF16)
        kT_b = B("kT")
        vT = sb("vT", [128, T], BF16)
        vT_b = B("vT")
        sz = sb("sz", [128, T], BF16)
        sz_b = B("sz")
        ktm = sb("ktm", [128, NT, 128], BF16)
        ktm_b = B("ktm")
        vtm = sb("vtm2", [128, NT, 128], BF16)
        vtm_b = B("vtm2")
        kg = [sb("kg%d" % d, [128, NT, 128], BF16) for d in range(2)]
        kg_b = [B("kg") for d in range(2)]
        ktl = [sb("ktl%d" % d, [128, NT, 128], BF16) for d in range(2)]
        ktl_b = [B("ktl") for d in range(2)]
        qdT = [sb("qdT%d" % d, [128, T], BF16) for d in range(2)]
        qdT_b = [B("qdT") for d in range(2)]
        VT = [sb("VT%d" % d, [128, T], BF16) for d in range(2)]
        VT_b = [B("VT") for d in range(2)]
        atT = [sb("atT%d" % d, [128, T], BF16) for d in range(2)]
        atT_b = [B("atT") for d in range(2)]
        wpn = [sb("wpn%d" % d, [128, T], BF16) for d in range(2)]
        wpn_b = [B("wpn") for d in range(2)]
        dcl = sb("dcl2", [128, 2, NT], F32)
        dcl_b = [B("dcl20"), B("dcl21")]
        gU = sb("gU", [128, NT, 128], F32)
        gU_b = B("gU")
        gneg = sb("gneg", [128, NT, 128], F32)
        gneg_b = B("gneg")
        DT = sb("DT", [128, 512], F32)
        DT_b = B("DT")
        DTb = sb("DTb", [128, 512], F32)
        DTb_b = B("DTb")
        eb = sb("eb", [128, 512], F32)
        eb_b = B("eb")
        NTm = sb("NTm", [128, 512], BF16)
        NTm_b = B("NTm")
        MT = sb("MT", [128, 512], BF16)
        MT_b = B("MT")
        Xm = sb("Xm", [128, 512], BF16)
        Xm_b = B("Xm")
        Ym = sb("Ym", [128, 512], BF16)
        Ym_b = B("Ym")
        Pm = sb("Pm", [128, 512], BF16)
        Pm_b = B("Pm")
        S32 = [sb("S32_%d" % d, [128, 128], F32) for d in range(2)]
        S32_b = [B("S32") for d in range(2)]
        Sbf = [sb("Sbf2_%d" % d, [128, 128], BF16) for d in range(2)]
        Sbf_b = [B("Sbf2") for d in range(2)]
        vnb = [sb("vnb%d" % d, [128, 128], BF16) for d in range(2)]
        vnb_b = [B("vnb") for d in range(2)]
        oacc = [sb("oacc%d" % d, [128, T], F32) for d in range(2)]
        oacc_b = [B("oacc") for d in range(2)]
        sq = sb("sq2", [128, 512], F32)
        sq_b = B("sq2")
        rs = sb("rs2", [128, 512], F32)
        rs_b = B("rs2")
        tt = sb("tt2", [128, 512], F32)
        tt_b = B("tt2")
        yb = [sb("yb2_%d" % i, [128, T], BF16) for i in range(2)]
        yb_b = [B("yb2") for i in range(2)]

        pc = [0]

        def bank():
            pi = pc[0] % 8
            pc[0] += 1
            return pi

        pinc = [0]

        def silu_inplace(x, x_b, n=T):
            S.op("act", lambda e: e.activation(out=tB[:, 0:n], in_=x[:, 0:n], func=AF.Exp, scale=-1.0), reads=[x_b], writes=[tB_b])
            S.op("act", lambda e: e.activation(out=tB[:, 0:n], in_=tB[:, 0:n], func=AF.Ln, bias=1.0), reads=[tB_b], writes=[tB_b])
            S.op("act", lambda e: e.activation(out=tB[:, 0:n], in_=tB[:, 0:n], func=AF.Exp, scale=-1.0), reads=[tB_b], writes=[tB_b])
            S.op("dve", lambda e: e.tensor_tensor(out=x[:, 0:n], in0=x[:, 0:n], in1=tB[:, 0:n], op=ALU.mult),
                 reads=[x_b, tB_b], writes=[x_b])

        def conv_silu(row0, blk):
            i = pinc[0] % 2
            pinc[0] += 1
            S.dma("sp", pin[i][:, 1:T + 1], scr[row0:row0 + 128, :], writes=[pin_b[i]])
            w0 = cw[:, blk * 3:blk * 3 + 1]
            w1 = cw[:, blk * 3 + 1:blk * 3 + 2]
            w2 = cw[:, blk * 3 + 2:blk * 3 + 3]
            S.op("act", lambda e, i=i, w0=w0: e.activation(out=c1[:, :], in_=pin[i][:, 0:T], func=AF.Copy, scale=w0),
                 reads=[pin_b[i], cw_b], writes=[c1_b])
            S.op("dve", lambda e, i=i, w1=w1: e.scalar_tensor_tensor(
                out=c1[:, :], in0=pin[i][:, 1:T + 1], scalar=w1, in1=c1[:, :], op0=ALU.mult, op1=ALU.add),
                reads=[pin_b[i], cw_b, c1_b], writes=[c1_b])
            S.op("dve", lambda e, i=i, w2=w2: e.scalar_tensor_tensor(
                out=c1[:, :], in0=pin[i][:, 2:T + 2], scalar=w2, in1=c1[:, :], op0=ALU.mult, op1=ALU.add),
                reads=[pin_b[i], cw_b, c1_b], writes=[c1_b])
            silu_inplace(c1, c1_b)

        def l2norm_to(dst, dst_b, scale):
            S.op("act", lambda e: e.activation(out=tB[:, :], in_=c1[:, :], func=AF.Square), reads=[c1_b], writes=[tB_b])
            for blk in range(4):
                sl = slice(blk * 512, (blk + 1) * 512)
                pi = bank()
                S.op("pe", lambda e, pi=pi, sl=sl: e.matmul(out=psum[pi][:, :], lhsT=ONES, rhs=tB[:, sl], start=True, stop=True),
                     reads=[tB_b, cst_b], writes=psum_b[pi], skip_same=True)
                S.op("act", lambda e, pi=pi: e.activation(out=rs[:, :], in_=psum[pi][:, :], func=AF.Ln, bias=EPS),
                     reads=psum_b[pi], writes=[rs_b])
                S.op("act", lambda e: e.activation(out=rs[:, :], in_=rs[:, :], func=AF.Exp, scale=-0.5), reads=[rs_b], writes=[rs_b])
                S.op("dve", lambda e, sl=sl: e.scalar_tensor_tensor(
                    out=dst[:, sl], in0=c1[:, sl], scalar=float(scale), in1=rs[:, :], op0=ALU.mult, op1=ALU.mult),
                    reads=[c1_b, rs_b], writes=[dst_b])

        def to_token_major(src, src_b, dst, dst_b):
            for g in range(4):
                pi = bank()
                pt = psum[pi].bitcast(BF16)
                for q in range(4):
                    c = 4 * g + q
                    S.op("pe", lambda e, pt=pt, q=q, c=c: e.transpose(
                        out=pt[:, q * 128:(q + 1) * 128], in_=src[:, c * 128:(c + 1) * 128], identity=ident_bf[:, :]),
                        reads=[src_b, ident_bf_b], writes=[psum_b[pi][q]], skip_same=True)
                S.op("act", lambda e, pt=pt, g=g: e.activation(
                    out=dst[:, 4 * g:4 * g + 4, :], in_=pt[:, 0:512].rearrange("p (a b) -> p a b", a=4), func=AF.Copy),
                    reads=psum_b[pi], writes=[dst_b])

        def bc4(ap2d):
            return ap2d.unsqueeze(1).broadcast_to([128, 4, 128])

        def r4(ap):
            return ap.rearrange("p (a b) -> p a b", a=4)

        for h in range(nheads):
            conv_silu(C_DQ + h * 128, h)
            l2norm_to(qT, qT_b, 128 ** -0.5)
            conv_silu(C_DK + h * 128, 8 + h)
            l2norm_to(kT, kT_b, 1.0)
            conv_silu(C_DV + h * 128, 16 + h)
            S.op("act", lambda e: e.activation(out=vT[:, :], in_=c1[:, :], func=AF.Copy), reads=[c1_b], writes=[vT_b])
            to_token_major(kT, kT_b, ktm, ktm_b)
            to_token_major(vT, vT_b, vtm, vtm_b)
            S.dma("sp", c1[:, :], scr[C_DZ + h * 128:C_DZ + (h + 1) * 128, :], writes=[c1_b])
            silu_inplace(c1, c1_b)
            S.op("act", lambda e: e.activation(out=sz[:, :], in_=c1[:, :], func=AF.Copy), reads=[c1_b], writes=[sz_b])
            if "gdn_qkv" in C.dbg_out and h == 0:
                S.dma("sp", C.dbg_out["gdn_qkv"][0:128, :], qT[:, :], reads=[qT_b])
                S.dma("sp", C.dbg_out["gdn_qkv"][128:256, :], kT[:, :], reads=[kT_b])
                S.dma("sp", C.dbg_out["gdn_qkv"][256:384, :], vT[:, :], reads=[vT_b])

            for d in range(2):
                col = d * 8 + h
                gcol = gtm[:, :, col:col + 1]
                bcol = gtm[:, :, 32 + col:32 + col + 1]
                S.op("dve", lambda e, d=d, col=col: e.tensor_tensor(
                    out=kg[d][:, :, :], in0=ktm[:, :, :], in1=egtm[:, :, col:col + 1].broadcast_to([128, NT, 128]), op=ALU.mult),
                    reads=[ktm_b, egtm_b], writes=[kg_b[d]])
                S.op("dve", lambda e, d=d, col=col: e.tensor_tensor(
                    out=ktl[d][:, :, :], in0=ktm[:, :, :], in1=ektm[:, :, col:col + 1].broadcast_to([128, NT, 128]), op=ALU.mult),
                    reads=[ktm_b, ektm_b], writes=[ktl_b[d]])
                S.op("pool", lambda e, d=d, gcol=gcol: e.tensor_tensor(
                    out=gU[:, :, :], in0=U[d].unsqueeze(1).broadcast_to([128, NT, 128]), in1=gcol.broadcast_to([128, NT, 128]), op=ALU.mult),
                    reads=[c2_b, gtm_b], writes=[gU_b])
                S.op("pool", lambda e, gcol=gcol: e.tensor_scalar(
                    out=gneg[:, :, :], in0=gcol.broadcast_to([128, NT, 128]), scalar1=-1.0, scalar2=None, op0=ALU.mult),
                    reads=[gtm_b], writes=[gneg_b])
                last = 127 if d == 0 else 0
                for g in range(4):
                    sl = slice(g * 512, (g + 1) * 512)
                    pA = bank()
                    for q in range(4):
                        c = 4 * g + q
                        osl = slice(q * 128, (q + 1) * 128)
                        S.op("pe", lambda e, pA=pA, osl=osl, c=c: e.matmul(
                            out=psum[pA][:, osl], lhsT=ONES, rhs=gU[:, c, :], start=True, stop=False),
                            reads=[gU_b, cst_b], writes=[psum_b[pA][q]], skip_same=True)
                        S.op("pe", lambda e, pA=pA, osl=osl, c=c, d=d: e.matmul(
                            out=psum[pA][:, osl], lhsT=U[d], rhs=gneg[:, c, :], start=False, stop=False),
                            reads=[gneg_b, c2_b], writes=[psum_b[pA][q]], skip_same=True)
                        S.op("pe", lambda e, pA=pA, osl=osl, d=d: e.matmul(
                            out=psum[pA][:, osl], lhsT=IDF, rhs=PEN[d], start=False, stop=True),
                            reads=[c2_b], writes=[psum_b[pA][q]], skip_same=True)
                    S.op("act", lambda e, pA=pA: e.activation(out=DT[:, :], in_=psum[pA][:, :], func=AF.Exp),
                         reads=psum_b[pA], writes=[DT_b])
                    pB = bank()
                    for q in range(4):
                        c = 4 * g + q
                        osl = slice(q * 128, (q + 1) * 128)
                        S.op("pe", lambda e, pB=pB, osl=osl, c=c, d=d: e.matmul(
                            out=psum[pB][:, osl], lhsT=gneg[:, c, :], rhs=U[d], start=True, stop=True),
                            reads=[gneg_b, c2_b], writes=[psum_b[pB][q]], skip_same=True)
                    S.op("act", lambda e, pB=pB: e.activation(out=eb[:, :], in_=psum[pB][:, :], func=AF.Exp, scale=-1.0),
                         reads=psum_b[pB], writes=[eb_b])
                    S.op("act", lambda e, pB=pB, g=g, d=d, last=last: e.activation(
                        out=dcl[:, d, 4 * g:4 * g + 4], in_=r4(psum[pB][:, :])[:, :, last], func=AF.Exp, scale=-1.0),
                        reads=psum_b[pB], writes=[dcl_b[d]])
                    S.op("dve", lambda e, d=d, sl=sl: e.tensor_tensor(out=qdT[d][:, sl], in0=qT[:, sl], in1=eb[:, :], op=ALU.mult),
                         reads=[qT_b, eb_b], writes=[qdT_b[d]])
                    S.op("dve", lambda e, g=g, bcol=bcol: e.tensor_tensor(
                        out=r4(DTb[:, :]), in0=r4(DT[:, :]), in1=bcol[:, 4 * g:4 * g + 4, :].broadcast_to([128, 4, 128]), op=ALU.mult),
                        reads=[DT_b, gtm_b], writes=[DTb_b])
                    pC = bank()
                    for q in range(4):
                        c = 4 * g + q
                        cs_ = slice(c * 128, (c + 1) * 128)
                        S.op("pe", lambda e, pC=pC, q=q, cs_=cs_: e.matmul(
                            out=psum[pC][:, q * 128:(q + 1) * 128], lhsT=kT[:, cs_], rhs=kT[:, cs_], start=True, stop=True),
                            reads=[kT_b], writes=[psum_b[pC][q]], skip_same=True)
                    S.op("dve", lambda e, pC=pC: e.tensor_tensor(out=NTm[:, :], in0=psum[pC][:, :], in1=DTb[:, :], op=ALU.mult),
                         reads=psum_b[pC] + [DTb_b], writes=[NTm_b])
                    pD = bank()
                    for q in range(4):
                        c = 4 * g + q
                        cs_ = slice(c * 128, (c + 1) * 128)
                        S.op("pe", lambda e, pD=pD, q=q, cs_=cs_: e.matmul(
                            out=psum[pD][:, q * 128:(q + 1) * 128], lhsT=kT[:, cs_], rhs=qT[:, cs_], start=True, stop=True),
                            reads=[kT_b, qT_b], writes=[psum_b[pD][q]], skip_same=True)
                    S.op("dve", lambda e, pD=pD, d=d, sl=sl: e.tensor_tensor(out=atT[d][:, sl], in0=psum[pD][:, :], in1=DT[:, :], op=ALU.mult),
                         reads=psum_b[pD] + [DT_b], writes=[atT_b[d]])
                    S.op("pool", lambda e, d=d: e.tensor_tensor(out=r4(MT[:, :]), in0=r4(NTm[:, :]), in1=bc4(lmask(d, 1)), op=ALU.mult),
                         reads=[NTm_b, cb_b], writes=[MT_b])
                    S.op("pool", lambda e: e.tensor_tensor(out=r4(Ym[:, :]), in0=bc4(ident_bf[:, :]), in1=r4(MT[:, :]), op=ALU.subtract),
                         reads=[MT_b, ident_bf_b], writes=[Ym_b])
                    pE = bank()
                    ptE = psum[pE].bitcast(BF16)
                    for q in range(4):
                        osl = slice(q * 128, (q + 1) * 128)
                        S.op("pe", lambda e, ptE=ptE, osl=osl: e.transpose(out=ptE[:, osl], in_=Ym[:, osl], identity=ident_bf[:, :]),
                             reads=[Ym_b, ident_bf_b], writes=[psum_b[pE][q]], skip_same=True)
                    S.op("act", lambda e, ptE=ptE: e.activation(out=Xm[:, :], in_=ptE[:, 0:512], func=AF.Copy),
                         reads=psum_b[pE], writes=[Xm_b])
                    for l in range(2, 8):
                        S.op("pool", lambda e, d=d, l=l: e.tensor_tensor(out=r4(MT[:, :]), in0=r4(NTm[:, :]), in1=bc4(lmask(d, l)), op=ALU.mult),
                             reads=[NTm_b, cb_b], writes=[MT_b])
                        pF = bank()
                        for q in range(4):
                            osl = slice(q * 128, (q + 1) * 128)
                            S.op("pe", lambda e, pF=pF, osl=osl: e.matmul(out=psum[pF][:, osl], lhsT=MT[:, osl], rhs=Xm[:, osl], start=True, stop=True),
                                 reads=[MT_b, Xm_b], writes=psum_b[pF], skip_same=True)
                        S.op("act", lambda e, pF=pF: e.activation(out=Pm[:, :], in_=psum[pF][:, :], func=AF.Copy),
                             reads=psum_b[pF], writes=[Pm_b])
                        pG = bank()
                        for q in range(4):
                            osl = slice(q * 128, (q + 1) * 128)
                            S.op("pe", lambda e, pG=pG, osl=osl: e.matmul(out=psum[pG][:, osl], lhsT=Ym[:, osl], rhs=Pm[:, osl], start=True, stop=True),
                                 reads=[Ym_b, Pm_b], writes=psum_b[pG], skip_same=True)
                        S.op("dve", lambda e, pG=pG: e.tensor_tensor(out=Xm[:, :], in0=Xm[:, :], in1=psum[pG][:, :], op=ALU.subtract),
                             reads=psum_b[pG] + [Xm_b], writes=[Xm_b])
                        pE = bank()
                        ptE = psum[pE].bitcast(BF16)
                        for q in range(4):
                            osl = slice(q * 128, (q + 1) * 128)
                            S.op("pe", lambda e, ptE=ptE, osl=osl: e.transpose(out=ptE[:, osl], in_=Xm[:, osl], identity=ident_bf[:, :]),
                                 reads=[Xm_b, ident_bf_b], writes=[psum_b[pE][q]], skip_same=True)
                        if l < 7:
                            S.op("act", lambda e, ptE=ptE: e.activation(out=Ym[:, :], in_=ptE[:, 0:512], func=AF.Copy),
                                 reads=psum_b[pE], writes=[Ym_b])
                        else:
                            S.op("act", lambda e, ptE=ptE, d=d, sl=sl: e.activation(out=VT[d][:, sl], in_=ptE[:, 0:512], func=AF.Copy),
                                 reads=psum_b[pE], writes=[VT_b[d]])
                    pH = bank()
                    for q in range(4):
                        c = 4 * g + q
                        cs_ = slice(c * 128, (c + 1) * 128)
                        S.op("pe", lambda e, pH=pH, q=q, c=c, cs_=cs_, d=d: e.matmul(
                            out=psum[pH][:, q * 128:(q + 1) * 128], lhsT=kg[d][:, c, :], rhs=VT[d][:, cs_], start=True, stop=True),
                            reads=[kg_b[d], VT_b[d]], writes=[psum_b[pH][q]], skip_same=True)
                    S.op("act", lambda e, pH=pH, d=d, sl=sl: e.activation(out=wpn[d][:, sl], in_=psum[pH][:, :], func=AF.Copy, scale=-1.0),
                         reads=psum_b[pH], writes=[wpn_b[d]])

            for d in range(2):
                S.op("pool", lambda e, d=d: e.memset(S32[d][:, :], 0.0), writes=[S32_b[d]])
                S.op("pool", lambda e, d=d: e.memset(Sbf[d][:, :], 0.0), writes=[Sbf_b[d]])
            for s in range(NT):
                for d in range(2):
                    c = s if d == 0 else NT - 1 - s
                    cs_ = slice(c * 128, (c + 1) * 128)
                    col = d * 8 + h
                    pv, pS, pO = 3 * d, 3 * d + 1, 3 * d + 2
                    S.op("pe", lambda e, pv=pv, d=d, c=c, cs_=cs_: e.matmul(
                        out=psum[pv][:, 0:128], lhsT=VT[d][:, cs_], rhs=vtm[:, c, :], start=True, stop=False),
                        reads=[VT_b[d], vtm_b], writes=[psum_b[pv][0]], skip_same=True)
                    S.op("pe", lambda e, pv=pv, d=d, cs_=cs_: e.matmul(
                        out=psum[pv][:, 0:128], lhsT=wpn[d][:, cs_], rhs=Sbf[d][:, :], start=False, stop=True),
                        reads=[wpn_b[d], Sbf_b[d]], writes=[psum_b[pv][0]], skip_same=True)
                    S.op("act", lambda e, pv=pv, d=d, c=c, col=col: e.activation(
                        out=vnb[d][:, :], in_=psum[pv][:, 0:128], func=AF.Copy, scale=gtm[:, c, 32 + col:32 + col + 1]),
                        reads=[psum_b[pv][0], gtm_b], writes=[vnb_b[d]])
                    S.op("pe", lambda e, pS=pS, d=d, c=c: e.matmul(
                        out=psum[pS][:, 0:128], lhsT=ktl[d][:, c, :], rhs=vnb[d][:, :], start=True, stop=True),
                        reads=[ktl_b[d], vnb_b[d]], writes=[psum_b[pS][1]], skip_same=True)
                    S.op("pe", lambda e, pO=pO, d=d, cs_=cs_: e.matmul(
                        out=psum[pO][:, 0:128], lhsT=Sbf[d][:, :], rhs=qdT[d][:, cs_], start=True, stop=False),
                        reads=[Sbf_b[d], qdT_b[d]], writes=[psum_b[pO][2]], skip_same=True)
                    S.op("pe", lambda e, pO=pO, d=d, cs_=cs_: e.matmul(
                        out=psum[pO][:, 0:128], lhsT=vnb[d][:, :], rhs=atT[d][:, cs_], start=False, stop=True),
                        reads=[vnb_b[d], atT_b[d]], writes=[psum_b[pO][2]], skip_same=True)
                    S.op("dve", lambda e, pS=pS, d=d, c=c: e.scalar_tensor_tensor(
                        out=S32[d][:, :], in0=S32[d][:, :], scalar=dcl[:, d, c:c + 1], in1=psum[pS][:, 0:128],
                        op0=ALU.mult, op1=ALU.add),
                        reads=[S32_b[d], dcl_b[d], psum_b[pS][1]], writes=[S32_b[d]])
                    S.op("act", lambda e, d=d: e.activation(out=Sbf[d][:, :], in_=S32[d][:, :], func=AF.Copy),
                         reads=[S32_b[d]], writes=[Sbf_b[d]])
                    S.op("dve", lambda e, pO=pO, d=d, cs_=cs_: e.tensor_copy(out=oacc[d][:, cs_], in_=psum[pO][:, 0:128]),
                         reads=[psum_b[pO][2]], writes=[oacc_b[d]])

            yi = h % 2
            S.op("dve", lambda e: e.tensor_tensor(out=oacc[0][:, :], in0=oacc[0][:, :], in1=oacc[1][:, :], op=ALU.add),
                 reads=[oacc_b[0], oacc_b[1]], writes=[oacc_b[0]])
            if "gdn_o" in C.dbg_out and h == 0:
                S.dma("sp", C.dbg_out["gdn_o"][:, :], oacc[0][:, :], reads=[oacc_b[0]])
            for blk in range(4):
                sl = slice(blk * 512, (blk + 1) * 512)
                pN = bank()
                S.op("act", lambda e, sl=sl: e.activation(out=sq[:, :], in_=oacc[0][:, sl], func=AF.Square),
                     reads=[oacc_b[0]], writes=[sq_b])
                S.op("pe", lambda e, pN=pN: e.matmul(out=psum[pN][:, :], lhsT=ONES, rhs=sq[:, :], start=True, stop=True),
                     reads=[sq_b, cst_b], writes=psum_b[pN], skip_same=True)
                S.op("act", lambda e, pN=pN: e.activation(out=rs[:, :], in_=psum[pN][:, :], func=AF.Ln, scale=1.0 / 128, bias=EPS),
                     reads=psum_b[pN], writes=[rs_b])
                S.op("act", lambda e: e.activation(out=rs[:, :], in_=rs[:, :], func=AF.Exp, scale=-0.5), reads=[rs_b], writes=[rs_b])
                S.op("dve", lambda e, sl=sl: e.scalar_tensor_tensor(
                    out=tt[:, :], in0=oacc[0][:, sl], scalar=gn[:, 0:1], in1=rs[:, :], op0=ALU.mult, op1=ALU.mult),
                    reads=[oacc_b[0], gn_b, rs_b], writes=[tt_b])
                S.op("dve", lambda e, sl=sl, yi=yi: e.tensor_tensor(out=yb[yi][:, sl], in0=tt[:, :], in1=sz[:, sl], op=ALU.mult),
                     reads=[tt_b, sz_b], writes=[yb_b[yi]])
            S.dma("sp", C.y_scr[1024 + h * 128:1024 + (h + 1) * 128, :], yb[yi][:, :], reads=[yb_b[yi]])


def phase_branch(C):
    nc, S, debug = C.nc, C.S, C.debug
    psum, psum_b = C.psum, C.psum_b
    scr = C.scr
    ident_bf, ident_bf_b = C.ident_bf, C.ident_bf_b
    with contextlib.ExitStack() as oes:
        mergedT = oes.enter_context(nc.sbuf_tensor("mergedT", [128, KC, T], BF16))
        mg_b = [Buf("mg%d" % t) for t in range(NT)]
        with contextlib.ExitStack() as pes:
            def sb(name, shape, dt):
                return pes.enter_context(nc.sbuf_tensor(name, list(shape), dt))
            yT = sb("yT", [128, KC, T], BF16)
            yT_b = [Buf("yT%d" % k) for k in range(KC)]
            y_r = C.y_scr.rearrange("(kc p) t -> p kc t", p=128)
            for k in range(KC):
                S.dma("sp", yT[:, k, :], y_r[:, k, :], writes=[yT_b[k]])
            NWB = 3
            wg = [sb("wbg%d" % i, [128, 8, 128], BF16) for i in range(NWB)]
            wd = [sb("wbd%d" % i, [128, 8, 128], BF16) for i in range(NWB)]
            wg_b = [Buf("wbg") for i in range(NWB)]
            wd_b = [Buf("wbd") for i in range(NWB)]
            gg = [sb("gg%d" % i, [128, T], F32) for i in range(2)]
            gd = [sb("gd%d" % i, [128, T], F32) for i in range(2)]
            gg_b = [Buf("gg") for i in range(2)]
            gd_b = [Buf("gd") for i in range(2)]
            sgg = [sb("sgg%d" % i, [128, 512], F32) for i in range(2)]
            sgd = [sb("sgd%d" % i, [128, 512], F32) for i in range(2)]
            sgg_b = [Buf("sgg") for i in range(2)]
            sgd_b = [Buf("sgd") for i in range(2)]
            t1 = [sb("t1_%d" % i, [128, 512], F32) for i in range(2)]
            t2 = [sb("t2_%d" % i, [128, 512], F32) for i in range(2)]
            t1_b = [Buf("t1") for i in range(2)]
            t2_b = [Buf("t2") for i in range(2)]
            wbg_r = C.w_branch_gla.rearrange("(kc p) n -> p kc n", p=128)
            wbd_r = C.w_branch_gdn.rearrange("(kc p) n -> p kc n", p=128)
            cnt = 0
            for db in range(KC):
                wi = db % NWB
                gi = db % 2
                S.dma("pool", wg[wi][:, :, :], wbg_r[:, :, db * 128:(db + 1) * 128], writes=[wg_b[wi]])
                S.dma("pool", wd[wi][:, :, :], wbd_r[:, :, db * 128:(db + 1) * 128], writes=[wd_b[wi]])
                S.dma("sp", gg[gi][:, :], scr[C_BG + db * 128:C_BG + (db + 1) * 128, :], writes=[gg_b[gi]])
                S.dma("sp", gd[gi][:, :], scr[C_BD + db * 128:C_BD + (db + 1) * 128, :], writes=[gd_b[gi]])
                for blk in range(4):
                    sl = slice(blk * 512, (blk + 1) * 512)
                    pG = (2 * cnt) % 8
                    pD = (2 * cnt + 1) % 8
                    j = cnt % 2
                    cnt += 1
                    for kc in range(8):
                        S.op("pe", lambda e, pG=pG, wi=wi, kc=kc, sl=sl: e.matmul(
                            out=psum[pG][:, :], lhsT=wg[wi][:, kc, :], rhs=yT[:, kc, sl], start=(kc == 0), stop=(kc == 7)),
                            reads=[wg_b[wi], yT_b[kc]], writes=psum_b[pG], skip_same=True)
                    for kc in range(8):
                        S.op("pe", lambda e, pD=pD, wi=wi, kc=kc, sl=sl: e.matmul(
                            out=psum[pD][:, :], lhsT=wd[wi][:, kc, :], rhs=yT[:, 8 + kc, sl], start=(kc == 0), stop=(kc == 7)),
                            reads=[wd_b[wi], yT_b[8 + kc]], writes=psum_b[pD], skip_same=True)
                    S.op("act", lambda e, j=j, gi=gi, sl=sl: e.activation(out=sgg[j][:, :], in_=gg[gi][:, sl], func=AF.Sigmoid),
                         reads=[gg_b[gi]], writes=[sgg_b[j]])
                    S.op("act", lambda e, j=j, gi=gi, sl=sl: e.activation(out=sgd[j][:, :], in_=gd[gi][:, sl], func=AF.Sigmoid),
                         reads=[gd_b[gi]], writes=[sgd_b[j]])
                    S.op("dve", lambda e, j=j, pG=pG: e.tensor_tensor(out=t1[j][:, :], in0=psum[pG][:, :], in1=sgg[j][:, :], op=ALU.mult),
                         reads=psum_b[pG] + [sgg_b[j]], writes=[t1_b[j]])
                    S.op("dve", lambda e, j=j, pD=pD: e.tensor_tensor(out=t2[j][:, :], in0=psum[pD][:, :], in1=sgd[j][:, :], op=ALU.mult),
                         reads=psum_b[pD] + [sgd_b[j]], writes=[t2_b[j]])
                    S.op("pool", lambda e, j=j, db=db, sl=sl: e.tensor_tensor(out=mergedT[:, db, sl], in0=t1[j][:, :], in1=t2[j][:, :], op=ALU.add),
                         reads=[t1_b[j], t2_b[j]], writes=mg_b[4 * blk:4 * blk + 4])
        S.barrier()
        if "merged" in C.dbg_out:
            S.dma("sp", C.dbg_out["merged"].rearrange("(kc p) t -> p kc t", p=128), mergedT[:, :, :], reads=mg_b)
        with contextlib.ExitStack() as pes:
            def sb(name, shape, dt):
                return pes.enter_context(nc.sbuf_tensor(name, list(shape), dt))
            Wout = sb("Wout", [128, KC, D], BF16)
            Wout_b = [Buf("Wout%d" % k) for k in range(KC)]
            wo_r = C.w_out.rearrange("(kc p) n -> p kc n", p=128)
            for k in range(KC):
                S.dma("pool", Wout[:, k, :], wo_r[:, k, :], writes=[Wout_b[k]], max_dma_last_dim=4096)
            g2b = sb("g2b", [128, D], F32)
            g2b_b = Buf("g2b")
            S.dma("sp", g2b[:, :], C.norm2_g.partition_broadcast(128), writes=[g2b_b])
            xt = [sb("xt2_%d" % i, [128, D], F32) for i in range(2)]
            xt_b = [Buf("xt2") for i in range(2)]
            x1t = [sb("x1t%d" % i, [128, D], F32) for i in range(2)]
            x1t_b = [Buf("x1t") for i in range(2)]
            junk = sb("junk2", [128, D], BF16)
            junk_b = Buf("junk2")
            hb = [sb("hb2_%d" % i, [128, D], BF16) for i in range(2)]
            hb_b = [Buf("hb2") for i in range(2)]
            hst = [sb("hst%d" % i, [128, KC, 128], BF16) for i in range(2)]
            hst_b = [Buf("hst") for i in range(2)]
            stat = sb("stat2", [128, 4 * NT], F32)
            stat_b = [Buf("stat2") for i in range(NT)]
            h2_r = C.h2_scr.rearrange("(kc p) t -> p kc t", p=128)
            pcnt = 0
            for t in range(NT):
                i = t % 2
                ts_ = slice(t * 128, (t + 1) * 128)
                S.dma("sp", xt[i][:, :], C.x[ts_, :], writes=[xt_b[i]])
                for cb_ in range(4):
                    pi = cb_ + 4 * (t % 2)
                    csl = slice(cb_ * 512, (cb_ + 1) * 512)
                    for kc in range(KC):
                        S.op("pe", lambda e, pi=pi, kc=kc, ts_=ts_, csl=csl: e.matmul(
                            out=psum[pi][:, :], lhsT=mergedT[:, kc, ts_], rhs=Wout[:, kc, csl], start=(kc == 0), stop=(kc == KC - 1)),
                            reads=[mg_b[t], Wout_b[kc]], writes=psum_b[pi], skip_same=True)
                    S.op("dve", lambda e, pi=pi, i=i, csl=csl: e.tensor_tensor(out=x1t[i][:, csl], in0=psum[pi][:, :], in1=xt[i][:, csl], op=ALU.add),
                         reads=psum_b[pi] + [xt_b[i]], writes=[x1t_b[i]])
                S.dma("sp", C.x1_scr[ts_, :], x1t[i][:, :], reads=[x1t_b[i]])
                ss = stat[:, 4 * t:4 * t + 1]
                lnv = stat[:, 4 * t + 1:4 * t + 2]
                rstd = stat[:, 4 * t + 2:4 * t + 3]
                S.op("act", lambda e, i=i, ss=ss: e.activation(out=junk[:, :], in_=x1t[i][:, :], func=AF.Square, accum_out=ss),
                     reads=[x1t_b[i]], writes=[junk_b, stat_b[t]])
                S.op("act", lambda e, ss=ss, lnv=lnv: e.activation(out=lnv, in_=ss, func=AF.Ln, scale=1.0 / D, bias=EPS),
                     reads=[stat_b[t]], writes=[stat_b[t]])
                S.op("act", lambda e, rstd=rstd, lnv=lnv: e.activation(out=rstd, in_=lnv, func=AF.Exp, scale=-0.5),
                     reads=[stat_b[t]], writes=[stat_b[t]])
                S.op("dve", lambda e, i=i, rstd=rstd: e.scalar_tensor_tensor(
                    out=hb[i][:, :], in0=x1t[i][:, :], scalar=rstd, in1=g2b[:, :], op0=ALU.mult, op1=ALU.mult),
                    reads=[x1t_b[i], stat_b[t], g2b_b], writes=[hb_b[i]])
                for g in range(4):
                    pi = (pcnt % 2) * 4 + (3 - g)
                    pt = psum[pi].bitcast(BF16)
                    for q in range(4):
                        kc = g * 4 + q
                        S.op("pe", lambda e, pt=pt, q=q, i=i, kc=kc: e.transpose(
                            out=pt[:, q * 128:(q + 1) * 128], in_=hb[i][:, kc * 128:(kc + 1) * 128], identity=ident_bf[:, :]),
                            reads=[hb_b[i], ident_bf_b], writes=psum_b[pi], skip_same=True)
                    S.op("act", lambda e, pt=pt, g=g, i=i: e.activation(
                        out=hst[i][:, g * 4:(g + 1) * 4, :], in_=pt[:, 0:512].rearrange("p (a b) -> p a b", a=4), func=AF.Copy),
                        reads=psum_b[pi], writes=[hst_b[i]])
                pcnt += 1
                S.dma("sp", h2_r[:, :, ts_], hst[i][:, :, :], reads=[hst_b[i]])


def phase_ffn(C):
    nc, S, debug = C.nc, C.S, C.debug
    psum, psum_b = C.psum, C.psum_b
    TH = 1024
    NB3 = 342
    with contextlib.ExitStack() as pes:
        def sb(name, shape, dt):
            return pes.enter_context(nc.sbuf_tensor(name, list(shape), dt))
        h2T = sb("h2T", [128, KC, TH + 2], BF16)
        h2T_b = Buf("h2T")
        aT = sb("aT", [128, 44, TH], BF16)
        aT_b = [Buf("aT%d" % j) for j in range(44)]
        NW = 4
        wu = [sb("wu%d" % i, [128, KC, 128], BF16) for i in range(NW)]
        wu_b = [Buf("wu") for i in range(NW)]
        pg = [sb("pg%d" % i, [128, TH + 2], F32) for i in range(2)]
        pg_b = [Buf("pg") for i in range(2)]
        cg = [sb("cg%d" % i, [128, TH], F32) for i in range(2)]
        cg_b = [Buf("cg") for i in range(2)]
        fw = sb("fw", [128, 88 * 4], F32)
        fw_b = Buf("fw")
        S.dma("sp", fw[:, :], C.ffn_cw_d[:, :], writes=[fw_b])
        NWD = 2
        wdn = [sb("wdn%d" % i, [128, 44, 128], BF16) for i in range(NWD)]
        wdn_b = [Buf("wdn") for i in range(NWD)]
        x1s = [sb("x1s%d" % i, [128, 512], F32) for i in range(2)]
        x1s_b = [Buf("x1s") for i in range(2)]
        xo = [sb("xo%d" % i, [128, 512], F32) for i in range(2)]
        xo_b = [Buf("xo") for i in range(2)]
        wup_r = C.w_up.rearrange("(kc p) n -> p kc n", p=128)
        wdn_r = C.w_down.rearrange("(fc p) n -> p fc n", p=128)
        h2_r = C.h2_scr.rearrange("(kc p) t -> p kc t", p=128)
        ucnt = 0
        pcnt = 0
        dcnt = 0
        for half in range(2):
            t0 = half * TH
            S.op("pool", lambda e: e.memset(h2T[:, :, :], 0.0), writes=[h2T_b])
            lo = max(t0 - 1, 0)
            hi = min(t0 + TH + 1, T)
            S.dma("sp", h2T[:, :, lo - (t0 - 1):hi - (t0 - 1)], h2_r[:, :, lo:hi], writes=[h2T_b])
            for j in range(44):
                for which in range(2):
                    blk = which * 44 + j
                    c0 = which * D_FF + j * 128
                    wi = ucnt % NW
                    pgi = ucnt % 2
                    ucnt += 1
                    S.dma("pool", wu[wi][:, :, :], wup_r[:, :, c0:c0 + 128], writes=[wu_b[wi]])
                    for b3 in range(3):
                        pi = pcnt % 8
                        pcnt += 1
                        s3 = slice(b3 * NB3, (b3 + 1) * NB3)
                        for kc in range(KC):
                            S.op("pe", lambda e, pi=pi, wi=wi, kc=kc, s3=s3: e.matmul(
                                out=psum[pi][:, 0:NB3], lhsT=wu[wi][:, kc, :], rhs=h2T[:, kc, s3], start=(kc == 0), stop=(kc == KC - 1)),
                                reads=[wu_b[wi], h2T_b], writes=psum_b[pi], skip_same=True)
                        if pcnt % 2 == 0:
                            S.op("act", lambda e, pi=pi, pgi=pgi, s3=s3: e.activation(out=pg[pgi][:, s3], in_=psum[pi][:, 0:NB3], func=AF.Copy),
                                 reads=psum_b[pi], writes=[pg_b[pgi]])
                        else:
                            S.op("dve", lambda e, pi=pi, pgi=pgi, s3=s3: e.tensor_copy(out=pg[pgi][:, s3], in_=psum[pi][:, 0:NB3]),
                                 reads=psum_b[pi], writes=[pg_b[pgi]])
                    w0 = fw[:, blk * 4:blk * 4 + 1]
                    w1 = fw[:, blk * 4 + 1:blk * 4 + 2]
                    w2 = fw[:, blk * 4 + 2:blk * 4 + 3]
                    bb = fw[:, blk * 4 + 3:blk * 4 + 4]
                    S.op("act", lambda e, pgi=pgi, which=which, w1=w1, bb=bb: e.activation(
                        out=cg[which][:, :], in_=pg[pgi][:, 1:TH + 1], func=AF.Identity, scale=w1, bias=bb),
                        reads=[pg_b[pgi], fw_b], writes=[cg_b[which]])
                    S.op("dve", lambda e, pgi=pgi, which=which, w0=w0: e.scalar_tensor_tensor(
                        out=cg[which][:, :], in0=pg[pgi][:, 0:TH], scalar=w0, in1=cg[which][:, :], op0=ALU.mult, op1=ALU.add),
                        reads=[pg_b[pgi], fw_b, cg_b[which]], writes=[cg_b[which]])
                    S.op("dve", lambda e, pgi=pgi, which=which, w2=w2: e.scalar_tensor_tensor(
                        out=cg[which][:, :], in0=pg[pgi][:, 2:TH + 2], scalar=w2, in1=cg[which][:, :], op0=ALU.mult, op1=ALU.add),
                        reads=[pg_b[pgi], fw_b, cg_b[which]], writes=[cg_b[which]])
                    if which == 0:
                        S.op("act", lambda e: e.activation(out=cg[0][:, :], in_=cg[0][:, :], func=AF.Silu),
                             reads=[cg_b[0]], writes=[cg_b[0]])
                S.op("pool", lambda e, j=j: e.tensor_tensor(out=aT[:, j, :], in0=cg[0][:, :], in1=cg[1][:, :], op=ALU.mult),
                     reads=[cg_b[0], cg_b[1]], writes=[aT_b[j]])
            for cgp in range(4):
                units = []
                for u in range(4):
                    c0 = cgp * 512 + u * 128
                    wi = dcnt % NWD
                    dcnt += 1
                    S.dma("pool", wdn[wi][:, :, :], wdn_r[:, :, c0:c0 + 128], writes=[wdn_b[wi]])
                    units.append(wi)
                    for tt in range(TH // 128):
                        pi = tt % 8
                        ts_ = slice(tt * 128, (tt + 1) * 128)
                        for fc in range(44):
                            S.op("pe", lambda e, pi=pi, u=u, fc=fc, ts_=ts_, wi=wi: e.matmul(
                                out=psum[pi][:, u * 128:(u + 1) * 128], lhsT=aT[:, fc, ts_], rhs=wdn[wi][:, fc, :],
                                start=(fc == 0), stop=(fc == 43)),
                                reads=[aT_b[fc], wdn_b[wi]], writes=psum_b[pi], skip_same=True)
                for tt in range(TH // 128):
                    pi = tt % 8
                    i = tt % 2
                    rows = slice(t0 + tt * 128, t0 + (tt + 1) * 128)
                    csl = slice(cgp * 512, (cgp + 1) * 512)
                    S.dma("sp", x1s[i][:, :], C.x1_scr[rows, csl], writes=[x1s_b[i]])
                    S.op("dve", lambda e, pi=pi, i=i: e.tensor_tensor(out=xo[i][:, :], in0=psum[pi][:, :], in1=x1s[i][:, :], op=ALU.add),
                         reads=psum_b[pi] + [x1s_b[i]], writes=[xo_b[i]])
                    S.dma("sp", C.x2_scr[rows, csl], xo[i][:, :], reads=[xo_b[i]])


def phase_final(C):
    nc, S, debug = C.nc, C.S, C.debug
    with contextlib.ExitStack() as pes:
        def sb(name, shape, dt):
            return pes.enter_context(nc.sbuf_tensor(name, list(shape), dt))
        gfb = sb("gfb", [128, D], F32)
        gfb_b = Buf("gfb")
        S.dma("sp", gfb[:, :], C.final_norm_g.partition_broadcast(128), writes=[gfb_b])
        NBF = 3
        xt = [sb("xf%d" % i, [128, D], F32) for i in range(NBF)]
        xt_b = [Buf("xf") for i in range(NBF)]
        ot = [sb("of%d" % i, [128, D], F32) for i in range(NBF)]
        ot_b = [Buf("of") for i in range(NBF)]
        junk = sb("junk3", [128, D], BF16)
        junk_b = Buf("junk3")
        stat = sb("stat3", [128, 4 * NT], F32)
        stat_b = [Buf("stat3") for i in range(NT)]
        for t in range(NT):
            i = t % NBF
            ts_ = slice(t * 128, (t + 1) * 128)
            S.dma("sp", xt[i][:, :], C.x2_scr[ts_, :], writes=[xt_b[i]])
            ss = stat[:, 4 * t:4 * t + 1]
            lnv = stat[:, 4 * t + 1:4 * t + 2]
            rstd = stat[:, 4 * t + 2:4 * t + 3]
            S.op("act", lambda e, i=i, ss=ss: e.activation(out=junk[:, :], in_=xt[i][:, :], func=AF.Square, accum_out=ss),
                 reads=[xt_b[i]], writes=[junk_b, stat_b[t]])
            S.op("act", lambda e, ss=ss, lnv=lnv: e.activation(out=lnv, in_=ss, func=AF.Ln, scale=1.0 / D, bias=EPS),
                 reads=[stat_b[t]], writes=[stat_b[t]])
            S.op("act", lambda e, rstd=rstd, lnv=lnv: e.activation(out=rstd, in_=lnv, func=AF.Exp, scale=-0.5),
                 reads=[stat_b[t]], writes=[stat_b[t]])
            S.op("dve", lambda e, i=i, rstd=rstd: e.scalar_tensor_tensor(
                out=ot[i][:, :], in0=xt[i][:, :], scalar=rstd, in1=gfb[:, :], op0=ALU.mult, op1=ALU.mult),
                reads=[xt_b[i], stat_b[t], gfb_b], writes=[ot_b[i]])
            S.dma("sp", C.out[ts_, :], ot[i][:, :], reads=[ot_b[i]])

NC2 = 896
NCB = 14 * 128


_NC_CACHE = {}


def _consts():
    j = np.arange(128)[:, None]
    i = np.arange(128)[None, :]
    cst = np.zeros((128, NCST), np.float32)
    cst[:, 0:128] = np.where(j <= i, -1.0 / 16, 0.0)
    cst[:, 128:256] = np.where(j >= i, -1.0 / 16, 0.0)
    cst[:, 256:384] = np.where(j > i, -1.0 / 16, 0.0)
    cst[:, 384:512] = np.where(j < i, -1.0 / 16, 0.0)
    cst[:, 512:640] = 1.0
    cst[:, 640:1152] = np.tile(np.where(j <= i, 1.0, 0.0), (1, 4))
    cst[:, 1152:1664] = np.tile(np.where(j >= i, 1.0, 0.0), (1, 4))
    c2 = np.zeros((128, NC2), np.float32)
    c2[:, 0:128] = np.where(j <= i, 1.0, 0.0)
    c2[:, 128:256] = np.where(j >= i, 1.0, 0.0)
    c2[:, 256:384] = np.where(j > i, 1.0, 0.0)
    c2[:, 384:512] = np.where(j < i, 1.0, 0.0)
    c2[:, 512:640] = np.where(j <= i, 0.0, -30000.0)
    c2[:, 640:768] = np.where(j >= i, 0.0, -30000.0)
    c2[:, 768:896] = np.eye(128)
    cb = np.zeros((128, NCB), np.float32)
    for d in range(2):
        for l in range(1, 8):
            b = 1 << (l - 1)
            same = (i // (2 * b)) == (j // (2 * b))
            if d == 0:
                m = same & ((i % (2 * b)) >= b) & ((j % (2 * b)) < b)
            else:
                m = same & ((i % (2 * b)) < b) & ((j % (2 * b)) >= b)
            o = (d * 7 + (l - 1)) * 128
            cb[:, o:o + 128] = m.astype(np.float32)
    return {
        "ident_bf": np.eye(128, dtype=np.float32).astype(ml_dtypes.bfloat16),
        "cst": cst,
        "cst2": c2,
        "cstb": cb.astype(ml_dtypes.bfloat16),
    }


def make_in_maps(inputs, n_cores=8):
    c = _consts()
    maps = []
    xs = np.ascontiguousarray(inputs["x"])
    for b in range(n_cores):
        m = {
            "x": xs[b],
            "norm1_g": np.ascontiguousarray(inputs["norm1_g"]).reshape(1, D),
            "w_in": np.ascontiguousarray(inputs["w_in"]).reshape(D, N_IN),
            "gla_decay_w_f": np.ascontiguousarray(inputs["gla_decay_w_f"]).reshape(16, 1024),
            "gla_decay_w_b": np.ascontiguousarray(inputs["gla_decay_w_b"]).reshape(16, 1024),
            "gla_decay_b_f": np.ascontiguousarray(inputs["gla_decay_b_f"]).reshape(1, 1024),
            "gla_decay_b_b": np.ascontiguousarray(inputs["gla_decay_b_b"]).reshape(1, 1024),
            "gla_norm_g": np.ascontiguousarray(inputs["gla_norm_g"]).reshape(1, 128),
            "gdn_norm_g": np.ascontiguousarray(inputs["gdn_norm_g"]).reshape(1, 128),
            "w_branch_gla": np.ascontiguousarray(inputs["w_branch_gla"]).reshape(1024, D),
            "w_branch_gdn": np.ascontiguousarray(inputs["w_branch_gdn"]).reshape(1024, D),
            "w_out": np.ascontiguousarray(inputs["w_out"]).reshape(D, D),
            "norm2_g": np.ascontiguousarray(inputs["norm2_g"]).reshape(1, D),
            "w_up": np.ascontiguousarray(inputs["w_up"]).reshape(D, 2 * D_FF),
            "w_down": np.ascontiguousarray(inputs["w_down"]).reshape(D_FF, D),
            "ffn_cw": np.ascontiguousarray(np.concatenate([
                np.asarray(inputs["ffn_conv_w"]).reshape(3, 88, 128), np.asarray(inputs["ffn_conv_b"]).reshape(1, 88, 128)],
                axis=0).transpose(2, 1, 0).reshape(128, 88 * 4)),
            "final_norm_g": np.ascontiguousarray(inputs["final_norm_g"]).reshape(1, D),
            "gdn_cw": np.ascontiguousarray(
                np.asarray(inputs["gdn_conv_w"]).reshape(3, 24, 128).transpose(2, 1, 0).reshape(128, 72)),
            "gdn_hp": np.ascontiguousarray(np.stack([
                np.concatenate([np.asarray(inputs["gdn_a_log_f"]).reshape(8), np.asarray(inputs["gdn_a_log_b"]).reshape(8)]),
                np.concatenate([np.asarray(inputs["gdn_dt_bias_f"]).reshape(8), np.asarray(inputs["gdn_dt_bias_b"]).reshape(8)]),
            ], axis=1).astype(np.float32)),
        }
        m.update(c)
        maps.append(m)
    return maps


def kernel(**inputs):
    nc = build_nc()
    in_maps = make_in_maps(inputs, 8)
    res = run_bass_kernel_spmd(nc, in_maps, core_ids=list(range(8)))
    return np.stack([np.asarray(r["out"]) for r in res.results], axis=0)
```

```python
import contextlib
import numpy as np
import ml_dtypes
import concourse.bass as bass
import concourse.mybir as mybir
from concourse.bass_utils import run_bass_kernel_spmd

F32 = mybir.dt.float32
BF16 = mybir.dt.bfloat16
AF = mybir.ActivationFunctionType
ALU = mybir.AluOpType

T = 2048
D = 2048
KC = 16
NT = 16
N_IN = 12352
D_FF = 5632
EPS = 1e-6

C_GQ, C_GK, C_GV, C_GG = 0, 1024, 2048, 3072
C_LR = 4096
C_DQ, C_DK, C_DV = 4128, 5152, 6176
C_DZ = 7200
C_AB = 8224
C_BG = 8256
C_BD = 10304


class Buf:
    __slots__ = ("w", "r", "name", "excl")

    def __init__(self, name="", excl=False):
        self.w = None
        self.r = {}
        self.name = name
        self.excl = excl


class DSem:
    __slots__ = ("handle", "count")

    def __init__(self, handle):
        self.handle = handle
        self.count = 0


class Sched:
    ENG = ["pe", "act", "dve", "pool", "sp"]

    def __init__(self, nc, es, n_dsem=24):
        self.nc = nc
        self.sem = {e: es.enter_context(nc.semaphore("s_" + e)) for e in self.ENG}
        self.cnt = {e: 0 for e in self.ENG}
        self.seen = {e: {} for e in self.ENG}
        self.prog = {e: [] for e in self.ENG}
        self.dsems = [DSem(es.enter_context(nc.semaphore("d%d" % i))) for i in range(n_dsem)]
        self.dnext = 0

    def _waits(self, eng, reads, writes, skip_same):
        deps = {}

        def add(k, v):
            if skip_same and k == eng:
                return
            if deps.get(k, 0) < v:
                deps[k] = v

        for b in reads:
            if b.w is not None:
                add(*b.w)
        for b in writes:
            if b.w is not None:
                add(*b.w)
            for k, v in b.r.items():
                add(k, v)
        out = []
        seen = self.seen[eng]
        for k, v in deps.items():
            if seen.get(k, 0) < v:
                seen[k] = v
                out.append((k.handle if isinstance(k, DSem) else self.sem[k], v))
        return out

    def _mark(self, d, reads, writes):
        k, v = d
        for b in reads:
            if b.r.get(k, 0) < v:
                b.r[k] = v
        for b in writes:
            b.w = d
            b.r = {}

    def op(self, eng, fn, reads=(), writes=(), skip_same=False):
        if any(b.excl for b in reads):
            writes = list(writes) + [b for b in reads if b.excl]
            reads = [b for b in reads if not b.excl]
        waits = self._waits(eng, reads, writes, skip_same)
        self.cnt[eng] += 1
        self.prog[eng].append((waits, fn, self.sem[eng], 1))
        self._mark((eng, self.cnt[eng]), reads, writes)

    def barrier(self):
        for eng in self.ENG:
            waits = []
            seen = self.seen[eng]
            for k in self.ENG:
                if k != eng and seen.get(k, 0) < self.cnt[k]:
                    seen[k] = self.cnt[k]
                    waits.append((self.sem[k], self.cnt[k]))
            for ds in self.dsems:
                if ds.count > 0 and seen.get(ds, 0) < ds.count:
                    seen[ds] = ds.count
                    waits.append((ds.handle, ds.count))
            if waits:
                self.prog[eng].append((waits, None, None, 0))

    def dma(self, q, out, in_, reads=(), writes=(), **kw):
        ds = self.dsems[self.dnext]
        self.dnext = (self.dnext + 1) % len(self.dsems)
        waits = self._waits(q, reads, writes, False)
        if ds.count > 0 and self.seen[q].get(ds, 0) < ds.count:
            self.seen[q][ds] = ds.count
            waits.append((ds.handle, ds.count))
        ds.count += 16
        self.prog[q].append((waits, (lambda e, o=out, i=in_, kw=kw: e.dma_start(out=o, in_=i, **kw)), ds.handle, 16))
        self._mark((ds, ds.count), reads, writes)

    def finish(self):
        waits = []
        for ds in self.dsems:
            if ds.count > 0:
                waits.append((ds.handle, ds.count))
        self.prog["sp"].append((waits, None, None, 0))

    def emit(self):
        nc = self.nc
        prog = self.prog

        def replay(name, e):
            for waits, fn, sem, inc in prog[name]:
                for s, v in waits:
                    e.wait_ge(s, v)
                if fn is not None:
                    ins = fn(e)
                    ins.then_inc(sem, inc)

        with nc.Block() as block:
            @block.tensor
            def _(e):
                replay("pe", e)

            @block.scalar
            def _(e):
                replay("act", e)

            @block.vector
            def _(e):
                replay("dve", e)

            @block.gpsimd
            def _(e):
                replay("pool", e)

            @block.sync
            def _(e):
                replay("sp", e)


def build_nc(debug=None):
    debug = debug or {}
    nc = bass.Bass("TRN2", target_bir_lowering=False)
    es = contextlib.ExitStack()
    with es:
        _build(nc, es, debug)
    return nc


def _dram_in(nc, name, shape, dt=F32):
    return nc.dram_tensor(name, list(shape), dt, kind="ExternalInput").ap()


class Ctx:
    pass


def _build(nc, es, debug):
    S = Sched(nc, es)
    C = Ctx()
    C.nc, C.S, C.debug = nc, S, debug
    stop_after = debug.get("stop_after", [[None], None])[0][0]

    C.x = _dram_in(nc, "x", [T, D])
    C.norm1_g = _dram_in(nc, "norm1_g", [1, D])
    C.w_in = _dram_in(nc, "w_in", [D, N_IN])
    C.ident_bf_d = _dram_in(nc, "ident_bf", [128, 128], BF16)
    C.cst_d = _dram_in(nc, "cst", [128, NCST])
    C.gla_dw = [_dram_in(nc, "gla_decay_w_f", [16, 1024]), _dram_in(nc, "gla_decay_w_b", [16, 1024])]
    C.gla_db = [_dram_in(nc, "gla_decay_b_f", [1, 1024]), _dram_in(nc, "gla_decay_b_b", [1, 1024])]
    C.gla_norm_g = _dram_in(nc, "gla_norm_g", [1, 128])
    C.gdn_norm_g = _dram_in(nc, "gdn_norm_g", [1, 128])
    C.w_branch_gla = _dram_in(nc, "w_branch_gla", [1024, D])
    C.w_branch_gdn = _dram_in(nc, "w_branch_gdn", [1024, D])
    C.w_out = _dram_in(nc, "w_out", [D, D])
    C.norm2_g = _dram_in(nc, "norm2_g", [1, D])
    C.w_up = _dram_in(nc, "w_up", [D, 2 * D_FF])
    C.w_down = _dram_in(nc, "w_down", [D_FF, D])
    C.ffn_cw_d = _dram_in(nc, "ffn_cw", [128, 88 * 4])
    C.final_norm_g = _dram_in(nc, "final_norm_g", [1, D])
    C.cst2_d = _dram_in(nc, "cst2", [128, NC2])
    C.cstb_d = _dram_in(nc, "cstb", [128, NCB], BF16)
    C.gdn_cw_d = _dram_in(nc, "gdn_cw", [128, 72])
    C.gdn_hp_d = _dram_in(nc, "gdn_hp", [16, 2])
    C.out = nc.dram_tensor("out", [T, D], F32, kind="ExternalOutput").ap()
    C.dbg_out = {}
    for name, (shape, dt) in debug.items():
        if dt is None:
            continue
        C.dbg_out[name] = nc.dram_tensor("dbg_" + name, list(shape), dt, kind="ExternalOutput").ap()
    C.scr = nc.dram_tensor("scr_proj", [N_IN, T], F32).ap()
    C.y_scr = nc.dram_tensor("scr_y", [2048, T], BF16).ap()
    C.h2_scr = nc.dram_tensor("scr_h2", [D, T], BF16).ap()
    C.x1_scr = nc.dram_tensor("scr_x1", [T, D], F32).ap()
    C.x2_scr = nc.dram_tensor("scr_x2", [T, D], F32).ap()

    C.psum = [es.enter_context(nc.psum_tensor("ps%d" % i, [128, 512], F32)) for i in range(8)]
    C.psum_b = [[Buf("ps%d" % i, excl=True)] * 4 for i in range(8)]
    C.ident_bf = es.enter_context(nc.sbuf_tensor("ident_bf_sb", [128, 128], BF16))
    C.ident_bf_b = Buf("ident_bf")
    C.cst = es.enter_context(nc.sbuf_tensor("cst_sb", [128, NCST], F32))
    C.cst_b = Buf("cst")
    S.dma("sp", C.ident_bf[:, :], C.ident_bf_d[:, :], writes=[C.ident_bf_b])
    S.dma("sp", C.cst[:, :], C.cst_d[:, :], writes=[C.cst_b])

    phase_proj(C)
    S.barrier()
    if stop_after != "proj":
        if debug.get("gla_heads", [[8], None])[0][0] > 0:
            phase_gla(C)
            S.barrier()
        if stop_after != "gla":
            if debug.get("gdn_heads", [[8], None])[0][0] > 0:
                phase_gdn(C)
                S.barrier()
            if stop_after != "gdn":
                phase_branch(C)
                S.barrier()
                if stop_after != "branch":
                    phase_ffn(C)
                    S.barrier()
                    phase_final(C)
                    S.barrier()

    if "proj" in C.dbg_out:
        S.dma("sp", C.dbg_out["proj"][:, :], C.scr[0:C.dbg_out["proj"].shape[0], :])
    if "y" in C.dbg_out:
        S.dma("sp", C.dbg_out["y"][:, :], C.y_scr[0:C.dbg_out["y"].shape[0], :])
    if "y2" in C.dbg_out:
        S.dma("sp", C.dbg_out["y2"][:, :], C.y_scr[1024:1024 + C.dbg_out["y2"].shape[0], :])
    for nm, ap_ in (("x1", C.x1_scr), ("x2", C.x2_scr)):
        if nm in C.dbg_out:
            S.dma("sp", C.dbg_out[nm][:, :], ap_[:, :])
    if "h2" in C.dbg_out:
        S.dma("sp", C.dbg_out["h2"][:, :], C.h2_scr[:, :])
    if stop_after is not None:
        S.dma("sp", C.out[0:128, :], C.x[0:128, :])
    S.finish()
    S.emit()


def phase_proj(C):
    nc, S, debug = C.nc, C.S, C.debug
    psum, psum_b = C.psum, C.psum_b
    with contextlib.ExitStack() as pes:
        def sb(name, shape, dt):
            return pes.enter_context(nc.sbuf_tensor(name, list(shape), dt))

        hT = sb("hT", [128, KC, T], BF16)
        hT_b = [Buf("hT%d" % t) for t in range(NT)]
        g1b = sb("g1b", [128, D], F32)
        g1b_b = Buf("g1b")
        S.dma("sp", g1b[:, :], C.norm1_g.partition_broadcast(128), writes=[g1b_b])

        xt = [sb("xt%d" % i, [128, D], F32) for i in range(2)]
        xt_b = [Buf("xt%d" % i) for i in range(2)]
        junk = sb("junk", [128, D], BF16)
        junk_b = Buf("junk")
        hb = [sb("hb%d" % i, [128, D], BF16) for i in range(2)]
        hb_b = [Buf("hb%d" % i) for i in range(2)]
        stat = sb("stat", [128, 4 * NT], F32)
        stat_b = [Buf("stat%d" % i) for i in range(NT)]
        ident_bf, ident_bf_b = C.ident_bf, C.ident_bf_b
        pcnt = 0
        for t in range(NT):
            i = t % 2
            S.dma("sp", xt[i][:, :], C.x[t * 128:(t + 1) * 128, :], writes=[xt_b[i]])
            ss = stat[:, 4 * t:4 * t + 1]
            lnv = stat[:, 4 * t + 1:4 * t + 2]
            rstd = stat[:, 4 * t + 2:4 * t + 3]
            S.op("act", lambda e, i=i, ss=ss: e.activation(out=junk[:, :], in_=xt[i][:, :], func=AF.Square, accum_out=ss),
                 reads=[xt_b[i]], writes=[junk_b, stat_b[t]])
            S.op("act", lambda e, ss=ss, lnv=lnv: e.activation(out=lnv, in_=ss, func=AF.Ln, scale=1.0 / D, bias=EPS),
                 reads=[stat_b[t]], writes=[stat_b[t]])
            S.op("act", lambda e, rstd=rstd, lnv=lnv: e.activation(out=rstd, in_=lnv, func=AF.Exp, scale=-0.5),
                 reads=[stat_b[t]], writes=[stat_b[t]])
            S.op("dve", lambda e, i=i, rstd=rstd: e.scalar_tensor_tensor(
                out=hb[i][:, :], in0=xt[i][:, :], scalar=rstd, in1=g1b[:, :], op0=ALU.mult, op1=ALU.mult),
                reads=[xt_b[i], stat_b[t], g1b_b], writes=[hb_b[i]])
            for g in range(4):
                pi = 4 + (pcnt % 4)
                pcnt += 1
                pt = psum[pi].bitcast(BF16)
                for q in range(4):
                    kc = g * 4 + q
                    S.op("pe", lambda e, pt=pt, q=q, i=i, kc=kc: e.transpose(
                        out=pt[:, q * 128:(q + 1) * 128], in_=hb[i][:, kc * 128:(kc + 1) * 128], identity=ident_bf[:, :]),
                        reads=[hb_b[i], ident_bf_b], writes=[psum_b[pi][q]], skip_same=True)
                if g % 2 == 0:
                    S.op("act", lambda e, pt=pt, g=g, t=t: e.activation(
                        out=hT[:, g * 4:(g + 1) * 4, t * 128:(t + 1) * 128],
                        in_=pt[:, 0:512].rearrange("p (a b) -> p a b", a=4), func=AF.Copy),
                        reads=psum_b[pi], writes=[hT_b[t]])
                else:
                    S.op("dve", lambda e, pt=pt, g=g, t=t: e.tensor_copy(
                        out=hT[:, g * 4:(g + 1) * 4, t * 128:(t + 1) * 128],
                        in_=pt[:, 0:512].rearrange("p (a b) -> p a b", a=4)),
                        reads=psum_b[pi], writes=[hT_b[t]])

        scr = C.scr
        w_in_r = C.w_in.rearrange("(kc p) n -> p kc n", p=128)
        units = []
        for c0 in range(0, 4096, 128):
            units.append((c0, 128))
        units.append((C_LR, 32))
        for c0 in range(C_DQ, C_AB, 128):
            units.append((c0, 128))
        units.append((C_AB, 32))
        for c0 in range(C_BG, N_IN, 128):
            units.append((c0, 128))
        if "nunits" in debug:
            units = units[:debug["nunits"][0][0]]
        NW = 8
        wring = [sb("wr%d" % i, [128, KC, 128], BF16) for i in range(NW)]
        wring_b = [Buf("wr%d" % i) for i in range(NW)]
        NSTG = 3
        stg = [sb("stg%d" % i, [128, T], F32) for i in range(NSTG)]
        stg_b = [Buf("stg%d" % i) for i in range(NSTG)]
        ecnt = 0
        for u, (c0, ncol) in enumerate(units):
            wt, wb = wring[u % NW], wring_b[u % NW]
            S.dma("pool", wt[:, :, 0:ncol], w_in_r[:, :, c0:c0 + ncol], writes=[wb])
            st, stb = stg[u % NSTG], stg_b[u % NSTG]
            for blk in range(4):
                pi = (4 * u + blk) % 8
                for kc in range(KC):
                    S.op("pe", lambda e, pi=pi, wt=wt, kc=kc, blk=blk, ncol=ncol: e.matmul(
                        out=psum[pi][0:ncol, :], lhsT=wt[:, kc, 0:ncol], rhs=hT[:, kc, blk * 512:(blk + 1) * 512],
                        start=(kc == 0), stop=(kc == KC - 1)),
                        reads=[wb] + hT_b[4 * blk:4 * blk + 4], writes=psum_b[pi], skip_same=True)
                if ecnt % 2 == 0:
                    S.op("act", lambda e, pi=pi, st=st, blk=blk, ncol=ncol: e.activation(
                        out=st[0:ncol, blk * 512:(blk + 1) * 512], in_=psum[pi][0:ncol, :], func=AF.Copy),
                        reads=psum_b[pi], writes=[stb])
                else:
                    S.op("dve", lambda e, pi=pi, st=st, blk=blk, ncol=ncol: e.tensor_copy(
                        out=st[0:ncol, blk * 512:(blk + 1) * 512], in_=psum[pi][0:ncol, :]),
                        reads=psum_b[pi], writes=[stb])
                ecnt += 1
            S.dma("sp", scr[c0:c0 + ncol, :], st[0:ncol, :], reads=[stb])


def phase_gla(C):
    nc, S, debug = C.nc, C.S, C.debug
    psum, psum_b = C.psum, C.psum_b
    cst, cst_b = C.cst, C.cst_b
    scr = C.scr
    nheads = debug.get("gla_heads", [[8], None])[0][0]
    with contextlib.ExitStack() as pes:
        def sb(name, shape, dt):
            return pes.enter_context(nc.sbuf_tensor(name, list(shape), dt))

        def B(name):
            return Buf(name)

        lr = sb("lr", [49, T], F32)
        lr_b = B("lr")
        dw = sb("dw", [49, 1024], F32)
        dw_b = B("dw")
        gn = sb("gn", [128, 1], F32)
        gn_b = B("gn")
        S.op("pool", lambda e: e.memset(lr[:, :], 1.0), writes=[lr_b])
        for d in range(2):
            S.dma("sp", lr[32 * d:32 * d + 16, :], scr[C_LR + 16 * d:C_LR + 16 * d + 16, :], writes=[lr_b])
            S.dma("sp", dw[32 * d:32 * d + 16, :], C.gla_dw[d][:, :], writes=[dw_b])
            S.dma("sp", dw[32 * d + 16:32 * d + 17, :], C.gla_db[d][:, :], writes=[dw_b])
        S.dma("sp", gn[:, :], C.gla_norm_g.rearrange("o d -> d o"), writes=[gn_b])

        NB = 2
        qbf = [sb("qbf%d" % i, [128, T], BF16) for i in range(NB)]
        kbf = [sb("kbf%d" % i, [128, T], BF16) for i in range(NB)]
        vbf = [sb("vbf%d" % i, [128, T], BF16) for i in range(NB)]
        g32 = [sb("g32%d" % i, [128, T], F32) for i in range(NB)]
        qbf_b = [B("qbf") for i in range(NB)]
        kbf_b = [B("kbf") for i in range(NB)]
        vbf_b = [B("vbf") for i in range(NB)]
        g32_b = [B("g32") for i in range(NB)]
        vtm = sb("vtm", [128, NT, 128], BF16)
        vtm_b = B("vtm")
        sg = sb("sg", [128, T], BF16)
        sg_b = B("sg")
        L = sb("L", [128, NT, 128], F32)
        L_b = B("L")
        EG = sb("EG", [128, T], BF16)
        EG_b = B("EG")
        EGi = sb("EGi", [128, T], BF16)
        EGi_b = B("EGi")
        EKT = sb("EKT", [128, NT, 128], BF16)
        EKT_b = B("EKT")
        dcl = sb("dcl", [128, 2, NT], F32)
        dcl_b = [B("dcl0"), B("dcl1")]
        qd = [sb("qd%d" % d, [128, T], BF16) for d in range(2)]
        qd_b = [B("qd") for d in range(2)]
        ki = [sb("ki%d" % d, [128, T], BF16) for d in range(2)]
        ki_b = [B("ki") for d in range(2)]
        kt = [sb("kt%d" % d, [128, NT, 128], BF16) for d in range(2)]
        kt_b = [B("kt") for d in range(2)]
        CS = sb("CS", [128, NT, 128], F32)
        CS_b = B("CS")
        Sst = [sb("Sst%d" % d, [128, NT, 128], F32) for d in range(2)]
        Sst_b = [B("Sst") for d in range(2)]
        Sbf = [sb("Sbf%d" % d, [128, NT, 128], BF16) for d in range(2)]
        Sbf_b = [B("Sbf") for d in range(2)]
        tE = [sb("tE%d" % i, [128, 512], F32) for i in range(2)]
        tE_b = [B("tE") for i in range(2)]
        sc1 = sb("sc1", [128, 512], F32)
        sc1_b = B("sc1")
        sc2 = sb("sc2", [128, 512], F32)
        sc2_b = B("sc2")
        scT = sb("scT", [128, 512], BF16)
        scT_b = B("scT")
        sq = sb("sq", [128, 512], F32)
        sq_b = B("sq")
        rs = sb("rs", [128, 512], F32)
        rs_b = B("rs")
        tt = sb("tt", [128, 512], F32)
        tt_b = B("tt")
        yb = [sb("yb%d" % i, [128, T], BF16) for i in range(2)]
        yb_b = [B("yb") for i in range(2)]

        ident_bf, ident_bf_b = C.ident_bf, C.ident_bf_b
        TRI = [cst[:, 0:128], cst[:, 128:256]]
        STRI = [cst[:, 256:384], cst[:, 384:512]]
        ONES = cst[:, 512:640]
        MASK = [cst[:, 640:1152], cst[:, 1152:1664]]
        pc = [0]
        tec = [0]

        def bank():
            pi = 4 + (pc[0] % 4)
            pc[0] += 1
            return pi

        def load_head(h):
            i = h % NB
            S.dma("pool", qbf[i][:, :], scr[C_GQ + h * 128:C_GQ + (h + 1) * 128, :], writes=[qbf_b[i]], max_dma_last_dim=4096)
            S.dma("pool", kbf[i][:, :], scr[C_GK + h * 128:C_GK + (h + 1) * 128, :], writes=[kbf_b[i]], max_dma_last_dim=4096)
            S.dma("pool", vbf[i][:, :], scr[C_GV + h * 128:C_GV + (h + 1) * 128, :], writes=[vbf_b[i]], max_dma_last_dim=4096)
            S.dma("sp", g32[i][:, :], scr[C_GG + h * 128:C_GG + (h + 1) * 128, :], writes=[g32_b[i]])

        load_head(0)
        for h in range(nheads):
            i = h % NB
            if h + 1 < nheads:
                load_head(h + 1)
            for g in range(4):
                pi = bank()
                pt = psum[pi].bitcast(BF16)
                for q in range(4):
                    c = 4 * g + q
                    S.op("pe", lambda e, pt=pt, q=q, c=c, i=i: e.transpose(
                        out=pt[:, q * 128:(q + 1) * 128], in_=vbf[i][:, c * 128:(c + 1) * 128], identity=ident_bf[:, :]),
                        reads=[vbf_b[i], ident_bf_b], writes=[psum_b[pi][q]], skip_same=True)
                S.op("act", lambda e, pt=pt, g=g: e.activation(
                    out=vtm[:, 4 * g:4 * g + 4, :], in_=pt[:, 0:512].rearrange("p (a b) -> p a b", a=4), func=AF.Copy),
                    reads=psum_b[pi], writes=[vtm_b])
            for blk in range(4):
                sl = slice(blk * 512, (blk + 1) * 512)
                j = tec[0] % 2
                tec[0] += 1
                S.op("act", lambda e, j=j, sl=sl, i=i: e.activation(out=tE[j][:, :], in_=g32[i][:, sl], func=AF.Exp, scale=-1.0),
                     reads=[g32_b[i]], writes=[tE_b[j]])
                S.op("act", lambda e, j=j: e.activation(out=tE[j][:, :], in_=tE[j][:, :], func=AF.Ln, bias=1.0),
                     reads=[tE_b[j]], writes=[tE_b[j]])
                S.op("act", lambda e, j=j: e.activation(out=tE[j][:, :], in_=tE[j][:, :], func=AF.Exp, scale=-1.0),
                     reads=[tE_b[j]], writes=[tE_b[j]])
                S.op("dve", lambda e, j=j, sl=sl, i=i: e.tensor_tensor(out=sg[:, sl], in0=g32[i][:, sl], in1=tE[j][:, :], op=ALU.mult),
                     reads=[g32_b[i], tE_b[j]], writes=[sg_b])
            for d in range(2):
                p0 = 32 * d
                for g in range(4):
                    pi = bank()
                    for q in range(4):
                        c = 4 * g + q
                        S.op("pe", lambda e, pi=pi, q=q, c=c, p0=p0, h=h: e.matmul(
                            out=psum[pi][:, q * 128:(q + 1) * 128], lhsT=lr[p0:p0 + 17, c * 128:(c + 1) * 128],
                            rhs=dw[p0:p0 + 17, h * 128:(h + 1) * 128], start=True, stop=True),
                            reads=[lr_b, dw_b], writes=[psum_b[pi][q]], skip_same=True)
                    j = tec[0] % 2
                    tec[0] += 1
                    S.op("act", lambda e, j=j, pi=pi: e.activation(out=tE[j][:, :], in_=psum[pi][:, :], func=AF.Exp, scale=-1.0),
                         reads=psum_b[pi], writes=[tE_b[j]])
                    S.op("act", lambda e, j=j, g=g: e.activation(
                        out=L[:, 4 * g:4 * g + 4, :], in_=tE[j][:, :].rearrange("p (a b) -> p a b", a=4), func=AF.Ln, bias=1.0),
                        reads=[tE_b[j]], writes=[L_b])
                for g in range(4):
                    pi = bank()
                    sl = slice(g * 512, (g + 1) * 512)
                    for q in range(4):
                        c = 4 * g + q
                        S.op("pe", lambda e, pi=pi, q=q, c=c, d=d: e.matmul(
                            out=psum[pi][:, q * 128:(q + 1) * 128], lhsT=L[:, c, :], rhs=TRI[d], start=True, stop=True),
                            reads=[L_b, cst_b], writes=[psum_b[pi][q]], skip_same=True)
                    S.op("act", lambda e, pi=pi, sl=sl: e.activation(out=EG[:, sl], in_=psum[pi][:, :], func=AF.Exp),
                         reads=psum_b[pi], writes=[EG_b])
                    S.op("act", lambda e, pi=pi, sl=sl: e.activation(out=EGi[:, sl], in_=psum[pi][:, :], func=AF.Exp, scale=-1.0),
                         reads=psum_b[pi], writes=[EGi_b])
                    col = 127 if d == 0 else 0
                    S.op("act", lambda e, pi=pi, g=g, d=d, col=col: e.activation(
                        out=dcl[:, d, 4 * g:4 * g + 4], in_=psum[pi][:, :].rearrange("p (a b) -> p a b", a=4)[:, :, col], func=AF.Exp),
                        reads=psum_b[pi], writes=[dcl_b[d]])
                for g in range(4):
                    pi = bank()
                    for q in range(4):
                        c = 4 * g + q
                        S.op("pe", lambda e, pi=pi, q=q, c=c, d=d: e.matmul(
                            out=psum[pi][:, q * 128:(q + 1) * 128], lhsT=STRI[d], rhs=L[:, c, :], start=True, stop=True),
                            reads=[L_b, cst_b], writes=[psum_b[pi][q]], skip_same=True)
                    S.op("act", lambda e, pi=pi, g=g: e.activation(
                        out=EKT[:, 4 * g:4 * g + 4, :], in_=psum[pi][:, :].rearrange("p (a b) -> p a b", a=4), func=AF.Exp),
                        reads=psum_b[pi], writes=[EKT_b])
                S.op("dve", lambda e, d=d, i=i: e.scalar_tensor_tensor(
                    out=qd[d][:, :], in0=qbf[i][:, :], scalar=float(128 ** -0.5), in1=EG[:, :], op0=ALU.mult, op1=ALU.mult),
                    reads=[qbf_b[i], EG_b], writes=[qd_b[d]])
                S.op("dve", lambda e, d=d, i=i: e.tensor_tensor(out=ki[d][:, :], in0=kbf[i][:, :], in1=EGi[:, :], op=ALU.mult),
                     reads=[kbf_b[i], EGi_b], writes=[ki_b[d]])
                for g in range(4):
                    pi = bank()
                    pt = psum[pi].bitcast(BF16)
                    for q in range(4):
                        c = 4 * g + q
                        S.op("pe", lambda e, pt=pt, q=q, c=c, i=i: e.transpose(
                            out=pt[:, q * 128:(q + 1) * 128], in_=kbf[i][:, c * 128:(c + 1) * 128], identity=ident_bf[:, :]),
                            reads=[kbf_b[i], ident_bf_b], writes=[psum_b[pi][q]], skip_same=True)
                    S.op("dve", lambda e, pt=pt, g=g, d=d: e.tensor_tensor(
                        out=kt[d][:, 4 * g:4 * g + 4, :], in0=pt[:, 0:512].rearrange("p (a b) -> p a b", a=4),
                        in1=EKT[:, 4 * g:4 * g + 4, :], op=ALU.mult),
                        reads=psum_b[pi] + [EKT_b], writes=[kt_b[d]])
                for g in range(4):
                    pi = bank()
                    for q in range(4):
                        c = 4 * g + q
                        S.op("pe", lambda e, pi=pi, q=q, c=c, d=d: e.matmul(
                            out=psum[pi][:, q * 128:(q + 1) * 128], lhsT=kt[d][:, c, :], rhs=vtm[:, c, :], start=True, stop=True),
                            reads=[kt_b[d], vtm_b], writes=[psum_b[pi][q]], skip_same=True)
                    S.op("act", lambda e, pi=pi, g=g: e.activation(
                        out=CS[:, 4 * g:4 * g + 4, :], in_=psum[pi][:, :].rearrange("p (a b) -> p a b", a=4), func=AF.Copy),
                        reads=psum_b[pi], writes=[CS_b])
                if d == 0:
                    S.op("pool", lambda e: e.memset(Sst[0][:, 0, :], 0.0), writes=[Sst_b[0]])
                    for c in range(1, NT):
                        S.op("dve", lambda e, c=c: e.scalar_tensor_tensor(
                            out=Sst[0][:, c, :], in0=Sst[0][:, c - 1, :], scalar=dcl[:, 0, c - 1:c], in1=CS[:, c - 1, :],
                            op0=ALU.mult, op1=ALU.add),
                            reads=[Sst_b[0], dcl_b[0], CS_b], writes=[Sst_b[0]])
                else:
                    S.op("pool", lambda e: e.memset(Sst[1][:, NT - 1, :], 0.0), writes=[Sst_b[1]])
                    for c in range(NT - 2, -1, -1):
                        S.op("dve", lambda e, c=c: e.scalar_tensor_tensor(
                            out=Sst[1][:, c, :], in0=Sst[1][:, c + 1, :], scalar=dcl[:, 1, c + 1:c + 2], in1=CS[:, c + 1, :],
                            op0=ALU.mult, op1=ALU.add),
                            reads=[Sst_b[1], dcl_b[1], CS_b], writes=[Sst_b[1]])
                S.op("act", lambda e, d=d: e.activation(out=Sbf[d][:, :, :], in_=Sst[d][:, :, :], func=AF.Copy),
                     reads=[Sst_b[d]], writes=[Sbf_b[d]])
            yi = h % 2
            for g in range(4):
                sl = slice(g * 512, (g + 1) * 512)
                pA, pB, pO, pN = 4, 5, 6, 7
                for q in range(4):
                    c = 4 * g + q
                    cs_ = slice(c * 128, (c + 1) * 128)
                    S.op("pe", lambda e, q=q, cs_=cs_: e.matmul(
                        out=psum[pA][:, q * 128:(q + 1) * 128], lhsT=ki[0][:, cs_], rhs=qd[0][:, cs_], start=True, stop=True),
                        reads=[ki_b[0], qd_b[0]], writes=[psum_b[pA][q]], skip_same=True)
                    S.op("pe", lambda e, q=q, cs_=cs_: e.matmul(
                        out=psum[pB][:, q * 128:(q + 1) * 128], lhsT=ki[1][:, cs_], rhs=qd[1][:, cs_], start=True, stop=True),
                        reads=[ki_b[1], qd_b[1]], writes=[psum_b[pB][q]], skip_same=True)
                S.op("dve", lambda e: e.tensor_tensor(out=sc1[:, :], in0=psum[pA][:, :], in1=MASK[0], op=ALU.mult),
                     reads=psum_b[pA] + [cst_b], writes=[sc1_b])
                S.op("dve", lambda e: e.tensor_tensor(out=sc2[:, :], in0=psum[pB][:, :], in1=MASK[1], op=ALU.mult),
                     reads=psum_b[pB] + [cst_b], writes=[sc2_b])
                S.op("pool", lambda e: e.tensor_tensor(out=scT[:, :], in0=sc1[:, :], in1=sc2[:, :], op=ALU.add),
                     reads=[sc1_b, sc2_b], writes=[scT_b])
                for q in range(4):
                    c = 4 * g + q
                    cs_ = slice(c * 128, (c + 1) * 128)
                    osl = slice(q * 128, (q + 1) * 128)
                    last_f = (c == 0)
                    has_f = c >= 1
                    has_b = c <= NT - 2
                    S.op("pe", lambda e, osl=osl, c=c, has_f=has_f, has_b=has_b: e.matmul(
                        out=psum[pO][:, osl], lhsT=vtm[:, c, :], rhs=scT[:, osl], start=True, stop=not (has_f or has_b)),
                        reads=[vtm_b, scT_b], writes=[psum_b[pO][q]], skip_same=True)
                    if has_f:
                        S.op("pe", lambda e, osl=osl, c=c, cs_=cs_, has_b=has_b: e.matmul(
                            out=psum[pO][:, osl], lhsT=Sbf[0][:, c, :], rhs=qd[0][:, cs_], start=False, stop=not has_b),
                            reads=[Sbf_b[0], qd_b[0]], writes=[psum_b[pO][q]], skip_same=True)
                    if has_b:
                        S.op("pe", lambda e, osl=osl, c=c, cs_=cs_: e.matmul(
                            out=psum[pO][:, osl], lhsT=Sbf[1][:, c, :], rhs=qd[1][:, cs_], start=False, stop=True),
                            reads=[Sbf_b[1], qd_b[1]], writes=[psum_b[pO][q]], skip_same=True)
                S.op("act", lambda e: e.activation(out=sq[:, :], in_=psum[pO][:, :], func=AF.Square),
                     reads=psum_b[pO], writes=[sq_b])
                S.op("pe", lambda e: e.matmul(out=psum[pN][:, :], lhsT=ONES, rhs=sq[:, :], start=True, stop=True),
                     reads=[sq_b, cst_b], writes=psum_b[pN], skip_same=True)
                S.op("act", lambda e: e.activation(out=rs[:, :], in_=psum[pN][:, :], func=AF.Ln, scale=1.0 / 128, bias=EPS),
                     reads=psum_b[pN], writes=[rs_b])
                S.op("act", lambda e: e.activation(out=rs[:, :], in_=rs[:, :], func=AF.Exp, scale=-0.5),
                     reads=[rs_b], writes=[rs_b])
                S.op("dve", lambda e: e.scalar_tensor_tensor(
                    out=tt[:, :], in0=psum[pO][:, :], scalar=gn[:, 0:1], in1=rs[:, :], op0=ALU.mult, op1=ALU.mult),
                    reads=psum_b[pO] + [gn_b, rs_b], writes=[tt_b])
                S.op("dve", lambda e, sl=sl, yi=yi: e.tensor_tensor(out=yb[yi][:, sl], in0=tt[:, :], in1=sg[:, sl], op=ALU.mult),
                     reads=[tt_b, sg_b], writes=[yb_b[yi]])
            S.dma("sp", C.y_scr[h * 128:(h + 1) * 128, :], yb[yi][:, :], reads=[yb_b[yi]])


NCST = 1664

def phase_gdn(C):
    nc, S, debug = C.nc, C.S, C.debug
    psum, psum_b = C.psum, C.psum_b
    cst, cst_b = C.cst, C.cst_b
    scr = C.scr
    nheads = debug.get("gdn_heads", [[8], None])[0][0]
    with contextlib.ExitStack() as pes:
        def sb(name, shape, dt):
            return pes.enter_context(nc.sbuf_tensor(name, list(shape), dt))

        def B(name):
            return Buf(name)

        ident_bf, ident_bf_b = C.ident_bf, C.ident_bf_b
        ONES = cst[:, 512:640]
        c2 = sb("c2", [128, NC2], F32)
        c2_b = B("c2")
        S.dma("sp", c2[:, :], C.cst2_d[:, :], writes=[c2_b])
        U = [c2[:, 0:128], c2[:, 128:256]]
        SU = [c2[:, 256:384], c2[:, 384:512]]
        PEN = [c2[:, 512:640], c2[:, 640:768]]
        IDF = c2[:, 768:896]
        NEGONES = c2[:, 896:1024]
        cb = sb("cb", [128, NCB], BF16)
        cb_b = B("cb")
        S.dma("sp", cb[:, :], C.cstb_d[:, :], writes=[cb_b])

        def lmask(d, l):
            o = (d * 7 + (l - 1)) * 128
            return cb[:, o:o + 128]

        cw = sb("cw", [128, 24 * 3], F32)
        cw_b = B("cw")
        S.dma("sp", cw[:, :], C.gdn_cw_d[:, :], writes=[cw_b])
        gn = sb("gn2", [128, 1], F32)
        gn_b = B("gn2")
        S.dma("sp", gn[:, :], C.gdn_norm_g.rearrange("o d -> d o"), writes=[gn_b])
        hp = sb("hp", [16, 4], F32)
        hp_b = B("hp")
        S.dma("sp", hp[:, 0:2], C.gdn_hp_d[:, :], writes=[hp_b])

        c1 = sb("c1", [128, T], F32)
        c1_b = B("c1")
        gb48 = c1
        gb48_b = c1_b
        S.op("pool", lambda e: e.memset(gb48[:, :], 0.0), writes=[gb48_b])
        S.dma("sp", gb48[0:16, :], scr[C_AB:C_AB + 16, :], writes=[gb48_b])
        S.dma("sp", gb48[32:48, :], scr[C_AB + 16:C_AB + 32, :], writes=[gb48_b])
        S.op("act", lambda e: e.activation(out=hp[:, 2:3], in_=hp[:, 0:1], func=AF.Exp), reads=[hp_b], writes=[hp_b])
        S.op("dve", lambda e: e.tensor_scalar(out=hp[:, 2:3], in0=hp[:, 2:3], scalar1=-1.0, scalar2=None, op0=ALU.mult),
             reads=[hp_b], writes=[hp_b])
        S.op("act", lambda e: e.activation(out=gb48[0:16, :], in_=gb48[0:16, :], func=AF.Exp, bias=hp[:, 1:2]),
             reads=[gb48_b, hp_b], writes=[gb48_b])
        S.op("act", lambda e: e.activation(out=gb48[0:16, :], in_=gb48[0:16, :], func=AF.Ln, bias=1.0),
             reads=[gb48_b], writes=[gb48_b])
        S.op("dve", lambda e: e.tensor_scalar(out=gb48[0:16, :], in0=gb48[0:16, :], scalar1=hp[:, 2:3], scalar2=None, op0=ALU.mult),
             reads=[gb48_b, hp_b], writes=[gb48_b])
        S.op("act", lambda e: e.activation(out=gb48[32:48, :], in_=gb48[32:48, :], func=AF.Exp, scale=-1.0),
             reads=[gb48_b], writes=[gb48_b])
        S.op("act", lambda e: e.activation(out=gb48[32:48, :], in_=gb48[32:48, :], func=AF.Ln, bias=1.0),
             reads=[gb48_b], writes=[gb48_b])
        S.op("act", lambda e: e.activation(out=gb48[32:48, :], in_=gb48[32:48, :], func=AF.Exp, scale=-1.0),
             reads=[gb48_b], writes=[gb48_b])
        gtm = sb("gtm", [128, NT, 48], F32)
        gtm_b = B("gtm")
        for g in range(4):
            pi = 2 + g
            for q in range(4):
                t = 4 * g + q
                S.op("pe", lambda e, pi=pi, q=q, t=t: e.transpose(
                    out=psum[pi][:, q * 48:(q + 1) * 48], in_=gb48[0:48, t * 128:(t + 1) * 128], identity=IDF[0:48, 0:48]),
                    reads=[gb48_b, c2_b], writes=[psum_b[pi][0]], skip_same=True)
            S.op("act", lambda e, pi=pi, g=g: e.activation(
                out=gtm[:, 4 * g:4 * g + 4, :], in_=psum[pi][:, 0:192].rearrange("p (a b) -> p a b", a=4), func=AF.Copy),
                reads=[psum_b[pi][0]], writes=[gtm_b])
        egtm = sb("egtm", [128, NT, 16], F32)
        egtm_b = B("egtm")
        ektm = sb("ektm", [128, NT, 16], F32)
        ektm_b = B("ektm")
        for (mats, dst, dst_b, pi) in ((U, egtm, egtm_b, 6), (SU, ektm, ektm_b, 7)):
            for t in range(NT):
                for d in range(2):
                    S.op("pe", lambda e, pi=pi, t=t, d=d, mats=mats: e.matmul(
                        out=psum[pi][:, t * 16 + d * 8:t * 16 + d * 8 + 8], lhsT=mats[d], rhs=gtm[:, t, d * 8:d * 8 + 8],
                        start=True, stop=True),
                        reads=[gtm_b, c2_b], writes=[psum_b[pi][0]], skip_same=True)
            S.op("act", lambda e, pi=pi, dst=dst: e.activation(
                out=dst[:, :, :], in_=psum[pi][:, 0:256].rearrange("p (a b) -> p a b", a=NT), func=AF.Exp),
                reads=[psum_b[pi][0]], writes=[dst_b])

        pin = [sb("pin%d" % i, [128, T + 2], F32) for i in range(1)]
        pin_b = [B("pin") for i in range(1)]
        for i in range(1):
            S.op("pool", lambda e, i=i: e.memset(pin[i][:, :], 0.0), writes=[pin_b[i]])
        tB = sb("tB", [128, T], F32)
        tB_b = B("tB")
        qT = sb("qT", [128, T], BF16)
        qT_b = B("qT")
        kT = sb("kT", [128, T], BF16)
        kT_b = B("kT")
        vT = sb("vT", [128, T], BF16)
        vT_b = B("vT")
        sz = sb("sz", [128, T], BF16)
        sz_b = B("sz")
        ktm = sb("ktm", [128, NT, 128], BF16)
        ktm_b = B("ktm")
        vtm = sb("vtm2", [128, NT, 128], BF16)
        vtm_b = B("vtm2")
        kg = [sb("kg%d" % d, [128, NT, 128], BF16) for d in range(2)]
        kg_b = [B("kg") for d in range(2)]
        ktl = [sb("ktl%d" % d, [128, NT, 128], BF16) for d in range(2)]
        ktl_b = [B("ktl") for d in range(2)]
        qdT = [sb("qdT%d" % d, [128, T], BF16) for d in range(2)]
        qdT_b = [B("qdT") for d in range(2)]
        VT = [sb("VT%d" % d, [128, T], BF16) for d in range(2)]
        VT_b = [B("VT") for d in range(2)]
        atT = [sb("atT%d" % d, [128, T], BF16) for d in range(2)]
        atT_b = [B("atT") for d in range(2)]
        wpn = [sb("wpn%d" % d, [128, T], BF16) for d in range(2)]
        wpn_b = [B("wpn") for d in range(2)]
        dcl = sb("dcl2", [128, 2, NT], F32)
        dcl_b = [B("dcl20"), B("dcl21")]
        gU = sb("gU", [128, NT, 128], F32)
        gU_b = B("gU")
        DTs = [sb("DT%d" % k, [128, 512], F32) for k in range(4)]
        DTs_b = [B("DT") for k in range(4)]
        ebs = [sb("eb%d" % k, [128, 512], F32) for k in range(4)]
        ebs_b = [B("eb") for k in range(4)]
        NTs = [sb("NT%d" % k, [128, 512], BF16) for k in range(4)]
        NTs_b = [B("NT") for k in range(4)]
        MTs = [sb("MT%d" % k, [128, 512], BF16) for k in range(4)]
        MTs_b = [B("MT") for k in range(4)]
        Xs = [sb("Xs%d" % k, [128, 512], BF16) for k in range(4)]
        Xs_b = [B("Xs") for k in range(4)]
        Ys = [sb("Ys%d" % k, [128, 512], BF16) for k in range(4)]
        Ys_b = [B("Ys") for k in range(4)]
        Ps = [sb("Ps%d" % k, [128, 512], BF16) for k in range(4)]
        Ps_b = [B("Ps") for k in range(4)]
        S32 = [sb("S32_%d" % d, [128, 128], F32) for d in range(2)]
        S32_b = [B("S32") for d in range(2)]
        Sbf = [sb("Sbf2_%d" % d, [128, 128], BF16) for d in range(2)]
        Sbf_b = [B("Sbf2") for d in range(2)]
        vnb = [sb("vnb%d" % d, [128, 128], BF16) for d in range(2)]
        vnb_b = [B("vnb") for d in range(2)]
        oacc = [sb("oacc%d" % d, [128, T], F32) for d in range(2)]
        oacc_b = [B("oacc") for d in range(2)]
        sq = sb("sq2", [128, 512], F32)
        sq_b = B("sq2")
        rs = sb("rs2", [128, 512], F32)
        rs_b = B("rs2")
        tt = sb("tt2", [128, 512], F32)
        tt_b = B("tt2")
        yb = [sb("yb2_%d" % i, [128, T], BF16) for i in range(1)] * 2
        yb_b = [B("yb2")] * 2

        pc = [0]

        def bank():
            pi = pc[0] % 8
            pc[0] += 1
            return pi

        pinc = [0]

        def silu_inplace(x, x_b, n=T):
            S.op("act", lambda e: e.activation(out=tB[:, 0:n], in_=x[:, 0:n], func=AF.Exp, scale=-1.0), reads=[x_b], writes=[tB_b])
            S.op("act", lambda e: e.activation(out=tB[:, 0:n], in_=tB[:, 0:n], func=AF.Ln, bias=1.0), reads=[tB_b], writes=[tB_b])
            S.op("act", lambda e: e.activation(out=tB[:, 0:n], in_=tB[:, 0:n], func=AF.Exp, scale=-1.0), reads=[tB_b], writes=[tB_b])
            S.op("dve", lambda e: e.tensor_tensor(out=x[:, 0:n], in0=x[:, 0:n], in1=tB[:, 0:n], op=ALU.mult),
                 reads=[x_b, tB_b], writes=[x_b])

        def conv_silu(row0, blk):
            i = 0
            S.dma("sp", pin[i][:, 1:T + 1], scr[row0:row0 + 128, :], writes=[pin_b[i]])
            w0 = cw[:, blk * 3:blk * 3 + 1]
            w1 = cw[:, blk * 3 + 1:blk * 3 + 2]
            w2 = cw[:, blk * 3 + 2:blk * 3 + 3]
            S.op("act", lambda e, i=i, w0=w0: e.activation(out=c1[:, :], in_=pin[i][:, 0:T], func=AF.Copy, scale=w0),
                 reads=[pin_b[i], cw_b], writes=[c1_b])
            S.op("dve", lambda e, i=i, w1=w1: e.scalar_tensor_tensor(
                out=c1[:, :], in0=pin[i][:, 1:T + 1], scalar=w1, in1=c1[:, :], op0=ALU.mult, op1=ALU.add),
                reads=[pin_b[i], cw_b, c1_b], writes=[c1_b])
            S.op("dve", lambda e, i=i, w2=w2: e.scalar_tensor_tensor(
                out=c1[:, :], in0=pin[i][:, 2:T + 2], scalar=w2, in1=c1[:, :], op0=ALU.mult, op1=ALU.add),
                reads=[pin_b[i], cw_b, c1_b], writes=[c1_b])
            silu_inplace(c1, c1_b)

        def l2norm_to(dst, dst_b, scale):
            S.op("act", lambda e: e.activation(out=tB[:, :], in_=c1[:, :], func=AF.Square), reads=[c1_b], writes=[tB_b])
            for blk in range(4):
                sl = slice(blk * 512, (blk + 1) * 512)
                pi = bank()
                S.op("pe", lambda e, pi=pi, sl=sl: e.matmul(out=psum[pi][:, :], lhsT=ONES, rhs=tB[:, sl], start=True, stop=True),
                     reads=[tB_b, cst_b], writes=psum_b[pi], skip_same=True)
                S.op("act", lambda e, pi=pi: e.activation(out=rs[:, :], in_=psum[pi][:, :], func=AF.Ln, bias=EPS),
                     reads=psum_b[pi], writes=[rs_b])
                S.op("act", lambda e: e.activation(out=rs[:, :], in_=rs[:, :], func=AF.Exp, scale=-0.5), reads=[rs_b], writes=[rs_b])
                S.op("dve", lambda e, sl=sl: e.scalar_tensor_tensor(
                    out=dst[:, sl], in0=c1[:, sl], scalar=float(scale), in1=rs[:, :], op0=ALU.mult, op1=ALU.mult),
                    reads=[c1_b, rs_b], writes=[dst_b])

        def to_token_major(src, src_b, dst, dst_b):
            for g in range(4):
                pi = bank()
                pt = psum[pi].bitcast(BF16)
                for q in range(4):
                    c = 4 * g + q
                    S.op("pe", lambda e, pt=pt, q=q, c=c: e.transpose(
                        out=pt[:, q * 128:(q + 1) * 128], in_=src[:, c * 128:(c + 1) * 128], identity=ident_bf[:, :]),
                        reads=[src_b, ident_bf_b], writes=[psum_b[pi][q]], skip_same=True)
                S.op("act", lambda e, pt=pt, g=g: e.activation(
                    out=dst[:, 4 * g:4 * g + 4, :], in_=pt[:, 0:512].rearrange("p (a b) -> p a b", a=4), func=AF.Copy),
                    reads=psum_b[pi], writes=[dst_b])

        def bc4(ap2d):
            return ap2d.unsqueeze(1).broadcast_to([128, 4, 128])

        def r4(ap):
            return ap.rearrange("p (a b) -> p a b", a=4)

        for h in range(nheads):
            conv_silu(C_DQ + h * 128, h)
            l2norm_to(qT, qT_b, 128 ** -0.5)
            conv_silu(C_DK + h * 128, 8 + h)
            l2norm_to(kT, kT_b, 1.0)
            conv_silu(C_DV + h * 128, 16 + h)
            S.op("act", lambda e: e.activation(out=vT[:, :], in_=c1[:, :], func=AF.Copy), reads=[c1_b], writes=[vT_b])
            to_token_major(kT, kT_b, ktm, ktm_b)
            to_token_major(vT, vT_b, vtm, vtm_b)
            S.dma("sp", c1[:, :], scr[C_DZ + h * 128:C_DZ + (h + 1) * 128, :], writes=[c1_b])
            silu_inplace(c1, c1_b)
            S.op("act", lambda e: e.activation(out=sz[:, :], in_=c1[:, :], func=AF.Copy), reads=[c1_b], writes=[sz_b])
            if "gdn_qkv" in C.dbg_out and h == 0:
                S.dma("sp", C.dbg_out["gdn_qkv"][0:128, :], qT[:, :], reads=[qT_b])
                S.dma("sp", C.dbg_out["gdn_qkv"][128:256, :], kT[:, :], reads=[kT_b])
                S.dma("sp", C.dbg_out["gdn_qkv"][256:384, :], vT[:, :], reads=[vT_b])

            for d in range(2):
                col = d * 8 + h
                gcol = gtm[:, :, col:col + 1]
                bcol = gtm[:, :, 32 + col:32 + col + 1]
                S.op("dve", lambda e, d=d, col=col: e.tensor_tensor(
                    out=kg[d][:, :, :], in0=ktm[:, :, :], in1=egtm[:, :, col:col + 1].broadcast_to([128, NT, 128]), op=ALU.mult),
                    reads=[ktm_b, egtm_b], writes=[kg_b[d]])
                S.op("dve", lambda e, d=d, col=col: e.tensor_tensor(
                    out=ktl[d][:, :, :], in0=ktm[:, :, :], in1=ektm[:, :, col:col + 1].broadcast_to([128, NT, 128]), op=ALU.mult),
                    reads=[ktm_b, ektm_b], writes=[ktl_b[d]])
                S.op("pool", lambda e, d=d, gcol=gcol: e.tensor_tensor(
                    out=gU[:, :, :], in0=U[d].unsqueeze(1).broadcast_to([128, NT, 128]), in1=gcol.broadcast_to([128, NT, 128]), op=ALU.mult),
                    reads=[c2_b, gtm_b], writes=[gU_b])
                last = 127 if d == 0 else 0
                KB = 4
                G = list(range(4))

                def qsl(q):
                    return slice(q * 128, (q + 1) * 128)

                pA = [bank() for k in G]
                for k in G:
                    for q in range(4):
                        c = 4 * k + q
                        S.op("pe", lambda e, p=pA[k], q=q, c=c: e.matmul(
                            out=psum[p][:, qsl(q)], lhsT=ONES, rhs=gU[:, c, :], start=True, stop=False),
                            reads=[gU_b, cst_b], writes=psum_b[pA[k]], skip_same=True)
                        S.op("pe", lambda e, p=pA[k], q=q, c=c: e.matmul(
                            out=psum[p][:, qsl(q)], lhsT=gU[:, c, :], rhs=NEGONES, start=False, stop=False),
                            reads=[gU_b, c2_b], writes=psum_b[pA[k]], skip_same=True)
                        S.op("pe", lambda e, p=pA[k], q=q, d=d: e.matmul(
                            out=psum[p][:, qsl(q)], lhsT=IDF, rhs=PEN[d], start=False, stop=True),
                            reads=[c2_b], writes=psum_b[pA[k]], skip_same=True)
                for k in G:
                    S.op("act", lambda e, p=pA[k], k=k: e.activation(out=DTs[k][:, :], in_=psum[p][:, :], func=AF.Exp),
                         reads=psum_b[pA[k]], writes=[DTs_b[k]])
                pB = [bank() for k in G]
                for k in G:
                    for q in range(4):
                        c = 4 * k + q
                        S.op("pe", lambda e, p=pB[k], q=q, c=c: e.matmul(
                            out=psum[p][:, qsl(q)], lhsT=NEGONES, rhs=gU[:, c, :], start=True, stop=True),
                            reads=[gU_b, c2_b], writes=psum_b[pB[k]], skip_same=True)
                for k in G:
                    S.op("act", lambda e, p=pB[k], k=k: e.activation(out=ebs[k][:, :], in_=psum[p][:, :], func=AF.Exp, scale=-1.0),
                         reads=psum_b[pB[k]], writes=[ebs_b[k]])
                    S.op("act", lambda e, p=pB[k], k=k, d=d, last=last: e.activation(
                        out=dcl[:, d, 4 * k:4 * k + 4], in_=r4(psum[p][:, :])[:, :, last], func=AF.Exp, scale=-1.0),
                        reads=psum_b[pB[k]], writes=[dcl_b[d]])
                for k in G:
                    sl = slice(k * 512, (k + 1) * 512)
                    S.op("dve", lambda e, d=d, sl=sl, k=k: e.tensor_tensor(out=qdT[d][:, sl], in0=qT[:, sl], in1=ebs[k][:, :], op=ALU.mult),
                         reads=[qT_b, ebs_b[k]], writes=[qdT_b[d]])
                pD = [bank() for k in G]
                for k in G:
                    for q in range(4):
                        cs_ = slice((4 * k + q) * 128, (4 * k + q + 1) * 128)
                        S.op("pe", lambda e, p=pD[k], q=q, cs_=cs_: e.matmul(
                            out=psum[p][:, qsl(q)], lhsT=kT[:, cs_], rhs=qT[:, cs_], start=True, stop=True),
                            reads=[kT_b, qT_b], writes=psum_b[pD[k]], skip_same=True)
                for k in G:
                    sl = slice(k * 512, (k + 1) * 512)
                    S.op("dve", lambda e, p=pD[k], d=d, sl=sl, k=k: e.tensor_tensor(out=atT[d][:, sl], in0=psum[p][:, :], in1=DTs[k][:, :], op=ALU.mult),
                         reads=psum_b[pD[k]] + [DTs_b[k]], writes=[atT_b[d]])
                for k in G:
                    S.op("pool", lambda e, k=k, bcol=bcol: e.tensor_tensor(
                        out=r4(DTs[k][:, :]), in0=r4(DTs[k][:, :]), in1=bcol[:, 4 * k:4 * k + 4, :].broadcast_to([128, 4, 128]), op=ALU.mult),
                        reads=[DTs_b[k], gtm_b], writes=[DTs_b[k]])
                pC = [bank() for k in G]
                for k in G:
                    for q in range(4):
                        cs_ = slice((4 * k + q) * 128, (4 * k + q + 1) * 128)
                        S.op("pe", lambda e, p=pC[k], q=q, cs_=cs_: e.matmul(
                            out=psum[p][:, qsl(q)], lhsT=kT[:, cs_], rhs=kT[:, cs_], start=True, stop=True),
                            reads=[kT_b], writes=psum_b[pC[k]], skip_same=True)
                for k in G:
                    S.op("dve", lambda e, p=pC[k], k=k: e.tensor_tensor(out=NTs[k][:, :], in0=psum[p][:, :], in1=DTs[k][:, :], op=ALU.mult),
                         reads=psum_b[pC[k]] + [DTs_b[k]], writes=[NTs_b[k]])
                for k in G:
                    S.op("pool", lambda e, d=d, k=k: e.tensor_tensor(out=r4(MTs[k][:, :]), in0=r4(NTs[k][:, :]), in1=bc4(lmask(d, 1)), op=ALU.mult),
                         reads=[NTs_b[k], cb_b], writes=[MTs_b[k]])
                    S.op("pool", lambda e, k=k: e.tensor_tensor(out=r4(Ys[k][:, :]), in0=bc4(ident_bf[:, :]), in1=r4(MTs[k][:, :]), op=ALU.subtract),
                         reads=[MTs_b[k], ident_bf_b], writes=[Ys_b[k]])
                pE = [bank() for k in G]
                for k in G:
                    ptE = psum[pE[k]].bitcast(BF16)
                    for q in range(4):
                        S.op("pe", lambda e, ptE=ptE, q=q, k=k: e.transpose(out=ptE[:, qsl(q)], in_=Ys[k][:, qsl(q)], identity=ident_bf[:, :]),
                             reads=[Ys_b[k], ident_bf_b], writes=psum_b[pE[k]], skip_same=True)
                for k in G:
                    ptE = psum[pE[k]].bitcast(BF16)
                    S.op("act", lambda e, ptE=ptE, k=k: e.activation(out=Xs[k][:, :], in_=ptE[:, 0:512], func=AF.Copy),
                         reads=psum_b[pE[k]], writes=[Xs_b[k]])
                for l in range(2, 8):
                    for k in G:
                        S.op("pool", lambda e, d=d, l=l, k=k: e.tensor_tensor(out=r4(MTs[k][:, :]), in0=r4(NTs[k][:, :]), in1=bc4(lmask(d, l)), op=ALU.mult),
                             reads=[NTs_b[k], cb_b], writes=[MTs_b[k]])
                    pF = [bank() for k in G]
                    for k in G:
                        for q in range(4):
                            S.op("pe", lambda e, p=pF[k], q=q, k=k: e.matmul(out=psum[p][:, qsl(q)], lhsT=MTs[k][:, qsl(q)], rhs=Xs[k][:, qsl(q)], start=True, stop=True),
                                 reads=[MTs_b[k], Xs_b[k]], writes=psum_b[pF[k]], skip_same=True)
                    for k in G:
                        S.op("act", lambda e, p=pF[k], k=k: e.activation(out=Ps[k][:, :], in_=psum[p][:, :], func=AF.Copy),
                             reads=psum_b[pF[k]], writes=[Ps_b[k]])
                    pG = [bank() for k in G]
                    for k in G:
                        for q in range(4):
                            S.op("pe", lambda e, p=pG[k], q=q, k=k: e.matmul(out=psum[p][:, qsl(q)], lhsT=Ys[k][:, qsl(q)], rhs=Ps[k][:, qsl(q)], start=True, stop=True),
                                 reads=[Ys_b[k], Ps_b[k]], writes=psum_b[pG[k]], skip_same=True)
                    for k in G:
                        S.op("dve", lambda e, p=pG[k], k=k: e.tensor_tensor(out=Xs[k][:, :], in0=Xs[k][:, :], in1=psum[p][:, :], op=ALU.subtract),
                             reads=psum_b[pG[k]] + [Xs_b[k]], writes=[Xs_b[k]])
                    pE = [bank() for k in G]
                    for k in G:
                        ptE = psum[pE[k]].bitcast(BF16)
                        for q in range(4):
                            S.op("pe", lambda e, ptE=ptE, q=q, k=k: e.transpose(out=ptE[:, qsl(q)], in_=Xs[k][:, qsl(q)], identity=ident_bf[:, :]),
                                 reads=[Xs_b[k], ident_bf_b], writes=psum_b[pE[k]], skip_same=True)
                    for k in G:
                        ptE = psum[pE[k]].bitcast(BF16)
                        sl = slice(k * 512, (k + 1) * 512)
                        if l < 7:
                            S.op("act", lambda e, ptE=ptE, k=k: e.activation(out=Ys[k][:, :], in_=ptE[:, 0:512], func=AF.Copy),
                                 reads=psum_b[pE[k]], writes=[Ys_b[k]])
                        else:
                            S.op("act", lambda e, ptE=ptE, d=d, sl=sl: e.activation(out=VT[d][:, sl], in_=ptE[:, 0:512], func=AF.Copy),
                                 reads=psum_b[pE[k]], writes=[VT_b[d]])
                pH = [bank() for k in G]
                for k in G:
                    for q in range(4):
                        c = 4 * k + q
                        cs_ = slice(c * 128, (c + 1) * 128)
                        S.op("pe", lambda e, p=pH[k], q=q, c=c, cs_=cs_, d=d: e.matmul(
                            out=psum[p][:, qsl(q)], lhsT=kg[d][:, c, :], rhs=VT[d][:, cs_], start=True, stop=True),
                            reads=[kg_b[d], VT_b[d]], writes=psum_b[pH[k]], skip_same=True)
                for k in G:
                    sl = slice(k * 512, (k + 1) * 512)
                    S.op("act", lambda e, p=pH[k], d=d, sl=sl: e.activation(out=wpn[d][:, sl], in_=psum[p][:, :], func=AF.Copy, scale=-1.0),
                         reads=psum_b[pH[k]], writes=[wpn_b[d]])

            for d in range(2):
                S.op("pool", lambda e, d=d: e.memset(S32[d][:, :], 0.0), writes=[S32_b[d]])
                S.op("pool", lambda e, d=d: e.memset(Sbf[d][:, :], 0.0), writes=[Sbf_b[d]])
            for s in range(NT):
                for d in range(2):
                    c = s if d == 0 else NT - 1 - s
                    cs_ = slice(c * 128, (c + 1) * 128)
                    col = d * 8 + h
                    pv, pS, pO = 3 * d, 3 * d + 1, 3 * d + 2
                    S.op("pe", lambda e, pv=pv, d=d, c=c, cs_=cs_: e.matmul(
                        out=psum[pv][:, 0:128], lhsT=VT[d][:, cs_], rhs=vtm[:, c, :], start=True, stop=False),
                        reads=[VT_b[d], vtm_b], writes=[psum_b[pv][0]], skip_same=True)
                    S.op("pe", lambda e, pv=pv, d=d, cs_=cs_: e.matmul(
                        out=psum[pv][:, 0:128], lhsT=wpn[d][:, cs_], rhs=Sbf[d][:, :], start=False, stop=True),
                        reads=[wpn_b[d], Sbf_b[d]], writes=[psum_b[pv][0]], skip_same=True)
                    S.op("act", lambda e, pv=pv, d=d, c=c, col=col: e.activation(
                        out=vnb[d][:, :], in_=psum[pv][:, 0:128], func=AF.Copy, scale=gtm[:, c, 32 + col:32 + col + 1]),
                        reads=[psum_b[pv][0], gtm_b], writes=[vnb_b[d]])
                    S.op("pe", lambda e, pS=pS, d=d, c=c: e.matmul(
                        out=psum[pS][:, 0:128], lhsT=ktl[d][:, c, :], rhs=vnb[d][:, :], start=True, stop=True),
                        reads=[ktl_b[d], vnb_b[d]], writes=[psum_b[pS][1]], skip_same=True)
                    S.op("pe", lambda e, pO=pO, d=d, cs_=cs_: e.matmul(
                        out=psum[pO][:, 0:128], lhsT=Sbf[d][:, :], rhs=qdT[d][:, cs_], start=True, stop=False),
                        reads=[Sbf_b[d], qdT_b[d]], writes=[psum_b[pO][2]], skip_same=True)
                    S.op("pe", lambda e, pO=pO, d=d, cs_=cs_: e.matmul(
                        out=psum[pO][:, 0:128], lhsT=vnb[d][:, :], rhs=atT[d][:, cs_], start=False, stop=True),
                        reads=[vnb_b[d], atT_b[d]], writes=[psum_b[pO][2]], skip_same=True)
                    S.op("dve", lambda e, pS=pS, d=d, c=c: e.scalar_tensor_tensor(
                        out=S32[d][:, :], in0=S32[d][:, :], scalar=dcl[:, d, c:c + 1], in1=psum[pS][:, 0:128],
                        op0=ALU.mult, op1=ALU.add),
                        reads=[S32_b[d], dcl_b[d], psum_b[pS][1]], writes=[S32_b[d]])
                    S.op("act", lambda e, d=d: e.activation(out=Sbf[d][:, :], in_=S32[d][:, :], func=AF.Copy),
                         reads=[S32_b[d]], writes=[Sbf_b[d]])
                    S.op("dve", lambda e, pO=pO, d=d, cs_=cs_: e.tensor_copy(out=oacc[d][:, cs_], in_=psum[pO][:, 0:128]),
                         reads=[psum_b[pO][2]], writes=[oacc_b[d]])

            yi = h % 2
            S.op("dve", lambda e: e.tensor_tensor(out=oacc[0][:, :], in0=oacc[0][:, :], in1=oacc[1][:, :], op=ALU.add),
                 reads=[oacc_b[0], oacc_b[1]], writes=[oacc_b[0]])
            if "gdn_o" in C.dbg_out and h == 0:
                S.dma("sp", C.dbg_out["gdn_o"][:, :], oacc[0][:, :], reads=[oacc_b[0]])
            for blk in range(4):
                sl = slice(blk * 512, (blk + 1) * 512)
                pN = bank()
                S.op("act", lambda e, sl=sl: e.activation(out=sq[:, :], in_=oacc[0][:, sl], func=AF.Square),
                     reads=[oacc_b[0]], writes=[sq_b])
                S.op("pe", lambda e, pN=pN: e.matmul(out=psum[pN][:, :], lhsT=ONES, rhs=sq[:, :], start=True, stop=True),
                     reads=[sq_b, cst_b], writes=psum_b[pN], skip_same=True)
                S.op("act", lambda e, pN=pN: e.activation(out=rs[:, :], in_=psum[pN][:, :], func=AF.Ln, scale=1.0 / 128, bias=EPS),
                     reads=psum_b[pN], writes=[rs_b])
                S.op("act", lambda e: e.activation(out=rs[:, :], in_=rs[:, :], func=AF.Exp, scale=-0.5), reads=[rs_b], writes=[rs_b])
                S.op("dve", lambda e, sl=sl: e.scalar_tensor_tensor(
                    out=tt[:, :], in0=oacc[0][:, sl], scalar=gn[:, 0:1], in1=rs[:, :], op0=ALU.mult, op1=ALU.mult),
                    reads=[oacc_b[0], gn_b, rs_b], writes=[tt_b])
                S.op("dve", lambda e, sl=sl, yi=yi: e.tensor_tensor(out=yb[yi][:, sl], in0=tt[:, :], in1=sz[:, sl], op=ALU.mult),
                     reads=[tt_b, sz_b], writes=[yb_b[yi]])
            S.dma("sp", C.y_scr[1024 + h * 128:1024 + (h + 1) * 128, :], yb[yi][:, :], reads=[yb_b[yi]])


def phase_branch(C):
    nc, S, debug = C.nc, C.S, C.debug
    psum, psum_b = C.psum, C.psum_b
    scr = C.scr
    ident_bf, ident_bf_b = C.ident_bf, C.ident_bf_b
    with contextlib.ExitStack() as oes:
        mergedT = oes.enter_context(nc.sbuf_tensor("mergedT", [128, KC, T], BF16))
        mg_b = [Buf("mg%d" % t) for t in range(NT)]
        with contextlib.ExitStack() as pes:
            def sb(name, shape, dt):
                return pes.enter_context(nc.sbuf_tensor(name, list(shape), dt))
            yT = sb("yT", [128, KC, T], BF16)
            yT_b = [Buf("yT%d" % k) for k in range(KC)]
            y_r = C.y_scr.rearrange("(kc p) t -> p kc t", p=128)
            for k in range(KC):
                S.dma("sp", yT[:, k, :], y_r[:, k, :], writes=[yT_b[k]])
            NWB = 3
            wg = [sb("wbg%d" % i, [128, 8, 128], BF16) for i in range(NWB)]
            wd = [sb("wbd%d" % i, [128, 8, 128], BF16) for i in range(NWB)]
            wg_b = [Buf("wbg") for i in range(NWB)]
            wd_b = [Buf("wbd") for i in range(NWB)]
            gg = [sb("gg%d" % i, [128, T], F32) for i in range(2)]
            gd = [sb("gd%d" % i, [128, T], F32) for i in range(2)]
            gg_b = [Buf("gg") for i in range(2)]
            gd_b = [Buf("gd") for i in range(2)]
            sgg = [sb("sgg%d" % i, [128, 512], F32) for i in range(2)]
            sgd = [sb("sgd%d" % i, [128, 512], F32) for i in range(2)]
            sgg_b = [Buf("sgg") for i in range(2)]
            sgd_b = [Buf("sgd") for i in range(2)]
            t1 = [sb("t1_%d" % i, [128, 512], F32) for i in range(2)]
            t2 = [sb("t2_%d" % i, [128, 512], F32) for i in range(2)]
            t1_b = [Buf("t1") for i in range(2)]
            t2_b = [Buf("t2") for i in range(2)]
            wbg_r = C.w_branch_gla.rearrange("(kc p) n -> p kc n", p=128)
            wbd_r = C.w_branch_gdn.rearrange("(kc p) n -> p kc n", p=128)
            cnt = 0
            for db in range(KC):
                wi = db % NWB
                gi = db % 2
                S.dma("pool", wg[wi][:, :, :], wbg_r[:, :, db * 128:(db + 1) * 128], writes=[wg_b[wi]])
                S.dma("pool", wd[wi][:, :, :], wbd_r[:, :, db * 128:(db + 1) * 128], writes=[wd_b[wi]])
                S.dma("sp", gg[gi][:, :], scr[C_BG + db * 128:C_BG + (db + 1) * 128, :], writes=[gg_b[gi]])
                S.dma("sp", gd[gi][:, :], scr[C_BD + db * 128:C_BD + (db + 1) * 128, :], writes=[gd_b[gi]])
                for blk in range(4):
                    sl = slice(blk * 512, (blk + 1) * 512)
                    pG = (2 * cnt) % 8
                    pD = (2 * cnt + 1) % 8
                    j = cnt % 2
                    cnt += 1
                    for kc in range(8):
                        S.op("pe", lambda e, pG=pG, wi=wi, kc=kc, sl=sl: e.matmul(
                            out=psum[pG][:, :], lhsT=wg[wi][:, kc, :], rhs=yT[:, kc, sl], start=(kc == 0), stop=(kc == 7)),
                            reads=[wg_b[wi], yT_b[kc]], writes=psum_b[pG], skip_same=True)
                    for kc in range(8):
                        S.op("pe", lambda e, pD=pD, wi=wi, kc=kc, sl=sl: e.matmul(
                            out=psum[pD][:, :], lhsT=wd[wi][:, kc, :], rhs=yT[:, 8 + kc, sl], start=(kc == 0), stop=(kc == 7)),
                            reads=[wd_b[wi], yT_b[8 + kc]], writes=psum_b[pD], skip_same=True)
                    S.op("act", lambda e, j=j, gi=gi, sl=sl: e.activation(out=sgg[j][:, :], in_=gg[gi][:, sl], func=AF.Sigmoid),
                         reads=[gg_b[gi]], writes=[sgg_b[j]])
                    S.op("act", lambda e, j=j, gi=gi, sl=sl: e.activation(out=sgd[j][:, :], in_=gd[gi][:, sl], func=AF.Sigmoid),
                         reads=[gd_b[gi]], writes=[sgd_b[j]])
                    S.op("dve", lambda e, j=j, pG=pG: e.tensor_tensor(out=t1[j][:, :], in0=psum[pG][:, :], in1=sgg[j][:, :], op=ALU.mult),
                         reads=psum_b[pG] + [sgg_b[j]], writes=[t1_b[j]])
                    S.op("dve", lambda e, j=j, pD=pD: e.tensor_tensor(out=t2[j][:, :], in0=psum[pD][:, :], in1=sgd[j][:, :], op=ALU.mult),
                         reads=psum_b[pD] + [sgd_b[j]], writes=[t2_b[j]])
                    S.op("dve", lambda e, j=j, db=db, sl=sl: e.tensor_tensor(out=mergedT[:, db, sl], in0=t1[j][:, :], in1=t2[j][:, :], op=ALU.add),
                         reads=[t1_b[j], t2_b[j]], writes=mg_b[4 * blk:4 * blk + 4])
        S.barrier()
        if "merged" in C.dbg_out:
            S.dma("sp", C.dbg_out["merged"].rearrange("(kc p) t -> p kc t", p=128), mergedT[:, :, :], reads=mg_b)
        with contextlib.ExitStack() as pes:
            def sb(name, shape, dt):
                return pes.enter_context(nc.sbuf_tensor(name, list(shape), dt))
            Wout = sb("Wout", [128, KC, D], BF16)
            Wout_b = [Buf("Wout%d" % k) for k in range(KC)]
            wo_r = C.w_out.rearrange("(kc p) n -> p kc n", p=128)
            for k in range(KC):
                S.dma("pool", Wout[:, k, :], wo_r[:, k, :], writes=[Wout_b[k]], max_dma_last_dim=4096)
            g2b = sb("g2b", [128, D], F32)
            g2b_b = Buf("g2b")
            S.dma("sp", g2b[:, :], C.norm2_g.partition_broadcast(128), writes=[g2b_b])
            xt = [sb("xt2_%d" % i, [128, D], F32) for i in range(2)]
            xt_b = [Buf("xt2") for i in range(2)]
            x1t = [sb("x1t%d" % i, [128, D], F32) for i in range(2)]
            x1t_b = [Buf("x1t") for i in range(2)]
            junk = sb("junk2", [128, D], BF16)
            junk_b = Buf("junk2")
            hb = [sb("hb2_%d" % i, [128, D], BF16) for i in range(2)]
            hb_b = [Buf("hb2") for i in range(2)]
            hst = [sb("hst%d" % i, [128, KC, 128], BF16) for i in range(2)]
            hst_b = [Buf("hst") for i in range(2)]
            stat = sb("stat2", [128, 4 * NT], F32)
            stat_b = [Buf("stat2") for i in range(NT)]
            h2_r = C.h2_scr.rearrange("(kc p) t -> p kc t", p=128)
            pcnt = 0
            for t in range(NT):
                i = t % 2
                ts_ = slice(t * 128, (t + 1) * 128)
                S.dma("sp", xt[i][:, :], C.x[ts_, :], writes=[xt_b[i]])
                for cb_ in range(4):
                    pi = cb_ + 4 * (t % 2)
                    csl = slice(cb_ * 512, (cb_ + 1) * 512)
                    for kc in range(KC):
                        S.op("pe", lambda e, pi=pi, kc=kc, ts_=ts_, csl=csl: e.matmul(
                            out=psum[pi][:, :], lhsT=mergedT[:, kc, ts_], rhs=Wout[:, kc, csl], start=(kc == 0), stop=(kc == KC - 1)),
                            reads=[mg_b[t], Wout_b[kc]], writes=psum_b[pi], skip_same=True)
                    S.op("dve", lambda e, pi=pi, i=i, csl=csl: e.tensor_tensor(out=x1t[i][:, csl], in0=psum[pi][:, :], in1=xt[i][:, csl], op=ALU.add),
                         reads=psum_b[pi] + [xt_b[i]], writes=[x1t_b[i]])
                S.dma("sp", C.x1_scr[ts_, :], x1t[i][:, :], reads=[x1t_b[i]])
                ss = stat[:, 4 * t:4 * t + 1]
                lnv = stat[:, 4 * t + 1:4 * t + 2]
                rstd = stat[:, 4 * t + 2:4 * t + 3]
                S.op("act", lambda e, i=i, ss=ss: e.activation(out=junk[:, :], in_=x1t[i][:, :], func=AF.Square, accum_out=ss),
                     reads=[x1t_b[i]], writes=[junk_b, stat_b[t]])
                S.op("act", lambda e, ss=ss, lnv=lnv: e.activation(out=lnv, in_=ss, func=AF.Ln, scale=1.0 / D, bias=EPS),
                     reads=[stat_b[t]], writes=[stat_b[t]])
                S.op("act", lambda e, rstd=rstd, lnv=lnv: e.activation(out=rstd, in_=lnv, func=AF.Exp, scale=-0.5),
                     reads=[stat_b[t]], writes=[stat_b[t]])
                S.op("dve", lambda e, i=i, rstd=rstd: e.scalar_tensor_tensor(
                    out=hb[i][:, :], in0=x1t[i][:, :], scalar=rstd, in1=g2b[:, :], op0=ALU.mult, op1=ALU.mult),
                    reads=[x1t_b[i], stat_b[t], g2b_b], writes=[hb_b[i]])
                for g in range(4):
                    pi = (pcnt % 2) * 4 + (3 - g)
                    pt = psum[pi].bitcast(BF16)
                    for q in range(4):
                        kc = g * 4 + q
                        S.op("pe", lambda e, pt=pt, q=q, i=i, kc=kc: e.transpose(
                            out=pt[:, q * 128:(q + 1) * 128], in_=hb[i][:, kc * 128:(kc + 1) * 128], identity=ident_bf[:, :]),
                            reads=[hb_b[i], ident_bf_b], writes=psum_b[pi], skip_same=True)
                    S.op("act", lambda e, pt=pt, g=g, i=i: e.activation(
                        out=hst[i][:, g * 4:(g + 1) * 4, :], in_=pt[:, 0:512].rearrange("p (a b) -> p a b", a=4), func=AF.Copy),
                        reads=psum_b[pi], writes=[hst_b[i]])
                pcnt += 1
                S.dma("sp", h2_r[:, :, ts_], hst[i][:, :, :], reads=[hst_b[i]])


def phase_ffn(C):
    nc, S, debug = C.nc, C.S, C.debug
    psum, psum_b = C.psum, C.psum_b
    TH = 1024
    NB3 = 342
    with contextlib.ExitStack() as pes:
        def sb(name, shape, dt):
            return pes.enter_context(nc.sbuf_tensor(name, list(shape), dt))
        h2T = sb("h2T", [128, KC, TH + 2], BF16)
        h2T_b = Buf("h2T")
        aT = sb("aT", [128, 44, TH], BF16)
        aT_b = [Buf("aT%d" % j) for j in range(44)]
        NW = 5
        wu = [sb("wu%d" % i, [128, KC, 128], BF16) for i in range(NW)]
        wu_b = [Buf("wu") for i in range(NW)]
        pg = [sb("pg%d" % i, [128, TH + 2], F32) for i in range(2)]
        pg_b = [Buf("pg") for i in range(2)]
        cg = [sb("cg%d" % i, [128, TH], F32) for i in range(2)]
        cg_b = [Buf("cg") for i in range(2)]
        fw = sb("fw", [128, 88 * 4], F32)
        fw_b = Buf("fw")
        S.dma("sp", fw[:, :], C.ffn_cw_d[:, :], writes=[fw_b])
        NWD = 2
        wdn = [sb("wdn%d" % i, [128, 44, 128], BF16) for i in range(NWD)]
        wdn_b = [Buf("wdn") for i in range(NWD)]
        x1s = [sb("x1s%d" % i, [128, 512], F32) for i in range(2)]
        x1s_b = [Buf("x1s") for i in range(2)]
        xo = [sb("xo%d" % i, [128, 512], F32) for i in range(2)]
        xo_b = [Buf("xo") for i in range(2)]
        wup_r = C.w_up.rearrange("(kc p) n -> p kc n", p=128)
        wdn_r = C.w_down.rearrange("(fc p) n -> p fc n", p=128)
        h2_r = C.h2_scr.rearrange("(kc p) t -> p kc t", p=128)
        ucnt = 0
        pcnt = 0
        dcnt = 0
        for half in range(2):
            t0 = half * TH
            S.op("pool", lambda e: e.memset(h2T[:, :, :], 0.0), writes=[h2T_b])
            lo = max(t0 - 1, 0)
            hi = min(t0 + TH + 1, T)
            S.dma("sp", h2T[:, :, lo - (t0 - 1):hi - (t0 - 1)], h2_r[:, :, lo:hi], writes=[h2T_b])
            ulist = [(j, which) for j in range(44) for which in range(2)]
            PF = NW - 1

            def issue_unit(k):
                j_, which_ = ulist[k]
                c0_ = which_ * D_FF + j_ * 128
                wi_ = (ucnt + k) % NW
                S.dma("pool", wu[wi_][:, :, :], wup_r[:, :, c0_:c0_ + 128], writes=[wu_b[wi_]])

            for k in range(min(PF, len(ulist))):
                issue_unit(k)
            for k, (j, which) in enumerate(ulist):
                if k + PF < len(ulist):
                    issue_unit(k + PF)
                blk = which * 44 + j
                wi = (ucnt + k) % NW
                pgi = k % 2
                for b3 in range(3):
                    pi = pcnt % 8
                    pcnt += 1
                    s3 = slice(b3 * NB3, (b3 + 1) * NB3)
                    for kc in range(KC):
                        S.op("pe", lambda e, pi=pi, wi=wi, kc=kc, s3=s3: e.matmul(
                            out=psum[pi][:, 0:NB3], lhsT=wu[wi][:, kc, :], rhs=h2T[:, kc, s3], start=(kc == 0), stop=(kc == KC - 1)),
                            reads=[wu_b[wi], h2T_b], writes=psum_b[pi], skip_same=True)
                    if pcnt % 2 == 0:
                        S.op("act", lambda e, pi=pi, pgi=pgi, s3=s3: e.activation(out=pg[pgi][:, s3], in_=psum[pi][:, 0:NB3], func=AF.Copy),
                             reads=psum_b[pi], writes=[pg_b[pgi]])
                    else:
                        S.op("dve", lambda e, pi=pi, pgi=pgi, s3=s3: e.tensor_copy(out=pg[pgi][:, s3], in_=psum[pi][:, 0:NB3]),
                             reads=psum_b[pi], writes=[pg_b[pgi]])
                w0 = fw[:, blk * 4:blk * 4 + 1]
                w1 = fw[:, blk * 4 + 1:blk * 4 + 2]
                w2 = fw[:, blk * 4 + 2:blk * 4 + 3]
                bb = fw[:, blk * 4 + 3:blk * 4 + 4]
                S.op("act", lambda e, pgi=pgi, which=which, w1=w1, bb=bb: e.activation(
                    out=cg[which][:, :], in_=pg[pgi][:, 1:TH + 1], func=AF.Identity, scale=w1, bias=bb),
                    reads=[pg_b[pgi], fw_b], writes=[cg_b[which]])
                S.op("dve", lambda e, pgi=pgi, which=which, w0=w0: e.scalar_tensor_tensor(
                    out=cg[which][:, :], in0=pg[pgi][:, 0:TH], scalar=w0, in1=cg[which][:, :], op0=ALU.mult, op1=ALU.add),
                    reads=[pg_b[pgi], fw_b, cg_b[which]], writes=[cg_b[which]])
                S.op("dve", lambda e, pgi=pgi, which=which, w2=w2: e.scalar_tensor_tensor(
                    out=cg[which][:, :], in0=pg[pgi][:, 2:TH + 2], scalar=w2, in1=cg[which][:, :], op0=ALU.mult, op1=ALU.add),
                    reads=[pg_b[pgi], fw_b, cg_b[which]], writes=[cg_b[which]])
                if which == 0:
                    S.op("act", lambda e: e.activation(out=cg[0][:, :], in_=cg[0][:, :], func=AF.Silu),
                         reads=[cg_b[0]], writes=[cg_b[0]])
                else:
                    S.op("dve", lambda e, j=j: e.tensor_tensor(out=aT[:, j, :], in0=cg[0][:, :], in1=cg[1][:, :], op=ALU.mult),
                         reads=[cg_b[0], cg_b[1]], writes=[aT_b[j]])
            ucnt += len(ulist)
            for cgp in range(4):
                units = []
                for u in range(4):
                    c0 = cgp * 512 + u * 128
                    wi = dcnt % NWD
                    dcnt += 1
                    S.dma("pool", wdn[wi][:, :, :], wdn_r[:, :, c0:c0 + 128], writes=[wdn_b[wi]])
                    units.append(wi)
                    for tt in range(TH // 128):
                        pi = tt % 8
                        ts_ = slice(tt * 128, (tt + 1) * 128)
                        for fc in range(44):
                            S.op("pe", lambda e, pi=pi, u=u, fc=fc, ts_=ts_, wi=wi: e.matmul(
                                out=psum[pi][:, u * 128:(u + 1) * 128], lhsT=aT[:, fc, ts_], rhs=wdn[wi][:, fc, :],
                                start=(fc == 0), stop=(fc == 43)),
                                reads=[aT_b[fc], wdn_b[wi]], writes=psum_b[pi], skip_same=True)
                for tt in range(TH // 128):
                    pi = tt % 8
                    i = tt % 2
                    rows = slice(t0 + tt * 128, t0 + (tt + 1) * 128)
                    csl = slice(cgp * 512, (cgp + 1) * 512)
                    S.dma("sp", x1s[i][:, :], C.x1_scr[rows, csl], writes=[x1s_b[i]])
                    S.op("dve", lambda e, pi=pi, i=i: e.tensor_tensor(out=xo[i][:, :], in0=psum[pi][:, :], in1=x1s[i][:, :], op=ALU.add),
                         reads=psum_b[pi] + [x1s_b[i]], writes=[xo_b[i]])
                    S.dma("sp", C.x2_scr[rows, csl], xo[i][:, :], reads=[xo_b[i]])


def phase_final(C):
    nc, S, debug = C.nc, C.S, C.debug
    with contextlib.ExitStack() as pes:
        def sb(name, shape, dt):
            return pes.enter_context(nc.sbuf_tensor(name, list(shape), dt))
        gfb = sb("gfb", [128, D], F32)
        gfb_b = Buf("gfb")
        S.dma("sp", gfb[:, :], C.final_norm_g.partition_broadcast(128), writes=[gfb_b])
        NBF = 3
        xt = [sb("xf%d" % i, [128, D], F32) for i in range(NBF)]
        xt_b = [Buf("xf") for i in range(NBF)]
        ot = [sb("of%d" % i, [128, D], F32) for i in range(NBF)]
        ot_b = [Buf("of") for i in range(NBF)]
        junk = sb("junk3", [128, D], BF16)
        junk_b = Buf("junk3")
        stat = sb("stat3", [128, 4 * NT], F32)
        stat_b = [Buf("stat3") for i in range(NT)]
        for t in range(NT):
            i = t % NBF
            ts_ = slice(t * 128, (t + 1) * 128)
            S.dma("sp", xt[i][:, :], C.x2_scr[ts_, :], writes=[xt_b[i]])
            ss = stat[:, 4 * t:4 * t + 1]
            lnv = stat[:, 4 * t + 1:4 * t + 2]
            rstd = stat[:, 4 * t + 2:4 * t + 3]
            S.op("act", lambda e, i=i, ss=ss: e.activation(out=junk[:, :], in_=xt[i][:, :], func=AF.Square, accum_out=ss),
                 reads=[xt_b[i]], writes=[junk_b, stat_b[t]])
            S.op("act", lambda e, ss=ss, lnv=lnv: e.activation(out=lnv, in_=ss, func=AF.Ln, scale=1.0 / D, bias=EPS),
                 reads=[stat_b[t]], writes=[stat_b[t]])
            S.op("act", lambda e, rstd=rstd, lnv=lnv: e.activation(out=rstd, in_=lnv, func=AF.Exp, scale=-0.5),
                 reads=[stat_b[t]], writes=[stat_b[t]])
            S.op("dve", lambda e, i=i, rstd=rstd: e.scalar_tensor_tensor(
                out=ot[i][:, :], in0=xt[i][:, :], scalar=rstd, in1=gfb[:, :], op0=ALU.mult, op1=ALU.mult),
                reads=[xt_b[i], stat_b[t], gfb_b], writes=[ot_b[i]])
            S.dma("sp", C.out[ts_, :], ot[i][:, :], reads=[ot_b[i]])

NC2 = 1024
NCB = 14 * 128


_NC_CACHE = {}


def _consts():
    j = np.arange(128)[:, None]
    i = np.arange(128)[None, :]
    cst = np.zeros((128, NCST), np.float32)
    cst[:, 0:128] = np.where(j <= i, -1.0 / 16, 0.0)
    cst[:, 128:256] = np.where(j >= i, -1.0 / 16, 0.0)
    cst[:, 256:384] = np.where(j > i, -1.0 / 16, 0.0)
    cst[:, 384:512] = np.where(j < i, -1.0 / 16, 0.0)
    cst[:, 512:640] = 1.0
    cst[:, 640:1152] = np.tile(np.where(j <= i, 1.0, 0.0), (1, 4))
    cst[:, 1152:1664] = np.tile(np.where(j >= i, 1.0, 0.0), (1, 4))
    c2 = np.zeros((128, NC2), np.float32)
    c2[:, 0:128] = np.where(j <= i, 1.0, 0.0)
    c2[:, 128:256] = np.where(j >= i, 1.0, 0.0)
    c2[:, 256:384] = np.where(j > i, 1.0, 0.0)
    c2[:, 384:512] = np.where(j < i, 1.0, 0.0)
    c2[:, 512:640] = np.where(j <= i, 0.0, -30000.0)
    c2[:, 640:768] = np.where(j >= i, 0.0, -30000.0)
    c2[:, 768:896] = np.eye(128)
    c2[:, 896:1024] = -1.0
    cb = np.zeros((128, NCB), np.float32)
    for d in range(2):
        for l in range(1, 8):
            b = 1 << (l - 1)
            same = (i // (2 * b)) == (j // (2 * b))
            if d == 0:
                m = same & ((i % (2 * b)) >= b) & ((j % (2 * b)) < b)
            else:
                m = same & ((i % (2 * b)) < b) & ((j % (2 * b)) >= b)
            o = (d * 7 + (l - 1)) * 128
            cb[:, o:o + 128] = m.astype(np.float32)
    return {
        "ident_bf": np.eye(128, dtype=np.float32).astype(ml_dtypes.bfloat16),
        "cst": cst,
        "cst2": c2,
        "cstb": cb.astype(ml_dtypes.bfloat16),
    }


def make_in_maps(inputs, n_cores=8):
    c = _consts()
    maps = []
    xs = np.ascontiguousarray(inputs["x"])
    for b in range(n_cores):
        m = {
            "x": xs[b],
            "norm1_g": np.ascontiguousarray(inputs["norm1_g"]).reshape(1, D),
            "w_in": np.ascontiguousarray(inputs["w_in"]).reshape(D, N_IN),
            "gla_decay_w_f": np.ascontiguousarray(inputs["gla_decay_w_f"]).reshape(16, 1024),
            "gla_decay_w_b": np.ascontiguousarray(inputs["gla_decay_w_b"]).reshape(16, 1024),
            "gla_decay_b_f": np.ascontiguousarray(inputs["gla_decay_b_f"]).reshape(1, 1024),
            "gla_decay_b_b": np.ascontiguousarray(inputs["gla_decay_b_b"]).reshape(1, 1024),
            "gla_norm_g": np.ascontiguousarray(inputs["gla_norm_g"]).reshape(1, 128),
            "gdn_norm_g": np.ascontiguousarray(inputs["gdn_norm_g"]).reshape(1, 128),
            "w_branch_gla": np.ascontiguousarray(inputs["w_branch_gla"]).reshape(1024, D),
            "w_branch_gdn": np.ascontiguousarray(inputs["w_branch_gdn"]).reshape(1024, D),
            "w_out": np.ascontiguousarray(inputs["w_out"]).reshape(D, D),
            "norm2_g": np.ascontiguousarray(inputs["norm2_g"]).reshape(1, D),
            "w_up": np.ascontiguousarray(inputs["w_up"]).reshape(D, 2 * D_FF),
            "w_down": np.ascontiguousarray(inputs["w_down"]).reshape(D_FF, D),
            "ffn_cw": np.ascontiguousarray(np.concatenate([
                np.asarray(inputs["ffn_conv_w"]).reshape(3, 88, 128), np.asarray(inputs["ffn_conv_b"]).reshape(1, 88, 128)],
                axis=0).transpose(2, 1, 0).reshape(128, 88 * 4)),
            "final_norm_g": np.ascontiguousarray(inputs["final_norm_g"]).reshape(1, D),
            "gdn_cw": np.ascontiguousarray(
                np.asarray(inputs["gdn_conv_w"]).reshape(3, 24, 128).transpose(2, 1, 0).reshape(128, 72)),
            "gdn_hp": np.ascontiguousarray(np.stack([
                np.concatenate([np.asarray(inputs["gdn_a_log_f"]).reshape(8), np.asarray(inputs["gdn_a_log_b"]).reshape(8)]),
                np.concatenate([np.asarray(inputs["gdn_dt_bias_f"]).reshape(8), np.asarray(inputs["gdn_dt_bias_b"]).reshape(8)]),
            ], axis=1).astype(np.float32)),
        }
        m.update(c)
        maps.append(m)
    return maps


def kernel(**inputs):
    nc = build_nc()
    in_maps = make_in_maps(inputs, 8)
    res = run_bass_kernel_spmd(nc, in_maps, core_ids=list(range(8)))
    return np.stack([np.asarray(r["out"]) for r in res.results], axis=0)
```

```python
import contextlib
import numpy as np
import ml_dtypes
import concourse.bass as bass
import concourse.mybir as mybir
from concourse.bass_utils import run_bass_kernel_spmd

F32 = mybir.dt.float32
BF16 = mybir.dt.bfloat16
AF = mybir.ActivationFunctionType
ALU = mybir.AluOpType

T = 2048
D = 2048
KC = 16
NT = 16
N_IN = 12352
D_FF = 5632
EPS = 1e-6

C_GQ, C_GK, C_GV, C_GG = 0, 1024, 2048, 3072
C_LR = 4096
C_DQ, C_DK, C_DV = 4128, 5152, 6176
C_DZ = 7200
C_AB = 8224
C_BG = 8256
C_BD = 10304


class Buf:
    __slots__ = ("w", "r", "name", "excl")

    def __init__(self, name="", excl=False):
        self.w = None
        self.r = {}
        self.name = name
        self.excl = excl


class DSem:
    __slots__ = ("handle", "count")

    def __init__(self, handle):
        self.handle = handle
        self.count = 0


class Sched:
    ENG = ["pe", "act", "dve", "pool", "sp"]

    def __init__(self, nc, es, n_dsem=24):
        self.nc = nc
        self.sem = {e: es.enter_context(nc.semaphore("s_" + e)) for e in self.ENG}
        self.cnt = {e: 0 for e in self.ENG}
        self.seen = {e: {} for e in self.ENG}
        self.prog = {e: [] for e in self.ENG}
        self.dsems = [DSem(es.enter_context(nc.semaphore("d%d" % i))) for i in range(n_dsem)]
        self.dnext = 0

    def _waits(self, eng, reads, writes, skip_same):
        deps = {}

        def add(k, v):
            if skip_same and k == eng:
                return
            if deps.get(k, 0) < v:
                deps[k] = v

        for b in reads:
            if b.w is not None:
                add(*b.w)
        for b in writes:
            if b.w is not None:
                add(*b.w)
            for k, v in b.r.items():
                add(k, v)
        out = []
        seen = self.seen[eng]
        for k, v in deps.items():
            if seen.get(k, 0) < v:
                seen[k] = v
                out.append((k.handle if isinstance(k, DSem) else self.sem[k], v))
        return out

    def _mark(self, d, reads, writes):
        k, v = d
        for b in reads:
            if b.r.get(k, 0) < v:
                b.r[k] = v
        for b in writes:
            b.w = d
            b.r = {}

    def rec_begin(self):
        self._rec = []

    def rec_end(self):
        r, self._rec = self._rec, None
        return r

    def play(self, lst, n=None):
        n = len(lst) if n is None else min(n, len(lst))
        for _ in range(n):
            kind, a, kw = lst.pop(0)
            (self.op if kind == "op" else self.dma)(*a, **kw)

    def op(self, eng, fn, reads=(), writes=(), skip_same=False):
        if getattr(self, "_rec", None) is not None:
            self._rec.append(("op", (eng, fn, list(reads), list(writes), skip_same), {}))
            return
        if any(b.excl for b in reads):
            writes = list(writes) + [b for b in reads if b.excl]
            reads = [b for b in reads if not b.excl]
        waits = self._waits(eng, reads, writes, skip_same)
        self.cnt[eng] += 1
        self.prog[eng].append((waits, fn, self.sem[eng], 1))
        self._mark((eng, self.cnt[eng]), reads, writes)

    def barrier(self):
        for eng in self.ENG:
            waits = []
            seen = self.seen[eng]
            for k in self.ENG:
                if k != eng and seen.get(k, 0) < self.cnt[k]:
                    seen[k] = self.cnt[k]
                    waits.append((self.sem[k], self.cnt[k]))
            for ds in self.dsems:
                if ds.count > 0 and seen.get(ds, 0) < ds.count:
                    seen[ds] = ds.count
                    waits.append((ds.handle, ds.count))
            if waits:
                self.prog[eng].append((waits, None, None, 0))

    def dma(self, q, out, in_, reads=(), writes=(), **kw):
        if getattr(self, "_rec", None) is not None:
            self._rec.append(("dma", (q, out, in_, list(reads), list(writes)), dict(kw)))
            return
        ds = self.dsems[self.dnext]
        self.dnext = (self.dnext + 1) % len(self.dsems)
        waits = self._waits(q, reads, writes, False)
        if ds.count > 0 and self.seen[q].get(ds, 0) < ds.count:
            self.seen[q][ds] = ds.count
            waits.append((ds.handle, ds.count))
        ds.count += 16
        self.prog[q].append((waits, (lambda e, o=out, i=in_, kw=kw: e.dma_start(out=o, in_=i, **kw)), ds.handle, 16))
        self._mark((ds, ds.count), reads, writes)

    def finish(self):
        waits = []
        for ds in self.dsems:
            if ds.count > 0:
                waits.append((ds.handle, ds.count))
        self.prog["sp"].append((waits, None, None, 0))

    def emit(self):
        nc = self.nc
        prog = self.prog

        def replay(name, e):
            for waits, fn, sem, inc in prog[name]:
                for s, v in waits:
                    e.wait_ge(s, v)
                if fn is not None:
                    ins = fn(e)
                    ins.then_inc(sem, inc)

        with nc.Block() as block:
            @block.tensor
            def _(e):
                replay("pe", e)

            @block.scalar
            def _(e):
                replay("act", e)

            @block.vector
            def _(e):
                replay("dve", e)

            @block.gpsimd
            def _(e):
                replay("pool", e)

            @block.sync
            def _(e):
                replay("sp", e)


def build_nc(debug=None):
    debug = debug or {}
    nc = bass.Bass("TRN2", target_bir_lowering=False)
    es = contextlib.ExitStack()
    with es:
        _build(nc, es, debug)
    return nc


def _dram_in(nc, name, shape, dt=F32):
    return nc.dram_tensor(name, list(shape), dt, kind="ExternalInput").ap()


class Ctx:
    pass


def _build(nc, es, debug):
    S = Sched(nc, es)
    C = Ctx()
    C.nc, C.S, C.debug = nc, S, debug
    stop_after = debug.get("stop_after", [[None], None])[0][0]

    C.x = _dram_in(nc, "x", [T, D])
    C.norm1_g = _dram_in(nc, "norm1_g", [1, D])
    C.w_in = _dram_in(nc, "w_in", [D, N_IN])
    C.ident_bf_d = _dram_in(nc, "ident_bf", [128, 128], BF16)
    C.cst_d = _dram_in(nc, "cst", [128, NCST])
    C.gla_dw = [_dram_in(nc, "gla_decay_w_f", [16, 1024]), _dram_in(nc, "gla_decay_w_b", [16, 1024])]
    C.gla_db = [_dram_in(nc, "gla_decay_b_f", [1, 1024]), _dram_in(nc, "gla_decay_b_b", [1, 1024])]
    C.gla_norm_g = _dram_in(nc, "gla_norm_g", [1, 128])
    C.gdn_norm_g = _dram_in(nc, "gdn_norm_g", [1, 128])
    C.w_branch_gla = _dram_in(nc, "w_branch_gla", [1024, D])
    C.w_branch_gdn = _dram_in(nc, "w_branch_gdn", [1024, D])
    C.w_out = _dram_in(nc, "w_out", [D, D])
    C.norm2_g = _dram_in(nc, "norm2_g", [1, D])
    C.w_up = _dram_in(nc, "w_up", [D, 2 * D_FF])
    C.w_down = _dram_in(nc, "w_down", [D_FF, D])
    C.ffn_cw_d = _dram_in(nc, "ffn_cw", [128, 88 * 4])
    C.final_norm_g = _dram_in(nc, "final_norm_g", [1, D])
    C.cst2_d = _dram_in(nc, "cst2", [128, NC2])
    C.cstb_d = _dram_in(nc, "cstb", [128, NCB], BF16)
    C.gdn_cw_d = _dram_in(nc, "gdn_cw", [128, 72])
    C.gdn_hp_d = _dram_in(nc, "gdn_hp", [16, 2])
    C.out = nc.dram_tensor("out", [T, D], F32, kind="ExternalOutput").ap()
    C.dbg_out = {}
    for name, (shape, dt) in debug.items():
        if dt is None:
            continue
        C.dbg_out[name] = nc.dram_tensor("dbg_" + name, list(shape), dt, kind="ExternalOutput").ap()
    C.scr = nc.dram_tensor("scr_proj", [N_IN, T], F32).ap()
    C.y_scr = nc.dram_tensor("scr_y", [2048, T], BF16).ap()
    C.h2_scr = nc.dram_tensor("scr_h2", [D, T], BF16).ap()
    C.x1_scr = nc.dram_tensor("scr_x1", [T, D], F32).ap()
    C.x2_scr = nc.dram_tensor("scr_x2", [T, D], F32).ap()

    C.psum = [es.enter_context(nc.psum_tensor("ps%d" % i, [128, 512], F32)) for i in range(8)]
    C.psum_b = [[Buf("ps%d" % i, excl=True)] * 4 for i in range(8)]
    C.ident_bf = es.enter_context(nc.sbuf_tensor("ident_bf_sb", [128, 128], BF16))
    C.ident_bf_b = Buf("ident_bf")
    C.cst = es.enter_context(nc.sbuf_tensor("cst_sb", [128, NCST], F32))
    C.cst_b = Buf("cst")
    S.dma("sp", C.ident_bf[:, :], C.ident_bf_d[:, :], writes=[C.ident_bf_b])
    S.dma("sp", C.cst[:, :], C.cst_d[:, :], writes=[C.cst_b])

    phase_proj(C)
    S.barrier()
    if stop_after != "proj":
        if debug.get("gla_heads", [[8], None])[0][0] > 0:
            phase_gla(C)
            S.barrier()
        if stop_after != "gla":
            if debug.get("gdn_heads", [[8], None])[0][0] > 0:
                phase_gdn(C)
                S.barrier()
            if stop_after != "gdn":
                phase_branch(C)
                S.barrier()
                if stop_after != "branch":
                    phase_ffn(C)
                    S.barrier()
                    phase_final(C)
                    S.barrier()

    if "proj" in C.dbg_out:
        S.dma("sp", C.dbg_out["proj"][:, :], C.scr[0:C.dbg_out["proj"].shape[0], :])
    if "y" in C.dbg_out:
        S.dma("sp", C.dbg_out["y"][:, :], C.y_scr[0:C.dbg_out["y"].shape[0], :])
    if "y2" in C.dbg_out:
        S.dma("sp", C.dbg_out["y2"][:, :], C.y_scr[1024:1024 + C.dbg_out["y2"].shape[0], :])
    for nm, ap_ in (("x1", C.x1_scr), ("x2", C.x2_scr)):
        if nm in C.dbg_out:
            S.dma("sp", C.dbg_out[nm][:, :], ap_[:, :])
    if "h2" in C.dbg_out:
        S.dma("sp", C.dbg_out["h2"][:, :], C.h2_scr[:, :])
    if stop_after is not None:
        S.dma("sp", C.out[0:128, :], C.x[0:128, :])
    S.finish()
    S.emit()


def phase_proj(C):
    nc, S, debug = C.nc, C.S, C.debug
    psum, psum_b = C.psum, C.psum_b
    with contextlib.ExitStack() as pes:
        def sb(name, shape, dt):
            return pes.enter_context(nc.sbuf_tensor(name, list(shape), dt))

        hT = sb("hT", [128, KC, T], BF16)
        hT_b = [Buf("hT%d" % t) for t in range(NT)]
        g1b = sb("g1b", [128, D], F32)
        g1b_b = Buf("g1b")
        S.dma("sp", g1b[:, :], C.norm1_g.partition_broadcast(128), writes=[g1b_b])

        xt = [sb("xt%d" % i, [128, D], F32) for i in range(2)]
        xt_b = [Buf("xt%d" % i) for i in range(2)]
        junk = sb("junk", [128, D], BF16)
        junk_b = Buf("junk")
        hb = [sb("hb%d" % i, [128, D], BF16) for i in range(2)]
        hb_b = [Buf("hb%d" % i) for i in range(2)]
        stat = sb("stat", [128, 4 * NT], F32)
        stat_b = [Buf("stat%d" % i) for i in range(NT)]
        ident_bf, ident_bf_b = C.ident_bf, C.ident_bf_b
        pcnt = 0
        for t in range(NT):
            i = t % 2
            S.dma("sp", xt[i][:, :], C.x[t * 128:(t + 1) * 128, :], writes=[xt_b[i]])
            ss = stat[:, 4 * t:4 * t + 1]
            lnv = stat[:, 4 * t + 1:4 * t + 2]
            rstd = stat[:, 4 * t + 2:4 * t + 3]
            S.op("act", lambda e, i=i, ss=ss: e.activation(out=junk[:, :], in_=xt[i][:, :], func=AF.Square, accum_out=ss),
                 reads=[xt_b[i]], writes=[junk_b, stat_b[t]])
            S.op("act", lambda e, ss=ss, lnv=lnv: e.activation(out=lnv, in_=ss, func=AF.Ln, scale=1.0 / D, bias=EPS),
                 reads=[stat_b[t]], writes=[stat_b[t]])
            S.op("act", lambda e, rstd=rstd, lnv=lnv: e.activation(out=rstd, in_=lnv, func=AF.Exp, scale=-0.5),
                 reads=[stat_b[t]], writes=[stat_b[t]])
            S.op("dve", lambda e, i=i, rstd=rstd: e.scalar_tensor_tensor(
                out=hb[i][:, :], in0=xt[i][:, :], scalar=rstd, in1=g1b[:, :], op0=ALU.mult, op1=ALU.mult),
                reads=[xt_b[i], stat_b[t], g1b_b], writes=[hb_b[i]])
            for g in range(4):
                pi = 4 + (pcnt % 4)
                pcnt += 1
                pt = psum[pi].bitcast(BF16)
                for q in range(4):
                    kc = g * 4 + q
                    S.op("pe", lambda e, pt=pt, q=q, i=i, kc=kc: e.transpose(
                        out=pt[:, q * 128:(q + 1) * 128], in_=hb[i][:, kc * 128:(kc + 1) * 128], identity=ident_bf[:, :]),
                        reads=[hb_b[i], ident_bf_b], writes=[psum_b[pi][q]], skip_same=True)
                if g % 2 == 0:
                    S.op("act", lambda e, pt=pt, g=g, t=t: e.activation(
                        out=hT[:, g * 4:(g + 1) * 4, t * 128:(t + 1) * 128],
                        in_=pt[:, 0:512].rearrange("p (a b) -> p a b", a=4), func=AF.Copy),
                        reads=psum_b[pi], writes=[hT_b[t]])
                else:
                    S.op("dve", lambda e, pt=pt, g=g, t=t: e.tensor_copy(
                        out=hT[:, g * 4:(g + 1) * 4, t * 128:(t + 1) * 128],
                        in_=pt[:, 0:512].rearrange("p (a b) -> p a b", a=4)),
                        reads=psum_b[pi], writes=[hT_b[t]])

        scr = C.scr
        w_in_r = C.w_in.rearrange("(kc p) n -> p kc n", p=128)
        units = []
        for c0 in range(0, 4096, 128):
            units.append((c0, 128))
        units.append((C_LR, 32))
        for c0 in range(C_DQ, C_AB, 128):
            units.append((c0, 128))
        units.append((C_AB, 32))
        for c0 in range(C_BG, N_IN, 128):
            units.append((c0, 128))
        if "nunits" in debug:
            units = units[:debug["nunits"][0][0]]
        NW = 8
        wring = [sb("wr%d" % i, [128, KC, 128], BF16) for i in range(NW)]
        wring_b = [Buf("wr%d" % i) for i in range(NW)]
        NSTG = 3
        stg = [sb("stg%d" % i, [128, T], F32) for i in range(NSTG)]
        stg_b = [Buf("stg%d" % i) for i in range(NSTG)]
        ecnt = 0
        for u, (c0, ncol) in enumerate(units):
            wt, wb = wring[u % NW], wring_b[u % NW]
            S.dma("pool", wt[:, :, 0:ncol], w_in_r[:, :, c0:c0 + ncol], writes=[wb])
            st, stb = stg[u % NSTG], stg_b[u % NSTG]
            for blk in range(4):
                pi = (4 * u + blk) % 8
                for kc in range(KC):
                    S.op("pe", lambda e, pi=pi, wt=wt, kc=kc, blk=blk, ncol=ncol: e.matmul(
                        out=psum[pi][0:ncol, :], lhsT=wt[:, kc, 0:ncol], rhs=hT[:, kc, blk * 512:(blk + 1) * 512],
                        start=(kc == 0), stop=(kc == KC - 1)),
                        reads=[wb] + hT_b[4 * blk:4 * blk + 4], writes=psum_b[pi], skip_same=True)
                if ecnt % 2 == 0:
                    S.op("act", lambda e, pi=pi, st=st, blk=blk, ncol=ncol: e.activation(
                        out=st[0:ncol, blk * 512:(blk + 1) * 512], in_=psum[pi][0:ncol, :], func=AF.Copy),
                        reads=psum_b[pi], writes=[stb])
                else:
                    S.op("dve", lambda e, pi=pi, st=st, blk=blk, ncol=ncol: e.tensor_copy(
                        out=st[0:ncol, blk * 512:(blk + 1) * 512], in_=psum[pi][0:ncol, :]),
                        reads=psum_b[pi], writes=[stb])
                ecnt += 1
            S.dma("sp", scr[c0:c0 + ncol, :], st[0:ncol, :], reads=[stb])


def phase_gla(C):
    nc, S, debug = C.nc, C.S, C.debug
    psum, psum_b = C.psum, C.psum_b
    cst, cst_b = C.cst, C.cst_b
    scr = C.scr
    nheads = debug.get("gla_heads", [[8], None])[0][0]
    with contextlib.ExitStack() as pes:
        def sb(name, shape, dt):
            return pes.enter_context(nc.sbuf_tensor(name, list(shape), dt))

        def B(name):
            return Buf(name)

        lr = sb("lr", [49, T], F32)
        lr_b = B("lr")
        dw = sb("dw", [49, 1024], F32)
        dw_b = B("dw")
        gn = sb("gn", [128, 1], F32)
        gn_b = B("gn")
        S.op("pool", lambda e: e.memset(lr[:, :], 1.0), writes=[lr_b])
        for d in range(2):
            S.dma("sp", lr[32 * d:32 * d + 16, :], scr[C_LR + 16 * d:C_LR + 16 * d + 16, :], writes=[lr_b])
            S.dma("sp", dw[32 * d:32 * d + 16, :], C.gla_dw[d][:, :], writes=[dw_b])
            S.dma("sp", dw[32 * d + 16:32 * d + 17, :], C.gla_db[d][:, :], writes=[dw_b])
        S.dma("sp", gn[:, :], C.gla_norm_g.rearrange("o d -> d o"), writes=[gn_b])

        NB = 2
        qbf = [sb("qbf%d" % i, [128, T], BF16) for i in range(NB)]
        kbf = [sb("kbf%d" % i, [128, T], BF16) for i in range(NB)]
        vbf = [sb("vbf%d" % i, [128, T], BF16) for i in range(NB)]
        g32 = [sb("g32%d" % i, [128, T], F32) for i in range(NB)]
        qbf_b = [B("qbf") for i in range(NB)]
        kbf_b = [B("kbf") for i in range(NB)]
        vbf_b = [B("vbf") for i in range(NB)]
        g32_b = [B("g32") for i in range(NB)]
        vtm = sb("vtm", [128, NT, 128], BF16)
        vtm_b = B("vtm")
        sg = sb("sg", [128, T], BF16)
        sg_b = B("sg")
        L = sb("L", [128, NT, 128], F32)
        L_b = B("L")
        EG = sb("EG", [128, T], BF16)
        EG_b = B("EG")
        EGi = sb("EGi", [128, T], BF16)
        EGi_b = B("EGi")
        EKT = sb("EKT", [128, NT, 128], BF16)
        EKT_b = B("EKT")
        dcl = sb("dcl", [128, 2, NT], F32)
        dcl_b = [B("dcl0"), B("dcl1")]
        qd = [sb("qd%d" % d, [128, T], BF16) for d in range(2)]
        qd_b = [B("qd") for d in range(2)]
        ki = [sb("ki%d" % d, [128, T], BF16) for d in range(2)]
        ki_b = [B("ki") for d in range(2)]
        kt = [sb("kt%d" % d, [128, NT, 128], BF16) for d in range(2)]
        kt_b = [B("kt") for d in range(2)]
        CS = sb("CS", [128, NT, 128], F32)
        CS_b = B("CS")
        Sst = [sb("Sst%d" % d, [128, NT, 128], F32) for d in range(2)]
        Sst_b = [B("Sst") for d in range(2)]
        Sbf = [sb("Sbf%d" % d, [128, NT, 128], BF16) for d in range(2)]
        Sbf_b = [B("Sbf") for d in range(2)]
        tE = [sb("tE%d" % i, [128, 512], F32) for i in range(2)]
        tE_b = [B("tE") for i in range(2)]
        sc1 = sb("sc1", [128, 512], F32)
        sc1_b = B("sc1")
        sc2 = sb("sc2", [128, 512], F32)
        sc2_b = B("sc2")
        scT = sb("scT", [128, 512], BF16)
        scT_b = B("scT")
        sq = sb("sq", [128, 512], F32)
        sq_b = B("sq")
        rs = sb("rs", [128, 512], F32)
        rs_b = B("rs")
        tt = sb("tt", [128, 512], F32)
        tt_b = B("tt")
        yb = [sb("yb%d" % i, [128, T], BF16) for i in range(2)]
        yb_b = [B("yb") for i in range(2)]

        ident_bf, ident_bf_b = C.ident_bf, C.ident_bf_b
        TRI = [cst[:, 0:128], cst[:, 128:256]]
        STRI = [cst[:, 256:384], cst[:, 384:512]]
        ONES = cst[:, 512:640]
        MASK = [cst[:, 640:1152], cst[:, 1152:1664]]
        pc = [0]
        tec = [0]

        def bank():
            pi = 4 + (pc[0] % 4)
            pc[0] += 1
            return pi

        def load_head(h):
            i = h % NB
            S.dma("pool", qbf[i][:, :], scr[C_GQ + h * 128:C_GQ + (h + 1) * 128, :], writes=[qbf_b[i]], max_dma_last_dim=4096)
            S.dma("pool", kbf[i][:, :], scr[C_GK + h * 128:C_GK + (h + 1) * 128, :], writes=[kbf_b[i]], max_dma_last_dim=4096)
            S.dma("pool", vbf[i][:, :], scr[C_GV + h * 128:C_GV + (h + 1) * 128, :], writes=[vbf_b[i]], max_dma_last_dim=4096)
            S.dma("sp", g32[i][:, :], scr[C_GG + h * 128:C_GG + (h + 1) * 128, :], writes=[g32_b[i]])

        load_head(0)
        for h in range(nheads):
            i = h % NB
            if h + 1 < nheads:
                load_head(h + 1)
            for g in range(4):
                pi = bank()
                pt = psum[pi].bitcast(BF16)
                for q in range(4):
                    c = 4 * g + q
                    S.op("pe", lambda e, pt=pt, q=q, c=c, i=i: e.transpose(
                        out=pt[:, q * 128:(q + 1) * 128], in_=vbf[i][:, c * 128:(c + 1) * 128], identity=ident_bf[:, :]),
                        reads=[vbf_b[i], ident_bf_b], writes=[psum_b[pi][q]], skip_same=True)
                S.op("act", lambda e, pt=pt, g=g: e.activation(
                    out=vtm[:, 4 * g:4 * g + 4, :], in_=pt[:, 0:512].rearrange("p (a b) -> p a b", a=4), func=AF.Copy),
                    reads=psum_b[pi], writes=[vtm_b])
            for blk in range(4):
                sl = slice(blk * 512, (blk + 1) * 512)
                j = tec[0] % 2
                tec[0] += 1
                S.op("act", lambda e, j=j, sl=sl, i=i: e.activation(out=tE[j][:, :], in_=g32[i][:, sl], func=AF.Exp, scale=-1.0),
                     reads=[g32_b[i]], writes=[tE_b[j]])
                S.op("act", lambda e, j=j: e.activation(out=tE[j][:, :], in_=tE[j][:, :], func=AF.Ln, bias=1.0),
                     reads=[tE_b[j]], writes=[tE_b[j]])
                S.op("act", lambda e, j=j: e.activation(out=tE[j][:, :], in_=tE[j][:, :], func=AF.Exp, scale=-1.0),
                     reads=[tE_b[j]], writes=[tE_b[j]])
                S.op("dve", lambda e, j=j, sl=sl, i=i: e.tensor_tensor(out=sg[:, sl], in0=g32[i][:, sl], in1=tE[j][:, :], op=ALU.mult),
                     reads=[g32_b[i], tE_b[j]], writes=[sg_b])
            for d in range(2):
                p0 = 32 * d
                for g in range(4):
                    pi = bank()
                    for q in range(4):
                        c = 4 * g + q
                        S.op("pe", lambda e, pi=pi, q=q, c=c, p0=p0, h=h: e.matmul(
                            out=psum[pi][:, q * 128:(q + 1) * 128], lhsT=lr[p0:p0 + 17, c * 128:(c + 1) * 128],
                            rhs=dw[p0:p0 + 17, h * 128:(h + 1) * 128], start=True, stop=True),
                            reads=[lr_b, dw_b], writes=[psum_b[pi][q]], skip_same=True)
                    j = tec[0] % 2
                    tec[0] += 1
                    S.op("act", lambda e, j=j, pi=pi: e.activation(out=tE[j][:, :], in_=psum[pi][:, :], func=AF.Exp, scale=-1.0),
                         reads=psum_b[pi], writes=[tE_b[j]])
                    S.op("act", lambda e, j=j, g=g: e.activation(
                        out=L[:, 4 * g:4 * g + 4, :], in_=tE[j][:, :].rearrange("p (a b) -> p a b", a=4), func=AF.Ln, bias=1.0),
                        reads=[tE_b[j]], writes=[L_b])
                for g in range(4):
                    pi = bank()
                    sl = slice(g * 512, (g + 1) * 512)
                    for q in range(4):
                        c = 4 * g + q
                        S.op("pe", lambda e, pi=pi, q=q, c=c, d=d: e.matmul(
                            out=psum[pi][:, q * 128:(q + 1) * 128], lhsT=L[:, c, :], rhs=TRI[d], start=True, stop=True),
                            reads=[L_b, cst_b], writes=[psum_b[pi][q]], skip_same=True)
                    S.op("act", lambda e, pi=pi, sl=sl: e.activation(out=EG[:, sl], in_=psum[pi][:, :], func=AF.Exp),
                         reads=psum_b[pi], writes=[EG_b])
                    S.op("act", lambda e, pi=pi, sl=sl: e.activation(out=EGi[:, sl], in_=psum[pi][:, :], func=AF.Exp, scale=-1.0),
                         reads=psum_b[pi], writes=[EGi_b])
                    col = 127 if d == 0 else 0
                    S.op("act", lambda e, pi=pi, g=g, d=d, col=col: e.activation(
                        out=dcl[:, d, 4 * g:4 * g + 4], in_=psum[pi][:, :].rearrange("p (a b) -> p a b", a=4)[:, :, col], func=AF.Exp),
                        reads=psum_b[pi], writes=[dcl_b[d]])
                for g in range(4):
                    pi = bank()
                    for q in range(4):
                        c = 4 * g + q
                        S.op("pe", lambda e, pi=pi, q=q, c=c, d=d: e.matmul(
                            out=psum[pi][:, q * 128:(q + 1) * 128], lhsT=STRI[d], rhs=L[:, c, :], start=True, stop=True),
                            reads=[L_b, cst_b], writes=[psum_b[pi][q]], skip_same=True)
                    S.op("act", lambda e, pi=pi, g=g: e.activation(
                        out=EKT[:, 4 * g:4 * g + 4, :], in_=psum[pi][:, :].rearrange("p (a b) -> p a b", a=4), func=AF.Exp),
                        reads=psum_b[pi], writes=[EKT_b])
                S.op("dve", lambda e, d=d, i=i: e.scalar_tensor_tensor(
                    out=qd[d][:, :], in0=qbf[i][:, :], scalar=float(128 ** -0.5), in1=EG[:, :], op0=ALU.mult, op1=ALU.mult),
                    reads=[qbf_b[i], EG_b], writes=[qd_b[d]])
                S.op("dve", lambda e, d=d, i=i: e.tensor_tensor(out=ki[d][:, :], in0=kbf[i][:, :], in1=EGi[:, :], op=ALU.mult),
                     reads=[kbf_b[i], EGi_b], writes=[ki_b[d]])
                for g in range(4):
                    pi = bank()
                    pt = psum[pi].bitcast(BF16)
                    for q in range(4):
                        c = 4 * g + q
                        S.op("pe", lambda e, pt=pt, q=q, c=c, i=i: e.transpose(
                            out=pt[:, q * 128:(q + 1) * 128], in_=kbf[i][:, c * 128:(c + 1) * 128], identity=ident_bf[:, :]),
                            reads=[kbf_b[i], ident_bf_b], writes=[psum_b[pi][q]], skip_same=True)
                    S.op("dve", lambda e, pt=pt, g=g, d=d: e.tensor_tensor(
                        out=kt[d][:, 4 * g:4 * g + 4, :], in0=pt[:, 0:512].rearrange("p (a b) -> p a b", a=4),
                        in1=EKT[:, 4 * g:4 * g + 4, :], op=ALU.mult),
                        reads=psum_b[pi] + [EKT_b], writes=[kt_b[d]])
                for g in range(4):
                    pi = bank()
                    for q in range(4):
                        c = 4 * g + q
                        S.op("pe", lambda e, pi=pi, q=q, c=c, d=d: e.matmul(
                            out=psum[pi][:, q * 128:(q + 1) * 128], lhsT=kt[d][:, c, :], rhs=vtm[:, c, :], start=True, stop=True),
                            reads=[kt_b[d], vtm_b], writes=[psum_b[pi][q]], skip_same=True)
                    S.op("act", lambda e, pi=pi, g=g: e.activation(
                        out=CS[:, 4 * g:4 * g + 4, :], in_=psum[pi][:, :].rearrange("p (a b) -> p a b", a=4), func=AF.Copy),
                        reads=psum_b[pi], writes=[CS_b])
                if d == 0:
                    S.op("pool", lambda e: e.memset(Sst[0][:, 0, :], 0.0), writes=[Sst_b[0]])
                    for c in range(1, NT):
                        S.op("dve", lambda e, c=c: e.scalar_tensor_tensor(
                            out=Sst[0][:, c, :], in0=Sst[0][:, c - 1, :], scalar=dcl[:, 0, c - 1:c], in1=CS[:, c - 1, :],
                            op0=ALU.mult, op1=ALU.add),
                            reads=[Sst_b[0], dcl_b[0], CS_b], writes=[Sst_b[0]])
                else:
                    S.op("pool", lambda e: e.memset(Sst[1][:, NT - 1, :], 0.0), writes=[Sst_b[1]])
                    for c in range(NT - 2, -1, -1):
                        S.op("dve", lambda e, c=c: e.scalar_tensor_tensor(
                            out=Sst[1][:, c, :], in0=Sst[1][:, c + 1, :], scalar=dcl[:, 1, c + 1:c + 2], in1=CS[:, c + 1, :],
                            op0=ALU.mult, op1=ALU.add),
                            reads=[Sst_b[1], dcl_b[1], CS_b], writes=[Sst_b[1]])
                S.op("act", lambda e, d=d: e.activation(out=Sbf[d][:, :, :], in_=Sst[d][:, :, :], func=AF.Copy),
                     reads=[Sst_b[d]], writes=[Sbf_b[d]])
            yi = h % 2
            for g in range(4):
                sl = slice(g * 512, (g + 1) * 512)
                pA, pB, pO, pN = 4, 5, 6, 7
                for q in range(4):
                    c = 4 * g + q
                    cs_ = slice(c * 128, (c + 1) * 128)
                    S.op("pe", lambda e, q=q, cs_=cs_: e.matmul(
                        out=psum[pA][:, q * 128:(q + 1) * 128], lhsT=ki[0][:, cs_], rhs=qd[0][:, cs_], start=True, stop=True),
                        reads=[ki_b[0], qd_b[0]], writes=[psum_b[pA][q]], skip_same=True)
                    S.op("pe", lambda e, q=q, cs_=cs_: e.matmul(
                        out=psum[pB][:, q * 128:(q + 1) * 128], lhsT=ki[1][:, cs_], rhs=qd[1][:, cs_], start=True, stop=True),
                        reads=[ki_b[1], qd_b[1]], writes=[psum_b[pB][q]], skip_same=True)
                S.op("dve", lambda e: e.tensor_tensor(out=sc1[:, :], in0=psum[pA][:, :], in1=MASK[0], op=ALU.mult),
                     reads=psum_b[pA] + [cst_b], writes=[sc1_b])
                S.op("dve", lambda e: e.tensor_tensor(out=sc2[:, :], in0=psum[pB][:, :], in1=MASK[1], op=ALU.mult),
                     reads=psum_b[pB] + [cst_b], writes=[sc2_b])
                S.op("pool", lambda e: e.tensor_tensor(out=scT[:, :], in0=sc1[:, :], in1=sc2[:, :], op=ALU.add),
                     reads=[sc1_b, sc2_b], writes=[scT_b])
                for q in range(4):
                    c = 4 * g + q
                    cs_ = slice(c * 128, (c + 1) * 128)
                    osl = slice(q * 128, (q + 1) * 128)
                    last_f = (c == 0)
                    has_f = c >= 1
                    has_b = c <= NT - 2
                    S.op("pe", lambda e, osl=osl, c=c, has_f=has_f, has_b=has_b: e.matmul(
                        out=psum[pO][:, osl], lhsT=vtm[:, c, :], rhs=scT[:, osl], start=True, stop=not (has_f or has_b)),
                        reads=[vtm_b, scT_b], writes=[psum_b[pO][q]], skip_same=True)
                    if has_f:
                        S.op("pe", lambda e, osl=osl, c=c, cs_=cs_, has_b=has_b: e.matmul(
                            out=psum[pO][:, osl], lhsT=Sbf[0][:, c, :], rhs=qd[0][:, cs_], start=False, stop=not has_b),
                            reads=[Sbf_b[0], qd_b[0]], writes=[psum_b[pO][q]], skip_same=True)
                    if has_b:
                        S.op("pe", lambda e, osl=osl, c=c, cs_=cs_: e.matmul(
                            out=psum[pO][:, osl], lhsT=Sbf[1][:, c, :], rhs=qd[1][:, cs_], start=False, stop=True),
                            reads=[Sbf_b[1], qd_b[1]], writes=[psum_b[pO][q]], skip_same=True)
                S.op("act", lambda e: e.activation(out=sq[:, :], in_=psum[pO][:, :], func=AF.Square),
                     reads=psum_b[pO], writes=[sq_b])
                S.op("pe", lambda e: e.matmul(out=psum[pN][:, :], lhsT=ONES, rhs=sq[:, :], start=True, stop=True),
                     reads=[sq_b, cst_b], writes=psum_b[pN], skip_same=True)
                S.op("act", lambda e: e.activation(out=rs[:, :], in_=psum[pN][:, :], func=AF.Ln, scale=1.0 / 128, bias=EPS),
                     reads=psum_b[pN], writes=[rs_b])
                S.op("act", lambda e: e.activation(out=rs[:, :], in_=rs[:, :], func=AF.Exp, scale=-0.5),
                     reads=[rs_b], writes=[rs_b])
                S.op("dve", lambda e: e.scalar_tensor_tensor(
                    out=tt[:, :], in0=psum[pO][:, :], scalar=gn[:, 0:1], in1=rs[:, :], op0=ALU.mult, op1=ALU.mult),
                    reads=psum_b[pO] + [gn_b, rs_b], writes=[tt_b])
                S.op("dve", lambda e, sl=sl, yi=yi: e.tensor_tensor(out=yb[yi][:, sl], in0=tt[:, :], in1=sg[:, sl], op=ALU.mult),
                     reads=[tt_b, sg_b], writes=[yb_b[yi]])
            S.dma("sp", C.y_scr[h * 128:(h + 1) * 128, :], yb[yi][:, :], reads=[yb_b[yi]])


NCST = 1664

def phase_gdn(C):
    nc, S, debug = C.nc, C.S, C.debug
    psum, psum_b = C.psum, C.psum_b
    cst, cst_b = C.cst, C.cst_b
    scr = C.scr
    nheads = debug.get("gdn_heads", [[8], None])[0][0]
    with contextlib.ExitStack() as pes:
        def sb(name, shape, dt):
            return pes.enter_context(nc.sbuf_tensor(name, list(shape), dt))

        def B(name):
            return Buf(name)

        ident_bf, ident_bf_b = C.ident_bf, C.ident_bf_b
        ONES = cst[:, 512:640]
        c2 = sb("c2", [128, NC2], F32)
        c2_b = B("c2")
        S.dma("sp", c2[:, :], C.cst2_d[:, :], writes=[c2_b])
        U = [c2[:, 0:128], c2[:, 128:256]]
        SU = [c2[:, 256:384], c2[:, 384:512]]
        PEN = [c2[:, 512:640], c2[:, 640:768]]
        IDF = c2[:, 768:896]
        NEGONES = c2[:, 896:1024]
        cb = sb("cb", [128, NCB], BF16)
        cb_b = B("cb")
        S.dma("sp", cb[:, :], C.cstb_d[:, :], writes=[cb_b])

        def lmask(d, l):
            o = (d * 7 + (l - 1)) * 128
            return cb[:, o:o + 128]

        cw = sb("cw", [128, 24 * 3], F32)
        cw_b = B("cw")
        S.dma("sp", cw[:, :], C.gdn_cw_d[:, :], writes=[cw_b])
        gn = sb("gn2", [128, 1], F32)
        gn_b = B("gn2")
        S.dma("sp", gn[:, :], C.gdn_norm_g.rearrange("o d -> d o"), writes=[gn_b])
        hp = sb("hp", [16, 4], F32)
        hp_b = B("hp")
        S.dma("sp", hp[:, 0:2], C.gdn_hp_d[:, :], writes=[hp_b])

        c1 = sb("c1", [128, T], F32)
        c1_b = B("c1")
        gb48 = c1
        gb48_b = c1_b
        S.op("pool", lambda e: e.memset(gb48[:, :], 0.0), writes=[gb48_b])
        S.dma("sp", gb48[0:16, :], scr[C_AB:C_AB + 16, :], writes=[gb48_b])
        S.dma("sp", gb48[32:48, :], scr[C_AB + 16:C_AB + 32, :], writes=[gb48_b])
        S.op("act", lambda e: e.activation(out=hp[:, 2:3], in_=hp[:, 0:1], func=AF.Exp), reads=[hp_b], writes=[hp_b])
        S.op("dve", lambda e: e.tensor_scalar(out=hp[:, 2:3], in0=hp[:, 2:3], scalar1=-1.0, scalar2=None, op0=ALU.mult),
             reads=[hp_b], writes=[hp_b])
        S.op("act", lambda e: e.activation(out=gb48[0:16, :], in_=gb48[0:16, :], func=AF.Exp, bias=hp[:, 1:2]),
             reads=[gb48_b, hp_b], writes=[gb48_b])
        S.op("act", lambda e: e.activation(out=gb48[0:16, :], in_=gb48[0:16, :], func=AF.Ln, bias=1.0),
             reads=[gb48_b], writes=[gb48_b])
        S.op("dve", lambda e: e.tensor_scalar(out=gb48[0:16, :], in0=gb48[0:16, :], scalar1=hp[:, 2:3], scalar2=None, op0=ALU.mult),
             reads=[gb48_b, hp_b], writes=[gb48_b])
        S.op("act", lambda e: e.activation(out=gb48[32:48, :], in_=gb48[32:48, :], func=AF.Exp, scale=-1.0),
             reads=[gb48_b], writes=[gb48_b])
        S.op("act", lambda e: e.activation(out=gb48[32:48, :], in_=gb48[32:48, :], func=AF.Ln, bias=1.0),
             reads=[gb48_b], writes=[gb48_b])
        S.op("act", lambda e: e.activation(out=gb48[32:48, :], in_=gb48[32:48, :], func=AF.Exp, scale=-1.0),
             reads=[gb48_b], writes=[gb48_b])
        gtm = sb("gtm", [128, NT, 48], F32)
        gtm_b = B("gtm")
        for g in range(4):
            pi = 2 + g
            for q in range(4):
                t = 4 * g + q
                S.op("pe", lambda e, pi=pi, q=q, t=t: e.transpose(
                    out=psum[pi][:, q * 48:(q + 1) * 48], in_=gb48[0:48, t * 128:(t + 1) * 128], identity=IDF[0:48, 0:48]),
                    reads=[gb48_b, c2_b], writes=[psum_b[pi][0]], skip_same=True)
            S.op("act", lambda e, pi=pi, g=g: e.activation(
                out=gtm[:, 4 * g:4 * g + 4, :], in_=psum[pi][:, 0:192].rearrange("p (a b) -> p a b", a=4), func=AF.Copy),
                reads=[psum_b[pi][0]], writes=[gtm_b])
        egtm = sb("egtm", [128, NT, 16], F32)
        egtm_b = B("egtm")
        ektm = sb("ektm", [128, NT, 16], F32)
        ektm_b = B("ektm")
        for (mats, dst, dst_b, pi) in ((U, egtm, egtm_b, 6), (SU, ektm, ektm_b, 7)):
            for t in range(NT):
                for d in range(2):
                    S.op("pe", lambda e, pi=pi, t=t, d=d, mats=mats: e.matmul(
                        out=psum[pi][:, t * 16 + d * 8:t * 16 + d * 8 + 8], lhsT=mats[d], rhs=gtm[:, t, d * 8:d * 8 + 8],
                        start=True, stop=True),
                        reads=[gtm_b, c2_b], writes=[psum_b[pi][0]], skip_same=True)
            S.op("act", lambda e, pi=pi, dst=dst: e.activation(
                out=dst[:, :, :], in_=psum[pi][:, 0:256].rearrange("p (a b) -> p a b", a=NT), func=AF.Exp),
                reads=[psum_b[pi][0]], writes=[dst_b])

        pin = [sb("pin%d" % i, [128, T + 2], F32) for i in range(1)]
        pin_b = [B("pin") for i in range(1)]
        for i in range(1):
            S.op("pool", lambda e, i=i: e.memset(pin[i][:, :], 0.0), writes=[pin_b[i]])
        tB = sb("tB", [128, T], F32)
        tB_b = B("tB")
        qT = sb("qT", [128, T], BF16)
        qT_b = B("qT")
        kT = sb("kT", [128, T], BF16)
        kT_b = B("kT")
        vT = sb("vT", [128, T], BF16)
        vT_b = B("vT")
        szs = [sb("sz%d" % i, [128, T], BF16) for i in range(2)]
        szs_b = [B("sz") for i in range(2)]
        ktm = sb("ktm", [128, NT, 128], BF16)
        ktm_b = B("ktm")
        vtms = [sb("vtm2_%d" % i, [128, NT, 128], BF16) for i in range(2)]
        vtms_b = [B("vtm2") for i in range(2)]
        rsi = sb("rsi", [128, 512], F32)
        rsi_b = B("rsi")
        kg = [sb("kg%d" % d, [128, NT, 128], BF16) for d in range(2)]
        kg_b = [B("kg") for d in range(2)]
        ktl = [sb("ktl%d" % d, [128, NT, 128], BF16) for d in range(2)]
        ktl_b = [B("ktl") for d in range(2)]
        qdT = [sb("qdT%d" % d, [128, T], BF16) for d in range(2)]
        qdT_b = [B("qdT") for d in range(2)]
        VT = [sb("VT%d" % d, [128, T], BF16) for d in range(2)]
        VT_b = [B("VT") for d in range(2)]
        atT = [sb("atT%d" % d, [128, T], BF16) for d in range(2)]
        atT_b = [B("atT") for d in range(2)]
        wpn = [sb("wpn%d" % d, [128, T], BF16) for d in range(2)]
        wpn_b = [B("wpn") for d in range(2)]
        dcl = sb("dcl2", [128, 2, NT], F32)
        dcl_b = [B("dcl20"), B("dcl21")]
        gU = sb("gU", [128, NT, 128], F32)
        gU_b = B("gU")
        DTs = [sb("DT%d" % k, [128, 512], F32) for k in range(4)]
        DTs_b = [B("DT") for k in range(4)]
        ebs = [sb("eb%d" % k, [128, 512], F32) for k in range(4)]
        ebs_b = [B("eb") for k in range(4)]
        NTs = [sb("NT%d" % k, [128, 512], BF16) for k in range(4)]
        NTs_b = [B("NT") for k in range(4)]
        Xs = [sb("Xs%d" % k, [128, 512], BF16) for k in range(4)]
        Xs_b = [B("Xs") for k in range(4)]
        Ys = [sb("Ys%d" % k, [128, 512], BF16) for k in range(4)]
        Ys_b = [B("Ys") for k in range(4)]
        Ps = [sb("Ps%d" % k, [128, 512], BF16) for k in range(4)]
        Ps_b = [B("Ps") for k in range(4)]
        S32 = [sb("S32_%d" % d, [128, 128], F32) for d in range(2)]
        S32_b = [B("S32") for d in range(2)]
        Sbf = [sb("Sbf2_%d" % d, [128, 128], BF16) for d in range(2)]
        Sbf_b = [B("Sbf2") for d in range(2)]
        vnb = [sb("vnb%d" % d, [128, 128], BF16) for d in range(2)]
        vnb_b = [B("vnb") for d in range(2)]
        oacc = [sb("oacc%d" % d, [128, T], F32) for d in range(2)]
        oacc_b = [B("oacc") for d in range(2)]
        sq = sb("sq2", [128, 512], F32)
        sq_b = B("sq2")
        rs = sb("rs2", [128, 512], F32)
        rs_b = B("rs2")
        tt = sb("tt2", [128, 512], F32)
        tt_b = B("tt2")
        yb = [sb("yb2_%d" % i, [128, T], BF16) for i in range(1)] * 2
        yb_b = [B("yb2")] * 2

        pc = [0]

        def bank():
            pi = pc[0] % 8
            pc[0] += 1
            return pi

        pinc = [0]
        ipc = [0]

        def ibank():
            pi = 6 + (ipc[0] % 2)
            ipc[0] += 1
            return pi

        def silu_inplace(x, x_b, n=T):
            for hs in (slice(0, n // 2), slice(n // 2, n)):
                S.op("act", lambda e, hs=hs: e.activation(out=tB[:, hs], in_=x[:, hs], func=AF.Exp, scale=-1.0), reads=[x_b], writes=[tB_b])
                S.op("act", lambda e, hs=hs: e.activation(out=tB[:, hs], in_=tB[:, hs], func=AF.Ln, bias=1.0), reads=[tB_b], writes=[tB_b])
                S.op("act", lambda e, hs=hs: e.activation(out=tB[:, hs], in_=tB[:, hs], func=AF.Exp, scale=-1.0), reads=[tB_b], writes=[tB_b])
                S.op("dve", lambda e, hs=hs: e.tensor_tensor(out=x[:, hs], in0=x[:, hs], in1=tB[:, hs], op=ALU.mult),
                     reads=[x_b, tB_b], writes=[x_b])

        def conv_silu(row0, blk):
            i = 0
            S.dma("sp", pin[i][:, 1:T + 1], scr[row0:row0 + 128, :], writes=[pin_b[i]])
            w0 = cw[:, blk * 3:blk * 3 + 1]
            w1 = cw[:, blk * 3 + 1:blk * 3 + 2]
            w2 = cw[:, blk * 3 + 2:blk * 3 + 3]
            S.op("act", lambda e, i=i, w0=w0: e.activation(out=c1[:, :], in_=pin[i][:, 0:T], func=AF.Copy, scale=w0),
                 reads=[pin_b[i], cw_b], writes=[c1_b])
            S.op("dve", lambda e, i=i, w1=w1: e.scalar_tensor_tensor(
                out=c1[:, :], in0=pin[i][:, 1:T + 1], scalar=w1, in1=c1[:, :], op0=ALU.mult, op1=ALU.add),
                reads=[pin_b[i], cw_b, c1_b], writes=[c1_b])
            S.op("dve", lambda e, i=i, w2=w2: e.scalar_tensor_tensor(
                out=c1[:, :], in0=pin[i][:, 2:T + 2], scalar=w2, in1=c1[:, :], op0=ALU.mult, op1=ALU.add),
                reads=[pin_b[i], cw_b, c1_b], writes=[c1_b])
            silu_inplace(c1, c1_b)

        def l2norm_to(dst, dst_b, scale):
            for hs in (slice(0, T // 2), slice(T // 2, T)):
                S.op("act", lambda e, hs=hs: e.activation(out=tB[:, hs], in_=c1[:, hs], func=AF.Square), reads=[c1_b], writes=[tB_b])
            for blk in range(4):
                sl = slice(blk * 512, (blk + 1) * 512)
                pi = ibank()
                S.op("pe", lambda e, pi=pi, sl=sl: e.matmul(out=psum[pi][:, :], lhsT=ONES, rhs=tB[:, sl], start=True, stop=True),
                     reads=[tB_b, cst_b], writes=psum_b[pi], skip_same=True)
                S.op("act", lambda e, pi=pi: e.activation(out=rsi[:, :], in_=psum[pi][:, :], func=AF.Ln, bias=EPS),
                     reads=psum_b[pi], writes=[rsi_b])
                S.op("act", lambda e: e.activation(out=rsi[:, :], in_=rsi[:, :], func=AF.Exp, scale=-0.5), reads=[rsi_b], writes=[rsi_b])
                S.op("dve", lambda e, sl=sl: e.scalar_tensor_tensor(
                    out=dst[:, sl], in0=c1[:, sl], scalar=float(scale), in1=rsi[:, :], op0=ALU.mult, op1=ALU.mult),
                    reads=[c1_b, rsi_b], writes=[dst_b])

        def to_token_major(src, src_b, dst, dst_b):
            for g in range(4):
                pi = ibank()
                pt = psum[pi].bitcast(BF16)
                for q in range(4):
                    c = 4 * g + q
                    S.op("pe", lambda e, pt=pt, q=q, c=c: e.transpose(
                        out=pt[:, q * 128:(q + 1) * 128], in_=src[:, c * 128:(c + 1) * 128], identity=ident_bf[:, :]),
                        reads=[src_b, ident_bf_b], writes=[psum_b[pi][q]], skip_same=True)
                S.op("act", lambda e, pt=pt, g=g: e.activation(
                    out=dst[:, 4 * g:4 * g + 4, :], in_=pt[:, 0:512].rearrange("p (a b) -> p a b", a=4), func=AF.Copy),
                    reads=psum_b[pi], writes=[dst_b])

        def bc4(ap2d):
            return ap2d.unsqueeze(1).broadcast_to([128, 4, 128])

        def r4(ap):
            return ap.rearrange("p (a b) -> p a b", a=4)

        def emit_inputs(h):
            sz, sz_b = szs[h % 2], szs_b[h % 2]
            vtm, vtm_b = vtms[h % 2], vtms_b[h % 2]
            conv_silu(C_DQ + h * 128, h)
            l2norm_to(qT, qT_b, 128 ** -0.5)
            conv_silu(C_DK + h * 128, 8 + h)
            l2norm_to(kT, kT_b, 1.0)
            conv_silu(C_DV + h * 128, 16 + h)
            for hs in (slice(0, T // 2), slice(T // 2, T)):
                S.op("act", lambda e, hs=hs: e.activation(out=vT[:, hs], in_=c1[:, hs], func=AF.Copy), reads=[c1_b], writes=[vT_b])
            to_token_major(kT, kT_b, ktm, ktm_b)
            to_token_major(vT, vT_b, vtm, vtm_b)
            S.dma("sp", c1[:, :], scr[C_DZ + h * 128:C_DZ + (h + 1) * 128, :], writes=[c1_b])
            silu_inplace(c1, c1_b)
            for hs in (slice(0, T // 2), slice(T // 2, T)):
                S.op("act", lambda e, hs=hs: e.activation(out=sz[:, hs], in_=c1[:, hs], func=AF.Copy), reads=[c1_b], writes=[sz_b])
            if "gdn_qkv" in C.dbg_out and h == 0:
                S.dma("sp", C.dbg_out["gdn_qkv"][0:128, :], qT[:, :], reads=[qT_b])
                S.dma("sp", C.dbg_out["gdn_qkv"][128:256, :], kT[:, :], reads=[kT_b])
                S.dma("sp", C.dbg_out["gdn_qkv"][256:384, :], vT[:, :], reads=[vT_b])

        emit_inputs(0)
        for h in range(nheads):
            sz, sz_b = szs[h % 2], szs_b[h % 2]
            vtm, vtm_b = vtms[h % 2], vtms_b[h % 2]

            for d in range(2):
                col = d * 8 + h
                gcol = gtm[:, :, col:col + 1]
                bcol = gtm[:, :, 32 + col:32 + col + 1]
                S.op("dve", lambda e, d=d, col=col: e.tensor_tensor(
                    out=kg[d][:, :, :], in0=ktm[:, :, :], in1=egtm[:, :, col:col + 1].broadcast_to([128, NT, 128]), op=ALU.mult),
                    reads=[ktm_b, egtm_b], writes=[kg_b[d]])
                S.op("dve", lambda e, d=d, col=col: e.tensor_tensor(
                    out=ktl[d][:, :, :], in0=ktm[:, :, :], in1=ektm[:, :, col:col + 1].broadcast_to([128, NT, 128]), op=ALU.mult),
                    reads=[ktm_b, ektm_b], writes=[ktl_b[d]])
                S.op("pool", lambda e, d=d, gcol=gcol: e.tensor_tensor(
                    out=gU[:, :, :], in0=U[d].unsqueeze(1).broadcast_to([128, NT, 128]), in1=gcol.broadcast_to([128, NT, 128]), op=ALU.mult),
                    reads=[c2_b, gtm_b], writes=[gU_b])
                last = 127 if d == 0 else 0
                KB = 4
                G = list(range(4))

                def qsl(q):
                    return slice(q * 128, (q + 1) * 128)

                pA = [bank() for k in G]
                for k in G:
                    for q in range(4):
                        c = 4 * k + q
                        S.op("pe", lambda e, p=pA[k], q=q, c=c: e.matmul(
                            out=psum[p][:, qsl(q)], lhsT=ONES, rhs=gU[:, c, :], start=True, stop=False),
                            reads=[gU_b, cst_b], writes=psum_b[pA[k]], skip_same=True)
                        S.op("pe", lambda e, p=pA[k], q=q, c=c: e.matmul(
                            out=psum[p][:, qsl(q)], lhsT=gU[:, c, :], rhs=NEGONES, start=False, stop=False),
                            reads=[gU_b, c2_b], writes=psum_b[pA[k]], skip_same=True)
                        S.op("pe", lambda e, p=pA[k], q=q, d=d: e.matmul(
                            out=psum[p][:, qsl(q)], lhsT=IDF, rhs=PEN[d], start=False, stop=True),
                            reads=[c2_b], writes=psum_b[pA[k]], skip_same=True)
                for k in G:
                    S.op("act", lambda e, p=pA[k], k=k: e.activation(out=DTs[k][:, :], in_=psum[p][:, :], func=AF.Exp),
                         reads=psum_b[pA[k]], writes=[DTs_b[k]])
                pB = [bank() for k in G]
                for k in G:
                    for q in range(4):
                        c = 4 * k + q
                        S.op("pe", lambda e, p=pB[k], q=q, c=c: e.matmul(
                            out=psum[p][:, qsl(q)], lhsT=NEGONES, rhs=gU[:, c, :], start=True, stop=True),
                            reads=[gU_b, c2_b], writes=psum_b[pB[k]], skip_same=True)
                for k in G:
                    S.op("act", lambda e, p=pB[k], k=k: e.activation(out=ebs[k][:, :], in_=psum[p][:, :], func=AF.Exp, scale=-1.0),
                         reads=psum_b[pB[k]], writes=[ebs_b[k]])
                    S.op("act", lambda e, p=pB[k], k=k, d=d, last=last: e.activation(
                        out=dcl[:, d, 4 * k:4 * k + 4], in_=r4(psum[p][:, :])[:, :, last], func=AF.Exp, scale=-1.0),
                        reads=psum_b[pB[k]], writes=[dcl_b[d]])
                for k in G:
                    sl = slice(k * 512, (k + 1) * 512)
                    S.op("dve", lambda e, d=d, sl=sl, k=k: e.tensor_tensor(out=qdT[d][:, sl], in0=qT[:, sl], in1=ebs[k][:, :], op=ALU.mult),
                         reads=[qT_b, ebs_b[k]], writes=[qdT_b[d]])
                pD = [bank() for k in G]
                for k in G:
                    for q in range(4):
                        cs_ = slice((4 * k + q) * 128, (4 * k + q + 1) * 128)
                        S.op("pe", lambda e, p=pD[k], q=q, cs_=cs_: e.matmul(
                            out=psum[p][:, qsl(q)], lhsT=kT[:, cs_], rhs=qT[:, cs_], start=True, stop=True),
                            reads=[kT_b, qT_b], writes=psum_b[pD[k]], skip_same=True)
                for k in G:
                    sl = slice(k * 512, (k + 1) * 512)
                    S.op("dve", lambda e, p=pD[k], d=d, sl=sl, k=k: e.tensor_tensor(out=atT[d][:, sl], in0=psum[p][:, :], in1=DTs[k][:, :], op=ALU.mult),
                         reads=psum_b[pD[k]] + [DTs_b[k]], writes=[atT_b[d]])
                for k in G:
                    S.op("dve", lambda e, k=k, bcol=bcol: e.tensor_tensor(
                        out=r4(DTs[k][:, :]), in0=r4(DTs[k][:, :]), in1=bcol[:, 4 * k:4 * k + 4, :].broadcast_to([128, 4, 128]), op=ALU.mult),
                        reads=[DTs_b[k], gtm_b], writes=[DTs_b[k]])
                pC = [bank() for k in G]
                for k in G:
                    for q in range(4):
                        cs_ = slice((4 * k + q) * 128, (4 * k + q + 1) * 128)
                        S.op("pe", lambda e, p=pC[k], q=q, cs_=cs_: e.matmul(
                            out=psum[p][:, qsl(q)], lhsT=kT[:, cs_], rhs=kT[:, cs_], start=True, stop=True),
                            reads=[kT_b], writes=psum_b[pC[k]], skip_same=True)
                for k in G:
                    S.op("dve", lambda e, p=pC[k], k=k: e.tensor_tensor(out=NTs[k][:, :], in0=psum[p][:, :], in1=DTs[k][:, :], op=ALU.mult),
                         reads=psum_b[pC[k]] + [DTs_b[k]], writes=[NTs_b[k]])
                for l in range(1, 8):
                    def Xop(k, q, l=l):
                        return ident_bf[:, :] if l == 1 else Xs[k][:, qsl(q)]

                    def Yop(k, q, l=l):
                        return ident_bf[:, :] if l == 1 else Ys[k][:, qsl(q)]

                    pP = [bank() for k in G]
                    for k in G:
                        for q in range(4):
                            S.op("pe", lambda e, xo=Xop(k, q), yo=Yop(k, q), p=pP[k], q=q, k=k: e.matmul(out=psum[p][:, qsl(q)], lhsT=NTs[k][:, qsl(q)], rhs=xo, start=True, stop=True),
                                 reads=[NTs_b[k], Xs_b[k]], writes=psum_b[pP[k]], skip_same=True)
                    for k in G:
                        S.op("dve", lambda e, p=pP[k], k=k, d=d, l=l: e.tensor_tensor(
                            out=r4(Ps[k][:, :]), in0=r4(psum[p][:, :]), in1=bc4(lmask(d, l)), op=ALU.mult),
                            reads=psum_b[pP[k]] + [cb_b], writes=[Ps_b[k]])
                    if l < 7:
                        pX = [bank() for k in G]
                        for k in G:
                            for q in range(4):
                                S.op("pe", lambda e, xo=Xop(k, q), yo=Yop(k, q), p=pX[k], q=q, k=k: e.matmul(out=psum[p][:, qsl(q)], lhsT=ident_bf[:, :], rhs=xo, start=True, stop=False),
                                     reads=[ident_bf_b, Xs_b[k]], writes=psum_b[pX[k]], skip_same=True)
                                S.op("pe", lambda e, xo=Xop(k, q), yo=Yop(k, q), p=pX[k], q=q, k=k: e.matmul(out=psum[p][:, qsl(q)], lhsT=yo, rhs=Ps[k][:, qsl(q)], start=False, stop=True),
                                     reads=[Ys_b[k], Ps_b[k]], writes=psum_b[pX[k]], skip_same=True)
                    pY = [bank() for k in G]
                    for k in G:
                        via_act = (k >= 2)
                        for q in range(4):
                            if via_act:
                                S.op("pe", lambda e, xo=Xop(k, q), yo=Yop(k, q), p=pY[k], q=q, k=k: e.matmul(out=psum[p][:, qsl(q)], lhsT=ident_bf[:, :], rhs=yo, start=True, stop=False),
                                     reads=[ident_bf_b, Ys_b[k]], writes=psum_b[pY[k]], skip_same=True)
                            S.op("pe", lambda e, xo=Xop(k, q), yo=Yop(k, q), p=pY[k], q=q, k=k, via_act=via_act: e.matmul(out=psum[p][:, qsl(q)], lhsT=Ps[k][:, qsl(q)], rhs=yo, start=not via_act, stop=True),
                                 reads=[Ps_b[k], Ys_b[k]], writes=psum_b[pY[k]], skip_same=True)
                    if l < 7:
                        for k in G:
                            S.op("act", lambda e, p=pX[k], k=k: e.activation(out=Xs[k][:, :], in_=psum[p][:, :], func=AF.Copy),
                                 reads=psum_b[pX[k]], writes=[Xs_b[k]])
                    for k in G:
                        sl = slice(k * 512, (k + 1) * 512)
                        dst = Ys[k][:, :] if l < 7 else VT[d][:, sl]
                        dst_b = Ys_b[k] if l < 7 else VT_b[d]
                        if k >= 2:
                            S.op("act", lambda e, p=pY[k], dst=dst: e.activation(out=dst, in_=psum[p][:, :], func=AF.Copy),
                                 reads=psum_b[pY[k]], writes=[dst_b])
                        else:
                            yin = bc4(ident_bf[:, :]) if l == 1 else r4(Ys[k][:, :])
                            S.op("dve", lambda e, p=pY[k], dst=dst, yin=yin: e.tensor_tensor(out=r4(dst), in0=r4(psum[p][:, :]), in1=yin, op=ALU.add),
                                 reads=psum_b[pY[k]] + [Ys_b[k], ident_bf_b], writes=[dst_b])
                pH = [bank() for k in G]
                for k in G:
                    for q in range(4):
                        c = 4 * k + q
                        cs_ = slice(c * 128, (c + 1) * 128)
                        S.op("pe", lambda e, p=pH[k], q=q, c=c, cs_=cs_, d=d: e.matmul(
                            out=psum[p][:, qsl(q)], lhsT=kg[d][:, c, :], rhs=VT[d][:, cs_], start=True, stop=True),
                            reads=[kg_b[d], VT_b[d]], writes=psum_b[pH[k]], skip_same=True)
                for k in G:
                    sl = slice(k * 512, (k + 1) * 512)
                    S.op("act", lambda e, p=pH[k], d=d, sl=sl: e.activation(out=wpn[d][:, sl], in_=psum[p][:, :], func=AF.Copy, scale=-1.0),
                         reads=psum_b[pH[k]], writes=[wpn_b[d]])

            for d in range(2):
                S.op("pool", lambda e, d=d: e.memset(S32[d][:, :], 0.0), writes=[S32_b[d]])
                S.op("pool", lambda e, d=d: e.memset(Sbf[d][:, :], 0.0), writes=[Sbf_b[d]])
            pend = []
            if h + 1 < nheads:
                S.rec_begin()
                emit_inputs(h + 1)
                pend = S.rec_end()
            per_step = (len(pend) + 2 * NT - 1) // (2 * NT) if pend else 0
            for s in range(NT):
                for d in range(2):
                    c = s if d == 0 else NT - 1 - s
                    cs_ = slice(c * 128, (c + 1) * 128)
                    col = d * 8 + h
                    pv, pS, pO = 3 * d, 3 * d + 1, 3 * d + 2
                    S.op("pe", lambda e, pv=pv, d=d, c=c, cs_=cs_, vtm=vtm: e.matmul(
                        out=psum[pv][:, 0:128], lhsT=VT[d][:, cs_], rhs=vtm[:, c, :], start=True, stop=False),
                        reads=[VT_b[d], vtm_b], writes=[psum_b[pv][0]], skip_same=True)
                    S.op("pe", lambda e, pv=pv, d=d, cs_=cs_: e.matmul(
                        out=psum[pv][:, 0:128], lhsT=wpn[d][:, cs_], rhs=Sbf[d][:, :], start=False, stop=True),
                        reads=[wpn_b[d], Sbf_b[d]], writes=[psum_b[pv][0]], skip_same=True)
                    S.op("act", lambda e, pv=pv, d=d, c=c, col=col: e.activation(
                        out=vnb[d][:, :], in_=psum[pv][:, 0:128], func=AF.Copy, scale=gtm[:, c, 32 + col:32 + col + 1]),
                        reads=[psum_b[pv][0], gtm_b], writes=[vnb_b[d]])
                    S.op("pe", lambda e, pS=pS, d=d, c=c: e.matmul(
                        out=psum[pS][:, 0:128], lhsT=ktl[d][:, c, :], rhs=vnb[d][:, :], start=True, stop=True),
                        reads=[ktl_b[d], vnb_b[d]], writes=[psum_b[pS][1]], skip_same=True)
                    S.op("pe", lambda e, pO=pO, d=d, cs_=cs_: e.matmul(
                        out=psum[pO][:, 0:128], lhsT=Sbf[d][:, :], rhs=qdT[d][:, cs_], start=True, stop=False),
                        reads=[Sbf_b[d], qdT_b[d]], writes=[psum_b[pO][2]], skip_same=True)
                    S.op("pe", lambda e, pO=pO, d=d, cs_=cs_: e.matmul(
                        out=psum[pO][:, 0:128], lhsT=vnb[d][:, :], rhs=atT[d][:, cs_], start=False, stop=True),
                        reads=[vnb_b[d], atT_b[d]], writes=[psum_b[pO][2]], skip_same=True)
                    S.op("dve", lambda e, pS=pS, d=d, c=c: e.scalar_tensor_tensor(
                        out=S32[d][:, :], in0=S32[d][:, :], scalar=dcl[:, d, c:c + 1], in1=psum[pS][:, 0:128],
                        op0=ALU.mult, op1=ALU.add),
                        reads=[S32_b[d], dcl_b[d], psum_b[pS][1]], writes=[S32_b[d]])
                    S.op("act", lambda e, d=d: e.activation(out=Sbf[d][:, :], in_=S32[d][:, :], func=AF.Copy),
                         reads=[S32_b[d]], writes=[Sbf_b[d]])
                    S.op("dve", lambda e, pO=pO, d=d, cs_=cs_: e.tensor_copy(out=oacc[d][:, cs_], in_=psum[pO][:, 0:128]),
                         reads=[psum_b[pO][2]], writes=[oacc_b[d]])
                    S.play(pend, per_step)
            S.play(pend)

            yi = h % 2
            S.op("dve", lambda e: e.tensor_tensor(out=oacc[0][:, :], in0=oacc[0][:, :], in1=oacc[1][:, :], op=ALU.add),
                 reads=[oacc_b[0], oacc_b[1]], writes=[oacc_b[0]])
            if "gdn_o" in C.dbg_out and h == 0:
                S.dma("sp", C.dbg_out["gdn_o"][:, :], oacc[0][:, :], reads=[oacc_b[0]])
            for blk in range(4):
                sl = slice(blk * 512, (blk + 1) * 512)
                pN = bank()
                S.op("act", lambda e, sl=sl: e.activation(out=sq[:, :], in_=oacc[0][:, sl], func=AF.Square),
                     reads=[oacc_b[0]], writes=[sq_b])
                S.op("pe", lambda e, pN=pN: e.matmul(out=psum[pN][:, :], lhsT=ONES, rhs=sq[:, :], start=True, stop=True),
                     reads=[sq_b, cst_b], writes=psum_b[pN], skip_same=True)
                S.op("act", lambda e, pN=pN: e.activation(out=rs[:, :], in_=psum[pN][:, :], func=AF.Ln, scale=1.0 / 128, bias=EPS),
                     reads=psum_b[pN], writes=[rs_b])
                S.op("act", lambda e: e.activation(out=rs[:, :], in_=rs[:, :], func=AF.Exp, scale=-0.5), reads=[rs_b], writes=[rs_b])
                S.op("dve", lambda e, sl=sl: e.scalar_tensor_tensor(
                    out=tt[:, :], in0=oacc[0][:, sl], scalar=gn[:, 0:1], in1=rs[:, :], op0=ALU.mult, op1=ALU.mult),
                    reads=[oacc_b[0], gn_b, rs_b], writes=[tt_b])
                S.op("dve", lambda e, sl=sl, yi=yi, sz=sz: e.tensor_tensor(out=yb[yi][:, sl], in0=tt[:, :], in1=sz[:, sl], op=ALU.mult),
                     reads=[tt_b, sz_b], writes=[yb_b[yi]])
            S.dma("sp", C.y_scr[1024 + h * 128:1024 + (h + 1) * 128, :], yb[yi][:, :], reads=[yb_b[yi]])


def phase_branch(C):
    nc, S, debug = C.nc, C.S, C.debug
    psum, psum_b = C.psum, C.psum_b
    scr = C.scr
    ident_bf, ident_bf_b = C.ident_bf, C.ident_bf_b
    with contextlib.ExitStack() as oes:
        mergedT = oes.enter_context(nc.sbuf_tensor("mergedT", [128, KC, T], BF16))
        mg_b = [Buf("mg%d" % t) for t in range(NT)]
        with contextlib.ExitStack() as pes:
            def sb(name, shape, dt):
                return pes.enter_context(nc.sbuf_tensor(name, list(shape), dt))
            yT = sb("yT", [128, KC, T], BF16)
            yT_b = [Buf("yT%d" % k) for k in range(KC)]
            y_r = C.y_scr.rearrange("(kc p) t -> p kc t", p=128)
            for k in range(KC):
                S.dma("sp", yT[:, k, :], y_r[:, k, :], writes=[yT_b[k]])
            NWB = 3
            wg = [sb("wbg%d" % i, [128, 8, 128], BF16) for i in range(NWB)]
            wd = [sb("wbd%d" % i, [128, 8, 128], BF16) for i in range(NWB)]
            wg_b = [Buf("wbg") for i in range(NWB)]
            wd_b = [Buf("wbd") for i in range(NWB)]
            gg = [sb("gg%d" % i, [128, T], F32) for i in range(2)]
            gd = [sb("gd%d" % i, [128, T], F32) for i in range(2)]
            gg_b = [Buf("gg") for i in range(2)]
            gd_b = [Buf("gd") for i in range(2)]
            sgg = [sb("sgg%d" % i, [128, 512], F32) for i in range(2)]
            sgd = [sb("sgd%d" % i, [128, 512], F32) for i in range(2)]
            sgg_b = [Buf("sgg") for i in range(2)]
            sgd_b = [Buf("sgd") for i in range(2)]
            t1 = [sb("t1_%d" % i, [128, 512], F32) for i in range(2)]
            t2 = [sb("t2_%d" % i, [128, 512], F32) for i in range(2)]
            t1_b = [Buf("t1") for i in range(2)]
            t2_b = [Buf("t2") for i in range(2)]
            wbg_r = C.w_branch_gla.rearrange("(kc p) n -> p kc n", p=128)
            wbd_r = C.w_branch_gdn.rearrange("(kc p) n -> p kc n", p=128)
            cnt = 0
            for db in range(KC):
                wi = db % NWB
                gi = db % 2
                S.dma("pool", wg[wi][:, :, :], wbg_r[:, :, db * 128:(db + 1) * 128], writes=[wg_b[wi]])
                S.dma("pool", wd[wi][:, :, :], wbd_r[:, :, db * 128:(db + 1) * 128], writes=[wd_b[wi]])
                S.dma("sp", gg[gi][:, :], scr[C_BG + db * 128:C_BG + (db + 1) * 128, :], writes=[gg_b[gi]])
                S.dma("sp", gd[gi][:, :], scr[C_BD + db * 128:C_BD + (db + 1) * 128, :], writes=[gd_b[gi]])
                for blk in range(4):
                    sl = slice(blk * 512, (blk + 1) * 512)
                    pG = (2 * cnt) % 8
                    pD = (2 * cnt + 1) % 8
                    j = cnt % 2
                    cnt += 1
                    for kc in range(8):
                        S.op("pe", lambda e, pG=pG, wi=wi, kc=kc, sl=sl: e.matmul(
                            out=psum[pG][:, :], lhsT=wg[wi][:, kc, :], rhs=yT[:, kc, sl], start=(kc == 0), stop=(kc == 7)),
                            reads=[wg_b[wi], yT_b[kc]], writes=psum_b[pG], skip_same=True)
                    for kc in range(8):
                        S.op("pe", lambda e, pD=pD, wi=wi, kc=kc, sl=sl: e.matmul(
                            out=psum[pD][:, :], lhsT=wd[wi][:, kc, :], rhs=yT[:, 8 + kc, sl], start=(kc == 0), stop=(kc == 7)),
                            reads=[wd_b[wi], yT_b[8 + kc]], writes=psum_b[pD], skip_same=True)
                    S.op("act", lambda e, j=j, gi=gi, sl=sl: e.activation(out=sgg[j][:, :], in_=gg[gi][:, sl], func=AF.Sigmoid),
                         reads=[gg_b[gi]], writes=[sgg_b[j]])
                    S.op("act", lambda e, j=j, gi=gi, sl=sl: e.activation(out=sgd[j][:, :], in_=gd[gi][:, sl], func=AF.Sigmoid),
                         reads=[gd_b[gi]], writes=[sgd_b[j]])
                    S.op("dve", lambda e, j=j, pG=pG: e.tensor_tensor(out=t1[j][:, :], in0=psum[pG][:, :], in1=sgg[j][:, :], op=ALU.mult),
                         reads=psum_b[pG] + [sgg_b[j]], writes=[t1_b[j]])
                    S.op("dve", lambda e, j=j, pD=pD: e.tensor_tensor(out=t2[j][:, :], in0=psum[pD][:, :], in1=sgd[j][:, :], op=ALU.mult),
                         reads=psum_b[pD] + [sgd_b[j]], writes=[t2_b[j]])
                    S.op("dve", lambda e, j=j, db=db, sl=sl: e.tensor_tensor(out=mergedT[:, db, sl], in0=t1[j][:, :], in1=t2[j][:, :], op=ALU.add),
                         reads=[t1_b[j], t2_b[j]], writes=mg_b[4 * blk:4 * blk + 4])
        S.barrier()
        if "merged" in C.dbg_out:
            S.dma("sp", C.dbg_out["merged"].rearrange("(kc p) t -> p kc t", p=128), mergedT[:, :, :], reads=mg_b)
        with contextlib.ExitStack() as pes:
            def sb(name, shape, dt):
                return pes.enter_context(nc.sbuf_tensor(name, list(shape), dt))
            Wout = sb("Wout", [128, KC, D], BF16)
            Wout_b = [Buf("Wout%d" % k) for k in range(KC)]
            wo_r = C.w_out.rearrange("(kc p) n -> p kc n", p=128)
            for k in range(KC):
                S.dma("pool", Wout[:, k, :], wo_r[:, k, :], writes=[Wout_b[k]], max_dma_last_dim=4096)
            g2b = sb("g2b", [128, D], F32)
            g2b_b = Buf("g2b")
            S.dma("sp", g2b[:, :], C.norm2_g.partition_broadcast(128), writes=[g2b_b])
            xt = [sb("xt2_%d" % i, [128, D], F32) for i in range(2)]
            xt_b = [Buf("xt2") for i in range(2)]
            x1t = [sb("x1t%d" % i, [128, D], F32) for i in range(2)]
            x1t_b = [Buf("x1t") for i in range(2)]
            junk = sb("junk2", [128, D], BF16)
            junk_b = Buf("junk2")
            hb = [sb("hb2_%d" % i, [128, D], BF16) for i in range(2)]
            hb_b = [Buf("hb2") for i in range(2)]
            hst = [sb("hst%d" % i, [128, KC, 128], BF16) for i in range(2)]
            hst_b = [Buf("hst") for i in range(2)]
            stat = sb("stat2", [128, 4 * NT], F32)
            stat_b = [Buf("stat2") for i in range(NT)]
            h2_r = C.h2_scr.rearrange("(kc p) t -> p kc t", p=128)
            pcnt = 0
            for t in range(NT):
                i = t % 2
                ts_ = slice(t * 128, (t + 1) * 128)
                S.dma("sp", xt[i][:, :], C.x[ts_, :], writes=[xt_b[i]])
                for cb_ in range(4):
                    pi = cb_ + 4 * (t % 2)
                    csl = slice(cb_ * 512, (cb_ + 1) * 512)
                    for kc in range(KC):
                        S.op("pe", lambda e, pi=pi, kc=kc, ts_=ts_, csl=csl: e.matmul(
                            out=psum[pi][:, :], lhsT=mergedT[:, kc, ts_], rhs=Wout[:, kc, csl], start=(kc == 0), stop=(kc == KC - 1)),
                            reads=[mg_b[t], Wout_b[kc]], writes=psum_b[pi], skip_same=True)
                    S.op("dve", lambda e, pi=pi, i=i, csl=csl: e.tensor_tensor(out=x1t[i][:, csl], in0=psum[pi][:, :], in1=xt[i][:, csl], op=ALU.add),
                         reads=psum_b[pi] + [xt_b[i]], writes=[x1t_b[i]])
                S.dma("sp", C.x1_scr[ts_, :], x1t[i][:, :], reads=[x1t_b[i]])
                ss = stat[:, 4 * t:4 * t + 1]
                lnv = stat[:, 4 * t + 1:4 * t + 2]
                rstd = stat[:, 4 * t + 2:4 * t + 3]
                S.op("act", lambda e, i=i, ss=ss: e.activation(out=junk[:, :], in_=x1t[i][:, :], func=AF.Square, accum_out=ss),
                     reads=[x1t_b[i]], writes=[junk_b, stat_b[t]])
                S.op("act", lambda e, ss=ss, lnv=lnv: e.activation(out=lnv, in_=ss, func=AF.Ln, scale=1.0 / D, bias=EPS),
                     reads=[stat_b[t]], writes=[stat_b[t]])
                S.op("act", lambda e, rstd=rstd, lnv=lnv: e.activation(out=rstd, in_=lnv, func=AF.Exp, scale=-0.5),
                     reads=[stat_b[t]], writes=[stat_b[t]])
                S.op("dve", lambda e, i=i, rstd=rstd: e.scalar_tensor_tensor(
                    out=hb[i][:, :], in0=x1t[i][:, :], scalar=rstd, in1=g2b[:, :], op0=ALU.mult, op1=ALU.mult),
                    reads=[x1t_b[i], stat_b[t], g2b_b], writes=[hb_b[i]])
                for g in range(4):
                    pi = (pcnt % 2) * 4 + (3 - g)
                    pt = psum[pi].bitcast(BF16)
                    for q in range(4):
                        kc = g * 4 + q
                        S.op("pe", lambda e, pt=pt, q=q, i=i, kc=kc: e.transpose(
                            out=pt[:, q * 128:(q + 1) * 128], in_=hb[i][:, kc * 128:(kc + 1) * 128], identity=ident_bf[:, :]),
                            reads=[hb_b[i], ident_bf_b], writes=psum_b[pi], skip_same=True)
                    S.op("act", lambda e, pt=pt, g=g, i=i: e.activation(
                        out=hst[i][:, g * 4:(g + 1) * 4, :], in_=pt[:, 0:512].rearrange("p (a b) -> p a b", a=4), func=AF.Copy),
                        reads=psum_b[pi], writes=[hst_b[i]])
                pcnt += 1
                S.dma("sp", h2_r[:, :, ts_], hst[i][:, :, :], reads=[hst_b[i]])


def phase_ffn(C):
    nc, S, debug = C.nc, C.S, C.debug
    psum, psum_b = C.psum, C.psum_b
    TH = 1024
    NB3 = 342
    with contextlib.ExitStack() as pes:
        def sb(name, shape, dt):
            return pes.enter_context(nc.sbuf_tensor(name, list(shape), dt))
        h2T = sb("h2T", [128, KC, TH + 2], BF16)
        h2T_b = Buf("h2T")
        aT = sb("aT", [128, 44, TH], BF16)
        aT_b = [Buf("aT%d" % j) for j in range(44)]
        NW = 5
        wu = [sb("wu%d" % i, [128, KC, 128], BF16) for i in range(NW)]
        wu_b = [Buf("wu") for i in range(NW)]
        pg = [sb("pg%d" % i, [128, TH + 2], F32) for i in range(2)]
        pg_b = [Buf("pg") for i in range(2)]
        cg = [sb("cg%d" % i, [128, TH], F32) for i in range(2)]
        cg_b = [Buf("cg") for i in range(2)]
        fw = sb("fw", [128, 88 * 4], F32)
        fw_b = Buf("fw")
        S.dma("sp", fw[:, :], C.ffn_cw_d[:, :], writes=[fw_b])
        NWD = 2
        wdn = [sb("wdn%d" % i, [128, 44, 128], BF16) for i in range(NWD)]
        wdn_b = [Buf("wdn") for i in range(NWD)]
        x1s = [sb("x1s%d" % i, [128, 512], F32) for i in range(2)]
        x1s_b = [Buf("x1s") for i in range(2)]
        xo = [sb("xo%d" % i, [128, 512], F32) for i in range(2)]
        xo_b = [Buf("xo") for i in range(2)]
        wup_r = C.w_up.rearrange("(kc p) n -> p kc n", p=128)
        wdn_r = C.w_down.rearrange("(fc p) n -> p fc n", p=128)
        h2_r = C.h2_scr.rearrange("(kc p) t -> p kc t", p=128)
        ucnt = 0
        pcnt = 0
        dcnt = 0
        for half in range(2):
            t0 = half * TH
            S.op("pool", lambda e: e.memset(h2T[:, :, :], 0.0), writes=[h2T_b])
            lo = max(t0 - 1, 0)
            hi = min(t0 + TH + 1, T)
            S.dma("sp", h2T[:, :, lo - (t0 - 1):hi - (t0 - 1)], h2_r[:, :, lo:hi], writes=[h2T_b])
            ulist = [(j, which) for j in range(44) for which in range(2)]
            PF = NW - 1

            def issue_unit(k):
                j_, which_ = ulist[k]
                c0_ = which_ * D_FF + j_ * 128
                wi_ = (ucnt + k) % NW
                S.dma("pool", wu[wi_][:, :, :], wup_r[:, :, c0_:c0_ + 128], writes=[wu_b[wi_]])

            for k in range(min(PF, len(ulist))):
                issue_unit(k)
            for k, (j, which) in enumerate(ulist):
                if k + PF < len(ulist):
                    issue_unit(k + PF)
                blk = which * 44 + j
                wi = (ucnt + k) % NW
                pgi = k % 2
                for b3 in range(3):
                    pi = pcnt % 8
                    pcnt += 1
                    s3 = slice(b3 * NB3, (b3 + 1) * NB3)
                    for kc in range(KC):
                        S.op("pe", lambda e, pi=pi, wi=wi, kc=kc, s3=s3: e.matmul(
                            out=psum[pi][:, 0:NB3], lhsT=wu[wi][:, kc, :], rhs=h2T[:, kc, s3], start=(kc == 0), stop=(kc == KC - 1)),
                            reads=[wu_b[wi], h2T_b], writes=psum_b[pi], skip_same=True)
                    if pcnt % 2 == 0:
                        S.op("act", lambda e, pi=pi, pgi=pgi, s3=s3: e.activation(out=pg[pgi][:, s3], in_=psum[pi][:, 0:NB3], func=AF.Copy),
                             reads=psum_b[pi], writes=[pg_b[pgi]])
                    else:
                        S.op("dve", lambda e, pi=pi, pgi=pgi, s3=s3: e.tensor_copy(out=pg[pgi][:, s3], in_=psum[pi][:, 0:NB3]),
                             reads=psum_b[pi], writes=[pg_b[pgi]])
                w0 = fw[:, blk * 4:blk * 4 + 1]
                w1 = fw[:, blk * 4 + 1:blk * 4 + 2]
                w2 = fw[:, blk * 4 + 2:blk * 4 + 3]
                bb = fw[:, blk * 4 + 3:blk * 4 + 4]
                S.op("act", lambda e, pgi=pgi, which=which, w1=w1, bb=bb: e.activation(
                    out=cg[which][:, :], in_=pg[pgi][:, 1:TH + 1], func=AF.Identity, scale=w1, bias=bb),
                    reads=[pg_b[pgi], fw_b], writes=[cg_b[which]])
                S.op("dve", lambda e, pgi=pgi, which=which, w0=w0: e.scalar_tensor_tensor(
                    out=cg[which][:, :], in0=pg[pgi][:, 0:TH], scalar=w0, in1=cg[which][:, :], op0=ALU.mult, op1=ALU.add),
                    reads=[pg_b[pgi], fw_b, cg_b[which]], writes=[cg_b[which]])
                S.op("dve", lambda e, pgi=pgi, which=which, w2=w2: e.scalar_tensor_tensor(
                    out=cg[which][:, :], in0=pg[pgi][:, 2:TH + 2], scalar=w2, in1=cg[which][:, :], op0=ALU.mult, op1=ALU.add),
                    reads=[pg_b[pgi], fw_b, cg_b[which]], writes=[cg_b[which]])
                if which == 0:
                    S.op("act", lambda e: e.activation(out=cg[0][:, :], in_=cg[0][:, :], func=AF.Silu),
                         reads=[cg_b[0]], writes=[cg_b[0]])
                else:
                    S.op("dve", lambda e, j=j: e.tensor_tensor(out=aT[:, j, :], in0=cg[0][:, :], in1=cg[1][:, :], op=ALU.mult),
                         reads=[cg_b[0], cg_b[1]], writes=[aT_b[j]])
            ucnt += len(ulist)
            for cgp in range(4):
                units = []
                for u in range(4):
                    c0 = cgp * 512 + u * 128
                    wi = dcnt % NWD
                    dcnt += 1
                    S.dma("pool", wdn[wi][:, :, :], wdn_r[:, :, c0:c0 + 128], writes=[wdn_b[wi]])
                    units.append(wi)
                    for tt in range(TH // 128):
                        pi = tt % 8
                        ts_ = slice(tt * 128, (tt + 1) * 128)
                        for fc in range(44):
                            S.op("pe", lambda e, pi=pi, u=u, fc=fc, ts_=ts_, wi=wi: e.matmul(
                                out=psum[pi][:, u * 128:(u + 1) * 128], lhsT=aT[:, fc, ts_], rhs=wdn[wi][:, fc, :],
                                start=(fc == 0), stop=(fc == 43)),
                                reads=[aT_b[fc], wdn_b[wi]], writes=psum_b[pi], skip_same=True)
                for tt in range(TH // 128):
                    pi = tt % 8
                    i = tt % 2
                    rows = slice(t0 + tt * 128, t0 + (tt + 1) * 128)
                    csl = slice(cgp * 512, (cgp + 1) * 512)
                    S.dma("sp", x1s[i][:, :], C.x1_scr[rows, csl], writes=[x1s_b[i]])
                    S.op("dve", lambda e, pi=pi, i=i: e.tensor_tensor(out=xo[i][:, :], in0=psum[pi][:, :], in1=x1s[i][:, :], op=ALU.add),
                         reads=psum_b[pi] + [x1s_b[i]], writes=[xo_b[i]])
                    S.dma("sp", C.x2_scr[rows, csl], xo[i][:, :], reads=[xo_b[i]])


def phase_final(C):
    nc, S, debug = C.nc, C.S, C.debug
    with contextlib.ExitStack() as pes:
        def sb(name, shape, dt):
            return pes.enter_context(nc.sbuf_tensor(name, list(shape), dt))
        gfb = sb("gfb", [128, D], F32)
        gfb_b = Buf("gfb")
        S.dma("sp", gfb[:, :], C.final_norm_g.partition_broadcast(128), writes=[gfb_b])
        NBF = 3
        xt = [sb("xf%d" % i, [128, D], F32) for i in range(NBF)]
        xt_b = [Buf("xf") for i in range(NBF)]
        ot = [sb("of%d" % i, [128, D], F32) for i in range(NBF)]
        ot_b = [Buf("of") for i in range(NBF)]
        junk = sb("junk3", [128, D], BF16)
        junk_b = Buf("junk3")
        stat = sb("stat3", [128, 4 * NT], F32)
        stat_b = [Buf("stat3") for i in range(NT)]
        for t in range(NT):
            i = t % NBF
            ts_ = slice(t * 128, (t + 1) * 128)
            S.dma("sp", xt[i][:, :], C.x2_scr[ts_, :], writes=[xt_b[i]])
            ss = stat[:, 4 * t:4 * t + 1]
            lnv = stat[:, 4 * t + 1:4 * t + 2]
            rstd = stat[:, 4 * t + 2:4 * t + 3]
            S.op("act", lambda e, i=i, ss=ss: e.activation(out=junk[:, :], in_=xt[i][:, :], func=AF.Square, accum_out=ss),
                 reads=[xt_b[i]], writes=[junk_b, stat_b[t]])
            S.op("act", lambda e, ss=ss, lnv=lnv: e.activation(out=lnv, in_=ss, func=AF.Ln, scale=1.0 / D, bias=EPS),
                 reads=[stat_b[t]], writes=[stat_b[t]])
            S.op("act", lambda e, rstd=rstd, lnv=lnv: e.activation(out=rstd, in_=lnv, func=AF.Exp, scale=-0.5),
                 reads=[stat_b[t]], writes=[stat_b[t]])
            S.op("dve", lambda e, i=i, rstd=rstd: e.scalar_tensor_tensor(
                out=ot[i][:, :], in0=xt[i][:, :], scalar=rstd, in1=gfb[:, :], op0=ALU.mult, op1=ALU.mult),
                reads=[xt_b[i], stat_b[t], gfb_b], writes=[ot_b[i]])
            S.dma("sp", C.out[ts_, :], ot[i][:, :], reads=[ot_b[i]])

NC2 = 1024
NCB = 14 * 128


_NC_CACHE = {}


def _consts():
    j = np.arange(128)[:, None]
    i = np.arange(128)[None, :]
    cst = np.zeros((128, NCST), np.float32)
    cst[:, 0:128] = np.where(j <= i, -1.0 / 16, 0.0)
    cst[:, 128:256] = np.where(j >= i, -1.0 / 16, 0.0)
    cst[:, 256:384] = np.where(j > i, -1.0 / 16, 0.0)
    cst[:, 384:512] = np.where(j < i, -1.0 / 16, 0.0)
    cst[:, 512:640] = 1.0
    cst[:, 640:1152] = np.tile(np.where(j <= i, 1.0, 0.0), (1, 4))
    cst[:, 1152:1664] = np.tile(np.where(j >= i, 1.0, 0.0), (1, 4))
    c2 = np.zeros((128, NC2), np.float32)
    c2[:, 0:128] = np.where(j <= i, 1.0, 0.0)
    c2[:, 128:256] = np.where(j >= i, 1.0, 0.0)
    c2[:, 256:384] = np.where(j > i, 1.0, 0.0)
    c2[:, 384:512] = np.where(j < i, 1.0, 0.0)
    c2[:, 512:640] = np.where(j <= i, 0.0, -30000.0)
    c2[:, 640:768] = np.where(j >= i, 0.0, -30000.0)
    c2[:, 768:896] = np.eye(128)
    c2[:, 896:1024] = -1.0
    cb = np.zeros((128, NCB), np.float32)
    for d in range(2):
        for l in range(1, 8):
            b = 1 << (l - 1)
            same = (i // (2 * b)) == (j // (2 * b))
            if d == 0:
                m = same & ((j % (2 * b)) >= b) & ((i % (2 * b)) < b)
            else:
                m = same & ((j % (2 * b)) < b) & ((i % (2 * b)) >= b)
            o = (d * 7 + (l - 1)) * 128
            cb[:, o:o + 128] = -m.astype(np.float32)
    return {
        "ident_bf": np.eye(128, dtype=np.float32).astype(ml_dtypes.bfloat16),
        "cst": cst,
        "cst2": c2,
        "cstb": cb.astype(ml_dtypes.bfloat16),
    }


def make_in_maps(inputs, n_cores=8):
    c = _consts()
    maps = []
    xs = np.ascontiguousarray(inputs["x"])
    for b in range(n_cores):
        m = {
            "x": xs[b],
            "norm1_g": np.ascontiguousarray(inputs["norm1_g"]).reshape(1, D),
            "w_in": np.ascontiguousarray(inputs["w_in"]).reshape(D, N_IN),
            "gla_decay_w_f": np.ascontiguousarray(inputs["gla_decay_w_f"]).reshape(16, 1024),
            "gla_decay_w_b": np.ascontiguousarray(inputs["gla_decay_w_b"]).reshape(16, 1024),
            "gla_decay_b_f": np.ascontiguousarray(inputs["gla_decay_b_f"]).reshape(1, 1024),
            "gla_decay_b_b": np.ascontiguousarray(inputs["gla_decay_b_b"]).reshape(1, 1024),
            "gla_norm_g": np.ascontiguousarray(inputs["gla_norm_g"]).reshape(1, 128),
            "gdn_norm_g": np.ascontiguousarray(inputs["gdn_norm_g"]).reshape(1, 128),
            "w_branch_gla": np.ascontiguousarray(inputs["w_branch_gla"]).reshape(1024, D),
            "w_branch_gdn": np.ascontiguousarray(inputs["w_branch_gdn"]).reshape(1024, D),
            "w_out": np.ascontiguousarray(inputs["w_out"]).reshape(D, D),
            "norm2_g": np.ascontiguousarray(inputs["norm2_g"]).reshape(1, D),
            "w_up": np.ascontiguousarray(inputs["w_up"]).reshape(D, 2 * D_FF),
            "w_down": np.ascontiguousarray(inputs["w_down"]).reshape(D_FF, D),
            "ffn_cw": np.ascontiguousarray(np.concatenate([
                np.asarray(inputs["ffn_conv_w"]).reshape(3, 88, 128), np.asarray(inputs["ffn_conv_b"]).reshape(1, 88, 128)],
                axis=0).transpose(2, 1, 0).reshape(128, 88 * 4)),
            "final_norm_g": np.ascontiguousarray(inputs["final_norm_g"]).reshape(1, D),
            "gdn_cw": np.ascontiguousarray(
                np.asarray(inputs["gdn_conv_w"]).reshape(3, 24, 128).transpose(2, 1, 0).reshape(128, 72)),
            "gdn_hp": np.ascontiguousarray(np.stack([
                np.concatenate([np.asarray(inputs["gdn_a_log_f"]).reshape(8), np.asarray(inputs["gdn_a_log_b"]).reshape(8)]),
                np.concatenate([np.asarray(inputs["gdn_dt_bias_f"]).reshape(8), np.asarray(inputs["gdn_dt_bias_b"]).reshape(8)]),
            ], axis=1).astype(np.float32)),
        }
        m.update(c)
        maps.append(m)
    return maps


def kernel(**inputs):
    nc = build_nc()
    in_maps = make_in_maps(inputs, 8)
    res = run_bass_kernel_spmd(nc, in_maps, core_ids=list(range(8)))
    return np.stack([np.asarray(r["out"]) for r in res.results], axis=0)
```

```python
import contextlib
import numpy as np
import ml_dtypes
import concourse.bass as bass
import concourse.mybir as mybir
from concourse.bass_utils import run_bass_kernel_spmd

F32 = mybir.dt.float32
BF16 = mybir.dt.bfloat16
AF = mybir.ActivationFunctionType
ALU = mybir.AluOpType

T = 2048
D = 2048
KC = 16
NT = 16
N_IN = 12352
D_FF = 5632
EPS = 1e-6

C_GQ, C_GK, C_GV, C_GG = 0, 1024, 2048, 3072
C_LR = 4096
C_DQ, C_DK, C_DV = 4128, 5152, 6176
C_DZ = 7200
C_AB = 8224
C_BG = 8256
C_BD = 10304


class Buf:
    __slots__ = ("w", "r", "name", "excl")

    def __init__(self, name="", excl=False):
        self.w = None
        self.r = {}
        self.name = name
        self.excl = excl


class DSem:
    __slots__ = ("handle", "count")

    def __init__(self, handle):
        self.handle = handle
        self.count = 0


class Sched:
    ENG = ["pe", "act", "dve", "pool", "sp"]

    def __init__(self, nc, es, n_dsem=24):
        self.nc = nc
        self.sem = {e: es.enter_context(nc.semaphore("s_" + e)) for e in self.ENG}
        self.cnt = {e: 0 for e in self.ENG}
        self.seen = {e: {} for e in self.ENG}
        self.prog = {e: [] for e in self.ENG}
        self.dsems = [DSem(es.enter_context(nc.semaphore("d%d" % i))) for i in range(n_dsem)]
        self.dnext = 0

    def _waits(self, eng, reads, writes, skip_same):
        deps = {}

        def add(k, v):
            if skip_same and k == eng:
                return
            if deps.get(k, 0) < v:
                deps[k] = v

        for b in reads:
            if b.w is not None:
                add(*b.w)
        for b in writes:
            if b.w is not None:
                add(*b.w)
            for k, v in b.r.items():
                add(k, v)
        out = []
        seen = self.seen[eng]
        for k, v in deps.items():
            if seen.get(k, 0) < v:
                seen[k] = v
                out.append((k.handle if isinstance(k, DSem) else self.sem[k], v))
        return out

    def _mark(self, d, reads, writes):
        k, v = d
        for b in reads:
            if b.r.get(k, 0) < v:
                b.r[k] = v
        for b in writes:
            b.w = d
            b.r = {}

    def rec_begin(self):
        self._rec = []

    def rec_end(self):
        r, self._rec = self._rec, None
        return r

    def play(self, lst, n=None):
        n = len(lst) if n is None else min(n, len(lst))
        for _ in range(n):
            kind, a, kw = lst.pop(0)
            (self.op if kind == "op" else self.dma)(*a, **kw)

    def op(self, eng, fn, reads=(), writes=(), skip_same=False):
        if getattr(self, "_rec", None) is not None:
            self._rec.append(("op", (eng, fn, list(reads), list(writes), skip_same), {}))
            return
        if any(b.excl for b in reads):
            writes = list(writes) + [b for b in reads if b.excl]
            reads = [b for b in reads if not b.excl]
        waits = self._waits(eng, reads, writes, skip_same)
        self.cnt[eng] += 1
        self.prog[eng].append((waits, fn, self.sem[eng], 1))
        self._mark((eng, self.cnt[eng]), reads, writes)

    def barrier(self):
        for eng in self.ENG:
            waits = []
            seen = self.seen[eng]
            for k in self.ENG:
                if k != eng and seen.get(k, 0) < self.cnt[k]:
                    seen[k] = self.cnt[k]
                    waits.append((self.sem[k], self.cnt[k]))
            for ds in self.dsems:
                if ds.count > 0 and seen.get(ds, 0) < ds.count:
                    seen[ds] = ds.count
                    waits.append((ds.handle, ds.count))
            if waits:
                self.prog[eng].append((waits, None, None, 0))

    def dma(self, q, out, in_, reads=(), writes=(), **kw):
        if getattr(self, "_rec", None) is not None:
            self._rec.append(("dma", (q, out, in_, list(reads), list(writes)), dict(kw)))
            return
        ds = self.dsems[self.dnext]
        self.dnext = (self.dnext + 1) % len(self.dsems)
        waits = self._waits(q, reads, writes, False)
        if ds.count > 0 and self.seen[q].get(ds, 0) < ds.count:
            self.seen[q][ds] = ds.count
            waits.append((ds.handle, ds.count))
        ds.count += 16
        self.prog[q].append((waits, (lambda e, o=out, i=in_, kw=kw: e.dma_start(out=o, in_=i, **kw)), ds.handle, 16))
        self._mark((ds, ds.count), reads, writes)

    def finish(self):
        waits = []
        for ds in self.dsems:
            if ds.count > 0:
                waits.append((ds.handle, ds.count))
        self.prog["sp"].append((waits, None, None, 0))

    def emit(self):
        nc = self.nc
        prog = self.prog

        def replay(name, e):
            for waits, fn, sem, inc in prog[name]:
                for s, v in waits:
                    e.wait_ge(s, v)
                if fn is not None:
                    ins = fn(e)
                    ins.then_inc(sem, inc)

        with nc.Block() as block:
            @block.tensor
            def _(e):
                replay("pe", e)

            @block.scalar
            def _(e):
                replay("act", e)

            @block.vector
            def _(e):
                replay("dve", e)

            @block.gpsimd
            def _(e):
                replay("pool", e)

            @block.sync
            def _(e):
                replay("sp", e)


def build_nc(debug=None):
    debug = debug or {}
    nc = bass.Bass("TRN2", target_bir_lowering=False)
    es = contextlib.ExitStack()
    with es:
        _build(nc, es, debug)
    return nc


def _dram_in(nc, name, shape, dt=F32):
    return nc.dram_tensor(name, list(shape), dt, kind="ExternalInput").ap()


class Ctx:
    pass


def _build(nc, es, debug):
    S = Sched(nc, es)
    C = Ctx()
    C.nc, C.S, C.debug = nc, S, debug
    stop_after = debug.get("stop_after", [[None], None])[0][0]

    C.x = _dram_in(nc, "x", [T, D])
    C.norm1_g = _dram_in(nc, "norm1_g", [1, D])
    C.w_in = _dram_in(nc, "w_in", [D, N_IN])
    C.ident_bf_d = _dram_in(nc, "ident_bf", [128, 128], BF16)
    C.cst_d = _dram_in(nc, "cst", [128, NCST])
    C.gla_dw = [_dram_in(nc, "gla_decay_w_f", [16, 1024]), _dram_in(nc, "gla_decay_w_b", [16, 1024])]
    C.gla_db = [_dram_in(nc, "gla_decay_b_f", [1, 1024]), _dram_in(nc, "gla_decay_b_b", [1, 1024])]
    C.gla_norm_g = _dram_in(nc, "gla_norm_g", [1, 128])
    C.gdn_norm_g = _dram_in(nc, "gdn_norm_g", [1, 128])
    C.w_branch_gla = _dram_in(nc, "w_branch_gla", [1024, D])
    C.w_branch_gdn = _dram_in(nc, "w_branch_gdn", [1024, D])
    C.w_out = _dram_in(nc, "w_out", [D, D])
    C.norm2_g = _dram_in(nc, "norm2_g", [1, D])
    C.w_up = _dram_in(nc, "w_up", [D, 2 * D_FF])
    C.w_down = _dram_in(nc, "w_down", [D_FF, D])
    C.ffn_cw_d = _dram_in(nc, "ffn_cw", [128, 88 * 4])
    C.final_norm_g = _dram_in(nc, "final_norm_g", [1, D])
    C.cst2_d = _dram_in(nc, "cst2", [128, NC2])
    C.cstb_d = _dram_in(nc, "cstb", [128, NCB], BF16)
    C.gdn_cw_d = _dram_in(nc, "gdn_cw", [128, 72])
    C.gdn_hp_d = _dram_in(nc, "gdn_hp", [16, 2])
    C.out = nc.dram_tensor("out", [T, D], F32, kind="ExternalOutput").ap()
    C.dbg_out = {}
    for name, (shape, dt) in debug.items():
        if dt is None:
            continue
        C.dbg_out[name] = nc.dram_tensor("dbg_" + name, list(shape), dt, kind="ExternalOutput").ap()
    C.scr = nc.dram_tensor("scr_proj", [N_IN, T], F32).ap()
    C.y_scr = nc.dram_tensor("scr_y", [2048, T], BF16).ap()
    C.h2_scr = nc.dram_tensor("scr_h2", [D, T], BF16).ap()
    C.x1_scr = nc.dram_tensor("scr_x1", [T, D], F32).ap()
    C.x2_scr = nc.dram_tensor("scr_x2", [T, D], F32).ap()

    C.psum = [es.enter_context(nc.psum_tensor("ps%d" % i, [128, 512], F32)) for i in range(8)]
    C.psum_b = [[Buf("ps%d" % i, excl=True)] * 4 for i in range(8)]
    C.ident_bf = es.enter_context(nc.sbuf_tensor("ident_bf_sb", [128, 128], BF16))
    C.ident_bf_b = Buf("ident_bf")
    C.cst = es.enter_context(nc.sbuf_tensor("cst_sb", [128, NCST], F32))
    C.cst_b = Buf("cst")
    S.dma("sp", C.ident_bf[:, :], C.ident_bf_d[:, :], writes=[C.ident_bf_b])
    S.dma("sp", C.cst[:, :], C.cst_d[:, :], writes=[C.cst_b])

    phase_proj(C)
    S.barrier()
    if stop_after != "proj":
        if debug.get("gla_heads", [[8], None])[0][0] > 0:
            phase_gla(C)
            S.barrier()
        if stop_after != "gla":
            if debug.get("gdn_heads", [[8], None])[0][0] > 0:
                phase_gdn(C)
                S.barrier()
            if stop_after != "gdn":
                phase_branch(C)
                S.barrier()
                if stop_after != "branch":
                    phase_ffn(C)
                    S.barrier()
                    phase_final(C)
                    S.barrier()

    if "proj" in C.dbg_out:
        S.dma("sp", C.dbg_out["proj"][:, :], C.scr[0:C.dbg_out["proj"].shape[0], :])
    if "y" in C.dbg_out:
        S.dma("sp", C.dbg_out["y"][:, :], C.y_scr[0:C.dbg_out["y"].shape[0], :])
    if "y2" in C.dbg_out:
        S.dma("sp", C.dbg_out["y2"][:, :], C.y_scr[1024:1024 + C.dbg_out["y2"].shape[0], :])
    for nm, ap_ in (("x1", C.x1_scr), ("x2", C.x2_scr)):
        if nm in C.dbg_out:
            S.dma("sp", C.dbg_out[nm][:, :], ap_[:, :])
    if "h2" in C.dbg_out:
        S.dma("sp", C.dbg_out["h2"][:, :], C.h2_scr[:, :])
    if stop_after is not None:
        S.dma("sp", C.out[0:128, :], C.x[0:128, :])
    S.finish()
    S.emit()


def phase_proj(C):
    nc, S, debug = C.nc, C.S, C.debug
    psum, psum_b = C.psum, C.psum_b
    with contextlib.ExitStack() as pes:
        def sb(name, shape, dt):
            return pes.enter_context(nc.sbuf_tensor(name, list(shape), dt))

        hT = sb("hT", [128, KC, T], BF16)
        hT_b = [Buf("hT%d" % t) for t in range(NT)]
        g1b = sb("g1b", [128, D], F32)
        g1b_b = Buf("g1b")
        S.dma("sp", g1b[:, :], C.norm1_g.partition_broadcast(128), writes=[g1b_b])

        xt = [sb("xt%d" % i, [128, D], F32) for i in range(2)]
        xt_b = [Buf("xt%d" % i) for i in range(2)]
        junk = sb("junk", [128, D], BF16)
        junk_b = Buf("junk")
        hb = [sb("hb%d" % i, [128, D], BF16) for i in range(2)]
        hb_b = [Buf("hb%d" % i) for i in range(2)]
        stat = sb("stat", [128, 4 * NT], F32)
        stat_b = [Buf("stat%d" % i) for i in range(NT)]
        ident_bf, ident_bf_b = C.ident_bf, C.ident_bf_b
        pcnt = 0
        pend0 = []
        for t in range(NT):
            i = t % 2
            S.dma("sp", xt[i][:, :], C.x[t * 128:(t + 1) * 128, :], writes=[xt_b[i]])
            ss = stat[:, 4 * t:4 * t + 1]
            lnv = stat[:, 4 * t + 1:4 * t + 2]
            rstd = stat[:, 4 * t + 2:4 * t + 3]
            S.op("act", lambda e, i=i, ss=ss: e.activation(out=junk[:, :], in_=xt[i][:, :], func=AF.Square, accum_out=ss),
                 reads=[xt_b[i]], writes=[junk_b, stat_b[t]])
            S.op("act", lambda e, ss=ss, lnv=lnv: e.activation(out=lnv, in_=ss, func=AF.Ln, scale=1.0 / D, bias=EPS),
                 reads=[stat_b[t]], writes=[stat_b[t]])
            S.op("act", lambda e, rstd=rstd, lnv=lnv: e.activation(out=rstd, in_=lnv, func=AF.Exp, scale=-0.5),
                 reads=[stat_b[t]], writes=[stat_b[t]])
            S.op("dve", lambda e, i=i, rstd=rstd: e.scalar_tensor_tensor(
                out=hb[i][:, :], in0=xt[i][:, :], scalar=rstd, in1=g1b[:, :], op0=ALU.mult, op1=ALU.mult),
                reads=[xt_b[i], stat_b[t], g1b_b], writes=[hb_b[i]])
            S.play(pend0)
            S.rec_begin()
            for g in range(4):
                pi = 4 + (pcnt % 4)
                pcnt += 1
                pt = psum[pi].bitcast(BF16)
                for q in range(4):
                    kc = g * 4 + q
                    S.op("pe", lambda e, pt=pt, q=q, i=i, kc=kc: e.transpose(
                        out=pt[:, q * 128:(q + 1) * 128], in_=hb[i][:, kc * 128:(kc + 1) * 128], identity=ident_bf[:, :]),
                        reads=[hb_b[i], ident_bf_b], writes=[psum_b[pi][q]], skip_same=True)
                if g % 2 == 0:
                    S.op("act", lambda e, pt=pt, g=g, t=t: e.activation(
                        out=hT[:, g * 4:(g + 1) * 4, t * 128:(t + 1) * 128],
                        in_=pt[:, 0:512].rearrange("p (a b) -> p a b", a=4), func=AF.Copy),
                        reads=psum_b[pi], writes=[hT_b[t]])
                else:
                    S.op("dve", lambda e, pt=pt, g=g, t=t: e.tensor_copy(
                        out=hT[:, g * 4:(g + 1) * 4, t * 128:(t + 1) * 128],
                        in_=pt[:, 0:512].rearrange("p (a b) -> p a b", a=4)),
                        reads=psum_b[pi], writes=[hT_b[t]])
            pend0 = S.rec_end()
        S.play(pend0)

        scr = C.scr
        w_in_r = C.w_in.rearrange("(kc p) n -> p kc n", p=128)
        units = []
        for c0 in range(0, 4096, 128):
            units.append((c0, 128))
        units.append((C_LR, 32))
        for c0 in range(C_DQ, C_AB, 128):
            units.append((c0, 128))
        units.append((C_AB, 32))
        for c0 in range(C_BG, N_IN, 128):
            units.append((c0, 128))
        if "nunits" in debug:
            units = units[:debug["nunits"][0][0]]
        NW = 8
        wring = [sb("wr%d" % i, [128, KC, 128], BF16) for i in range(NW)]
        wring_b = [Buf("wr%d" % i) for i in range(NW)]
        NSTG = 3
        stg = [sb("stg%d" % i, [128, T], F32) for i in range(NSTG)]
        stg_b = [Buf("stg%d" % i) for i in range(NSTG)]
        ecnt = 0
        for u, (c0, ncol) in enumerate(units):
            wt, wb = wring[u % NW], wring_b[u % NW]
            S.dma("pool", wt[:, :, 0:ncol], w_in_r[:, :, c0:c0 + ncol], writes=[wb])
            st, stb = stg[u % NSTG], stg_b[u % NSTG]
            for blk in range(4):
                pi = (4 * u + blk) % 8
                for kc in range(KC):
                    S.op("pe", lambda e, pi=pi, wt=wt, kc=kc, blk=blk, ncol=ncol: e.matmul(
                        out=psum[pi][0:ncol, :], lhsT=wt[:, kc, 0:ncol], rhs=hT[:, kc, blk * 512:(blk + 1) * 512],
                        start=(kc == 0), stop=(kc == KC - 1)),
                        reads=[wb] + hT_b[4 * blk:4 * blk + 4], writes=psum_b[pi], skip_same=True)
                if ecnt % 2 == 0:
                    S.op("act", lambda e, pi=pi, st=st, blk=blk, ncol=ncol: e.activation(
                        out=st[0:ncol, blk * 512:(blk + 1) * 512], in_=psum[pi][0:ncol, :], func=AF.Copy),
                        reads=psum_b[pi], writes=[stb])
                else:
                    S.op("dve", lambda e, pi=pi, st=st, blk=blk, ncol=ncol: e.tensor_copy(
                        out=st[0:ncol, blk * 512:(blk + 1) * 512], in_=psum[pi][0:ncol, :]),
                        reads=psum_b[pi], writes=[stb])
                ecnt += 1
            S.dma("sp", scr[c0:c0 + ncol, :], st[0:ncol, :], reads=[stb])


def phase_gla(C):
    nc, S, debug = C.nc, C.S, C.debug
    psum, psum_b = C.psum, C.psum_b
    cst, cst_b = C.cst, C.cst_b
    scr = C.scr
    nheads = debug.get("gla_heads", [[8], None])[0][0]
    with contextlib.ExitStack() as pes:
        def sb(name, shape, dt):
            return pes.enter_context(nc.sbuf_tensor(name, list(shape), dt))

        def B(name):
            return Buf(name)

        lr = sb("lr", [49, T], F32)
        lr_b = B("lr")
        dw = sb("dw", [49, 1024], F32)
        dw_b = B("dw")
        gn = sb("gn", [128, 1], F32)
        gn_b = B("gn")
        S.op("pool", lambda e: e.memset(lr[:, :], 1.0), writes=[lr_b])
        for d in range(2):
            S.dma("sp", lr[32 * d:32 * d + 16, :], scr[C_LR + 16 * d:C_LR + 16 * d + 16, :], writes=[lr_b])
            S.dma("sp", dw[32 * d:32 * d + 16, :], C.gla_dw[d][:, :], writes=[dw_b])
            S.dma("sp", dw[32 * d + 16:32 * d + 17, :], C.gla_db[d][:, :], writes=[dw_b])
        S.dma("sp", gn[:, :], C.gla_norm_g.rearrange("o d -> d o"), writes=[gn_b])

        NB = 2
        qbf = [sb("qbf%d" % i, [128, T], BF16) for i in range(NB)]
        kbf = [sb("kbf%d" % i, [128, T], BF16) for i in range(NB)]
        vbf = [sb("vbf%d" % i, [128, T], BF16) for i in range(NB)]
        g32 = [sb("g32%d" % i, [128, T], F32) for i in range(NB)]
        qbf_b = [B("qbf") for i in range(NB)]
        kbf_b = [B("kbf") for i in range(NB)]
        vbf_b = [B("vbf") for i in range(NB)]
        g32_b = [B("g32") for i in range(NB)]
        vtm = sb("vtm", [128, NT, 128], BF16)
        vtm_b = B("vtm")
        sg = sb("sg", [128, T], BF16)
        sg_b = B("sg")
        Ls = [sb("L%d" % d, [128, NT, 128], F32) for d in range(2)]
        Ls_b = [B("L") for d in range(2)]
        EGs = [sb("EG%d" % d, [128, T], BF16) for d in range(2)]
        EGs_b = [B("EG") for d in range(2)]
        EGis = [sb("EGi%d" % d, [128, T], BF16) for d in range(2)]
        EGis_b = [B("EGi") for d in range(2)]
        EKTs = [sb("EKT%d" % d, [128, NT, 128], BF16) for d in range(2)]
        EKTs_b = [B("EKT") for d in range(2)]
        dcl = sb("dcl", [128, 2, NT], F32)
        dcl_b = [B("dcl0"), B("dcl1")]
        qd = [sb("qd%d" % d, [128, T], BF16) for d in range(2)]
        qd_b = [B("qd") for d in range(2)]
        ki = [sb("ki%d" % d, [128, T], BF16) for d in range(2)]
        ki_b = [B("ki") for d in range(2)]
        kt = [sb("kt%d" % d, [128, NT, 128], BF16) for d in range(2)]
        kt_b = [B("kt") for d in range(2)]
        CSs = [sb("CS%d" % d, [128, NT, 128], F32) for d in range(2)]
        CSs_b = [B("CS") for d in range(2)]
        Sst = [sb("Sst%d" % d, [128, NT, 128], F32) for d in range(2)]
        Sst_b = [B("Sst") for d in range(2)]
        Sbf = [sb("Sbf%d" % d, [128, NT, 128], BF16) for d in range(2)]
        Sbf_b = [B("Sbf") for d in range(2)]
        tE = [sb("tE%d" % i, [128, 512], F32) for i in range(2)]
        tE_b = [B("tE") for i in range(2)]
        sc1s = [sb("sc1_%d" % i, [128, 512], BF16) for i in range(2)]
        sc1s_b = [B("sc1") for i in range(2)]
        sc2s = [sb("sc2_%d" % i, [128, 512], BF16) for i in range(2)]
        sc2s_b = [B("sc2") for i in range(2)]
        sqs = [sb("sq_%d" % i, [128, 512], F32) for i in range(2)]
        sqs_b = [B("sq") for i in range(2)]
        rss = [sb("rs_%d" % i, [128, 512], F32) for i in range(2)]
        rss_b = [B("rs") for i in range(2)]
        tts = [sb("tt_%d" % i, [128, 512], F32) for i in range(2)]
        tts_b = [B("tt") for i in range(2)]
        yb = [sb("yb%d" % i, [128, T], BF16) for i in range(2)]
        yb_b = [B("yb") for i in range(2)]

        ident_bf, ident_bf_b = C.ident_bf, C.ident_bf_b
        TRI = [cst[:, 0:128], cst[:, 128:256]]
        STRI = [cst[:, 256:384], cst[:, 384:512]]
        ONES = cst[:, 512:640]
        MASK = [cst[:, 640:1152], cst[:, 1152:1664]]
        pc = [0]
        tec = [0]

        def bank():
            pi = 4 + (pc[0] % 4)
            pc[0] += 1
            return pi

        def load_head(h):
            i = h % NB
            S.dma("pool", qbf[i][:, :], scr[C_GQ + h * 128:C_GQ + (h + 1) * 128, :], writes=[qbf_b[i]], max_dma_last_dim=4096)
            S.dma("pool", kbf[i][:, :], scr[C_GK + h * 128:C_GK + (h + 1) * 128, :], writes=[kbf_b[i]], max_dma_last_dim=4096)
            S.dma("pool", vbf[i][:, :], scr[C_GV + h * 128:C_GV + (h + 1) * 128, :], writes=[vbf_b[i]], max_dma_last_dim=4096)
            S.dma("sp", g32[i][:, :], scr[C_GG + h * 128:C_GG + (h + 1) * 128, :], writes=[g32_b[i]])

        load_head(0)
        for h in range(nheads):
            i = h % NB
            if h + 1 < nheads:
                load_head(h + 1)
            for g in range(4):
                pi = bank()
                pt = psum[pi].bitcast(BF16)
                for q in range(4):
                    c = 4 * g + q
                    S.op("pe", lambda e, pt=pt, q=q, c=c, i=i: e.transpose(
                        out=pt[:, q * 128:(q + 1) * 128], in_=vbf[i][:, c * 128:(c + 1) * 128], identity=ident_bf[:, :]),
                        reads=[vbf_b[i], ident_bf_b], writes=[psum_b[pi][q]], skip_same=True)
                S.op("act", lambda e, pt=pt, g=g: e.activation(
                    out=vtm[:, 4 * g:4 * g + 4, :], in_=pt[:, 0:512].rearrange("p (a b) -> p a b", a=4), func=AF.Copy),
                    reads=psum_b[pi], writes=[vtm_b])
            for blk in range(4):
                sl = slice(blk * 512, (blk + 1) * 512)
                j = tec[0] % 2
                tec[0] += 1
                S.op("act", lambda e, j=j, sl=sl, i=i: e.activation(out=tE[j][:, :], in_=g32[i][:, sl], func=AF.Exp, scale=-1.0),
                     reads=[g32_b[i]], writes=[tE_b[j]])
                S.op("act", lambda e, j=j: e.activation(out=tE[j][:, :], in_=tE[j][:, :], func=AF.Ln, bias=1.0),
                     reads=[tE_b[j]], writes=[tE_b[j]])
                S.op("act", lambda e, j=j: e.activation(out=tE[j][:, :], in_=tE[j][:, :], func=AF.Exp, scale=-1.0),
                     reads=[tE_b[j]], writes=[tE_b[j]])
                S.op("dve", lambda e, j=j, sl=sl, i=i: e.tensor_tensor(out=sg[:, sl], in0=g32[i][:, sl], in1=tE[j][:, :], op=ALU.mult),
                     reads=[g32_b[i], tE_b[j]], writes=[sg_b])
            for d in range(2):
                p0 = 32 * d
                for g in range(4):
                    pi = bank()
                    for q in range(4):
                        c = 4 * g + q
                        S.op("pe", lambda e, d=d, pi=pi, q=q, c=c, p0=p0, h=h: e.matmul(
                            out=psum[pi][:, q * 128:(q + 1) * 128], lhsT=lr[p0:p0 + 17, c * 128:(c + 1) * 128],
                            rhs=dw[p0:p0 + 17, h * 128:(h + 1) * 128], start=True, stop=True),
                            reads=[lr_b, dw_b], writes=[psum_b[pi][q]], skip_same=True)
                    j = tec[0] % 2
                    tec[0] += 1
                    S.op("act", lambda e, d=d, j=j, pi=pi: e.activation(out=tE[j][:, :], in_=psum[pi][:, :], func=AF.Exp, scale=-1.0),
                         reads=psum_b[pi], writes=[tE_b[j]])
                    S.op("act", lambda e, d=d, j=j, g=g: e.activation(
                        out=Ls[d][:, 4 * g:4 * g + 4, :], in_=tE[j][:, :].rearrange("p (a b) -> p a b", a=4), func=AF.Ln, bias=1.0),
                        reads=[tE_b[j]], writes=[Ls_b[d]])
            for d in range(2):
                p0 = 32 * d
                for g in range(4):
                    pi = bank()
                    sl = slice(g * 512, (g + 1) * 512)
                    for q in range(4):
                        c = 4 * g + q
                        S.op("pe", lambda e, pi=pi, q=q, c=c, d=d: e.matmul(
                            out=psum[pi][:, q * 128:(q + 1) * 128], lhsT=Ls[d][:, c, :], rhs=TRI[d], start=True, stop=True),
                            reads=[Ls_b[d], cst_b], writes=[psum_b[pi][q]], skip_same=True)
                    S.op("act", lambda e, d=d, pi=pi, sl=sl: e.activation(out=EGs[d][:, sl], in_=psum[pi][:, :], func=AF.Exp),
                         reads=psum_b[pi], writes=[EGs_b[d]])
                    S.op("act", lambda e, d=d, pi=pi, sl=sl: e.activation(out=EGis[d][:, sl], in_=psum[pi][:, :], func=AF.Exp, scale=-1.0),
                         reads=psum_b[pi], writes=[EGis_b[d]])
                    col = 127 if d == 0 else 0
                    S.op("act", lambda e, pi=pi, g=g, d=d, col=col: e.activation(
                        out=dcl[:, d, 4 * g:4 * g + 4], in_=psum[pi][:, :].rearrange("p (a b) -> p a b", a=4)[:, :, col], func=AF.Exp),
                        reads=psum_b[pi], writes=[dcl_b[d]])
            for d in range(2):
                p0 = 32 * d
                for g in range(4):
                    pi = bank()
                    for q in range(4):
                        c = 4 * g + q
                        S.op("pe", lambda e, pi=pi, q=q, c=c, d=d: e.matmul(
                            out=psum[pi][:, q * 128:(q + 1) * 128], lhsT=STRI[d], rhs=Ls[d][:, c, :], start=True, stop=True),
                            reads=[Ls_b[d], cst_b], writes=[psum_b[pi][q]], skip_same=True)
                    S.op("act", lambda e, d=d, pi=pi, g=g: e.activation(
                        out=EKTs[d][:, 4 * g:4 * g + 4, :], in_=psum[pi][:, :].rearrange("p (a b) -> p a b", a=4), func=AF.Exp),
                        reads=psum_b[pi], writes=[EKTs_b[d]])
            for d in range(2):
                p0 = 32 * d
                S.op("dve", lambda e, d=d, i=i: e.scalar_tensor_tensor(
                    out=qd[d][:, :], in0=qbf[i][:, :], scalar=float(128 ** -0.5), in1=EGs[d][:, :], op0=ALU.mult, op1=ALU.mult),
                    reads=[qbf_b[i], EGs_b[d]], writes=[qd_b[d]])
                S.op("dve", lambda e, d=d, i=i: e.tensor_tensor(out=ki[d][:, :], in0=kbf[i][:, :], in1=EGis[d][:, :], op=ALU.mult),
                     reads=[kbf_b[i], EGis_b[d]], writes=[ki_b[d]])
            for d in range(2):
                p0 = 32 * d
                for g in range(4):
                    pi = bank()
                    pt = psum[pi].bitcast(BF16)
                    for q in range(4):
                        c = 4 * g + q
                        S.op("pe", lambda e, d=d, pt=pt, q=q, c=c, i=i: e.transpose(
                            out=pt[:, q * 128:(q + 1) * 128], in_=kbf[i][:, c * 128:(c + 1) * 128], identity=ident_bf[:, :]),
                            reads=[kbf_b[i], ident_bf_b], writes=[psum_b[pi][q]], skip_same=True)
                    S.op("dve", lambda e, pt=pt, g=g, d=d: e.tensor_tensor(
                        out=kt[d][:, 4 * g:4 * g + 4, :], in0=pt[:, 0:512].rearrange("p (a b) -> p a b", a=4),
                        in1=EKTs[d][:, 4 * g:4 * g + 4, :], op=ALU.mult),
                        reads=psum_b[pi] + [EKTs_b[d]], writes=[kt_b[d]])
            for d in range(2):
                p0 = 32 * d
                for g in range(4):
                    pi = bank()
                    for q in range(4):
                        c = 4 * g + q
                        S.op("pe", lambda e, pi=pi, q=q, c=c, d=d: e.matmul(
                            out=psum[pi][:, q * 128:(q + 1) * 128], lhsT=kt[d][:, c, :], rhs=vtm[:, c, :], start=True, stop=True),
                            reads=[kt_b[d], vtm_b], writes=[psum_b[pi][q]], skip_same=True)
                    S.op("act", lambda e, d=d, pi=pi, g=g: e.activation(
                        out=CSs[d][:, 4 * g:4 * g + 4, :], in_=psum[pi][:, :].rearrange("p (a b) -> p a b", a=4), func=AF.Copy),
                        reads=psum_b[pi], writes=[CSs_b[d]])
            S.op("pool", lambda e: e.memset(Sst[0][:, 0, :], 0.0), writes=[Sst_b[0]])
            S.op("pool", lambda e: e.memset(Sst[1][:, NT - 1, :], 0.0), writes=[Sst_b[1]])
            for s in range(1, NT):
                c = s
                S.op("dve", lambda e, c=c: e.scalar_tensor_tensor(
                    out=Sst[0][:, c, :], in0=Sst[0][:, c - 1, :], scalar=dcl[:, 0, c - 1:c], in1=CSs[0][:, c - 1, :],
                    op0=ALU.mult, op1=ALU.add),
                    reads=[Sst_b[0], dcl_b[0], CSs_b[0]], writes=[Sst_b[0]])
                c = NT - 1 - s
                S.op("dve", lambda e, c=c: e.scalar_tensor_tensor(
                    out=Sst[1][:, c, :], in0=Sst[1][:, c + 1, :], scalar=dcl[:, 1, c + 1:c + 2], in1=CSs[1][:, c + 1, :],
                    op0=ALU.mult, op1=ALU.add),
                    reads=[Sst_b[1], dcl_b[1], CSs_b[1]], writes=[Sst_b[1]])
            S.op("act", lambda e: e.activation(out=Sbf[0][:, :, :], in_=Sst[0][:, :, :], func=AF.Copy),
                 reads=[Sst_b[0]], writes=[Sbf_b[0]])
            S.op("dve", lambda e: e.tensor_copy(out=Sbf[1][:, :, :], in_=Sst[1][:, :, :]),
                 reads=[Sst_b[1]], writes=[Sbf_b[1]])
            yi = h % 2
            for gp in range(2):
                GG = [2 * gp, 2 * gp + 1]
                BK = {g: (4 * (g % 2), 4 * (g % 2) + 1, 4 * (g % 2) + 2, 4 * (g % 2) + 3) for g in GG}
                for g in GG:
                    pA, pB, pO, pN = BK[g]
                    for q in range(4):
                        c = 4 * g + q
                        cs_ = slice(c * 128, (c + 1) * 128)
                        S.op("pe", lambda e, q=q, cs_=cs_, pA=pA: e.matmul(
                            out=psum[pA][:, q * 128:(q + 1) * 128], lhsT=ki[0][:, cs_], rhs=qd[0][:, cs_], start=True, stop=True),
                            reads=[ki_b[0], qd_b[0]], writes=psum_b[pA], skip_same=True)
                        S.op("pe", lambda e, q=q, cs_=cs_, pB=pB: e.matmul(
                            out=psum[pB][:, q * 128:(q + 1) * 128], lhsT=ki[1][:, cs_], rhs=qd[1][:, cs_], start=True, stop=True),
                            reads=[ki_b[1], qd_b[1]], writes=psum_b[pB], skip_same=True)
                for g in GG:
                    pA, pB, pO, pN = BK[g]
                    k2 = g % 2
                    S.op("dve", lambda e, pA=pA, k2=k2: e.tensor_tensor(out=sc1s[k2][:, :], in0=psum[pA][:, :], in1=MASK[0], op=ALU.mult),
                         reads=psum_b[pA] + [cst_b], writes=[sc1s_b[k2]])
                    S.op("dve", lambda e, pB=pB, k2=k2: e.tensor_tensor(out=sc2s[k2][:, :], in0=psum[pB][:, :], in1=MASK[1], op=ALU.mult),
                         reads=psum_b[pB] + [cst_b], writes=[sc2s_b[k2]])
                for g in GG:
                    pA, pB, pO, pN = BK[g]
                    k2 = g % 2
                    for q in range(4):
                        c = 4 * g + q
                        cs_ = slice(c * 128, (c + 1) * 128)
                        osl = slice(q * 128, (q + 1) * 128)
                        has_f = c >= 1
                        has_b = c <= NT - 2
                        S.op("pe", lambda e, osl=osl, c=c, pO=pO, k2=k2: e.matmul(
                            out=psum[pO][:, osl], lhsT=vtm[:, c, :], rhs=sc1s[k2][:, osl], start=True, stop=False),
                            reads=[vtm_b, sc1s_b[k2]], writes=psum_b[pO], skip_same=True)
                        S.op("pe", lambda e, osl=osl, c=c, pO=pO, k2=k2, has_f=has_f, has_b=has_b: e.matmul(
                            out=psum[pO][:, osl], lhsT=vtm[:, c, :], rhs=sc2s[k2][:, osl], start=False, stop=not (has_f or has_b)),
                            reads=[vtm_b, sc2s_b[k2]], writes=psum_b[pO], skip_same=True)
                        if has_f:
                            S.op("pe", lambda e, osl=osl, c=c, cs_=cs_, has_b=has_b, pO=pO: e.matmul(
                                out=psum[pO][:, osl], lhsT=Sbf[0][:, c, :], rhs=qd[0][:, cs_], start=False, stop=not has_b),
                                reads=[Sbf_b[0], qd_b[0]], writes=psum_b[pO], skip_same=True)
                        if has_b:
                            S.op("pe", lambda e, osl=osl, c=c, cs_=cs_, pO=pO: e.matmul(
                                out=psum[pO][:, osl], lhsT=Sbf[1][:, c, :], rhs=qd[1][:, cs_], start=False, stop=True),
                                reads=[Sbf_b[1], qd_b[1]], writes=psum_b[pO], skip_same=True)
                for g in GG:
                    pA, pB, pO, pN = BK[g]
                    k2 = g % 2
                    S.op("act", lambda e, pO=pO, k2=k2: e.activation(out=sqs[k2][:, :], in_=psum[pO][:, :], func=AF.Square),
                         reads=psum_b[pO], writes=[sqs_b[k2]])
                for g in GG:
                    pA, pB, pO, pN = BK[g]
                    k2 = g % 2
                    S.op("pe", lambda e, pN=pN, k2=k2: e.matmul(out=psum[pN][:, :], lhsT=ONES, rhs=sqs[k2][:, :], start=True, stop=True),
                         reads=[sqs_b[k2], cst_b], writes=psum_b[pN], skip_same=True)
                for g in GG:
                    pA, pB, pO, pN = BK[g]
                    k2 = g % 2
                    S.op("act", lambda e, pN=pN, k2=k2: e.activation(out=rss[k2][:, :], in_=psum[pN][:, :], func=AF.Ln, scale=1.0 / 128, bias=EPS),
                         reads=psum_b[pN], writes=[rss_b[k2]])
                for g in GG:
                    k2 = g % 2
                    S.op("act", lambda e, k2=k2: e.activation(out=rss[k2][:, :], in_=rss[k2][:, :], func=AF.Exp, scale=-0.5),
                         reads=[rss_b[k2]], writes=[rss_b[k2]])
                for g in GG:
                    pA, pB, pO, pN = BK[g]
                    k2 = g % 2
                    S.op("dve", lambda e, pO=pO, k2=k2: e.scalar_tensor_tensor(
                        out=tts[k2][:, :], in0=psum[pO][:, :], scalar=gn[:, 0:1], in1=rss[k2][:, :], op0=ALU.mult, op1=ALU.mult),
                        reads=psum_b[pO] + [gn_b, rss_b[k2]], writes=[tts_b[k2]])
                for g in GG:
                    k2 = g % 2
                    sl = slice(g * 512, (g + 1) * 512)
                    S.op("dve", lambda e, sl=sl, yi=yi, k2=k2: e.tensor_tensor(out=yb[yi][:, sl], in0=tts[k2][:, :], in1=sg[:, sl], op=ALU.mult),
                         reads=[tts_b[k2], sg_b], writes=[yb_b[yi]])
            S.dma("sp", C.y_scr[h * 128:(h + 1) * 128, :], yb[yi][:, :], reads=[yb_b[yi]])


NCST = 1664

def phase_gdn(C):
    nc, S, debug = C.nc, C.S, C.debug
    psum, psum_b = C.psum, C.psum_b
    cst, cst_b = C.cst, C.cst_b
    scr = C.scr
    nheads = debug.get("gdn_heads", [[8], None])[0][0]
    with contextlib.ExitStack() as pes:
        def sb(name, shape, dt):
            return pes.enter_context(nc.sbuf_tensor(name, list(shape), dt))

        def B(name):
            return Buf(name)

        ident_bf, ident_bf_b = C.ident_bf, C.ident_bf_b
        ONES = cst[:, 512:640]
        c2 = sb("c2", [128, NC2], F32)
        c2_b = B("c2")
        S.dma("sp", c2[:, :], C.cst2_d[:, :], writes=[c2_b])
        U = [c2[:, 0:128], c2[:, 128:256]]
        SU = [c2[:, 256:384], c2[:, 384:512]]
        PEN = [c2[:, 512:640], c2[:, 640:768]]
        IDF = c2[:, 768:896]
        NEGONES = c2[:, 896:1024]
        cb = sb("cb", [128, NCB], BF16)
        cb_b = B("cb")
        S.dma("sp", cb[:, :], C.cstb_d[:, :], writes=[cb_b])

        def lmask(d, l):
            o = (d * 7 + (l - 1)) * 128
            return cb[:, o:o + 128]

        cw = sb("cw", [128, 24 * 3], F32)
        cw_b = B("cw")
        S.dma("sp", cw[:, :], C.gdn_cw_d[:, :], writes=[cw_b])
        gn = sb("gn2", [128, 1], F32)
        gn_b = B("gn2")
        S.dma("sp", gn[:, :], C.gdn_norm_g.rearrange("o d -> d o"), writes=[gn_b])
        hp = sb("hp", [16, 4], F32)
        hp_b = B("hp")
        S.dma("sp", hp[:, 0:2], C.gdn_hp_d[:, :], writes=[hp_b])

        c1 = sb("c1", [128, T], F32)
        c1_b = B("c1")
        gb48 = c1
        gb48_b = c1_b
        S.op("pool", lambda e: e.memset(gb48[:, :], 0.0), writes=[gb48_b])
        S.dma("sp", gb48[0:16, :], scr[C_AB:C_AB + 16, :], writes=[gb48_b])
        S.dma("sp", gb48[32:48, :], scr[C_AB + 16:C_AB + 32, :], writes=[gb48_b])
        S.op("act", lambda e: e.activation(out=hp[:, 2:3], in_=hp[:, 0:1], func=AF.Exp), reads=[hp_b], writes=[hp_b])
        S.op("dve", lambda e: e.tensor_scalar(out=hp[:, 2:3], in0=hp[:, 2:3], scalar1=-1.0, scalar2=None, op0=ALU.mult),
             reads=[hp_b], writes=[hp_b])
        S.op("act", lambda e: e.activation(out=gb48[0:16, :], in_=gb48[0:16, :], func=AF.Exp, bias=hp[:, 1:2]),
             reads=[gb48_b, hp_b], writes=[gb48_b])
        S.op("act", lambda e: e.activation(out=gb48[0:16, :], in_=gb48[0:16, :], func=AF.Ln, bias=1.0),
             reads=[gb48_b], writes=[gb48_b])
        S.op("dve", lambda e: e.tensor_scalar(out=gb48[0:16, :], in0=gb48[0:16, :], scalar1=hp[:, 2:3], scalar2=None, op0=ALU.mult),
             reads=[gb48_b, hp_b], writes=[gb48_b])
        S.op("act", lambda e: e.activation(out=gb48[32:48, :], in_=gb48[32:48, :], func=AF.Exp, scale=-1.0),
             reads=[gb48_b], writes=[gb48_b])
        S.op("act", lambda e: e.activation(out=gb48[32:48, :], in_=gb48[32:48, :], func=AF.Ln, bias=1.0),
             reads=[gb48_b], writes=[gb48_b])
        S.op("act", lambda e: e.activation(out=gb48[32:48, :], in_=gb48[32:48, :], func=AF.Exp, scale=-1.0),
             reads=[gb48_b], writes=[gb48_b])
        gtm = sb("gtm", [128, NT, 48], F32)
        gtm_b = B("gtm")
        for g in range(4):
            pi = 2 + g
            for q in range(4):
                t = 4 * g + q
                S.op("pe", lambda e, pi=pi, q=q, t=t: e.transpose(
                    out=psum[pi][:, q * 48:(q + 1) * 48], in_=gb48[0:48, t * 128:(t + 1) * 128], identity=IDF[0:48, 0:48]),
                    reads=[gb48_b, c2_b], writes=[psum_b[pi][0]], skip_same=True)
            S.op("act", lambda e, pi=pi, g=g: e.activation(
                out=gtm[:, 4 * g:4 * g + 4, :], in_=psum[pi][:, 0:192].rearrange("p (a b) -> p a b", a=4), func=AF.Copy),
                reads=[psum_b[pi][0]], writes=[gtm_b])
        egtm = sb("egtm", [128, NT, 16], F32)
        egtm_b = B("egtm")
        ektm = sb("ektm", [128, NT, 16], F32)
        ektm_b = B("ektm")
        for (mats, dst, dst_b, pi) in ((U, egtm, egtm_b, 6), (SU, ektm, ektm_b, 7)):
            for t in range(NT):
                for d in range(2):
                    S.op("pe", lambda e, pi=pi, t=t, d=d, mats=mats: e.matmul(
                        out=psum[pi][:, t * 16 + d * 8:t * 16 + d * 8 + 8], lhsT=mats[d], rhs=gtm[:, t, d * 8:d * 8 + 8],
                        start=True, stop=True),
                        reads=[gtm_b, c2_b], writes=[psum_b[pi][0]], skip_same=True)
            S.op("act", lambda e, pi=pi, dst=dst: e.activation(
                out=dst[:, :, :], in_=psum[pi][:, 0:256].rearrange("p (a b) -> p a b", a=NT), func=AF.Exp),
                reads=[psum_b[pi][0]], writes=[dst_b])

        pin = [sb("pin%d" % i, [128, T + 2], F32) for i in range(1)]
        pin_b = [B("pin") for i in range(1)]
        for i in range(1):
            S.op("pool", lambda e, i=i: e.memset(pin[i][:, :], 0.0), writes=[pin_b[i]])
        tB = sb("tB", [128, T], F32)
        tB_b = B("tB")
        qT = sb("qT", [128, T], BF16)
        qT_b = B("qT")
        kT = sb("kT", [128, T], BF16)
        kT_b = B("kT")
        vT = sb("vT", [128, T], BF16)
        vT_b = B("vT")
        szs = [sb("sz%d" % i, [128, T], BF16) for i in range(2)]
        szs_b = [B("sz") for i in range(2)]
        ktm = sb("ktm", [128, NT, 128], BF16)
        ktm_b = B("ktm")
        vtms = [sb("vtm2_%d" % i, [128, NT, 128], BF16) for i in range(2)]
        vtms_b = [B("vtm2") for i in range(2)]
        rsi = sb("rsi", [128, 512], F32)
        rsi_b = B("rsi")
        kg = [sb("kg%d" % d, [128, NT, 128], BF16) for d in range(2)]
        kg_b = [B("kg") for d in range(2)]
        ktl = [sb("ktl%d" % d, [128, NT, 128], BF16) for d in range(2)]
        ktl_b = [B("ktl") for d in range(2)]
        qdT = [sb("qdT%d" % d, [128, T], BF16) for d in range(2)]
        qdT_b = [B("qdT") for d in range(2)]
        VT = [sb("VT%d" % d, [128, T], BF16) for d in range(2)]
        VT_b = [B("VT") for d in range(2)]
        atT = [sb("atT%d" % d, [128, T], BF16) for d in range(2)]
        atT_b = [B("atT") for d in range(2)]
        wpn = [sb("wpn%d" % d, [128, T], BF16) for d in range(2)]
        wpn_b = [B("wpn") for d in range(2)]
        dcl = sb("dcl2", [128, 2, NT], F32)
        dcl_b = [B("dcl20"), B("dcl21")]
        gU = sb("gU", [128, NT, 128], F32)
        gU_b = B("gU")
        DTs = [sb("DT%d" % k, [128, 512], F32) for k in range(4)]
        DTs_b = [B("DT") for k in range(4)]
        ebs = [sb("eb%d" % k, [128, 512], F32) for k in range(4)]
        ebs_b = [B("eb") for k in range(4)]
        NTs = [sb("NT%d" % k, [128, 512], BF16) for k in range(4)]
        NTs_b = [B("NT") for k in range(4)]
        Xs = [sb("Xs%d" % k, [128, 512], BF16) for k in range(4)]
        Xs_b = [B("Xs") for k in range(4)]
        Ys = [sb("Ys%d" % k, [128, 512], BF16) for k in range(4)]
        Ys_b = [B("Ys") for k in range(4)]
        Ps = [sb("Ps%d" % k, [128, 512], BF16) for k in range(4)]
        Ps_b = [B("Ps") for k in range(4)]
        S32 = [sb("S32_%d" % d, [128, 128], F32) for d in range(2)]
        S32_b = [B("S32") for d in range(2)]
        Sbf = [sb("Sbf2_%d" % d, [128, 128], BF16) for d in range(2)]
        Sbf_b = [B("Sbf2") for d in range(2)]
        vnb = [sb("vnb%d" % d, [128, 128], BF16) for d in range(2)]
        vnb_b = [B("vnb") for d in range(2)]
        oacc = [sb("oacc%d" % d, [128, T], F32) for d in range(2)]
        oacc_b = [B("oacc") for d in range(2)]
        sqs2 = [sb("sq2_%d" % i, [128, 512], F32) for i in range(2)]
        sqs2_b = [B("sq2") for i in range(2)]
        rss2 = [sb("rs2_%d" % i, [128, 512], F32) for i in range(2)]
        rss2_b = [B("rs2") for i in range(2)]
        tts2 = [sb("tt2_%d" % i, [128, 512], F32) for i in range(2)]
        tts2_b = [B("tt2") for i in range(2)]
        yb = [sb("yb2_%d" % i, [128, T], BF16) for i in range(1)] * 2
        yb_b = [B("yb2")] * 2

        pc = [0]

        def bank():
            pi = pc[0] % 8
            pc[0] += 1
            return pi

        pinc = [0]
        ipc = [0]

        def ibank():
            pi = 6 + (ipc[0] % 2)
            ipc[0] += 1
            return pi

        def silu_inplace(x, x_b, n=T):
            for hs in (slice(0, n // 2), slice(n // 2, n)):
                S.op("act", lambda e, hs=hs: e.activation(out=tB[:, hs], in_=x[:, hs], func=AF.Exp, scale=-1.0), reads=[x_b], writes=[tB_b])
                S.op("act", lambda e, hs=hs: e.activation(out=tB[:, hs], in_=tB[:, hs], func=AF.Ln, bias=1.0), reads=[tB_b], writes=[tB_b])
                S.op("act", lambda e, hs=hs: e.activation(out=tB[:, hs], in_=tB[:, hs], func=AF.Exp, scale=-1.0), reads=[tB_b], writes=[tB_b])
                S.op("dve", lambda e, hs=hs: e.tensor_tensor(out=x[:, hs], in0=x[:, hs], in1=tB[:, hs], op=ALU.mult),
                     reads=[x_b, tB_b], writes=[x_b])

        def conv_silu(row0, blk):
            i = 0
            S.dma("sp", pin[i][:, 1:T + 1], scr[row0:row0 + 128, :], writes=[pin_b[i]])
            w0 = cw[:, blk * 3:blk * 3 + 1]
            w1 = cw[:, blk * 3 + 1:blk * 3 + 2]
            w2 = cw[:, blk * 3 + 2:blk * 3 + 3]
            S.op("act", lambda e, i=i, w0=w0: e.activation(out=c1[:, :], in_=pin[i][:, 0:T], func=AF.Copy, scale=w0),
                 reads=[pin_b[i], cw_b], writes=[c1_b])
            S.op("dve", lambda e, i=i, w1=w1: e.scalar_tensor_tensor(
                out=c1[:, :], in0=pin[i][:, 1:T + 1], scalar=w1, in1=c1[:, :], op0=ALU.mult, op1=ALU.add),
                reads=[pin_b[i], cw_b, c1_b], writes=[c1_b])
            S.op("dve", lambda e, i=i, w2=w2: e.scalar_tensor_tensor(
                out=c1[:, :], in0=pin[i][:, 2:T + 2], scalar=w2, in1=c1[:, :], op0=ALU.mult, op1=ALU.add),
                reads=[pin_b[i], cw_b, c1_b], writes=[c1_b])
            silu_inplace(c1, c1_b)

        def l2norm_to(dst, dst_b, scale):
            for hs in (slice(0, T // 2), slice(T // 2, T)):
                S.op("act", lambda e, hs=hs: e.activation(out=tB[:, hs], in_=c1[:, hs], func=AF.Square), reads=[c1_b], writes=[tB_b])
            for blk in range(4):
                sl = slice(blk * 512, (blk + 1) * 512)
                pi = ibank()
                S.op("pe", lambda e, pi=pi, sl=sl: e.matmul(out=psum[pi][:, :], lhsT=ONES, rhs=tB[:, sl], start=True, stop=True),
                     reads=[tB_b, cst_b], writes=psum_b[pi], skip_same=True)
                S.op("act", lambda e, pi=pi: e.activation(out=rsi[:, :], in_=psum[pi][:, :], func=AF.Ln, bias=EPS),
                     reads=psum_b[pi], writes=[rsi_b])
                S.op("act", lambda e: e.activation(out=rsi[:, :], in_=rsi[:, :], func=AF.Exp, scale=-0.5), reads=[rsi_b], writes=[rsi_b])
                S.op("dve", lambda e, sl=sl: e.scalar_tensor_tensor(
                    out=dst[:, sl], in0=c1[:, sl], scalar=float(scale), in1=rsi[:, :], op0=ALU.mult, op1=ALU.mult),
                    reads=[c1_b, rsi_b], writes=[dst_b])

        def to_token_major(src, src_b, dst, dst_b):
            for g in range(4):
                pi = ibank()
                pt = psum[pi].bitcast(BF16)
                for q in range(4):
                    c = 4 * g + q
                    S.op("pe", lambda e, pt=pt, q=q, c=c: e.transpose(
                        out=pt[:, q * 128:(q + 1) * 128], in_=src[:, c * 128:(c + 1) * 128], identity=ident_bf[:, :]),
                        reads=[src_b, ident_bf_b], writes=[psum_b[pi][q]], skip_same=True)
                S.op("act", lambda e, pt=pt, g=g: e.activation(
                    out=dst[:, 4 * g:4 * g + 4, :], in_=pt[:, 0:512].rearrange("p (a b) -> p a b", a=4), func=AF.Copy),
                    reads=psum_b[pi], writes=[dst_b])

        def bc4(ap2d):
            return ap2d.unsqueeze(1).broadcast_to([128, 4, 128])

        def r4(ap):
            return ap.rearrange("p (a b) -> p a b", a=4)

        def emit_inputs(h):
            sz, sz_b = szs[h % 2], szs_b[h % 2]
            vtm, vtm_b = vtms[h % 2], vtms_b[h % 2]
            conv_silu(C_DQ + h * 128, h)
            l2norm_to(qT, qT_b, 128 ** -0.5)
            conv_silu(C_DK + h * 128, 8 + h)
            l2norm_to(kT, kT_b, 1.0)
            conv_silu(C_DV + h * 128, 16 + h)
            for hs in (slice(0, T // 2), slice(T // 2, T)):
                S.op("act", lambda e, hs=hs: e.activation(out=vT[:, hs], in_=c1[:, hs], func=AF.Copy), reads=[c1_b], writes=[vT_b])
            to_token_major(kT, kT_b, ktm, ktm_b)
            to_token_major(vT, vT_b, vtm, vtm_b)
            S.dma("sp", c1[:, :], scr[C_DZ + h * 128:C_DZ + (h + 1) * 128, :], writes=[c1_b])
            silu_inplace(c1, c1_b)
            for hs in (slice(0, T // 2), slice(T // 2, T)):
                S.op("act", lambda e, hs=hs: e.activation(out=sz[:, hs], in_=c1[:, hs], func=AF.Copy), reads=[c1_b], writes=[sz_b])
            if "gdn_qkv" in C.dbg_out and h == 0:
                S.dma("sp", C.dbg_out["gdn_qkv"][0:128, :], qT[:, :], reads=[qT_b])
                S.dma("sp", C.dbg_out["gdn_qkv"][128:256, :], kT[:, :], reads=[kT_b])
                S.dma("sp", C.dbg_out["gdn_qkv"][256:384, :], vT[:, :], reads=[vT_b])

        emit_inputs(0)
        for h in range(nheads):
            sz, sz_b = szs[h % 2], szs_b[h % 2]
            vtm, vtm_b = vtms[h % 2], vtms_b[h % 2]

            for d in range(2):
                col = d * 8 + h
                gcol = gtm[:, :, col:col + 1]
                bcol = gtm[:, :, 32 + col:32 + col + 1]
                S.op("dve", lambda e, d=d, col=col: e.tensor_tensor(
                    out=kg[d][:, :, :], in0=ktm[:, :, :], in1=egtm[:, :, col:col + 1].broadcast_to([128, NT, 128]), op=ALU.mult),
                    reads=[ktm_b, egtm_b], writes=[kg_b[d]])
                S.op("dve", lambda e, d=d, col=col: e.tensor_tensor(
                    out=ktl[d][:, :, :], in0=ktm[:, :, :], in1=ektm[:, :, col:col + 1].broadcast_to([128, NT, 128]), op=ALU.mult),
                    reads=[ktm_b, ektm_b], writes=[ktl_b[d]])
                S.op("pool", lambda e, d=d, gcol=gcol: e.tensor_tensor(
                    out=gU[:, :, :], in0=U[d].unsqueeze(1).broadcast_to([128, NT, 128]), in1=gcol.broadcast_to([128, NT, 128]), op=ALU.mult),
                    reads=[c2_b, gtm_b], writes=[gU_b])
                last = 127 if d == 0 else 0
                KB = 4
                G = list(range(4))

                def qsl(q):
                    return slice(q * 128, (q + 1) * 128)

                pA = [bank() for k in G]
                for k in G:
                    for q in range(4):
                        c = 4 * k + q
                        S.op("pe", lambda e, p=pA[k], q=q, c=c: e.matmul(
                            out=psum[p][:, qsl(q)], lhsT=ONES, rhs=gU[:, c, :], start=True, stop=False),
                            reads=[gU_b, cst_b], writes=psum_b[pA[k]], skip_same=True)
                        S.op("pe", lambda e, p=pA[k], q=q, c=c: e.matmul(
                            out=psum[p][:, qsl(q)], lhsT=gU[:, c, :], rhs=NEGONES, start=False, stop=False),
                            reads=[gU_b, c2_b], writes=psum_b[pA[k]], skip_same=True)
                        S.op("pe", lambda e, p=pA[k], q=q, d=d: e.matmul(
                            out=psum[p][:, qsl(q)], lhsT=IDF, rhs=PEN[d], start=False, stop=True),
                            reads=[c2_b], writes=psum_b[pA[k]], skip_same=True)
                for k in G:
                    S.op("act", lambda e, p=pA[k], k=k: e.activation(out=DTs[k][:, :], in_=psum[p][:, :], func=AF.Exp),
                         reads=psum_b[pA[k]], writes=[DTs_b[k]])
                pB = [bank() for k in G]
                for k in G:
                    for q in range(4):
                        c = 4 * k + q
                        S.op("pe", lambda e, p=pB[k], q=q, c=c: e.matmul(
                            out=psum[p][:, qsl(q)], lhsT=NEGONES, rhs=gU[:, c, :], start=True, stop=True),
                            reads=[gU_b, c2_b], writes=psum_b[pB[k]], skip_same=True)
                for k in G:
                    S.op("act", lambda e, p=pB[k], k=k: e.activation(out=ebs[k][:, :], in_=psum[p][:, :], func=AF.Exp, scale=-1.0),
                         reads=psum_b[pB[k]], writes=[ebs_b[k]])
                    S.op("act", lambda e, p=pB[k], k=k, d=d, last=last: e.activation(
                        out=dcl[:, d, 4 * k:4 * k + 4], in_=r4(psum[p][:, :])[:, :, last], func=AF.Exp, scale=-1.0),
                        reads=psum_b[pB[k]], writes=[dcl_b[d]])
                for k in G:
                    sl = slice(k * 512, (k + 1) * 512)
                    S.op("dve", lambda e, d=d, sl=sl, k=k: e.tensor_tensor(out=qdT[d][:, sl], in0=qT[:, sl], in1=ebs[k][:, :], op=ALU.mult),
                         reads=[qT_b, ebs_b[k]], writes=[qdT_b[d]])
                pD = [bank() for k in G]
                for k in G:
                    for q in range(4):
                        cs_ = slice((4 * k + q) * 128, (4 * k + q + 1) * 128)
                        S.op("pe", lambda e, p=pD[k], q=q, cs_=cs_: e.matmul(
                            out=psum[p][:, qsl(q)], lhsT=kT[:, cs_], rhs=qT[:, cs_], start=True, stop=True),
                            reads=[kT_b, qT_b], writes=psum_b[pD[k]], skip_same=True)
                for k in G:
                    sl = slice(k * 512, (k + 1) * 512)
                    S.op("dve", lambda e, p=pD[k], d=d, sl=sl, k=k: e.tensor_tensor(out=atT[d][:, sl], in0=psum[p][:, :], in1=DTs[k][:, :], op=ALU.mult),
                         reads=psum_b[pD[k]] + [DTs_b[k]], writes=[atT_b[d]])
                for k in G:
                    S.op("dve", lambda e, k=k, bcol=bcol: e.tensor_tensor(
                        out=r4(DTs[k][:, :]), in0=r4(DTs[k][:, :]), in1=bcol[:, 4 * k:4 * k + 4, :].broadcast_to([128, 4, 128]), op=ALU.mult),
                        reads=[DTs_b[k], gtm_b], writes=[DTs_b[k]])
                pC = [bank() for k in G]
                for k in G:
                    for q in range(4):
                        cs_ = slice((4 * k + q) * 128, (4 * k + q + 1) * 128)
                        S.op("pe", lambda e, p=pC[k], q=q, cs_=cs_: e.matmul(
                            out=psum[p][:, qsl(q)], lhsT=kT[:, cs_], rhs=kT[:, cs_], start=True, stop=True),
                            reads=[kT_b], writes=psum_b[pC[k]], skip_same=True)
                for k in G:
                    S.op("dve", lambda e, p=pC[k], k=k: e.tensor_tensor(out=NTs[k][:, :], in0=psum[p][:, :], in1=DTs[k][:, :], op=ALU.mult),
                         reads=psum_b[pC[k]] + [DTs_b[k]], writes=[NTs_b[k]])
                for l in range(1, 8):
                    def Xop(k, q, l=l):
                        return ident_bf[:, :] if l == 1 else Xs[k][:, qsl(q)]

                    def Yop(k, q, l=l):
                        return ident_bf[:, :] if l == 1 else Ys[k][:, qsl(q)]

                    pP = [bank() for k in G]
                    for k in G:
                        for q in range(4):
                            S.op("pe", lambda e, xo=Xop(k, q), yo=Yop(k, q), p=pP[k], q=q, k=k: e.matmul(out=psum[p][:, qsl(q)], lhsT=NTs[k][:, qsl(q)], rhs=xo, start=True, stop=True),
                                 reads=[NTs_b[k], Xs_b[k]], writes=psum_b[pP[k]], skip_same=True)
                    for k in G:
                        S.op("dve", lambda e, p=pP[k], k=k, d=d, l=l: e.tensor_tensor(
                            out=r4(Ps[k][:, :]), in0=r4(psum[p][:, :]), in1=bc4(lmask(d, l)), op=ALU.mult),
                            reads=psum_b[pP[k]] + [cb_b], writes=[Ps_b[k]])
                    if l < 7:
                        pX = [bank() for k in G]
                        for k in G:
                            for q in range(4):
                                S.op("pe", lambda e, xo=Xop(k, q), yo=Yop(k, q), p=pX[k], q=q, k=k: e.matmul(out=psum[p][:, qsl(q)], lhsT=ident_bf[:, :], rhs=xo, start=True, stop=False),
                                     reads=[ident_bf_b, Xs_b[k]], writes=psum_b[pX[k]], skip_same=True)
                                S.op("pe", lambda e, xo=Xop(k, q), yo=Yop(k, q), p=pX[k], q=q, k=k: e.matmul(out=psum[p][:, qsl(q)], lhsT=yo, rhs=Ps[k][:, qsl(q)], start=False, stop=True),
                                     reads=[Ys_b[k], Ps_b[k]], writes=psum_b[pX[k]], skip_same=True)
                    pY = [bank() for k in G]
                    for k in G:
                        via_act = (k >= 2)
                        for q in range(4):
                            if via_act:
                                S.op("pe", lambda e, xo=Xop(k, q), yo=Yop(k, q), p=pY[k], q=q, k=k: e.matmul(out=psum[p][:, qsl(q)], lhsT=ident_bf[:, :], rhs=yo, start=True, stop=False),
                                     reads=[ident_bf_b, Ys_b[k]], writes=psum_b[pY[k]], skip_same=True)
                            S.op("pe", lambda e, xo=Xop(k, q), yo=Yop(k, q), p=pY[k], q=q, k=k, via_act=via_act: e.matmul(out=psum[p][:, qsl(q)], lhsT=Ps[k][:, qsl(q)], rhs=yo, start=not via_act, stop=True),
                                 reads=[Ps_b[k], Ys_b[k]], writes=psum_b[pY[k]], skip_same=True)
                    if l < 7:
                        for k in G:
                            S.op("act", lambda e, p=pX[k], k=k: e.activation(out=Xs[k][:, :], in_=psum[p][:, :], func=AF.Copy),
                                 reads=psum_b[pX[k]], writes=[Xs_b[k]])
                    for k in G:
                        sl = slice(k * 512, (k + 1) * 512)
                        dst = Ys[k][:, :] if l < 7 else VT[d][:, sl]
                        dst_b = Ys_b[k] if l < 7 else VT_b[d]
                        if k >= 2:
                            S.op("act", lambda e, p=pY[k], dst=dst: e.activation(out=dst, in_=psum[p][:, :], func=AF.Copy),
                                 reads=psum_b[pY[k]], writes=[dst_b])
                        else:
                            yin = bc4(ident_bf[:, :]) if l == 1 else r4(Ys[k][:, :])
                            S.op("dve", lambda e, p=pY[k], dst=dst, yin=yin: e.tensor_tensor(out=r4(dst), in0=r4(psum[p][:, :]), in1=yin, op=ALU.add),
                                 reads=psum_b[pY[k]] + [Ys_b[k], ident_bf_b], writes=[dst_b])
                pH = [bank() for k in G]
                for k in G:
                    for q in range(4):
                        c = 4 * k + q
                        cs_ = slice(c * 128, (c + 1) * 128)
                        S.op("pe", lambda e, p=pH[k], q=q, c=c, cs_=cs_, d=d: e.matmul(
                            out=psum[p][:, qsl(q)], lhsT=kg[d][:, c, :], rhs=VT[d][:, cs_], start=True, stop=True),
                            reads=[kg_b[d], VT_b[d]], writes=psum_b[pH[k]], skip_same=True)
                for k in G:
                    sl = slice(k * 512, (k + 1) * 512)
                    S.op("act", lambda e, p=pH[k], d=d, sl=sl: e.activation(out=wpn[d][:, sl], in_=psum[p][:, :], func=AF.Copy, scale=-1.0),
                         reads=psum_b[pH[k]], writes=[wpn_b[d]])

            for d in range(2):
                S.op("pool", lambda e, d=d: e.memset(S32[d][:, :], 0.0), writes=[S32_b[d]])
                S.op("pool", lambda e, d=d: e.memset(Sbf[d][:, :], 0.0), writes=[Sbf_b[d]])
            pend = []
            if h + 1 < nheads:
                S.rec_begin()
                emit_inputs(h + 1)
                pend = S.rec_end()
            per_step = (len(pend) + 2 * NT - 1) // (2 * NT) if pend else 0
            for s in range(NT):
                for d in range(2):
                    c = s if d == 0 else NT - 1 - s
                    cs_ = slice(c * 128, (c + 1) * 128)
                    col = d * 8 + h
                    pv, pS, pO = 3 * d, 3 * d + 1, 3 * d + 2
                    S.op("pe", lambda e, pv=pv, d=d, c=c, cs_=cs_, vtm=vtm: e.matmul(
                        out=psum[pv][:, 0:128], lhsT=VT[d][:, cs_], rhs=vtm[:, c, :], start=True, stop=False),
                        reads=[VT_b[d], vtm_b], writes=[psum_b[pv][0]], skip_same=True)
                    S.op("pe", lambda e, pv=pv, d=d, cs_=cs_: e.matmul(
                        out=psum[pv][:, 0:128], lhsT=wpn[d][:, cs_], rhs=Sbf[d][:, :], start=False, stop=True),
                        reads=[wpn_b[d], Sbf_b[d]], writes=[psum_b[pv][0]], skip_same=True)
                    S.op("act", lambda e, pv=pv, d=d, c=c, col=col: e.activation(
                        out=vnb[d][:, :], in_=psum[pv][:, 0:128], func=AF.Copy, scale=gtm[:, c, 32 + col:32 + col + 1]),
                        reads=[psum_b[pv][0], gtm_b], writes=[vnb_b[d]])
                    S.op("pe", lambda e, pS=pS, d=d, c=c: e.matmul(
                        out=psum[pS][:, 0:128], lhsT=ktl[d][:, c, :], rhs=vnb[d][:, :], start=True, stop=True),
                        reads=[ktl_b[d], vnb_b[d]], writes=[psum_b[pS][1]], skip_same=True)
                    S.op("pe", lambda e, pO=pO, d=d, cs_=cs_: e.matmul(
                        out=psum[pO][:, 0:128], lhsT=Sbf[d][:, :], rhs=qdT[d][:, cs_], start=True, stop=False),
                        reads=[Sbf_b[d], qdT_b[d]], writes=[psum_b[pO][2]], skip_same=True)
                    S.op("pe", lambda e, pO=pO, d=d, cs_=cs_: e.matmul(
                        out=psum[pO][:, 0:128], lhsT=vnb[d][:, :], rhs=atT[d][:, cs_], start=False, stop=True),
                        reads=[vnb_b[d], atT_b[d]], writes=[psum_b[pO][2]], skip_same=True)
                    S.op("dve", lambda e, pS=pS, d=d, c=c: e.scalar_tensor_tensor(
                        out=S32[d][:, :], in0=S32[d][:, :], scalar=dcl[:, d, c:c + 1], in1=psum[pS][:, 0:128],
                        op0=ALU.mult, op1=ALU.add),
                        reads=[S32_b[d], dcl_b[d], psum_b[pS][1]], writes=[S32_b[d]])
                    S.op("act", lambda e, d=d: e.activation(out=Sbf[d][:, :], in_=S32[d][:, :], func=AF.Copy),
                         reads=[S32_b[d]], writes=[Sbf_b[d]])
                    S.op("dve", lambda e, pO=pO, d=d, cs_=cs_: e.tensor_copy(out=oacc[d][:, cs_], in_=psum[pO][:, 0:128]),
                         reads=[psum_b[pO][2]], writes=[oacc_b[d]])
                    S.play(pend, per_step)
            S.play(pend)

            yi = h % 2
            S.op("dve", lambda e: e.tensor_tensor(out=oacc[0][:, :], in0=oacc[0][:, :], in1=oacc[1][:, :], op=ALU.add),
                 reads=[oacc_b[0], oacc_b[1]], writes=[oacc_b[0]])
            if "gdn_o" in C.dbg_out and h == 0:
                S.dma("sp", C.dbg_out["gdn_o"][:, :], oacc[0][:, :], reads=[oacc_b[0]])
            for bp in range(2):
                BL = [2 * bp, 2 * bp + 1]
                pNs = {blk: bank() for blk in BL}
                for blk in BL:
                    sl = slice(blk * 512, (blk + 1) * 512)
                    k2 = blk % 2
                    S.op("act", lambda e, sl=sl, k2=k2: e.activation(out=sqs2[k2][:, :], in_=oacc[0][:, sl], func=AF.Square),
                         reads=[oacc_b[0]], writes=[sqs2_b[k2]])
                for blk in BL:
                    k2 = blk % 2
                    S.op("pe", lambda e, pN=pNs[blk], k2=k2: e.matmul(out=psum[pN][:, :], lhsT=ONES, rhs=sqs2[k2][:, :], start=True, stop=True),
                         reads=[sqs2_b[k2], cst_b], writes=psum_b[pNs[blk]], skip_same=True)
                for blk in BL:
                    k2 = blk % 2
                    S.op("act", lambda e, pN=pNs[blk], k2=k2: e.activation(out=rss2[k2][:, :], in_=psum[pN][:, :], func=AF.Ln, scale=1.0 / 128, bias=EPS),
                         reads=psum_b[pNs[blk]], writes=[rss2_b[k2]])
                for blk in BL:
                    k2 = blk % 2
                    S.op("act", lambda e, k2=k2: e.activation(out=rss2[k2][:, :], in_=rss2[k2][:, :], func=AF.Exp, scale=-0.5),
                         reads=[rss2_b[k2]], writes=[rss2_b[k2]])
                for blk in BL:
                    sl = slice(blk * 512, (blk + 1) * 512)
                    k2 = blk % 2
                    S.op("dve", lambda e, sl=sl, k2=k2: e.scalar_tensor_tensor(
                        out=tts2[k2][:, :], in0=oacc[0][:, sl], scalar=gn[:, 0:1], in1=rss2[k2][:, :], op0=ALU.mult, op1=ALU.mult),
                        reads=[oacc_b[0], gn_b, rss2_b[k2]], writes=[tts2_b[k2]])
                for blk in BL:
                    sl = slice(blk * 512, (blk + 1) * 512)
                    k2 = blk % 2
                    S.op("dve", lambda e, sl=sl, yi=yi, sz=sz, k2=k2: e.tensor_tensor(out=yb[yi][:, sl], in0=tts2[k2][:, :], in1=sz[:, sl], op=ALU.mult),
                         reads=[tts2_b[k2], sz_b], writes=[yb_b[yi]])
            S.dma("sp", C.y_scr[1024 + h * 128:1024 + (h + 1) * 128, :], yb[yi][:, :], reads=[yb_b[yi]])


def phase_branch(C):
    nc, S, debug = C.nc, C.S, C.debug
    psum, psum_b = C.psum, C.psum_b
    scr = C.scr
    ident_bf, ident_bf_b = C.ident_bf, C.ident_bf_b
    with contextlib.ExitStack() as oes:
        mergedT = oes.enter_context(nc.sbuf_tensor("mergedT", [128, KC, T], BF16))
        mg_b = [Buf("mg%d" % t) for t in range(NT)]
        with contextlib.ExitStack() as pes:
            def sb(name, shape, dt):
                return pes.enter_context(nc.sbuf_tensor(name, list(shape), dt))
            yT = sb("yT", [128, KC, T], BF16)
            yT_b = [Buf("yT%d" % k) for k in range(KC)]
            y_r = C.y_scr.rearrange("(kc p) t -> p kc t", p=128)
            for k in range(KC):
                S.dma("sp", yT[:, k, :], y_r[:, k, :], writes=[yT_b[k]])
            NWB = 3
            wg = [sb("wbg%d" % i, [128, 8, 128], BF16) for i in range(NWB)]
            wd = [sb("wbd%d" % i, [128, 8, 128], BF16) for i in range(NWB)]
            wg_b = [Buf("wbg") for i in range(NWB)]
            wd_b = [Buf("wbd") for i in range(NWB)]
            gg = [sb("gg%d" % i, [128, T], F32) for i in range(2)]
            gd = [sb("gd%d" % i, [128, T], F32) for i in range(2)]
            gg_b = [Buf("gg") for i in range(2)]
            gd_b = [Buf("gd") for i in range(2)]
            sgg = [sb("sgg%d" % i, [128, 512], F32) for i in range(2)]
            sgd = [sb("sgd%d" % i, [128, 512], F32) for i in range(2)]
            sgg_b = [Buf("sgg") for i in range(2)]
            sgd_b = [Buf("sgd") for i in range(2)]
            t1 = [sb("t1_%d" % i, [128, 512], F32) for i in range(2)]
            t2 = [sb("t2_%d" % i, [128, 512], F32) for i in range(2)]
            t1_b = [Buf("t1") for i in range(2)]
            t2_b = [Buf("t2") for i in range(2)]
            wbg_r = C.w_branch_gla.rearrange("(kc p) n -> p kc n", p=128)
            wbd_r = C.w_branch_gdn.rearrange("(kc p) n -> p kc n", p=128)
            cnt = 0
            for db in range(KC):
                wi = db % NWB
                gi = db % 2
                S.dma("pool", wg[wi][:, :, :], wbg_r[:, :, db * 128:(db + 1) * 128], writes=[wg_b[wi]])
                S.dma("pool", wd[wi][:, :, :], wbd_r[:, :, db * 128:(db + 1) * 128], writes=[wd_b[wi]])
                S.dma("sp", gg[gi][:, :], scr[C_BG + db * 128:C_BG + (db + 1) * 128, :], writes=[gg_b[gi]])
                S.dma("sp", gd[gi][:, :], scr[C_BD + db * 128:C_BD + (db + 1) * 128, :], writes=[gd_b[gi]])
                for blk in range(4):
                    sl = slice(blk * 512, (blk + 1) * 512)
                    pG = (2 * cnt) % 8
                    pD = (2 * cnt + 1) % 8
                    j = cnt % 2
                    cnt += 1
                    for kc in range(8):
                        S.op("pe", lambda e, pG=pG, wi=wi, kc=kc, sl=sl: e.matmul(
                            out=psum[pG][:, :], lhsT=wg[wi][:, kc, :], rhs=yT[:, kc, sl], start=(kc == 0), stop=(kc == 7)),
                            reads=[wg_b[wi], yT_b[kc]], writes=psum_b[pG], skip_same=True)
                    for kc in range(8):
                        S.op("pe", lambda e, pD=pD, wi=wi, kc=kc, sl=sl: e.matmul(
                            out=psum[pD][:, :], lhsT=wd[wi][:, kc, :], rhs=yT[:, 8 + kc, sl], start=(kc == 0), stop=(kc == 7)),
                            reads=[wd_b[wi], yT_b[8 + kc]], writes=psum_b[pD], skip_same=True)
                    S.op("act", lambda e, j=j, gi=gi, sl=sl: e.activation(out=sgg[j][:, :], in_=gg[gi][:, sl], func=AF.Sigmoid),
                         reads=[gg_b[gi]], writes=[sgg_b[j]])
                    S.op("act", lambda e, j=j, gi=gi, sl=sl: e.activation(out=sgd[j][:, :], in_=gd[gi][:, sl], func=AF.Sigmoid),
                         reads=[gd_b[gi]], writes=[sgd_b[j]])
                    S.op("dve", lambda e, j=j, pG=pG: e.tensor_tensor(out=t1[j][:, :], in0=psum[pG][:, :], in1=sgg[j][:, :], op=ALU.mult),
                         reads=psum_b[pG] + [sgg_b[j]], writes=[t1_b[j]])
                    S.op("dve", lambda e, j=j, pD=pD: e.tensor_tensor(out=t2[j][:, :], in0=psum[pD][:, :], in1=sgd[j][:, :], op=ALU.mult),
                         reads=psum_b[pD] + [sgd_b[j]], writes=[t2_b[j]])
                    S.op("dve", lambda e, j=j, db=db, sl=sl: e.tensor_tensor(out=mergedT[:, db, sl], in0=t1[j][:, :], in1=t2[j][:, :], op=ALU.add),
                         reads=[t1_b[j], t2_b[j]], writes=mg_b[4 * blk:4 * blk + 4])
        S.barrier()
        if "merged" in C.dbg_out:
            S.dma("sp", C.dbg_out["merged"].rearrange("(kc p) t -> p kc t", p=128), mergedT[:, :, :], reads=mg_b)
        with contextlib.ExitStack() as pes:
            def sb(name, shape, dt):
                return pes.enter_context(nc.sbuf_tensor(name, list(shape), dt))
            Wout = sb("Wout", [128, KC, D], BF16)
            Wout_b = [Buf("Wout%d" % k) for k in range(KC)]
            wo_r = C.w_out.rearrange("(kc p) n -> p kc n", p=128)
            for k in range(KC):
                S.dma("pool", Wout[:, k, :], wo_r[:, k, :], writes=[Wout_b[k]], max_dma_last_dim=4096)
            g2b = sb("g2b", [128, D], F32)
            g2b_b = Buf("g2b")
            S.dma("sp", g2b[:, :], C.norm2_g.partition_broadcast(128), writes=[g2b_b])
            xt = [sb("xt2_%d" % i, [128, D], F32) for i in range(2)]
            xt_b = [Buf("xt2") for i in range(2)]
            x1t = [sb("x1t%d" % i, [128, D], F32) for i in range(2)]
            x1t_b = [Buf("x1t") for i in range(2)]
            junk = sb("junk2", [128, D], BF16)
            junk_b = Buf("junk2")
            hb = [sb("hb2_%d" % i, [128, D], BF16) for i in range(2)]
            hb_b = [Buf("hb2") for i in range(2)]
            hst = [sb("hst%d" % i, [128, KC, 128], BF16) for i in range(2)]
            hst_b = [Buf("hst") for i in range(2)]
            stat = sb("stat2", [128, 4 * NT], F32)
            stat_b = [Buf("stat2") for i in range(NT)]
            h2_r = C.h2_scr.rearrange("(kc p) t -> p kc t", p=128)
            pcnt = 0
            pend_t = []
            for t in range(NT):
                i = t % 2
                ts_ = slice(t * 128, (t + 1) * 128)
                S.dma("sp", xt[i][:, :], C.x[ts_, :], writes=[xt_b[i]])
                for cb_ in range(4):
                    pi = cb_ + 4 * (t % 2)
                    csl = slice(cb_ * 512, (cb_ + 1) * 512)
                    for kc in range(KC):
                        S.op("pe", lambda e, pi=pi, kc=kc, ts_=ts_, csl=csl: e.matmul(
                            out=psum[pi][:, :], lhsT=mergedT[:, kc, ts_], rhs=Wout[:, kc, csl], start=(kc == 0), stop=(kc == KC - 1)),
                            reads=[mg_b[t], Wout_b[kc]], writes=psum_b[pi], skip_same=True)
                    S.op("dve", lambda e, pi=pi, i=i, csl=csl: e.tensor_tensor(out=x1t[i][:, csl], in0=psum[pi][:, :], in1=xt[i][:, csl], op=ALU.add),
                         reads=psum_b[pi] + [xt_b[i]], writes=[x1t_b[i]])
                S.dma("pool", C.x1_scr[ts_, :], x1t[i][:, :], reads=[x1t_b[i]])
                ss = stat[:, 4 * t:4 * t + 1]
                lnv = stat[:, 4 * t + 1:4 * t + 2]
                rstd = stat[:, 4 * t + 2:4 * t + 3]
                S.op("act", lambda e, i=i, ss=ss: e.activation(out=junk[:, :], in_=x1t[i][:, :], func=AF.Square, accum_out=ss),
                     reads=[x1t_b[i]], writes=[junk_b, stat_b[t]])
                S.op("act", lambda e, ss=ss, lnv=lnv: e.activation(out=lnv, in_=ss, func=AF.Ln, scale=1.0 / D, bias=EPS),
                     reads=[stat_b[t]], writes=[stat_b[t]])
                S.op("act", lambda e, rstd=rstd, lnv=lnv: e.activation(out=rstd, in_=lnv, func=AF.Exp, scale=-0.5),
                     reads=[stat_b[t]], writes=[stat_b[t]])
                S.op("dve", lambda e, i=i, rstd=rstd: e.scalar_tensor_tensor(
                    out=hb[i][:, :], in0=x1t[i][:, :], scalar=rstd, in1=g2b[:, :], op0=ALU.mult, op1=ALU.mult),
                    reads=[x1t_b[i], stat_b[t], g2b_b], writes=[hb_b[i]])
                S.play(pend_t)
                S.rec_begin()
                for g in range(4):
                    pi = (pcnt % 2) * 4 + (3 - g)
                    pt = psum[pi].bitcast(BF16)
                    for q in range(4):
                        kc = g * 4 + q
                        S.op("pe", lambda e, pt=pt, q=q, i=i, kc=kc: e.transpose(
                            out=pt[:, q * 128:(q + 1) * 128], in_=hb[i][:, kc * 128:(kc + 1) * 128], identity=ident_bf[:, :]),
                            reads=[hb_b[i], ident_bf_b], writes=psum_b[pi], skip_same=True)
                    S.op("act", lambda e, pt=pt, g=g, i=i: e.activation(
                        out=hst[i][:, g * 4:(g + 1) * 4, :], in_=pt[:, 0:512].rearrange("p (a b) -> p a b", a=4), func=AF.Copy),
                        reads=psum_b[pi], writes=[hst_b[i]])
                pcnt += 1
                S.dma("pool", h2_r[:, :, ts_], hst[i][:, :, :], reads=[hst_b[i]])
                pend_t = S.rec_end()
            S.play(pend_t)


def phase_ffn(C):
    nc, S, debug = C.nc, C.S, C.debug
    psum, psum_b = C.psum, C.psum_b
    TH = 1024
    NB3 = 342
    with contextlib.ExitStack() as pes:
        def sb(name, shape, dt):
            return pes.enter_context(nc.sbuf_tensor(name, list(shape), dt))
        h2T = sb("h2T", [128, KC, TH + 2], BF16)
        h2T_b = Buf("h2T")
        aT = sb("aT", [128, 44, TH], BF16)
        aT_b = [Buf("aT%d" % j) for j in range(44)]
        NW = 5
        wu = [sb("wu%d" % i, [128, KC, 128], BF16) for i in range(NW)]
        wu_b = [Buf("wu") for i in range(NW)]
        pg = [sb("pg%d" % i, [128, TH + 2], F32) for i in range(2)]
        pg_b = [Buf("pg") for i in range(2)]
        cg = [sb("cg%d" % i, [128, TH], F32) for i in range(2)]
        cg_b = [Buf("cg") for i in range(2)]
        fw = sb("fw", [128, 88 * 4], F32)
        fw_b = Buf("fw")
        S.dma("sp", fw[:, :], C.ffn_cw_d[:, :], writes=[fw_b])
        NWD = 2
        wdn = [sb("wdn%d" % i, [128, 44, 128], BF16) for i in range(NWD)]
        wdn_b = [Buf("wdn") for i in range(NWD)]
        x1s = [sb("x1s%d" % i, [128, 512], F32) for i in range(2)]
        x1s_b = [Buf("x1s") for i in range(2)]
        xo = [sb("xo%d" % i, [128, 512], F32) for i in range(2)]
        xo_b = [Buf("xo") for i in range(2)]
        wup_r = C.w_up.rearrange("(kc p) n -> p kc n", p=128)
        wdn_r = C.w_down.rearrange("(fc p) n -> p fc n", p=128)
        h2_r = C.h2_scr.rearrange("(kc p) t -> p kc t", p=128)
        ucnt = 0
        pcnt = 0
        dcnt = 0
        for half in range(2):
            t0 = half * TH
            S.op("pool", lambda e: e.memset(h2T[:, :, :], 0.0), writes=[h2T_b])
            lo = max(t0 - 1, 0)
            hi = min(t0 + TH + 1, T)
            S.dma("sp", h2T[:, :, lo - (t0 - 1):hi - (t0 - 1)], h2_r[:, :, lo:hi], writes=[h2T_b])
            ulist = [(j, which) for j in range(44) for which in range(2)]
            PF = NW - 1

            def issue_unit(k):
                j_, which_ = ulist[k]
                c0_ = which_ * D_FF + j_ * 128
                wi_ = (ucnt + k) % NW
                S.dma("pool", wu[wi_][:, :, :], wup_r[:, :, c0_:c0_ + 128], writes=[wu_b[wi_]])

            for k in range(min(PF, len(ulist))):
                issue_unit(k)
            for k, (j, which) in enumerate(ulist):
                if k + PF < len(ulist):
                    issue_unit(k + PF)
                blk = which * 44 + j
                wi = (ucnt + k) % NW
                pgi = k % 2
                for b3 in range(3):
                    pi = pcnt % 8
                    pcnt += 1
                    s3 = slice(b3 * NB3, (b3 + 1) * NB3)
                    for kc in range(KC):
                        S.op("pe", lambda e, pi=pi, wi=wi, kc=kc, s3=s3: e.matmul(
                            out=psum[pi][:, 0:NB3], lhsT=wu[wi][:, kc, :], rhs=h2T[:, kc, s3], start=(kc == 0), stop=(kc == KC - 1)),
                            reads=[wu_b[wi], h2T_b], writes=psum_b[pi], skip_same=True)
                    if pcnt % 2 == 0:
                        S.op("act", lambda e, pi=pi, pgi=pgi, s3=s3: e.activation(out=pg[pgi][:, s3], in_=psum[pi][:, 0:NB3], func=AF.Copy),
                             reads=psum_b[pi], writes=[pg_b[pgi]])
                    else:
                        S.op("dve", lambda e, pi=pi, pgi=pgi, s3=s3: e.tensor_copy(out=pg[pgi][:, s3], in_=psum[pi][:, 0:NB3]),
                             reads=psum_b[pi], writes=[pg_b[pgi]])
                w0 = fw[:, blk * 4:blk * 4 + 1]
                w1 = fw[:, blk * 4 + 1:blk * 4 + 2]
                w2 = fw[:, blk * 4 + 2:blk * 4 + 3]
                bb = fw[:, blk * 4 + 3:blk * 4 + 4]
                S.op("act", lambda e, pgi=pgi, which=which, w1=w1, bb=bb: e.activation(
                    out=cg[which][:, :], in_=pg[pgi][:, 1:TH + 1], func=AF.Identity, scale=w1, bias=bb),
                    reads=[pg_b[pgi], fw_b], writes=[cg_b[which]])
                S.op("dve", lambda e, pgi=pgi, which=which, w0=w0: e.scalar_tensor_tensor(
                    out=cg[which][:, :], in0=pg[pgi][:, 0:TH], scalar=w0, in1=cg[which][:, :], op0=ALU.mult, op1=ALU.add),
                    reads=[pg_b[pgi], fw_b, cg_b[which]], writes=[cg_b[which]])
                S.op("dve", lambda e, pgi=pgi, which=which, w2=w2: e.scalar_tensor_tensor(
                    out=cg[which][:, :], in0=pg[pgi][:, 2:TH + 2], scalar=w2, in1=cg[which][:, :], op0=ALU.mult, op1=ALU.add),
                    reads=[pg_b[pgi], fw_b, cg_b[which]], writes=[cg_b[which]])
                if which == 0:
                    S.op("act", lambda e: e.activation(out=cg[0][:, :], in_=cg[0][:, :], func=AF.Silu),
                         reads=[cg_b[0]], writes=[cg_b[0]])
                else:
                    S.op("dve", lambda e, j=j: e.tensor_tensor(out=aT[:, j, :], in0=cg[0][:, :], in1=cg[1][:, :], op=ALU.mult),
                         reads=[cg_b[0], cg_b[1]], writes=[aT_b[j]])
            ucnt += len(ulist)
            for cgp in range(4):
                units = []
                for u in range(4):
                    c0 = cgp * 512 + u * 128
                    wi = dcnt % NWD
                    dcnt += 1
                    S.dma("pool", wdn[wi][:, :, :], wdn_r[:, :, c0:c0 + 128], writes=[wdn_b[wi]])
                    units.append(wi)
                    for tt in range(TH // 128):
                        pi = tt % 8
                        ts_ = slice(tt * 128, (tt + 1) * 128)
                        for fc in range(44):
                            S.op("pe", lambda e, pi=pi, u=u, fc=fc, ts_=ts_, wi=wi: e.matmul(
                                out=psum[pi][:, u * 128:(u + 1) * 128], lhsT=aT[:, fc, ts_], rhs=wdn[wi][:, fc, :],
                                start=(fc == 0), stop=(fc == 43)),
                                reads=[aT_b[fc], wdn_b[wi]], writes=psum_b[pi], skip_same=True)
                for tt in range(TH // 128):
                    pi = tt % 8
                    i = tt % 2
                    rows = slice(t0 + tt * 128, t0 + (tt + 1) * 128)
                    csl = slice(cgp * 512, (cgp + 1) * 512)
                    S.dma("sp", x1s[i][:, :], C.x1_scr[rows, csl], writes=[x1s_b[i]])
                    S.op("dve", lambda e, pi=pi, i=i: e.tensor_tensor(out=xo[i][:, :], in0=psum[pi][:, :], in1=x1s[i][:, :], op=ALU.add),
                         reads=psum_b[pi] + [x1s_b[i]], writes=[xo_b[i]])
                    S.dma("act", C.x2_scr[rows, csl], xo[i][:, :], reads=[xo_b[i]])


def phase_final(C):
    nc, S, debug = C.nc, C.S, C.debug
    with contextlib.ExitStack() as pes:
        def sb(name, shape, dt):
            return pes.enter_context(nc.sbuf_tensor(name, list(shape), dt))
        gfb = sb("gfb", [128, D], F32)
        gfb_b = Buf("gfb")
        S.dma("sp", gfb[:, :], C.final_norm_g.partition_broadcast(128), writes=[gfb_b])
        NBF = 3
        xt = [sb("xf%d" % i, [128, D], F32) for i in range(NBF)]
        xt_b = [Buf("xf") for i in range(NBF)]
        ot = [sb("of%d" % i, [128, D], F32) for i in range(NBF)]
        ot_b = [Buf("of") for i in range(NBF)]
        junk = sb("junk3", [128, D], BF16)
        junk_b = Buf("junk3")
        stat = sb("stat3", [128, 4 * NT], F32)
        stat_b = [Buf("stat3") for i in range(NT)]
        for t in range(NT):
            i = t % NBF
            ts_ = slice(t * 128, (t + 1) * 128)
            S.dma("sp", xt[i][:, :], C.x2_scr[ts_, :], writes=[xt_b[i]])
            ss = stat[:, 4 * t:4 * t + 1]
            lnv = stat[:, 4 * t + 1:4 * t + 2]
            rstd = stat[:, 4 * t + 2:4 * t + 3]
            S.op("act", lambda e, i=i, ss=ss: e.activation(out=junk[:, :], in_=xt[i][:, :], func=AF.Square, accum_out=ss),
                 reads=[xt_b[i]], writes=[junk_b, stat_b[t]])
            S.op("act", lambda e, ss=ss, lnv=lnv: e.activation(out=lnv, in_=ss, func=AF.Ln, scale=1.0 / D, bias=EPS),
                 reads=[stat_b[t]], writes=[stat_b[t]])
            S.op("act", lambda e, rstd=rstd, lnv=lnv: e.activation(out=rstd, in_=lnv, func=AF.Exp, scale=-0.5),
                 reads=[stat_b[t]], writes=[stat_b[t]])
            S.op("dve", lambda e, i=i, rstd=rstd: e.scalar_tensor_tensor(
                out=ot[i][:, :], in0=xt[i][:, :], scalar=rstd, in1=gfb[:, :], op0=ALU.mult, op1=ALU.mult),
                reads=[xt_b[i], stat_b[t], gfb_b], writes=[ot_b[i]])
            S.dma("pool", C.out[ts_, :], ot[i][:, :], reads=[ot_b[i]])

NC2 = 1024
NCB = 14 * 128


_NC_CACHE = {}


def _consts():
    j = np.arange(128)[:, None]
    i = np.arange(128)[None, :]
    cst = np.zeros((128, NCST), np.float32)
    cst[:, 0:128] = np.where(j <= i, -1.0 / 16, 0.0)
    cst[:, 128:256] = np.where(j >= i, -1.0 / 16, 0.0)
    cst[:, 256:384] = np.where(j > i, -1.0 / 16, 0.0)
    cst[:, 384:512] = np.where(j < i, -1.0 / 16, 0.0)
    cst[:, 512:640] = 1.0
    cst[:, 640:1152] = np.tile(np.where(j <= i, 1.0, 0.0), (1, 4))
    cst[:, 1152:1664] = np.tile(np.where(j >= i, 1.0, 0.0), (1, 4))
    c2 = np.zeros((128, NC2), np.float32)
    c2[:, 0:128] = np.where(j <= i, 1.0, 0.0)
    c2[:, 128:256] = np.where(j >= i, 1.0, 0.0)
    c2[:, 256:384] = np.where(j > i, 1.0, 0.0)
    c2[:, 384:512] = np.where(j < i, 1.0, 0.0)
    c2[:, 512:640] = np.where(j <= i, 0.0, -30000.0)
    c2[:, 640:768] = np.where(j >= i, 0.0, -30000.0)
    c2[:, 768:896] = np.eye(128)
    c2[:, 896:1024] = -1.0
    cb = np.zeros((128, NCB), np.float32)
    for d in range(2):
        for l in range(1, 8):
            b = 1 << (l - 1)
            same = (i // (2 * b)) == (j // (2 * b))
            if d == 0:
                m = same & ((j % (2 * b)) >= b) & ((i % (2 * b)) < b)
            else:
                m = same & ((j % (2 * b)) < b) & ((i % (2 * b)) >= b)
            o = (d * 7 + (l - 1)) * 128
            cb[:, o:o + 128] = -m.astype(np.float32)
    return {
        "ident_bf": np.eye(128, dtype=np.float32).astype(ml_dtypes.bfloat16),
        "cst": cst,
        "cst2": c2,
        "cstb": cb.astype(ml_dtypes.bfloat16),
    }


def make_in_maps(inputs, n_cores=8):
    c = _consts()
    maps = []
    xs = np.ascontiguousarray(inputs["x"])
    for b in range(n_cores):
        m = {
            "x": xs[b],
            "norm1_g": np.ascontiguousarray(inputs["norm1_g"]).reshape(1, D),
            "w_in": np.ascontiguousarray(inputs["w_in"]).reshape(D, N_IN),
            "gla_decay_w_f": np.ascontiguousarray(inputs["gla_decay_w_f"]).reshape(16, 1024),
            "gla_decay_w_b": np.ascontiguousarray(inputs["gla_decay_w_b"]).reshape(16, 1024),
            "gla_decay_b_f": np.ascontiguousarray(inputs["gla_decay_b_f"]).reshape(1, 1024),
            "gla_decay_b_b": np.ascontiguousarray(inputs["gla_decay_b_b"]).reshape(1, 1024),
            "gla_norm_g": np.ascontiguousarray(inputs["gla_norm_g"]).reshape(1, 128),
            "gdn_norm_g": np.ascontiguousarray(inputs["gdn_norm_g"]).reshape(1, 128),
            "w_branch_gla": np.ascontiguousarray(inputs["w_branch_gla"]).reshape(1024, D),
            "w_branch_gdn": np.ascontiguousarray(inputs["w_branch_gdn"]).reshape(1024, D),
            "w_out": np.ascontiguousarray(inputs["w_out"]).reshape(D, D),
            "norm2_g": np.ascontiguousarray(inputs["norm2_g"]).reshape(1, D),
            "w_up": np.ascontiguousarray(inputs["w_up"]).reshape(D, 2 * D_FF),
            "w_down": np.ascontiguousarray(inputs["w_down"]).reshape(D_FF, D),
            "ffn_cw": np.ascontiguousarray(np.concatenate([
                np.asarray(inputs["ffn_conv_w"]).reshape(3, 88, 128), np.asarray(inputs["ffn_conv_b"]).reshape(1, 88, 128)],
                axis=0).transpose(2, 1, 0).reshape(128, 88 * 4)),
            "final_norm_g": np.ascontiguousarray(inputs["final_norm_g"]).reshape(1, D),
            "gdn_cw": np.ascontiguousarray(
                np.asarray(inputs["gdn_conv_w"]).reshape(3, 24, 128).transpose(2, 1, 0).reshape(128, 72)),
            "gdn_hp": np.ascontiguousarray(np.stack([
                np.concatenate([np.asarray(inputs["gdn_a_log_f"]).reshape(8), np.asarray(inputs["gdn_a_log_b"]).reshape(8)]),
                np.concatenate([np.asarray(inputs["gdn_dt_bias_f"]).reshape(8), np.asarray(inputs["gdn_dt_bias_b"]).reshape(8)]),
            ], axis=1).astype(np.float32)),
        }
        m.update(c)
        maps.append(m)
    return maps


def kernel(**inputs):
    nc = build_nc()
    in_maps = make_in_maps(inputs, 8)
    res = run_bass_kernel_spmd(nc, in_maps, core_ids=list(range(8)))
    return np.stack([np.asarray(r["out"]) for r in res.results], axis=0)
```

```python
import contextlib
import numpy as np
import ml_dtypes
import concourse.bass as bass
import concourse.mybir as mybir
from concourse.bass_utils import run_bass_kernel_spmd

F32 = mybir.dt.float32
BF16 = mybir.dt.bfloat16
AF = mybir.ActivationFunctionType
ALU = mybir.AluOpType

T = 2048
D = 2048
KC = 16
NT = 16
N_IN = 12352
D_FF = 5632
EPS = 1e-6

C_GQ, C_GK, C_GV, C_GG = 0, 1024, 2048, 3072
C_LR = 4096
C_DQ, C_DK, C_DV = 4128, 5152, 6176
C_DZ = 7200
C_AB = 8224
C_BG = 8256
C_BD = 10304


class Buf:
    __slots__ = ("w", "r", "name", "excl")

    def __init__(self, name="", excl=False):
        self.w = None
        self.r = {}
        self.name = name
        self.excl = excl


class DSem:
    __slots__ = ("handle", "count")

    def __init__(self, handle):
        self.handle = handle
        self.count = 0


class Sched:
    ENG = ["pe", "act", "dve", "pool", "sp"]

    def __init__(self, nc, es, n_dsem=24):
        self.nc = nc
        self.sem = {e: es.enter_context(nc.semaphore("s_" + e)) for e in self.ENG}
        self.cnt = {e: 0 for e in self.ENG}
        self.seen = {e: {} for e in self.ENG}
        self.prog = {e: [] for e in self.ENG}
        self.dsems = [DSem(es.enter_context(nc.semaphore("d%d" % i))) for i in range(n_dsem)]
        self.dnext = 0

    def _waits(self, eng, reads, writes, skip_same):
        deps = {}

        def add(k, v):
            if skip_same and k == eng:
                return
            if deps.get(k, 0) < v:
                deps[k] = v

        for b in reads:
            if b.w is not None:
                add(*b.w)
        for b in writes:
            if b.w is not None:
                add(*b.w)
            for k, v in b.r.items():
                add(k, v)
        out = []
        seen = self.seen[eng]
        for k, v in deps.items():
            if seen.get(k, 0) < v:
                seen[k] = v
                out.append((k.handle if isinstance(k, DSem) else self.sem[k], v))
        return out

    def _mark(self, d, reads, writes):
        k, v = d
        for b in reads:
            if b.r.get(k, 0) < v:
                b.r[k] = v
        for b in writes:
            b.w = d
            b.r = {}

    def rec_begin(self):
        self._rec = []

    def rec_end(self):
        r, self._rec = self._rec, None
        return r

    def play(self, lst, n=None):
        n = len(lst) if n is None else min(n, len(lst))
        for _ in range(n):
            kind, a, kw = lst.pop(0)
            (self.op if kind == "op" else self.dma)(*a, **kw)

    def op(self, eng, fn, reads=(), writes=(), skip_same=False):
        if getattr(self, "_rec", None) is not None:
            self._rec.append(("op", (eng, fn, list(reads), list(writes), skip_same), {}))
            return
        if any(b.excl for b in reads):
            writes = list(writes) + [b for b in reads if b.excl]
            reads = [b for b in reads if not b.excl]
        waits = self._waits(eng, reads, writes, skip_same)
        self.cnt[eng] += 1
        self.prog[eng].append((waits, fn, self.sem[eng], 1))
        self._mark((eng, self.cnt[eng]), reads, writes)

    def barrier(self):
        for eng in self.ENG:
            waits = []
            seen = self.seen[eng]
            for k in self.ENG:
                if k != eng and seen.get(k, 0) < self.cnt[k]:
                    seen[k] = self.cnt[k]
                    waits.append((self.sem[k], self.cnt[k]))
            for ds in self.dsems:
                if ds.count > 0 and seen.get(ds, 0) < ds.count:
                    seen[ds] = ds.count
                    waits.append((ds.handle, ds.count))
            if waits:
                self.prog[eng].append((waits, None, None, 0))

    def dma(self, q, out, in_, reads=(), writes=(), **kw):
        if getattr(self, "_rec", None) is not None:
            self._rec.append(("dma", (q, out, in_, list(reads), list(writes)), dict(kw)))
            return
        ds = self.dsems[self.dnext]
        self.dnext = (self.dnext + 1) % len(self.dsems)
        waits = self._waits(q, reads, writes, False)
        if ds.count > 0 and self.seen[q].get(ds, 0) < ds.count:
            self.seen[q][ds] = ds.count
            waits.append((ds.handle, ds.count))
        ds.count += 16
        self.prog[q].append((waits, (lambda e, o=out, i=in_, kw=kw: e.dma_start(out=o, in_=i, **kw)), ds.handle, 16))
        self._mark((ds, ds.count), reads, writes)

    def finish(self):
        waits = []
        for ds in self.dsems:
            if ds.count > 0:
                waits.append((ds.handle, ds.count))
        self.prog["sp"].append((waits, None, None, 0))

    def emit(self):
        nc = self.nc
        prog = self.prog

        def replay(name, e):
            for waits, fn, sem, inc in prog[name]:
                for s, v in waits:
                    e.wait_ge(s, v)
                if fn is not None:
                    ins = fn(e)
                    ins.then_inc(sem, inc)

        with nc.Block() as block:
            @block.tensor
            def _(e):
                replay("pe", e)

            @block.scalar
            def _(e):
                replay("act", e)

            @block.vector
            def _(e):
                replay("dve", e)

            @block.gpsimd
            def _(e):
                replay("pool", e)

            @block.sync
            def _(e):
                replay("sp", e)


def build_nc(debug=None):
    debug = debug or {}
    nc = bass.Bass("TRN2", target_bir_lowering=False)
    es = contextlib.ExitStack()
    with es:
        _build(nc, es, debug)
    return nc


def _dram_in(nc, name, shape, dt=F32):
    return nc.dram_tensor(name, list(shape), dt, kind="ExternalInput").ap()


class Ctx:
    pass


def _build(nc, es, debug):
    S = Sched(nc, es)
    C = Ctx()
    C.nc, C.S, C.debug = nc, S, debug
    stop_after = debug.get("stop_after", [[None], None])[0][0]

    C.x = _dram_in(nc, "x", [T, D])
    C.norm1_g = _dram_in(nc, "norm1_g", [1, D])
    C.w_in = _dram_in(nc, "w_in", [D, N_IN])
    C.ident_bf_d = _dram_in(nc, "ident_bf", [128, 128], BF16)
    C.cst_d = _dram_in(nc, "cst", [128, NCST])
    C.gla_dw = [_dram_in(nc, "gla_decay_w_f", [16, 1024]), _dram_in(nc, "gla_decay_w_b", [16, 1024])]
    C.gla_db = [_dram_in(nc, "gla_decay_b_f", [1, 1024]), _dram_in(nc, "gla_decay_b_b", [1, 1024])]
    C.gla_norm_g = _dram_in(nc, "gla_norm_g", [1, 128])
    C.gdn_norm_g = _dram_in(nc, "gdn_norm_g", [1, 128])
    C.w_branch_gla = _dram_in(nc, "w_branch_gla", [1024, D])
    C.w_branch_gdn = _dram_in(nc, "w_branch_gdn", [1024, D])
    C.w_out = _dram_in(nc, "w_out", [D, D])
    C.norm2_g = _dram_in(nc, "norm2_g", [1, D])
    C.w_up = _dram_in(nc, "w_up", [D, 2 * D_FF])
    C.w_down = _dram_in(nc, "w_down", [D_FF, D])
    C.ffn_cw_d = _dram_in(nc, "ffn_cw", [128, 88 * 4])
    C.final_norm_g = _dram_in(nc, "final_norm_g", [1, D])
    C.cst2_d = _dram_in(nc, "cst2", [128, NC2])
    C.cstb_d = _dram_in(nc, "cstb", [128, NCB], BF16)
    C.gdn_cw_d = _dram_in(nc, "gdn_cw", [128, 72])
    C.gdn_hp_d = _dram_in(nc, "gdn_hp", [16, 2])
    C.out = nc.dram_tensor("out", [T, D], F32, kind="ExternalOutput").ap()
    C.dbg_out = {}
    for name, (shape, dt) in debug.items():
        if dt is None:
            continue
        C.dbg_out[name] = nc.dram_tensor("dbg_" + name, list(shape), dt, kind="ExternalOutput").ap()
    C.scr = nc.dram_tensor("scr_proj", [N_IN, T], F32).ap()
    C.y_scr = nc.dram_tensor("scr_y", [2048, T], BF16).ap()
    C.h2_scr = nc.dram_tensor("scr_h2", [D, T], BF16).ap()
    C.x1_scr = nc.dram_tensor("scr_x1", [T, D], F32).ap()
    C.x2_scr = nc.dram_tensor("scr_x2", [T, D], F32).ap()

    C.psum = [es.enter_context(nc.psum_tensor("ps%d" % i, [128, 512], F32)) for i in range(8)]
    C.psum_b = [[Buf("ps%d" % i, excl=True)] * 4 for i in range(8)]
    C.ident_bf = es.enter_context(nc.sbuf_tensor("ident_bf_sb", [128, 128], BF16))
    C.ident_bf_b = Buf("ident_bf")
    C.cst = es.enter_context(nc.sbuf_tensor("cst_sb", [128, NCST], F32))
    C.cst_b = Buf("cst")
    S.dma("sp", C.ident_bf[:, :], C.ident_bf_d[:, :], writes=[C.ident_bf_b])
    S.dma("sp", C.cst[:, :], C.cst_d[:, :], writes=[C.cst_b])

    phase_proj(C)
    S.barrier()
    if stop_after != "proj":
        if debug.get("gla_heads", [[8], None])[0][0] > 0:
            phase_gla(C)
            S.barrier()
        if stop_after != "gla":
            if debug.get("gdn_heads", [[8], None])[0][0] > 0:
                phase_gdn(C)
                S.barrier()
            if stop_after != "gdn":
                phase_branch(C)
                S.barrier()
                if stop_after != "branch":
                    phase_ffn(C)
                    S.barrier()
                    phase_final(C)
                    S.barrier()

    if "proj" in C.dbg_out:
        S.dma("sp", C.dbg_out["proj"][:, :], C.scr[0:C.dbg_out["proj"].shape[0], :])
    if "y" in C.dbg_out:
        S.dma("sp", C.dbg_out["y"][:, :], C.y_scr[0:C.dbg_out["y"].shape[0], :])
    if "y2" in C.dbg_out:
        S.dma("sp", C.dbg_out["y2"][:, :], C.y_scr[1024:1024 + C.dbg_out["y2"].shape[0], :])
    for nm, ap_ in (("x1", C.x1_scr), ("x2", C.x2_scr)):
        if nm in C.dbg_out:
            S.dma("sp", C.dbg_out[nm][:, :], ap_[:, :])
    if "h2" in C.dbg_out:
        S.dma("sp", C.dbg_out["h2"][:, :], C.h2_scr[:, :])
    if stop_after is not None:
        S.dma("sp", C.out[0:128, :], C.x[0:128, :])
    S.finish()
    S.emit()


def phase_proj(C):
    nc, S, debug = C.nc, C.S, C.debug
    psum, psum_b = C.psum, C.psum_b
    with contextlib.ExitStack() as pes:
        def sb(name, shape, dt):
            return pes.enter_context(nc.sbuf_tensor(name, list(shape), dt))

        hT = sb("hT", [128, KC, T], BF16)
        hT_b = [Buf("hT%d" % t) for t in range(NT)]
        g1b = sb("g1b", [128, D], F32)
        g1b_b = Buf("g1b")
        S.dma("sp", g1b[:, :], C.norm1_g.partition_broadcast(128), writes=[g1b_b])

        xt = [sb("xt%d" % i, [128, D], F32) for i in range(2)]
        xt_b = [Buf("xt%d" % i) for i in range(2)]
        junk = sb("junk", [128, D], BF16)
        junk_b = Buf("junk")
        hb = [sb("hb%d" % i, [128, D], BF16) for i in range(2)]
        hb_b = [Buf("hb%d" % i) for i in range(2)]
        stat = sb("stat", [128, 4 * NT], F32)
        stat_b = [Buf("stat%d" % i) for i in range(NT)]
        ident_bf, ident_bf_b = C.ident_bf, C.ident_bf_b
        pcnt = 0
        pend0 = []
        for t in range(NT):
            i = t % 2
            S.dma("sp", xt[i][:, :], C.x[t * 128:(t + 1) * 128, :], writes=[xt_b[i]])
            ss = stat[:, 4 * t:4 * t + 1]
            lnv = stat[:, 4 * t + 1:4 * t + 2]
            rstd = stat[:, 4 * t + 2:4 * t + 3]
            S.op("act", lambda e, i=i, ss=ss: e.activation(out=junk[:, :], in_=xt[i][:, :], func=AF.Square, accum_out=ss),
                 reads=[xt_b[i]], writes=[junk_b, stat_b[t]])
            S.op("act", lambda e, ss=ss, lnv=lnv: e.activation(out=lnv, in_=ss, func=AF.Ln, scale=1.0 / D, bias=EPS),
                 reads=[stat_b[t]], writes=[stat_b[t]])
            S.op("act", lambda e, rstd=rstd, lnv=lnv: e.activation(out=rstd, in_=lnv, func=AF.Exp, scale=-0.5),
                 reads=[stat_b[t]], writes=[stat_b[t]])
            S.op("dve", lambda e, i=i, rstd=rstd: e.scalar_tensor_tensor(
                out=hb[i][:, :], in0=xt[i][:, :], scalar=rstd, in1=g1b[:, :], op0=ALU.mult, op1=ALU.mult),
                reads=[xt_b[i], stat_b[t], g1b_b], writes=[hb_b[i]])
            S.play(pend0)
            S.rec_begin()
            for g in range(4):
                pi = 4 + (pcnt % 4)
                pcnt += 1
                pt = psum[pi].bitcast(BF16)
                for q in range(4):
                    kc = g * 4 + q
                    S.op("pe", lambda e, pt=pt, q=q, i=i, kc=kc: e.transpose(
                        out=pt[:, q * 128:(q + 1) * 128], in_=hb[i][:, kc * 128:(kc + 1) * 128], identity=ident_bf[:, :]),
                        reads=[hb_b[i], ident_bf_b], writes=[psum_b[pi][q]], skip_same=True)
                if g % 2 == 0:
                    S.op("act", lambda e, pt=pt, g=g, t=t: e.activation(
                        out=hT[:, g * 4:(g + 1) * 4, t * 128:(t + 1) * 128],
                        in_=pt[:, 0:512].rearrange("p (a b) -> p a b", a=4), func=AF.Copy),
                        reads=psum_b[pi], writes=[hT_b[t]])
                else:
                    S.op("dve", lambda e, pt=pt, g=g, t=t: e.tensor_copy(
                        out=hT[:, g * 4:(g + 1) * 4, t * 128:(t + 1) * 128],
                        in_=pt[:, 0:512].rearrange("p (a b) -> p a b", a=4)),
                        reads=psum_b[pi], writes=[hT_b[t]])
            pend0 = S.rec_end()
        S.play(pend0)

        scr = C.scr
        w_in_r = C.w_in.rearrange("(kc p) n -> p kc n", p=128)
        units = []
        for c0 in range(0, 4096, 128):
            units.append((c0, 128))
        units.append((C_LR, 32))
        for c0 in range(C_DQ, C_AB, 128):
            units.append((c0, 128))
        units.append((C_AB, 32))
        for c0 in range(C_BG, N_IN, 128):
            units.append((c0, 128))
        if "nunits" in debug:
            units = units[:debug["nunits"][0][0]]
        NW = 8
        wring = [sb("wr%d" % i, [128, KC, 128], BF16) for i in range(NW)]
        wring_b = [Buf("wr%d" % i) for i in range(NW)]
        NSTG = 3
        stg = [sb("stg%d" % i, [128, T], F32) for i in range(NSTG)]
        stg_b = [Buf("stg%d" % i) for i in range(NSTG)]
        ecnt = 0
        for u, (c0, ncol) in enumerate(units):
            wt, wb = wring[u % NW], wring_b[u % NW]
            S.dma("pool", wt[:, :, 0:ncol], w_in_r[:, :, c0:c0 + ncol], writes=[wb])
            st, stb = stg[u % NSTG], stg_b[u % NSTG]
            for blk in range(4):
                pi = (4 * u + blk) % 8
                for kc in range(KC):
                    S.op("pe", lambda e, pi=pi, wt=wt, kc=kc, blk=blk, ncol=ncol: e.matmul(
                        out=psum[pi][0:ncol, :], lhsT=wt[:, kc, 0:ncol], rhs=hT[:, kc, blk * 512:(blk + 1) * 512],
                        start=(kc == 0), stop=(kc == KC - 1)),
                        reads=[wb] + hT_b[4 * blk:4 * blk + 4], writes=psum_b[pi], skip_same=True)
                if ecnt % 2 == 0:
                    S.op("act", lambda e, pi=pi, st=st, blk=blk, ncol=ncol: e.activation(
                        out=st[0:ncol, blk * 512:(blk + 1) * 512], in_=psum[pi][0:ncol, :], func=AF.Copy),
                        reads=psum_b[pi], writes=[stb])
                else:
                    S.op("dve", lambda e, pi=pi, st=st, blk=blk, ncol=ncol: e.tensor_copy(
                        out=st[0:ncol, blk * 512:(blk + 1) * 512], in_=psum[pi][0:ncol, :]),
                        reads=psum_b[pi], writes=[stb])
                ecnt += 1
            S.dma("sp", scr[c0:c0 + ncol, :], st[0:ncol, :], reads=[stb])


def phase_gla(C):
    nc, S, debug = C.nc, C.S, C.debug
    psum, psum_b = C.psum, C.psum_b
    cst, cst_b = C.cst, C.cst_b
    scr = C.scr
    nheads = debug.get("gla_heads", [[8], None])[0][0]
    with contextlib.ExitStack() as pes:
        def sb(name, shape, dt):
            return pes.enter_context(nc.sbuf_tensor(name, list(shape), dt))

        def B(name):
            return Buf(name)

        lr = sb("lr", [49, T], F32)
        lr_b = B("lr")
        dw = sb("dw", [49, 1024], F32)
        dw_b = B("dw")
        gn = sb("gn", [128, 1], F32)
        gn_b = B("gn")
        S.op("pool", lambda e: e.memset(lr[:, :], 1.0), writes=[lr_b])
        for d in range(2):
            S.dma("sp", lr[32 * d:32 * d + 16, :], scr[C_LR + 16 * d:C_LR + 16 * d + 16, :], writes=[lr_b])
            S.dma("sp", dw[32 * d:32 * d + 16, :], C.gla_dw[d][:, :], writes=[dw_b])
            S.dma("sp", dw[32 * d + 16:32 * d + 17, :], C.gla_db[d][:, :], writes=[dw_b])
        S.dma("sp", gn[:, :], C.gla_norm_g.rearrange("o d -> d o"), writes=[gn_b])

        NB = 2
        qbf = [sb("qbf%d" % i, [128, T], BF16) for i in range(NB)]
        kbf = [sb("kbf%d" % i, [128, T], BF16) for i in range(NB)]
        vbf = [sb("vbf%d" % i, [128, T], BF16) for i in range(NB)]
        g32 = [sb("g32%d" % i, [128, T], F32) for i in range(NB)]
        qbf_b = [B("qbf") for i in range(NB)]
        kbf_b = [B("kbf") for i in range(NB)]
        vbf_b = [B("vbf") for i in range(NB)]
        g32_b = [B("g32") for i in range(NB)]
        vtm = sb("vtm", [128, NT, 128], BF16)
        vtm_b = B("vtm")
        sg = sb("sg", [128, T], BF16)
        sg_b = B("sg")
        Ls = [sb("L%d" % d, [128, NT, 128], F32) for d in range(2)]
        Ls_b = [B("L") for d in range(2)]
        EGs = [sb("EG%d" % d, [128, T], BF16) for d in range(2)]
        EGs_b = [B("EG") for d in range(2)]
        EGis = [sb("EGi%d" % d, [128, T], BF16) for d in range(2)]
        EGis_b = [B("EGi") for d in range(2)]
        EKTs = [sb("EKT%d" % d, [128, NT, 128], BF16) for d in range(2)]
        EKTs_b = [B("EKT") for d in range(2)]
        dcl = sb("dcl", [128, 2, NT], F32)
        dcl_b = [B("dcl0"), B("dcl1")]
        qd = [sb("qd%d" % d, [128, T], BF16) for d in range(2)]
        qd_b = [B("qd") for d in range(2)]
        ki = [sb("ki%d" % d, [128, T], BF16) for d in range(2)]
        ki_b = [B("ki") for d in range(2)]
        kt = [sb("kt%d" % d, [128, NT, 128], BF16) for d in range(2)]
        kt_b = [B("kt") for d in range(2)]
        CSs = [sb("CS%d" % d, [128, NT, 128], F32) for d in range(2)]
        CSs_b = [B("CS") for d in range(2)]
        Sst = [sb("Sst%d" % d, [128, NT, 128], F32) for d in range(2)]
        Sst_b = [B("Sst") for d in range(2)]
        Sbf = [sb("Sbf%d" % d, [128, NT, 128], BF16) for d in range(2)]
        Sbf_b = [B("Sbf") for d in range(2)]
        tE = [sb("tE%d" % i, [128, 512], F32) for i in range(2)]
        tE_b = [B("tE") for i in range(2)]
        sc1s = [sb("sc1_%d" % i, [128, 512], BF16) for i in range(2)]
        sc1s_b = [B("sc1") for i in range(2)]
        sc2s = [sb("sc2_%d" % i, [128, 512], BF16) for i in range(2)]
        sc2s_b = [B("sc2") for i in range(2)]
        sqs = [sb("sq_%d" % i, [128, 512], F32) for i in range(2)]
        sqs_b = [B("sq") for i in range(2)]
        rss = [sb("rs_%d" % i, [128, 512], F32) for i in range(2)]
        rss_b = [B("rs") for i in range(2)]
        tts = [sb("tt_%d" % i, [128, 512], F32) for i in range(2)]
        tts_b = [B("tt") for i in range(2)]
        yb = [sb("yb%d" % i, [128, T], BF16) for i in range(2)]
        yb_b = [B("yb") for i in range(2)]

        ident_bf, ident_bf_b = C.ident_bf, C.ident_bf_b
        TRI = [cst[:, 0:128], cst[:, 128:256]]
        STRI = [cst[:, 256:384], cst[:, 384:512]]
        ONES = cst[:, 512:640]
        MASK = [cst[:, 640:1152], cst[:, 1152:1664]]
        pc = [0]
        tec = [0]

        def bank():
            pi = 4 + (pc[0] % 4)
            pc[0] += 1
            return pi

        def load_head(h):
            i = h % NB
            S.dma("pool", qbf[i][:, :], scr[C_GQ + h * 128:C_GQ + (h + 1) * 128, :], writes=[qbf_b[i]], max_dma_last_dim=4096)
            S.dma("pool", kbf[i][:, :], scr[C_GK + h * 128:C_GK + (h + 1) * 128, :], writes=[kbf_b[i]], max_dma_last_dim=4096)
            S.dma("pool", vbf[i][:, :], scr[C_GV + h * 128:C_GV + (h + 1) * 128, :], writes=[vbf_b[i]], max_dma_last_dim=4096)
            S.dma("sp", g32[i][:, :], scr[C_GG + h * 128:C_GG + (h + 1) * 128, :], writes=[g32_b[i]])

        load_head(0)
        for h in range(nheads):
            i = h % NB
            if h + 1 < nheads:
                load_head(h + 1)
            for g in range(4):
                pi = bank()
                pt = psum[pi].bitcast(BF16)
                for q in range(4):
                    c = 4 * g + q
                    S.op("pe", lambda e, pt=pt, q=q, c=c, i=i: e.transpose(
                        out=pt[:, q * 128:(q + 1) * 128], in_=vbf[i][:, c * 128:(c + 1) * 128], identity=ident_bf[:, :]),
                        reads=[vbf_b[i], ident_bf_b], writes=[psum_b[pi][q]], skip_same=True)
                S.op("act", lambda e, pt=pt, g=g: e.activation(
                    out=vtm[:, 4 * g:4 * g + 4, :], in_=pt[:, 0:512].rearrange("p (a b) -> p a b", a=4), func=AF.Copy),
                    reads=psum_b[pi], writes=[vtm_b])
            for blk in range(4):
                sl = slice(blk * 512, (blk + 1) * 512)
                j = tec[0] % 2
                tec[0] += 1
                S.op("act", lambda e, j=j, sl=sl, i=i: e.activation(out=tE[j][:, :], in_=g32[i][:, sl], func=AF.Exp, scale=-1.0),
                     reads=[g32_b[i]], writes=[tE_b[j]])
                S.op("act", lambda e, j=j: e.activation(out=tE[j][:, :], in_=tE[j][:, :], func=AF.Ln, bias=1.0),
                     reads=[tE_b[j]], writes=[tE_b[j]])
                S.op("act", lambda e, j=j: e.activation(out=tE[j][:, :], in_=tE[j][:, :], func=AF.Exp, scale=-1.0),
                     reads=[tE_b[j]], writes=[tE_b[j]])
                S.op("dve", lambda e, j=j, sl=sl, i=i: e.tensor_tensor(out=sg[:, sl], in0=g32[i][:, sl], in1=tE[j][:, :], op=ALU.mult),
                     reads=[g32_b[i], tE_b[j]], writes=[sg_b])
            for d in range(2):
                p0 = 32 * d
                for g in range(4):
                    pi = bank()
                    for q in range(4):
                        c = 4 * g + q
                        S.op("pe", lambda e, d=d, pi=pi, q=q, c=c, p0=p0, h=h: e.matmul(
                            out=psum[pi][:, q * 128:(q + 1) * 128], lhsT=lr[p0:p0 + 17, c * 128:(c + 1) * 128],
                            rhs=dw[p0:p0 + 17, h * 128:(h + 1) * 128], start=True, stop=True),
                            reads=[lr_b, dw_b], writes=[psum_b[pi][q]], skip_same=True)
                    j = tec[0] % 2
                    tec[0] += 1
                    S.op("act", lambda e, d=d, j=j, pi=pi: e.activation(out=tE[j][:, :], in_=psum[pi][:, :], func=AF.Exp, scale=-1.0),
                         reads=psum_b[pi], writes=[tE_b[j]])
                    S.op("act", lambda e, d=d, j=j, g=g: e.activation(
                        out=Ls[d][:, 4 * g:4 * g + 4, :], in_=tE[j][:, :].rearrange("p (a b) -> p a b", a=4), func=AF.Ln, bias=1.0),
                        reads=[tE_b[j]], writes=[Ls_b[d]])
            for d in range(2):
                p0 = 32 * d
                for g in range(4):
                    pi = bank()
                    sl = slice(g * 512, (g + 1) * 512)
                    for q in range(4):
                        c = 4 * g + q
                        S.op("pe", lambda e, pi=pi, q=q, c=c, d=d: e.matmul(
                            out=psum[pi][:, q * 128:(q + 1) * 128], lhsT=Ls[d][:, c, :], rhs=TRI[d], start=True, stop=True),
                            reads=[Ls_b[d], cst_b], writes=[psum_b[pi][q]], skip_same=True)
                    S.op("act", lambda e, d=d, pi=pi, sl=sl: e.activation(out=EGs[d][:, sl], in_=psum[pi][:, :], func=AF.Exp),
                         reads=psum_b[pi], writes=[EGs_b[d]])
                    S.op("act", lambda e, d=d, pi=pi, sl=sl: e.activation(out=EGis[d][:, sl], in_=psum[pi][:, :], func=AF.Exp, scale=-1.0),
                         reads=psum_b[pi], writes=[EGis_b[d]])
                    col = 127 if d == 0 else 0
                    S.op("act", lambda e, pi=pi, g=g, d=d, col=col: e.activation(
                        out=dcl[:, d, 4 * g:4 * g + 4], in_=psum[pi][:, :].rearrange("p (a b) -> p a b", a=4)[:, :, col], func=AF.Exp),
                        reads=psum_b[pi], writes=[dcl_b[d]])
            for d in range(2):
                p0 = 32 * d
                for g in range(4):
                    pi = bank()
                    for q in range(4):
                        c = 4 * g + q
                        S.op("pe", lambda e, pi=pi, q=q, c=c, d=d: e.matmul(
                            out=psum[pi][:, q * 128:(q + 1) * 128], lhsT=STRI[d], rhs=Ls[d][:, c, :], start=True, stop=True),
                            reads=[Ls_b[d], cst_b], writes=[psum_b[pi][q]], skip_same=True)
                    S.op("act", lambda e, d=d, pi=pi, g=g: e.activation(
                        out=EKTs[d][:, 4 * g:4 * g + 4, :], in_=psum[pi][:, :].rearrange("p (a b) -> p a b", a=4), func=AF.Exp),
                        reads=psum_b[pi], writes=[EKTs_b[d]])
            for d in range(2):
                p0 = 32 * d
                S.op("dve", lambda e, d=d, i=i: e.scalar_tensor_tensor(
                    out=qd[d][:, :], in0=qbf[i][:, :], scalar=float(128 ** -0.5), in1=EGs[d][:, :], op0=ALU.mult, op1=ALU.mult),
                    reads=[qbf_b[i], EGs_b[d]], writes=[qd_b[d]])
                S.op("dve", lambda e, d=d, i=i: e.tensor_tensor(out=ki[d][:, :], in0=kbf[i][:, :], in1=EGis[d][:, :], op=ALU.mult),
                     reads=[kbf_b[i], EGis_b[d]], writes=[ki_b[d]])
            for d in range(2):
                p0 = 32 * d
                for g in range(4):
                    pi = bank()
                    pt = psum[pi].bitcast(BF16)
                    for q in range(4):
                        c = 4 * g + q
                        S.op("pe", lambda e, d=d, pt=pt, q=q, c=c, i=i: e.transpose(
                            out=pt[:, q * 128:(q + 1) * 128], in_=kbf[i][:, c * 128:(c + 1) * 128], identity=ident_bf[:, :]),
                            reads=[kbf_b[i], ident_bf_b], writes=[psum_b[pi][q]], skip_same=True)
                    S.op("dve", lambda e, pt=pt, g=g, d=d: e.tensor_tensor(
                        out=kt[d][:, 4 * g:4 * g + 4, :], in0=pt[:, 0:512].rearrange("p (a b) -> p a b", a=4),
                        in1=EKTs[d][:, 4 * g:4 * g + 4, :], op=ALU.mult),
                        reads=psum_b[pi] + [EKTs_b[d]], writes=[kt_b[d]])
            for d in range(2):
                p0 = 32 * d
                for g in range(4):
                    pi = bank()
                    for q in range(4):
                        c = 4 * g + q
                        S.op("pe", lambda e, pi=pi, q=q, c=c, d=d: e.matmul(
                            out=psum[pi][:, q * 128:(q + 1) * 128], lhsT=kt[d][:, c, :], rhs=vtm[:, c, :], start=True, stop=True),
                            reads=[kt_b[d], vtm_b], writes=[psum_b[pi][q]], skip_same=True)
                    S.op("act", lambda e, d=d, pi=pi, g=g: e.activation(
                        out=CSs[d][:, 4 * g:4 * g + 4, :], in_=psum[pi][:, :].rearrange("p (a b) -> p a b", a=4), func=AF.Copy),
                        reads=psum_b[pi], writes=[CSs_b[d]])
            S.op("pool", lambda e: e.memset(Sst[0][:, 0, :], 0.0), writes=[Sst_b[0]])
            S.op("pool", lambda e: e.memset(Sst[1][:, NT - 1, :], 0.0), writes=[Sst_b[1]])
            for s in range(1, NT):
                c = s
                S.op("dve", lambda e, c=c: e.scalar_tensor_tensor(
                    out=Sst[0][:, c, :], in0=Sst[0][:, c - 1, :], scalar=dcl[:, 0, c - 1:c], in1=CSs[0][:, c - 1, :],
                    op0=ALU.mult, op1=ALU.add),
                    reads=[Sst_b[0], dcl_b[0], CSs_b[0]], writes=[Sst_b[0]])
                c = NT - 1 - s
                S.op("dve", lambda e, c=c: e.scalar_tensor_tensor(
                    out=Sst[1][:, c, :], in0=Sst[1][:, c + 1, :], scalar=dcl[:, 1, c + 1:c + 2], in1=CSs[1][:, c + 1, :],
                    op0=ALU.mult, op1=ALU.add),
                    reads=[Sst_b[1], dcl_b[1], CSs_b[1]], writes=[Sst_b[1]])
            S.op("act", lambda e: e.activation(out=Sbf[0][:, :, :], in_=Sst[0][:, :, :], func=AF.Copy),
                 reads=[Sst_b[0]], writes=[Sbf_b[0]])
            S.op("dve", lambda e: e.tensor_copy(out=Sbf[1][:, :, :], in_=Sst[1][:, :, :]),
                 reads=[Sst_b[1]], writes=[Sbf_b[1]])
            yi = h % 2
            for gp in range(2):
                GG = [2 * gp, 2 * gp + 1]
                BK = {g: (4 * (g % 2), 4 * (g % 2) + 1, 4 * (g % 2) + 2, 4 * (g % 2) + 3) for g in GG}
                for g in GG:
                    pA, pB, pO, pN = BK[g]
                    for q in range(4):
                        c = 4 * g + q
                        cs_ = slice(c * 128, (c + 1) * 128)
                        S.op("pe", lambda e, q=q, cs_=cs_, pA=pA: e.matmul(
                            out=psum[pA][:, q * 128:(q + 1) * 128], lhsT=ki[0][:, cs_], rhs=qd[0][:, cs_], start=True, stop=True),
                            reads=[ki_b[0], qd_b[0]], writes=psum_b[pA], skip_same=True)
                        S.op("pe", lambda e, q=q, cs_=cs_, pB=pB: e.matmul(
                            out=psum[pB][:, q * 128:(q + 1) * 128], lhsT=ki[1][:, cs_], rhs=qd[1][:, cs_], start=True, stop=True),
                            reads=[ki_b[1], qd_b[1]], writes=psum_b[pB], skip_same=True)
                for g in GG:
                    pA, pB, pO, pN = BK[g]
                    k2 = g % 2
                    S.op("dve", lambda e, pA=pA, k2=k2: e.tensor_tensor(out=sc1s[k2][:, :], in0=psum[pA][:, :], in1=MASK[0], op=ALU.mult),
                         reads=psum_b[pA] + [cst_b], writes=[sc1s_b[k2]])
                    S.op("dve", lambda e, pB=pB, k2=k2: e.tensor_tensor(out=sc2s[k2][:, :], in0=psum[pB][:, :], in1=MASK[1], op=ALU.mult),
                         reads=psum_b[pB] + [cst_b], writes=[sc2s_b[k2]])
                for g in GG:
                    pA, pB, pO, pN = BK[g]
                    k2 = g % 2
                    for q in range(4):
                        c = 4 * g + q
                        cs_ = slice(c * 128, (c + 1) * 128)
                        osl = slice(q * 128, (q + 1) * 128)
                        has_f = c >= 1
                        has_b = c <= NT - 2
                        S.op("pe", lambda e, osl=osl, c=c, pO=pO, k2=k2: e.matmul(
                            out=psum[pO][:, osl], lhsT=vtm[:, c, :], rhs=sc1s[k2][:, osl], start=True, stop=False),
                            reads=[vtm_b, sc1s_b[k2]], writes=psum_b[pO], skip_same=True)
                        S.op("pe", lambda e, osl=osl, c=c, pO=pO, k2=k2, has_f=has_f, has_b=has_b: e.matmul(
                            out=psum[pO][:, osl], lhsT=vtm[:, c, :], rhs=sc2s[k2][:, osl], start=False, stop=not (has_f or has_b)),
                            reads=[vtm_b, sc2s_b[k2]], writes=psum_b[pO], skip_same=True)
                        if has_f:
                            S.op("pe", lambda e, osl=osl, c=c, cs_=cs_, has_b=has_b, pO=pO: e.matmul(
                                out=psum[pO][:, osl], lhsT=Sbf[0][:, c, :], rhs=qd[0][:, cs_], start=False, stop=not has_b),
                                reads=[Sbf_b[0], qd_b[0]], writes=psum_b[pO], skip_same=True)
                        if has_b:
                            S.op("pe", lambda e, osl=osl, c=c, cs_=cs_, pO=pO: e.matmul(
                                out=psum[pO][:, osl], lhsT=Sbf[1][:, c, :], rhs=qd[1][:, cs_], start=False, stop=True),
                                reads=[Sbf_b[1], qd_b[1]], writes=psum_b[pO], skip_same=True)
                for g in GG:
                    pA, pB, pO, pN = BK[g]
                    k2 = g % 2
                    S.op("act", lambda e, pO=pO, k2=k2: e.activation(out=sqs[k2][:, :], in_=psum[pO][:, :], func=AF.Square),
                         reads=psum_b[pO], writes=[sqs_b[k2]])
                for g in GG:
                    pA, pB, pO, pN = BK[g]
                    k2 = g % 2
                    S.op("pe", lambda e, pN=pN, k2=k2: e.matmul(out=psum[pN][:, :], lhsT=ONES, rhs=sqs[k2][:, :], start=True, stop=True),
                         reads=[sqs_b[k2], cst_b], writes=psum_b[pN], skip_same=True)
                for g in GG:
                    pA, pB, pO, pN = BK[g]
                    k2 = g % 2
                    S.op("act", lambda e, pN=pN, k2=k2: e.activation(out=rss[k2][:, :], in_=psum[pN][:, :], func=AF.Ln, scale=1.0 / 128, bias=EPS),
                         reads=psum_b[pN], writes=[rss_b[k2]])
                for g in GG:
                    k2 = g % 2
                    S.op("act", lambda e, k2=k2: e.activation(out=rss[k2][:, :], in_=rss[k2][:, :], func=AF.Exp, scale=-0.5),
                         reads=[rss_b[k2]], writes=[rss_b[k2]])
                for g in GG:
                    pA, pB, pO, pN = BK[g]
                    k2 = g % 2
                    S.op("dve", lambda e, pO=pO, k2=k2: e.scalar_tensor_tensor(
                        out=tts[k2][:, :], in0=psum[pO][:, :], scalar=gn[:, 0:1], in1=rss[k2][:, :], op0=ALU.mult, op1=ALU.mult),
                        reads=psum_b[pO] + [gn_b, rss_b[k2]], writes=[tts_b[k2]])
                for g in GG:
                    k2 = g % 2
                    sl = slice(g * 512, (g + 1) * 512)
                    S.op("dve", lambda e, sl=sl, yi=yi, k2=k2: e.tensor_tensor(out=yb[yi][:, sl], in0=tts[k2][:, :], in1=sg[:, sl], op=ALU.mult),
                         reads=[tts_b[k2], sg_b], writes=[yb_b[yi]])
            S.dma("sp", C.y_scr[h * 128:(h + 1) * 128, :], yb[yi][:, :], reads=[yb_b[yi]])


NCST = 1664

def phase_gdn(C):
    nc, S, debug = C.nc, C.S, C.debug
    psum, psum_b = C.psum, C.psum_b
    cst, cst_b = C.cst, C.cst_b
    scr = C.scr
    nheads = debug.get("gdn_heads", [[8], None])[0][0]
    with contextlib.ExitStack() as pes:
        def sb(name, shape, dt):
            return pes.enter_context(nc.sbuf_tensor(name, list(shape), dt))

        def B(name):
            return Buf(name)

        ident_bf, ident_bf_b = C.ident_bf, C.ident_bf_b
        ONES = cst[:, 512:640]
        c2 = sb("c2", [128, NC2], F32)
        c2_b = B("c2")
        S.dma("sp", c2[:, :], C.cst2_d[:, :], writes=[c2_b])
        U = [c2[:, 0:128], c2[:, 128:256]]
        SU = [c2[:, 256:384], c2[:, 384:512]]
        PEN = [c2[:, 512:640], c2[:, 640:768]]
        IDF = c2[:, 768:896]
        NEGONES = c2[:, 896:1024]
        cb = sb("cb", [128, NCB], BF16)
        cb_b = B("cb")
        S.dma("sp", cb[:, :], C.cstb_d[:, :], writes=[cb_b])

        def lmask(d, l):
            o = (d * 7 + (l - 1)) * 128
            return cb[:, o:o + 128]

        cw = sb("cw", [128, 24 * 3], F32)
        cw_b = B("cw")
        S.dma("sp", cw[:, :], C.gdn_cw_d[:, :], writes=[cw_b])
        gn = sb("gn2", [128, 1], F32)
        gn_b = B("gn2")
        S.dma("sp", gn[:, :], C.gdn_norm_g.rearrange("o d -> d o"), writes=[gn_b])
        hp = sb("hp", [16, 4], F32)
        hp_b = B("hp")
        S.dma("sp", hp[:, 0:2], C.gdn_hp_d[:, :], writes=[hp_b])

        c1 = sb("c1", [128, T], F32)
        c1_b = B("c1")
        gb48 = c1
        gb48_b = c1_b
        S.op("pool", lambda e: e.memset(gb48[:, :], 0.0), writes=[gb48_b])
        S.dma("sp", gb48[0:16, :], scr[C_AB:C_AB + 16, :], writes=[gb48_b])
        S.dma("sp", gb48[32:48, :], scr[C_AB + 16:C_AB + 32, :], writes=[gb48_b])
        S.op("act", lambda e: e.activation(out=hp[:, 2:3], in_=hp[:, 0:1], func=AF.Exp), reads=[hp_b], writes=[hp_b])
        S.op("dve", lambda e: e.tensor_scalar(out=hp[:, 2:3], in0=hp[:, 2:3], scalar1=-1.0, scalar2=None, op0=ALU.mult),
             reads=[hp_b], writes=[hp_b])
        S.op("act", lambda e: e.activation(out=gb48[0:16, :], in_=gb48[0:16, :], func=AF.Exp, bias=hp[:, 1:2]),
             reads=[gb48_b, hp_b], writes=[gb48_b])
        S.op("act", lambda e: e.activation(out=gb48[0:16, :], in_=gb48[0:16, :], func=AF.Ln, bias=1.0),
             reads=[gb48_b], writes=[gb48_b])
        S.op("dve", lambda e: e.tensor_scalar(out=gb48[0:16, :], in0=gb48[0:16, :], scalar1=hp[:, 2:3], scalar2=None, op0=ALU.mult),
             reads=[gb48_b, hp_b], writes=[gb48_b])
        S.op("act", lambda e: e.activation(out=gb48[32:48, :], in_=gb48[32:48, :], func=AF.Exp, scale=-1.0),
             reads=[gb48_b], writes=[gb48_b])
        S.op("act", lambda e: e.activation(out=gb48[32:48, :], in_=gb48[32:48, :], func=AF.Ln, bias=1.0),
             reads=[gb48_b], writes=[gb48_b])
        S.op("act", lambda e: e.activation(out=gb48[32:48, :], in_=gb48[32:48, :], func=AF.Exp, scale=-1.0),
             reads=[gb48_b], writes=[gb48_b])
        gtm = sb("gtm", [128, NT, 48], F32)
        gtm_b = B("gtm")
        for g in range(4):
            pi = 2 + g
            for q in range(4):
                t = 4 * g + q
                S.op("pe", lambda e, pi=pi, q=q, t=t: e.transpose(
                    out=psum[pi][:, q * 48:(q + 1) * 48], in_=gb48[0:48, t * 128:(t + 1) * 128], identity=IDF[0:48, 0:48]),
                    reads=[gb48_b, c2_b], writes=[psum_b[pi][0]], skip_same=True)
            S.op("act", lambda e, pi=pi, g=g: e.activation(
                out=gtm[:, 4 * g:4 * g + 4, :], in_=psum[pi][:, 0:192].rearrange("p (a b) -> p a b", a=4), func=AF.Copy),
                reads=[psum_b[pi][0]], writes=[gtm_b])
        egtm = sb("egtm", [128, NT, 16], F32)
        egtm_b = B("egtm")
        ektm = sb("ektm", [128, NT, 16], F32)
        ektm_b = B("ektm")
        for (mats, dst, dst_b, pi) in ((U, egtm, egtm_b, 6), (SU, ektm, ektm_b, 7)):
            for t in range(NT):
                for d in range(2):
                    S.op("pe", lambda e, pi=pi, t=t, d=d, mats=mats: e.matmul(
                        out=psum[pi][:, t * 16 + d * 8:t * 16 + d * 8 + 8], lhsT=mats[d], rhs=gtm[:, t, d * 8:d * 8 + 8],
                        start=True, stop=True),
                        reads=[gtm_b, c2_b], writes=[psum_b[pi][0]], skip_same=True)
            S.op("act", lambda e, pi=pi, dst=dst: e.activation(
                out=dst[:, :, :], in_=psum[pi][:, 0:256].rearrange("p (a b) -> p a b", a=NT), func=AF.Exp),
                reads=[psum_b[pi][0]], writes=[dst_b])

        pin = [sb("pin%d" % i, [128, T + 2], F32) for i in range(1)]
        pin_b = [B("pin") for i in range(1)]
        for i in range(1):
            S.op("pool", lambda e, i=i: e.memset(pin[i][:, :], 0.0), writes=[pin_b[i]])
        tB = sb("tB", [128, T], F32)
        tB_b = B("tB")
        qT = sb("qT", [128, T], BF16)
        qT_b = B("qT")
        kT = sb("kT", [128, T], BF16)
        kT_b = B("kT")
        vT = sb("vT", [128, T], BF16)
        vT_b = B("vT")
        szs = [sb("sz%d" % i, [128, T], BF16) for i in range(2)]
        szs_b = [B("sz") for i in range(2)]
        ktm = sb("ktm", [128, NT, 128], BF16)
        ktm_b = B("ktm")
        vtms = [sb("vtm2_%d" % i, [128, NT, 128], BF16) for i in range(2)]
        vtms_b = [B("vtm2") for i in range(2)]
        rsi = sb("rsi", [128, 512], F32)
        rsi_b = B("rsi")
        kg = [sb("kg%d" % d, [128, NT, 128], BF16) for d in range(2)]
        kg_b = [B("kg") for d in range(2)]
        ktl = [sb("ktl%d" % d, [128, NT, 128], BF16) for d in range(2)]
        ktl_b = [B("ktl") for d in range(2)]
        qdT = [sb("qdT%d" % d, [128, T], BF16) for d in range(2)]
        qdT_b = [B("qdT") for d in range(2)]
        VT = [sb("VT%d" % d, [128, T], BF16) for d in range(2)]
        VT_b = [B("VT") for d in range(2)]
        atT = [sb("atT%d" % d, [128, T], BF16) for d in range(2)]
        atT_b = [B("atT") for d in range(2)]
        wpn = [sb("wpn%d" % d, [128, T], BF16) for d in range(2)]
        wpn_b = [B("wpn") for d in range(2)]
        dcl = sb("dcl2", [128, 2, NT], F32)
        dcl_b = [B("dcl20"), B("dcl21")]
        gU = sb("gU", [128, NT, 128], F32)
        gU_b = B("gU")
        DTs = [sb("DT%d" % k, [128, 512], F32) for k in range(4)]
        DTs_b = [B("DT") for k in range(4)]
        ebs = [sb("eb%d" % k, [128, 512], F32) for k in range(4)]
        ebs_b = [B("eb") for k in range(4)]
        NTs = [sb("NT%d" % k, [128, 512], BF16) for k in range(4)]
        NTs_b = [B("NT") for k in range(4)]
        Xs = [sb("Xs%d" % k, [128, 512], BF16) for k in range(4)]
        Xs_b = [B("Xs") for k in range(4)]
        Ys = [sb("Ys%d" % k, [128, 512], BF16) for k in range(4)]
        Ys_b = [B("Ys") for k in range(4)]
        Ps = [sb("Ps%d" % k, [128, 512], BF16) for k in range(4)]
        Ps_b = [B("Ps") for k in range(4)]
        S32 = [sb("S32_%d" % d, [128, 128], F32) for d in range(2)]
        S32_b = [B("S32") for d in range(2)]
        Sbf = [sb("Sbf2_%d" % d, [128, 128], BF16) for d in range(2)]
        Sbf_b = [B("Sbf2") for d in range(2)]
        vnb = [sb("vnb%d" % d, [128, 128], BF16) for d in range(2)]
        vnb_b = [B("vnb") for d in range(2)]
        oacc = [sb("oacc%d" % d, [128, T], F32) for d in range(2)]
        oacc_b = [B("oacc") for d in range(2)]
        sqs2 = [sb("sq2_%d" % i, [128, 512], F32) for i in range(2)]
        sqs2_b = [B("sq2") for i in range(2)]
        rss2 = [sb("rs2_%d" % i, [128, 512], F32) for i in range(2)]
        rss2_b = [B("rs2") for i in range(2)]
        tts2 = [sb("tt2_%d" % i, [128, 512], F32) for i in range(2)]
        tts2_b = [B("tt2") for i in range(2)]
        yb = [sb("yb2_%d" % i, [128, T], BF16) for i in range(1)] * 2
        yb_b = [B("yb2")] * 2

        pc = [0]

        def bank():
            pi = pc[0] % 8
            pc[0] += 1
            return pi

        pinc = [0]
        ipc = [0]

        def ibank():
            pi = 6 + (ipc[0] % 2)
            ipc[0] += 1
            return pi

        def silu_inplace(x, x_b, n=T):
            for hs in (slice(0, n // 2), slice(n // 2, n)):
                S.op("act", lambda e, hs=hs: e.activation(out=tB[:, hs], in_=x[:, hs], func=AF.Exp, scale=-1.0), reads=[x_b], writes=[tB_b])
                S.op("act", lambda e, hs=hs: e.activation(out=tB[:, hs], in_=tB[:, hs], func=AF.Ln, bias=1.0), reads=[tB_b], writes=[tB_b])
                S.op("act", lambda e, hs=hs: e.activation(out=tB[:, hs], in_=tB[:, hs], func=AF.Exp, scale=-1.0), reads=[tB_b], writes=[tB_b])
                S.op("dve", lambda e, hs=hs: e.tensor_tensor(out=x[:, hs], in0=x[:, hs], in1=tB[:, hs], op=ALU.mult),
                     reads=[x_b, tB_b], writes=[x_b])

        def conv_silu(row0, blk):
            i = 0
            S.dma("sp", pin[i][:, 1:T + 1], scr[row0:row0 + 128, :], writes=[pin_b[i]])
            w0 = cw[:, blk * 3:blk * 3 + 1]
            w1 = cw[:, blk * 3 + 1:blk * 3 + 2]
            w2 = cw[:, blk * 3 + 2:blk * 3 + 3]
            S.op("act", lambda e, i=i, w0=w0: e.activation(out=c1[:, :], in_=pin[i][:, 0:T], func=AF.Copy, scale=w0),
                 reads=[pin_b[i], cw_b], writes=[c1_b])
            S.op("dve", lambda e, i=i, w1=w1: e.scalar_tensor_tensor(
                out=c1[:, :], in0=pin[i][:, 1:T + 1], scalar=w1, in1=c1[:, :], op0=ALU.mult, op1=ALU.add),
                reads=[pin_b[i], cw_b, c1_b], writes=[c1_b])
            S.op("dve", lambda e, i=i, w2=w2: e.scalar_tensor_tensor(
                out=c1[:, :], in0=pin[i][:, 2:T + 2], scalar=w2, in1=c1[:, :], op0=ALU.mult, op1=ALU.add),
                reads=[pin_b[i], cw_b, c1_b], writes=[c1_b])
            silu_inplace(c1, c1_b)

        def l2norm_to(dst, dst_b, scale):
            for hs in (slice(0, T // 2), slice(T // 2, T)):
                S.op("act", lambda e, hs=hs: e.activation(out=tB[:, hs], in_=c1[:, hs], func=AF.Square), reads=[c1_b], writes=[tB_b])
            for blk in range(4):
                sl = slice(blk * 512, (blk + 1) * 512)
                pi = ibank()
                S.op("pe", lambda e, pi=pi, sl=sl: e.matmul(out=psum[pi][:, :], lhsT=ONES, rhs=tB[:, sl], start=True, stop=True),
                     reads=[tB_b, cst_b], writes=psum_b[pi], skip_same=True)
                S.op("act", lambda e, pi=pi: e.activation(out=rsi[:, :], in_=psum[pi][:, :], func=AF.Ln, bias=EPS),
                     reads=psum_b[pi], writes=[rsi_b])
                S.op("act", lambda e: e.activation(out=rsi[:, :], in_=rsi[:, :], func=AF.Exp, scale=-0.5), reads=[rsi_b], writes=[rsi_b])
                S.op("dve", lambda e, sl=sl: e.scalar_tensor_tensor(
                    out=dst[:, sl], in0=c1[:, sl], scalar=float(scale), in1=rsi[:, :], op0=ALU.mult, op1=ALU.mult),
                    reads=[c1_b, rsi_b], writes=[dst_b])

        def to_token_major(src, src_b, dst, dst_b):
            for g in range(4):
                pi = ibank()
                pt = psum[pi].bitcast(BF16)
                for q in range(4):
                    c = 4 * g + q
                    S.op("pe", lambda e, pt=pt, q=q, c=c: e.transpose(
                        out=pt[:, q * 128:(q + 1) * 128], in_=src[:, c * 128:(c + 1) * 128], identity=ident_bf[:, :]),
                        reads=[src_b, ident_bf_b], writes=[psum_b[pi][q]], skip_same=True)
                S.op("act", lambda e, pt=pt, g=g: e.activation(
                    out=dst[:, 4 * g:4 * g + 4, :], in_=pt[:, 0:512].rearrange("p (a b) -> p a b", a=4), func=AF.Copy),
                    reads=psum_b[pi], writes=[dst_b])

        def bc4(ap2d):
            return ap2d.unsqueeze(1).broadcast_to([128, 4, 128])

        def r4(ap):
            return ap.rearrange("p (a b) -> p a b", a=4)

        def emit_inputs(h):
            sz, sz_b = szs[h % 2], szs_b[h % 2]
            vtm, vtm_b = vtms[h % 2], vtms_b[h % 2]
            conv_silu(C_DQ + h * 128, h)
            l2norm_to(qT, qT_b, 128 ** -0.5)
            conv_silu(C_DK + h * 128, 8 + h)
            l2norm_to(kT, kT_b, 1.0)
            conv_silu(C_DV + h * 128, 16 + h)
            for hs in (slice(0, T // 2), slice(T // 2, T)):
                S.op("act", lambda e, hs=hs: e.activation(out=vT[:, hs], in_=c1[:, hs], func=AF.Copy), reads=[c1_b], writes=[vT_b])
            to_token_major(kT, kT_b, ktm, ktm_b)
            to_token_major(vT, vT_b, vtm, vtm_b)
            S.dma("sp", c1[:, :], scr[C_DZ + h * 128:C_DZ + (h + 1) * 128, :], writes=[c1_b])
            silu_inplace(c1, c1_b)
            for hs in (slice(0, T // 2), slice(T // 2, T)):
                S.op("act", lambda e, hs=hs: e.activation(out=sz[:, hs], in_=c1[:, hs], func=AF.Copy), reads=[c1_b], writes=[sz_b])
            if "gdn_qkv" in C.dbg_out and h == 0:
                S.dma("sp", C.dbg_out["gdn_qkv"][0:128, :], qT[:, :], reads=[qT_b])
                S.dma("sp", C.dbg_out["gdn_qkv"][128:256, :], kT[:, :], reads=[kT_b])
                S.dma("sp", C.dbg_out["gdn_qkv"][256:384, :], vT[:, :], reads=[vT_b])

        emit_inputs(0)
        for h in range(nheads):
            sz, sz_b = szs[h % 2], szs_b[h % 2]
            vtm, vtm_b = vtms[h % 2], vtms_b[h % 2]

            for d in range(2):
                col = d * 8 + h
                gcol = gtm[:, :, col:col + 1]
                bcol = gtm[:, :, 32 + col:32 + col + 1]
                S.op("dve", lambda e, d=d, col=col: e.tensor_tensor(
                    out=kg[d][:, :, :], in0=ktm[:, :, :], in1=egtm[:, :, col:col + 1].broadcast_to([128, NT, 128]), op=ALU.mult),
                    reads=[ktm_b, egtm_b], writes=[kg_b[d]])
                S.op("dve", lambda e, d=d, col=col: e.tensor_tensor(
                    out=ktl[d][:, :, :], in0=ktm[:, :, :], in1=ektm[:, :, col:col + 1].broadcast_to([128, NT, 128]), op=ALU.mult),
                    reads=[ktm_b, ektm_b], writes=[ktl_b[d]])
                S.op("pool", lambda e, d=d, gcol=gcol: e.tensor_tensor(
                    out=gU[:, :, :], in0=U[d].unsqueeze(1).broadcast_to([128, NT, 128]), in1=gcol.broadcast_to([128, NT, 128]), op=ALU.mult),
                    reads=[c2_b, gtm_b], writes=[gU_b])
                last = 127 if d == 0 else 0
                KB = 4
                G = list(range(4))

                def qsl(q):
                    return slice(q * 128, (q + 1) * 128)

                pA = [bank() for k in G]
                for k in G:
                    for q in range(4):
                        c = 4 * k + q
                        S.op("pe", lambda e, p=pA[k], q=q, c=c: e.matmul(
                            out=psum[p][:, qsl(q)], lhsT=ONES, rhs=gU[:, c, :], start=True, stop=False),
                            reads=[gU_b, cst_b], writes=psum_b[pA[k]], skip_same=True)
                        S.op("pe", lambda e, p=pA[k], q=q, c=c: e.matmul(
                            out=psum[p][:, qsl(q)], lhsT=gU[:, c, :], rhs=NEGONES, start=False, stop=False),
                            reads=[gU_b, c2_b], writes=psum_b[pA[k]], skip_same=True)
                        S.op("pe", lambda e, p=pA[k], q=q, d=d: e.matmul(
                            out=psum[p][:, qsl(q)], lhsT=IDF, rhs=PEN[d], start=False, stop=True),
                            reads=[c2_b], writes=psum_b[pA[k]], skip_same=True)
                for k in G:
                    S.op("act", lambda e, p=pA[k], k=k: e.activation(out=DTs[k][:, :], in_=psum[p][:, :], func=AF.Exp),
                         reads=psum_b[pA[k]], writes=[DTs_b[k]])
                pB = [bank() for k in G]
                for k in G:
                    for q in range(4):
                        c = 4 * k + q
                        S.op("pe", lambda e, p=pB[k], q=q, c=c: e.matmul(
                            out=psum[p][:, qsl(q)], lhsT=NEGONES, rhs=gU[:, c, :], start=True, stop=True),
                            reads=[gU_b, c2_b], writes=psum_b[pB[k]], skip_same=True)
                for k in G:
                    S.op("act", lambda e, p=pB[k], k=k: e.activation(out=ebs[k][:, :], in_=psum[p][:, :], func=AF.Exp, scale=-1.0),
                         reads=psum_b[pB[k]], writes=[ebs_b[k]])
                    S.op("act", lambda e, p=pB[k], k=k, d=d, last=last: e.activation(
                        out=dcl[:, d, 4 * k:4 * k + 4], in_=r4(psum[p][:, :])[:, :, last], func=AF.Exp, scale=-1.0),
                        reads=psum_b[pB[k]], writes=[dcl_b[d]])
                for k in G:
                    sl = slice(k * 512, (k + 1) * 512)
                    S.op("dve", lambda e, d=d, sl=sl, k=k: e.tensor_tensor(out=qdT[d][:, sl], in0=qT[:, sl], in1=ebs[k][:, :], op=ALU.mult),
                         reads=[qT_b, ebs_b[k]], writes=[qdT_b[d]])
                pD = [bank() for k in G]
                for k in G:
                    for q in range(4):
                        cs_ = slice((4 * k + q) * 128, (4 * k + q + 1) * 128)
                        S.op("pe", lambda e, p=pD[k], q=q, cs_=cs_: e.matmul(
                            out=psum[p][:, qsl(q)], lhsT=kT[:, cs_], rhs=qT[:, cs_], start=True, stop=True),
                            reads=[kT_b, qT_b], writes=psum_b[pD[k]], skip_same=True)
                for k in G:
                    sl = slice(k * 512, (k + 1) * 512)
                    S.op("dve", lambda e, p=pD[k], d=d, sl=sl, k=k: e.tensor_tensor(out=atT[d][:, sl], in0=psum[p][:, :], in1=DTs[k][:, :], op=ALU.mult),
                         reads=psum_b[pD[k]] + [DTs_b[k]], writes=[atT_b[d]])
                for k in G:
                    S.op("dve", lambda e, k=k, bcol=bcol: e.tensor_tensor(
                        out=r4(DTs[k][:, :]), in0=r4(DTs[k][:, :]), in1=bcol[:, 4 * k:4 * k + 4, :].broadcast_to([128, 4, 128]), op=ALU.mult),
                        reads=[DTs_b[k], gtm_b], writes=[DTs_b[k]])
                pC = [bank() for k in G]
                for k in G:
                    for q in range(4):
                        cs_ = slice((4 * k + q) * 128, (4 * k + q + 1) * 128)
                        S.op("pe", lambda e, p=pC[k], q=q, cs_=cs_: e.matmul(
                            out=psum[p][:, qsl(q)], lhsT=kT[:, cs_], rhs=kT[:, cs_], start=True, stop=True),
                            reads=[kT_b], writes=psum_b[pC[k]], skip_same=True)
                for k in G:
                    S.op("dve", lambda e, p=pC[k], k=k: e.tensor_tensor(out=NTs[k][:, :], in0=psum[p][:, :], in1=DTs[k][:, :], op=ALU.mult),
                         reads=psum_b[pC[k]] + [DTs_b[k]], writes=[NTs_b[k]])
                for l in range(1, 8):
                    def Xop(k, q, l=l):
                        return ident_bf[:, :] if l == 1 else Xs[k][:, qsl(q)]

                    def Yop(k, q, l=l):
                        return ident_bf[:, :] if l == 1 else Ys[k][:, qsl(q)]

                    pP = [bank() for k in G]
                    for k in G:
                        for q in range(4):
                            S.op("pe", lambda e, xo=Xop(k, q), yo=Yop(k, q), p=pP[k], q=q, k=k: e.matmul(out=psum[p][:, qsl(q)], lhsT=NTs[k][:, qsl(q)], rhs=xo, start=True, stop=True),
                                 reads=[NTs_b[k], Xs_b[k]], writes=psum_b[pP[k]], skip_same=True)
                    for k in G:
                        S.op("dve", lambda e, p=pP[k], k=k, d=d, l=l: e.tensor_tensor(
                            out=r4(Ps[k][:, :]), in0=r4(psum[p][:, :]), in1=bc4(lmask(d, l)), op=ALU.mult),
                            reads=psum_b[pP[k]] + [cb_b], writes=[Ps_b[k]])
                    if l < 7:
                        pX = [bank() for k in G]
                        for k in G:
                            for q in range(4):
                                S.op("pe", lambda e, xo=Xop(k, q), yo=Yop(k, q), p=pX[k], q=q, k=k: e.matmul(out=psum[p][:, qsl(q)], lhsT=ident_bf[:, :], rhs=xo, start=True, stop=False),
                                     reads=[ident_bf_b, Xs_b[k]], writes=psum_b[pX[k]], skip_same=True)
                                S.op("pe", lambda e, xo=Xop(k, q), yo=Yop(k, q), p=pX[k], q=q, k=k: e.matmul(out=psum[p][:, qsl(q)], lhsT=yo, rhs=Ps[k][:, qsl(q)], start=False, stop=True),
                                     reads=[Ys_b[k], Ps_b[k]], writes=psum_b[pX[k]], skip_same=True)
                    pY = [bank() for k in G]
                    for k in G:
                        via_act = (k >= 2)
                        for q in range(4):
                            if via_act:
                                S.op("pe", lambda e, xo=Xop(k, q), yo=Yop(k, q), p=pY[k], q=q, k=k: e.matmul(out=psum[p][:, qsl(q)], lhsT=ident_bf[:, :], rhs=yo, start=True, stop=False),
                                     reads=[ident_bf_b, Ys_b[k]], writes=psum_b[pY[k]], skip_same=True)
                            S.op("pe", lambda e, xo=Xop(k, q), yo=Yop(k, q), p=pY[k], q=q, k=k, via_act=via_act: e.matmul(out=psum[p][:, qsl(q)], lhsT=Ps[k][:, qsl(q)], rhs=yo, start=not via_act, stop=True),
                                 reads=[Ps_b[k], Ys_b[k]], writes=psum_b[pY[k]], skip_same=True)
                    if l < 7:
                        for k in G:
                            S.op("act", lambda e, p=pX[k], k=k: e.activation(out=Xs[k][:, :], in_=psum[p][:, :], func=AF.Copy),
                                 reads=psum_b[pX[k]], writes=[Xs_b[k]])
                    for k in G:
                        sl = slice(k * 512, (k + 1) * 512)
                        dst = Ys[k][:, :] if l < 7 else VT[d][:, sl]
                        dst_b = Ys_b[k] if l < 7 else VT_b[d]
                        if k >= 2:
                            S.op("act", lambda e, p=pY[k], dst=dst: e.activation(out=dst, in_=psum[p][:, :], func=AF.Copy),
                                 reads=psum_b[pY[k]], writes=[dst_b])
                        else:
                            yin = bc4(ident_bf[:, :]) if l == 1 else r4(Ys[k][:, :])
                            S.op("dve", lambda e, p=pY[k], dst=dst, yin=yin: e.tensor_tensor(out=r4(dst), in0=r4(psum[p][:, :]), in1=yin, op=ALU.add),
                                 reads=psum_b[pY[k]] + [Ys_b[k], ident_bf_b], writes=[dst_b])
                pH = [bank() for k in G]
                for k in G:
                    for q in range(4):
                        c = 4 * k + q
                        cs_ = slice(c * 128, (c + 1) * 128)
                        S.op("pe", lambda e, p=pH[k], q=q, c=c, cs_=cs_, d=d: e.matmul(
                            out=psum[p][:, qsl(q)], lhsT=kg[d][:, c, :], rhs=VT[d][:, cs_], start=True, stop=True),
                            reads=[kg_b[d], VT_b[d]], writes=psum_b[pH[k]], skip_same=True)
                for k in G:
                    sl = slice(k * 512, (k + 1) * 512)
                    S.op("act", lambda e, p=pH[k], d=d, sl=sl: e.activation(out=wpn[d][:, sl], in_=psum[p][:, :], func=AF.Copy, scale=-1.0),
                         reads=psum_b[pH[k]], writes=[wpn_b[d]])

            for d in range(2):
                S.op("pool", lambda e, d=d: e.memset(S32[d][:, :], 0.0), writes=[S32_b[d]])
                S.op("pool", lambda e, d=d: e.memset(Sbf[d][:, :], 0.0), writes=[Sbf_b[d]])
            pend = []
            if h + 1 < nheads:
                S.rec_begin()
                emit_inputs(h + 1)
                pend = S.rec_end()
            per_step = (len(pend) + 2 * NT - 1) // (2 * NT) if pend else 0
            for s in range(NT):
                for d in range(2):
                    c = s if d == 0 else NT - 1 - s
                    cs_ = slice(c * 128, (c + 1) * 128)
                    col = d * 8 + h
                    pv, pS, pO = 3 * d, 3 * d + 1, 3 * d + 2
                    S.op("pe", lambda e, pv=pv, d=d, c=c, cs_=cs_, vtm=vtm: e.matmul(
                        out=psum[pv][:, 0:128], lhsT=VT[d][:, cs_], rhs=vtm[:, c, :], start=True, stop=False),
                        reads=[VT_b[d], vtm_b], writes=[psum_b[pv][0]], skip_same=True)
                    S.op("pe", lambda e, pv=pv, d=d, cs_=cs_: e.matmul(
                        out=psum[pv][:, 0:128], lhsT=wpn[d][:, cs_], rhs=Sbf[d][:, :], start=False, stop=True),
                        reads=[wpn_b[d], Sbf_b[d]], writes=[psum_b[pv][0]], skip_same=True)
                    S.op("act", lambda e, pv=pv, d=d, c=c, col=col: e.activation(
                        out=vnb[d][:, :], in_=psum[pv][:, 0:128], func=AF.Copy, scale=gtm[:, c, 32 + col:32 + col + 1]),
                        reads=[psum_b[pv][0], gtm_b], writes=[vnb_b[d]])
                    S.op("pe", lambda e, pS=pS, d=d, c=c: e.matmul(
                        out=psum[pS][:, 0:128], lhsT=ktl[d][:, c, :], rhs=vnb[d][:, :], start=True, stop=True),
                        reads=[ktl_b[d], vnb_b[d]], writes=[psum_b[pS][1]], skip_same=True)
                    S.op("pe", lambda e, pO=pO, d=d, cs_=cs_: e.matmul(
                        out=psum[pO][:, 0:128], lhsT=Sbf[d][:, :], rhs=qdT[d][:, cs_], start=True, stop=False),
                        reads=[Sbf_b[d], qdT_b[d]], writes=[psum_b[pO][2]], skip_same=True)
                    S.op("pe", lambda e, pO=pO, d=d, cs_=cs_: e.matmul(
                        out=psum[pO][:, 0:128], lhsT=vnb[d][:, :], rhs=atT[d][:, cs_], start=False, stop=True),
                        reads=[vnb_b[d], atT_b[d]], writes=[psum_b[pO][2]], skip_same=True)
                    S.op("dve", lambda e, pS=pS, d=d, c=c: e.scalar_tensor_tensor(
                        out=S32[d][:, :], in0=S32[d][:, :], scalar=dcl[:, d, c:c + 1], in1=psum[pS][:, 0:128],
                        op0=ALU.mult, op1=ALU.add),
                        reads=[S32_b[d], dcl_b[d], psum_b[pS][1]], writes=[S32_b[d]])
                    S.op("pool", lambda e, d=d: e.tensor_copy(out=Sbf[d][:, :], in_=S32[d][:, :]),
                         reads=[S32_b[d]], writes=[Sbf_b[d]])
                    S.op("dve", lambda e, pO=pO, d=d, cs_=cs_: e.tensor_copy(out=oacc[d][:, cs_], in_=psum[pO][:, 0:128]),
                         reads=[psum_b[pO][2]], writes=[oacc_b[d]])
                    S.play(pend, per_step)
            S.play(pend)

            yi = h % 2
            S.op("dve", lambda e: e.tensor_tensor(out=oacc[0][:, :], in0=oacc[0][:, :], in1=oacc[1][:, :], op=ALU.add),
                 reads=[oacc_b[0], oacc_b[1]], writes=[oacc_b[0]])
            if "gdn_o" in C.dbg_out and h == 0:
                S.dma("sp", C.dbg_out["gdn_o"][:, :], oacc[0][:, :], reads=[oacc_b[0]])
            for bp in range(2):
                BL = [2 * bp, 2 * bp + 1]
                pNs = {blk: bank() for blk in BL}
                for blk in BL:
                    sl = slice(blk * 512, (blk + 1) * 512)
                    k2 = blk % 2
                    S.op("act", lambda e, sl=sl, k2=k2: e.activation(out=sqs2[k2][:, :], in_=oacc[0][:, sl], func=AF.Square),
                         reads=[oacc_b[0]], writes=[sqs2_b[k2]])
                for blk in BL:
                    k2 = blk % 2
                    S.op("pe", lambda e, pN=pNs[blk], k2=k2: e.matmul(out=psum[pN][:, :], lhsT=ONES, rhs=sqs2[k2][:, :], start=True, stop=True),
                         reads=[sqs2_b[k2], cst_b], writes=psum_b[pNs[blk]], skip_same=True)
                for blk in BL:
                    k2 = blk % 2
                    S.op("act", lambda e, pN=pNs[blk], k2=k2: e.activation(out=rss2[k2][:, :], in_=psum[pN][:, :], func=AF.Ln, scale=1.0 / 128, bias=EPS),
                         reads=psum_b[pNs[blk]], writes=[rss2_b[k2]])
                for blk in BL:
                    k2 = blk % 2
                    S.op("act", lambda e, k2=k2: e.activation(out=rss2[k2][:, :], in_=rss2[k2][:, :], func=AF.Exp, scale=-0.5),
                         reads=[rss2_b[k2]], writes=[rss2_b[k2]])
                for blk in BL:
                    sl = slice(blk * 512, (blk + 1) * 512)
                    k2 = blk % 2
                    S.op("dve", lambda e, sl=sl, k2=k2: e.scalar_tensor_tensor(
                        out=tts2[k2][:, :], in0=oacc[0][:, sl], scalar=gn[:, 0:1], in1=rss2[k2][:, :], op0=ALU.mult, op1=ALU.mult),
                        reads=[oacc_b[0], gn_b, rss2_b[k2]], writes=[tts2_b[k2]])
                for blk in BL:
                    sl = slice(blk * 512, (blk + 1) * 512)
                    k2 = blk % 2
                    S.op("dve", lambda e, sl=sl, yi=yi, sz=sz, k2=k2: e.tensor_tensor(out=yb[yi][:, sl], in0=tts2[k2][:, :], in1=sz[:, sl], op=ALU.mult),
                         reads=[tts2_b[k2], sz_b], writes=[yb_b[yi]])
            S.dma("sp", C.y_scr[1024 + h * 128:1024 + (h + 1) * 128, :], yb[yi][:, :], reads=[yb_b[yi]])


def phase_branch(C):
    nc, S, debug = C.nc, C.S, C.debug
    psum, psum_b = C.psum, C.psum_b
    scr = C.scr
    ident_bf, ident_bf_b = C.ident_bf, C.ident_bf_b
    with contextlib.ExitStack() as oes:
        mergedT = oes.enter_context(nc.sbuf_tensor("mergedT", [128, KC, T], BF16))
        mg_b = [Buf("mg%d" % t) for t in range(NT)]
        with contextlib.ExitStack() as pes:
            def sb(name, shape, dt):
                return pes.enter_context(nc.sbuf_tensor(name, list(shape), dt))
            yT = sb("yT", [128, KC, T], BF16)
            yT_b = [Buf("yT%d" % k) for k in range(KC)]
            y_r = C.y_scr.rearrange("(kc p) t -> p kc t", p=128)
            for k in range(KC):
                S.dma("sp", yT[:, k, :], y_r[:, k, :], writes=[yT_b[k]])
            NWB = 3
            wg = [sb("wbg%d" % i, [128, 8, 128], BF16) for i in range(NWB)]
            wd = [sb("wbd%d" % i, [128, 8, 128], BF16) for i in range(NWB)]
            wg_b = [Buf("wbg") for i in range(NWB)]
            wd_b = [Buf("wbd") for i in range(NWB)]
            gg = [sb("gg%d" % i, [128, T], F32) for i in range(2)]
            gd = [sb("gd%d" % i, [128, T], F32) for i in range(2)]
            gg_b = [Buf("gg") for i in range(2)]
            gd_b = [Buf("gd") for i in range(2)]
            sgg = [sb("sgg%d" % i, [128, 512], F32) for i in range(2)]
            sgd = [sb("sgd%d" % i, [128, 512], F32) for i in range(2)]
            sgg_b = [Buf("sgg") for i in range(2)]
            sgd_b = [Buf("sgd") for i in range(2)]
            t1 = [sb("t1_%d" % i, [128, 512], F32) for i in range(2)]
            t2 = [sb("t2_%d" % i, [128, 512], F32) for i in range(2)]
            t1_b = [Buf("t1") for i in range(2)]
            t2_b = [Buf("t2") for i in range(2)]
            wbg_r = C.w_branch_gla.rearrange("(kc p) n -> p kc n", p=128)
            wbd_r = C.w_branch_gdn.rearrange("(kc p) n -> p kc n", p=128)
            cnt = 0
            for db in range(KC):
                wi = db % NWB
                gi = db % 2
                S.dma("pool", wg[wi][:, :, :], wbg_r[:, :, db * 128:(db + 1) * 128], writes=[wg_b[wi]])
                S.dma("pool", wd[wi][:, :, :], wbd_r[:, :, db * 128:(db + 1) * 128], writes=[wd_b[wi]])
                S.dma("sp", gg[gi][:, :], scr[C_BG + db * 128:C_BG + (db + 1) * 128, :], writes=[gg_b[gi]])
                S.dma("sp", gd[gi][:, :], scr[C_BD + db * 128:C_BD + (db + 1) * 128, :], writes=[gd_b[gi]])
                for blk in range(4):
                    sl = slice(blk * 512, (blk + 1) * 512)
                    pG = (2 * cnt) % 8
                    pD = (2 * cnt + 1) % 8
                    j = cnt % 2
                    cnt += 1
                    for kc in range(8):
                        S.op("pe", lambda e, pG=pG, wi=wi, kc=kc, sl=sl: e.matmul(
                            out=psum[pG][:, :], lhsT=wg[wi][:, kc, :], rhs=yT[:, kc, sl], start=(kc == 0), stop=(kc == 7)),
                            reads=[wg_b[wi], yT_b[kc]], writes=psum_b[pG], skip_same=True)
                    for kc in range(8):
                        S.op("pe", lambda e, pD=pD, wi=wi, kc=kc, sl=sl: e.matmul(
                            out=psum[pD][:, :], lhsT=wd[wi][:, kc, :], rhs=yT[:, 8 + kc, sl], start=(kc == 0), stop=(kc == 7)),
                            reads=[wd_b[wi], yT_b[8 + kc]], writes=psum_b[pD], skip_same=True)
                    S.op("act", lambda e, j=j, gi=gi, sl=sl: e.activation(out=sgg[j][:, :], in_=gg[gi][:, sl], func=AF.Sigmoid),
                         reads=[gg_b[gi]], writes=[sgg_b[j]])
                    S.op("act", lambda e, j=j, gi=gi, sl=sl: e.activation(out=sgd[j][:, :], in_=gd[gi][:, sl], func=AF.Sigmoid),
                         reads=[gd_b[gi]], writes=[sgd_b[j]])
                    S.op("dve", lambda e, j=j, pG=pG: e.tensor_tensor(out=t1[j][:, :], in0=psum[pG][:, :], in1=sgg[j][:, :], op=ALU.mult),
                         reads=psum_b[pG] + [sgg_b[j]], writes=[t1_b[j]])
                    S.op("dve", lambda e, j=j, pD=pD: e.tensor_tensor(out=t2[j][:, :], in0=psum[pD][:, :], in1=sgd[j][:, :], op=ALU.mult),
                         reads=psum_b[pD] + [sgd_b[j]], writes=[t2_b[j]])
                    S.op("dve", lambda e, j=j, db=db, sl=sl: e.tensor_tensor(out=mergedT[:, db, sl], in0=t1[j][:, :], in1=t2[j][:, :], op=ALU.add),
                         reads=[t1_b[j], t2_b[j]], writes=mg_b[4 * blk:4 * blk + 4])
        S.barrier()
        if "merged" in C.dbg_out:
            S.dma("sp", C.dbg_out["merged"].rearrange("(kc p) t -> p kc t", p=128), mergedT[:, :, :], reads=mg_b)
        with contextlib.ExitStack() as pes:
            def sb(name, shape, dt):
                return pes.enter_context(nc.sbuf_tensor(name, list(shape), dt))
            Wout = sb("Wout", [128, KC, D], BF16)
            Wout_b = [Buf("Wout%d" % k) for k in range(KC)]
            wo_r = C.w_out.rearrange("(kc p) n -> p kc n", p=128)
            for k in range(KC):
                S.dma("pool", Wout[:, k, :], wo_r[:, k, :], writes=[Wout_b[k]], max_dma_last_dim=4096)
            g2b = sb("g2b", [128, D], F32)
            g2b_b = Buf("g2b")
            S.dma("sp", g2b[:, :], C.norm2_g.partition_broadcast(128), writes=[g2b_b])
            xt = [sb("xt2_%d" % i, [128, D], F32) for i in range(2)]
            xt_b = [Buf("xt2") for i in range(2)]
            x1t = [sb("x1t%d" % i, [128, D], F32) for i in range(2)]
            x1t_b = [Buf("x1t") for i in range(2)]
            junk = sb("junk2", [128, D], BF16)
            junk_b = Buf("junk2")
            hb = [sb("hb2_%d" % i, [128, D], BF16) for i in range(2)]
            hb_b = [Buf("hb2") for i in range(2)]
            hst = [sb("hst%d" % i, [128, KC, 128], BF16) for i in range(2)]
            hst_b = [Buf("hst") for i in range(2)]
            stat = sb("stat2", [128, 4 * NT], F32)
            stat_b = [Buf("stat2") for i in range(NT)]
            h2_r = C.h2_scr.rearrange("(kc p) t -> p kc t", p=128)
            pcnt = 0
            pend_t = []
            for t in range(NT):
                i = t % 2
                ts_ = slice(t * 128, (t + 1) * 128)
                S.dma("sp", xt[i][:, :], C.x[ts_, :], writes=[xt_b[i]])
                for cb_ in range(4):
                    pi = cb_ + 4 * (t % 2)
                    csl = slice(cb_ * 512, (cb_ + 1) * 512)
                    for kc in range(KC):
                        S.op("pe", lambda e, pi=pi, kc=kc, ts_=ts_, csl=csl: e.matmul(
                            out=psum[pi][:, :], lhsT=mergedT[:, kc, ts_], rhs=Wout[:, kc, csl], start=(kc == 0), stop=(kc == KC - 1)),
                            reads=[mg_b[t], Wout_b[kc]], writes=psum_b[pi], skip_same=True)
                    S.op("dve", lambda e, pi=pi, i=i, csl=csl: e.tensor_tensor(out=x1t[i][:, csl], in0=psum[pi][:, :], in1=xt[i][:, csl], op=ALU.add),
                         reads=psum_b[pi] + [xt_b[i]], writes=[x1t_b[i]])
                S.dma("pool", C.x1_scr[ts_, :], x1t[i][:, :], reads=[x1t_b[i]])
                ss = stat[:, 4 * t:4 * t + 1]
                lnv = stat[:, 4 * t + 1:4 * t + 2]
                rstd = stat[:, 4 * t + 2:4 * t + 3]
                S.op("act", lambda e, i=i, ss=ss: e.activation(out=junk[:, :], in_=x1t[i][:, :], func=AF.Square, accum_out=ss),
                     reads=[x1t_b[i]], writes=[junk_b, stat_b[t]])
                S.op("act", lambda e, ss=ss, lnv=lnv: e.activation(out=lnv, in_=ss, func=AF.Ln, scale=1.0 / D, bias=EPS),
                     reads=[stat_b[t]], writes=[stat_b[t]])
                S.op("act", lambda e, rstd=rstd, lnv=lnv: e.activation(out=rstd, in_=lnv, func=AF.Exp, scale=-0.5),
                     reads=[stat_b[t]], writes=[stat_b[t]])
                S.op("dve", lambda e, i=i, rstd=rstd: e.scalar_tensor_tensor(
                    out=hb[i][:, :], in0=x1t[i][:, :], scalar=rstd, in1=g2b[:, :], op0=ALU.mult, op1=ALU.mult),
                    reads=[x1t_b[i], stat_b[t], g2b_b], writes=[hb_b[i]])
                S.play(pend_t)
                S.rec_begin()
                for g in range(4):
                    pi = (pcnt % 2) * 4 + (3 - g)
                    pt = psum[pi].bitcast(BF16)
                    for q in range(4):
                        kc = g * 4 + q
                        S.op("pe", lambda e, pt=pt, q=q, i=i, kc=kc: e.transpose(
                            out=pt[:, q * 128:(q + 1) * 128], in_=hb[i][:, kc * 128:(kc + 1) * 128], identity=ident_bf[:, :]),
                            reads=[hb_b[i], ident_bf_b], writes=psum_b[pi], skip_same=True)
                    S.op("act", lambda e, pt=pt, g=g, i=i: e.activation(
                        out=hst[i][:, g * 4:(g + 1) * 4, :], in_=pt[:, 0:512].rearrange("p (a b) -> p a b", a=4), func=AF.Copy),
                        reads=psum_b[pi], writes=[hst_b[i]])
                pcnt += 1
                S.dma("pool", h2_r[:, :, ts_], hst[i][:, :, :], reads=[hst_b[i]])
                pend_t = S.rec_end()
            S.play(pend_t)


def phase_ffn(C):
    nc, S, debug = C.nc, C.S, C.debug
    psum, psum_b = C.psum, C.psum_b
    TH = 1024
    NB3 = 342
    with contextlib.ExitStack() as pes:
        def sb(name, shape, dt):
            return pes.enter_context(nc.sbuf_tensor(name, list(shape), dt))
        h2T = sb("h2T", [128, KC, TH + 2], BF16)
        h2T_b = Buf("h2T")
        aT = sb("aT", [128, 44, TH], BF16)
        aT_b = [Buf("aT%d" % j) for j in range(44)]
        NW = 5
        wu = [sb("wu%d" % i, [128, KC, 128], BF16) for i in range(NW)]
        wu_b = [Buf("wu") for i in range(NW)]
        pg = [sb("pg%d" % i, [128, TH + 2], F32) for i in range(2)]
        pg_b = [Buf("pg") for i in range(2)]
        cg = [sb("cg%d" % i, [128, TH], F32) for i in range(2)]
        cg_b = [Buf("cg") for i in range(2)]
        fw = sb("fw", [128, 88 * 4], F32)
        fw_b = Buf("fw")
        S.dma("sp", fw[:, :], C.ffn_cw_d[:, :], writes=[fw_b])
        NWD = 2
        wdn = [sb("wdn%d" % i, [128, 44, 128], BF16) for i in range(NWD)]
        wdn_b = [Buf("wdn") for i in range(NWD)]
        x1s = [sb("x1s%d" % i, [128, 512], F32) for i in range(2)]
        x1s_b = [Buf("x1s") for i in range(2)]
        xo = [sb("xo%d" % i, [128, 512], F32) for i in range(2)]
        xo_b = [Buf("xo") for i in range(2)]
        wup_r = C.w_up.rearrange("(kc p) n -> p kc n", p=128)
        wdn_r = C.w_down.rearrange("(fc p) n -> p fc n", p=128)
        h2_r = C.h2_scr.rearrange("(kc p) t -> p kc t", p=128)
        ucnt = 0
        pcnt = 0
        dcnt = 0
        for half in range(2):
            t0 = half * TH
            S.op("pool", lambda e: e.memset(h2T[:, :, :], 0.0), writes=[h2T_b])
            lo = max(t0 - 1, 0)
            hi = min(t0 + TH + 1, T)
            S.dma("sp", h2T[:, :, lo - (t0 - 1):hi - (t0 - 1)], h2_r[:, :, lo:hi], writes=[h2T_b])
            ulist = [(j, which) for j in range(44) for which in range(2)]
            PF = NW - 1

            def issue_unit(k):
                j_, which_ = ulist[k]
                c0_ = which_ * D_FF + j_ * 128
                wi_ = (ucnt + k) % NW
                S.dma("pool", wu[wi_][:, :, :], wup_r[:, :, c0_:c0_ + 128], writes=[wu_b[wi_]])

            for k in range(min(PF, len(ulist))):
                issue_unit(k)
            for k, (j, which) in enumerate(ulist):
                if k + PF < len(ulist):
                    issue_unit(k + PF)
                blk = which * 44 + j
                wi = (ucnt + k) % NW
                pgi = k % 2
                for b3 in range(3):
                    pi = pcnt % 8
                    pcnt += 1
                    s3 = slice(b3 * NB3, (b3 + 1) * NB3)
                    for kc in range(KC):
                        S.op("pe", lambda e, pi=pi, wi=wi, kc=kc, s3=s3: e.matmul(
                            out=psum[pi][:, 0:NB3], lhsT=wu[wi][:, kc, :], rhs=h2T[:, kc, s3], start=(kc == 0), stop=(kc == KC - 1)),
                            reads=[wu_b[wi], h2T_b], writes=psum_b[pi], skip_same=True)
                    if pcnt % 2 == 0:
                        S.op("act", lambda e, pi=pi, pgi=pgi, s3=s3: e.activation(out=pg[pgi][:, s3], in_=psum[pi][:, 0:NB3], func=AF.Copy),
                             reads=psum_b[pi], writes=[pg_b[pgi]])
                    else:
                        S.op("dve", lambda e, pi=pi, pgi=pgi, s3=s3: e.tensor_copy(out=pg[pgi][:, s3], in_=psum[pi][:, 0:NB3]),
                             reads=psum_b[pi], writes=[pg_b[pgi]])
                w0 = fw[:, blk * 4:blk * 4 + 1]
                w1 = fw[:, blk * 4 + 1:blk * 4 + 2]
                w2 = fw[:, blk * 4 + 2:blk * 4 + 3]
                bb = fw[:, blk * 4 + 3:blk * 4 + 4]
                S.op("act", lambda e, pgi=pgi, which=which, w1=w1, bb=bb: e.activation(
                    out=cg[which][:, :], in_=pg[pgi][:, 1:TH + 1], func=AF.Identity, scale=w1, bias=bb),
                    reads=[pg_b[pgi], fw_b], writes=[cg_b[which]])
                S.op("dve", lambda e, pgi=pgi, which=which, w0=w0: e.scalar_tensor_tensor(
                    out=cg[which][:, :], in0=pg[pgi][:, 0:TH], scalar=w0, in1=cg[which][:, :], op0=ALU.mult, op1=ALU.add),
                    reads=[pg_b[pgi], fw_b, cg_b[which]], writes=[cg_b[which]])
                S.op("dve", lambda e, pgi=pgi, which=which, w2=w2: e.scalar_tensor_tensor(
                    out=cg[which][:, :], in0=pg[pgi][:, 2:TH + 2], scalar=w2, in1=cg[which][:, :], op0=ALU.mult, op1=ALU.add),
                    reads=[pg_b[pgi], fw_b, cg_b[which]], writes=[cg_b[which]])
                if which == 0:
                    S.op("act", lambda e: e.activation(out=cg[0][:, :], in_=cg[0][:, :], func=AF.Silu),
                         reads=[cg_b[0]], writes=[cg_b[0]])
                else:
                    S.op("dve", lambda e, j=j: e.tensor_tensor(out=aT[:, j, :], in0=cg[0][:, :], in1=cg[1][:, :], op=ALU.mult),
                         reads=[cg_b[0], cg_b[1]], writes=[aT_b[j]])
            ucnt += len(ulist)
            for cgp in range(4):
                units = []
                for u in range(4):
                    c0 = cgp * 512 + u * 128
                    wi = dcnt % NWD
                    dcnt += 1
                    S.dma("pool", wdn[wi][:, :, :], wdn_r[:, :, c0:c0 + 128], writes=[wdn_b[wi]])
                    units.append(wi)
                    for tt in range(TH // 128):
                        pi = tt % 8
                        ts_ = slice(tt * 128, (tt + 1) * 128)
                        for fc in range(44):
                            S.op("pe", lambda e, pi=pi, u=u, fc=fc, ts_=ts_, wi=wi: e.matmul(
                                out=psum[pi][:, u * 128:(u + 1) * 128], lhsT=aT[:, fc, ts_], rhs=wdn[wi][:, fc, :],
                                start=(fc == 0), stop=(fc == 43)),
                                reads=[aT_b[fc], wdn_b[wi]], writes=psum_b[pi], skip_same=True)
                for tt in range(TH // 128):
                    pi = tt % 8
                    i = tt % 2
                    rows = slice(t0 + tt * 128, t0 + (tt + 1) * 128)
                    csl = slice(cgp * 512, (cgp + 1) * 512)
                    S.dma("sp", x1s[i][:, :], C.x1_scr[rows, csl], writes=[x1s_b[i]])
                    S.op("dve", lambda e, pi=pi, i=i: e.tensor_tensor(out=xo[i][:, :], in0=psum[pi][:, :], in1=x1s[i][:, :], op=ALU.add),
                         reads=psum_b[pi] + [x1s_b[i]], writes=[xo_b[i]])
                    S.dma("act", C.x2_scr[rows, csl], xo[i][:, :], reads=[xo_b[i]])


def phase_final(C):
    nc, S, debug = C.nc, C.S, C.debug
    with contextlib.ExitStack() as pes:
        def sb(name, shape, dt):
            return pes.enter_context(nc.sbuf_tensor(name, list(shape), dt))
        gfb = sb("gfb", [128, D], F32)
        gfb_b = Buf("gfb")
        S.dma("sp", gfb[:, :], C.final_norm_g.partition_broadcast(128), writes=[gfb_b])
        NBF = 3
        xt = [sb("xf%d" % i, [128, D], F32) for i in range(NBF)]
        xt_b = [Buf("xf") for i in range(NBF)]
        ot = [sb("of%d" % i, [128, D], F32) for i in range(NBF)]
        ot_b = [Buf("of") for i in range(NBF)]
        junk = sb("junk3", [128, D], BF16)
        junk_b = Buf("junk3")
        stat = sb("stat3", [128, 4 * NT], F32)
        stat_b = [Buf("stat3") for i in range(NT)]
        for t in range(NT):
            i = t % NBF
            ts_ = slice(t * 128, (t + 1) * 128)
            S.dma("sp", xt[i][:, :], C.x2_scr[ts_, :], writes=[xt_b[i]])
            ss = stat[:, 4 * t:4 * t + 1]
            lnv = stat[:, 4 * t + 1:4 * t + 2]
            rstd = stat[:, 4 * t + 2:4 * t + 3]
            S.op("act", lambda e, i=i, ss=ss: e.activation(out=junk[:, :], in_=xt[i][:, :], func=AF.Square, accum_out=ss),
                 reads=[xt_b[i]], writes=[junk_b, stat_b[t]])
            S.op("act", lambda e, ss=ss, lnv=lnv: e.activation(out=lnv, in_=ss, func=AF.Ln, scale=1.0 / D, bias=EPS),
                 reads=[stat_b[t]], writes=[stat_b[t]])
            S.op("act", lambda e, rstd=rstd, lnv=lnv: e.activation(out=rstd, in_=lnv, func=AF.Exp, scale=-0.5),
                 reads=[stat_b[t]], writes=[stat_b[t]])
            S.op("dve", lambda e, i=i, rstd=rstd: e.scalar_tensor_tensor(
                out=ot[i][:, :], in0=xt[i][:, :], scalar=rstd, in1=gfb[:, :], op0=ALU.mult, op1=ALU.mult),
                reads=[xt_b[i], stat_b[t], gfb_b], writes=[ot_b[i]])
            S.dma("pool", C.out[ts_, :], ot[i][:, :], reads=[ot_b[i]])

NC2 = 1024
NCB = 14 * 128


_NC_CACHE = {}


def _consts():
    j = np.arange(128)[:, None]
    i = np.arange(128)[None, :]
    cst = np.zeros((128, NCST), np.float32)
    cst[:, 0:128] = np.where(j <= i, -1.0 / 16, 0.0)
    cst[:, 128:256] = np.where(j >= i, -1.0 / 16, 0.0)
    cst[:, 256:384] = np.where(j > i, -1.0 / 16, 0.0)
    cst[:, 384:512] = np.where(j < i, -1.0 / 16, 0.0)
    cst[:, 512:640] = 1.0
    cst[:, 640:1152] = np.tile(np.where(j <= i, 1.0, 0.0), (1, 4))
    cst[:, 1152:1664] = np.tile(np.where(j >= i, 1.0, 0.0), (1, 4))
    c2 = np.zeros((128, NC2), np.float32)
    c2[:, 0:128] = np.where(j <= i, 1.0, 0.0)
    c2[:, 128:256] = np.where(j >= i, 1.0, 0.0)
    c2[:, 256:384] = np.where(j > i, 1.0, 0.0)
    c2[:, 384:512] = np.where(j < i, 1.0, 0.0)
    c2[:, 512:640] = np.where(j <= i, 0.0, -30000.0)
    c2[:, 640:768] = np.where(j >= i, 0.0, -30000.0)
    c2[:, 768:896] = np.eye(128)
    c2[:, 896:1024] = -1.0
    cb = np.zeros((128, NCB), np.float32)
    for d in range(2):
        for l in range(1, 8):
            b = 1 << (l - 1)
            same = (i // (2 * b)) == (j // (2 * b))
            if d == 0:
                m = same & ((j % (2 * b)) >= b) & ((i % (2 * b)) < b)
            else:
                m = same & ((j % (2 * b)) < b) & ((i % (2 * b)) >= b)
            o = (d * 7 + (l - 1)) * 128
            cb[:, o:o + 128] = -m.astype(np.float32)
    return {
        "ident_bf": np.eye(128, dtype=np.float32).astype(ml_dtypes.bfloat16),
        "cst": cst,
        "cst2": c2,
        "cstb": cb.astype(ml_dtypes.bfloat16),
    }


def make_in_maps(inputs, n_cores=8):
    c = _consts()
    maps = []
    xs = np.ascontiguousarray(inputs["x"])
    for b in range(n_cores):
        m = {
            "x": xs[b],
            "norm1_g": np.ascontiguousarray(inputs["norm1_g"]).reshape(1, D),
            "w_in": np.ascontiguousarray(inputs["w_in"]).reshape(D, N_IN),
            "gla_decay_w_f": np.ascontiguousarray(inputs["gla_decay_w_f"]).reshape(16, 1024),
            "gla_decay_w_b": np.ascontiguousarray(inputs["gla_decay_w_b"]).reshape(16, 1024),
            "gla_decay_b_f": np.ascontiguousarray(inputs["gla_decay_b_f"]).reshape(1, 1024),
            "gla_decay_b_b": np.ascontiguousarray(inputs["gla_decay_b_b"]).reshape(1, 1024),
            "gla_norm_g": np.ascontiguousarray(inputs["gla_norm_g"]).reshape(1, 128),
            "gdn_norm_g": np.ascontiguousarray(inputs["gdn_norm_g"]).reshape(1, 128),
            "w_branch_gla": np.ascontiguousarray(inputs["w_branch_gla"]).reshape(1024, D),
            "w_branch_gdn": np.ascontiguousarray(inputs["w_branch_gdn"]).reshape(1024, D),
            "w_out": np.ascontiguousarray(inputs["w_out"]).reshape(D, D),
            "norm2_g": np.ascontiguousarray(inputs["norm2_g"]).reshape(1, D),
            "w_up": np.ascontiguousarray(inputs["w_up"]).reshape(D, 2 * D_FF),
            "w_down": np.ascontiguousarray(inputs["w_down"]).reshape(D_FF, D),
            "ffn_cw": np.ascontiguousarray(np.concatenate([
                np.asarray(inputs["ffn_conv_w"]).reshape(3, 88, 128), np.asarray(inputs["ffn_conv_b"]).reshape(1, 88, 128)],
                axis=0).transpose(2, 1, 0).reshape(128, 88 * 4)),
            "final_norm_g": np.ascontiguousarray(inputs["final_norm_g"]).reshape(1, D),
            "gdn_cw": np.ascontiguousarray(
                np.asarray(inputs["gdn_conv_w"]).reshape(3, 24, 128).transpose(2, 1, 0).reshape(128, 72)),
            "gdn_hp": np.ascontiguousarray(np.stack([
                np.concatenate([np.asarray(inputs["gdn_a_log_f"]).reshape(8), np.asarray(inputs["gdn_a_log_b"]).reshape(8)]),
                np.concatenate([np.asarray(inputs["gdn_dt_bias_f"]).reshape(8), np.asarray(inputs["gdn_dt_bias_b"]).reshape(8)]),
            ], axis=1).astype(np.float32)),
        }
        m.update(c)
        maps.append(m)
    return maps


def kernel(**inputs):
    nc = build_nc()
    in_maps = make_in_maps(inputs, 8)
    res = run_bass_kernel_spmd(nc, in_maps, core_ids=list(range(8)))
    return np.stack([np.asarray(r["out"]) for r in res.results], axis=0)
```

```python
import contextlib
import numpy as np
import ml_dtypes
import concourse.bass as bass
import concourse.mybir as mybir
from concourse.bass_utils import run_bass_kernel_spmd

F32 = mybir.dt.float32
BF16 = mybir.dt.bfloat16
AF = mybir.ActivationFunctionType
ALU = mybir.AluOpType

T = 2048
D = 2048
KC = 16
NT = 16
N_IN = 12352
D_FF = 5632
EPS = 1e-6

C_GQ, C_GK, C_GV, C_GG = 0, 1024, 2048, 3072
C_LR = 4096
C_DQ, C_DK, C_DV = 4128, 5152, 6176
C_DZ = 7200
C_AB = 8224
C_BG = 8256
C_BD = 10304


class Buf:
    __slots__ = ("w", "r", "name", "excl")

    def __init__(self, name="", excl=False):
        self.w = None
        self.r = {}
        self.name = name
        self.excl = excl


class DSem:
    __slots__ = ("handle", "count")

    def __init__(self, handle):
        self.handle = handle
        self.count = 0


class Sched:
    ENG = ["pe", "act", "dve", "pool", "sp"]

    def __init__(self, nc, es, n_dsem=24):
        self.nc = nc
        self.sem = {e: es.enter_context(nc.semaphore("s_" + e)) for e in self.ENG}
        self.cnt = {e: 0 for e in self.ENG}
        self.seen = {e: {} for e in self.ENG}
        self.prog = {e: [] for e in self.ENG}
        self.dsems = [DSem(es.enter_context(nc.semaphore("d%d" % i))) for i in range(n_dsem)]
        self.dnext = 0

    def _waits(self, eng, reads, writes, skip_same):
        deps = {}

        def add(k, v):
            if skip_same and k == eng:
                return
            if deps.get(k, 0) < v:
                deps[k] = v

        for b in reads:
            if b.w is not None:
                add(*b.w)
        for b in writes:
            if b.w is not None:
                add(*b.w)
            for k, v in b.r.items():
                add(k, v)
        out = []
        seen = self.seen[eng]
        for k, v in deps.items():
            if seen.get(k, 0) < v:
                seen[k] = v
                out.append((k.handle if isinstance(k, DSem) else self.sem[k], v))
        return out

    def _mark(self, d, reads, writes):
        k, v = d
        for b in reads:
            if b.r.get(k, 0) < v:
                b.r[k] = v
        for b in writes:
            b.w = d
            b.r = {}

    def rec_begin(self):
        self._rec = []

    def rec_end(self):
        r, self._rec = self._rec, None
        return r

    def play(self, lst, n=None):
        n = len(lst) if n is None else min(n, len(lst))
        for _ in range(n):
            kind, a, kw = lst.pop(0)
            (self.op if kind == "op" else self.dma)(*a, **kw)

    def op(self, eng, fn, reads=(), writes=(), skip_same=False):
        if getattr(self, "_rec", None) is not None:
            self._rec.append(("op", (eng, fn, list(reads), list(writes), skip_same), {}))
            return
        if any(b.excl for b in reads):
            writes = list(writes) + [b for b in reads if b.excl]
            reads = [b for b in reads if not b.excl]
        waits = self._waits(eng, reads, writes, skip_same)
        self.cnt[eng] += 1
        self.prog[eng].append((waits, fn, self.sem[eng], 1))
        self._mark((eng, self.cnt[eng]), reads, writes)

    def barrier(self):
        for eng in self.ENG:
            waits = []
            seen = self.seen[eng]
            for k in self.ENG:
                if k != eng and seen.get(k, 0) < self.cnt[k]:
                    seen[k] = self.cnt[k]
                    waits.append((self.sem[k], self.cnt[k]))
            for ds in self.dsems:
                if ds.count > 0 and seen.get(ds, 0) < ds.count:
                    seen[ds] = ds.count
                    waits.append((ds.handle, ds.count))
            if waits:
                self.prog[eng].append((waits, None, None, 0))

    def dma(self, q, out, in_, reads=(), writes=(), **kw):
        if getattr(self, "_rec", None) is not None:
            self._rec.append(("dma", (q, out, in_, list(reads), list(writes)), dict(kw)))
            return
        ds = self.dsems[self.dnext]
        self.dnext = (self.dnext + 1) % len(self.dsems)
        waits = self._waits(q, reads, writes, False)
        if ds.count > 0 and self.seen[q].get(ds, 0) < ds.count:
            self.seen[q][ds] = ds.count
            waits.append((ds.handle, ds.count))
        ds.count += 16
        self.prog[q].append((waits, (lambda e, o=out, i=in_, kw=kw: e.dma_start(out=o, in_=i, **kw)), ds.handle, 16))
        self._mark((ds, ds.count), reads, writes)

    def finish(self):
        waits = []
        for ds in self.dsems:
            if ds.count > 0:
                waits.append((ds.handle, ds.count))
        self.prog["sp"].append((waits, None, None, 0))

    def emit(self):
        nc = self.nc
        prog = self.prog

        def replay(name, e):
            for waits, fn, sem, inc in prog[name]:
                for s, v in waits:
                    e.wait_ge(s, v)
                if fn is not None:
                    ins = fn(e)
                    ins.then_inc(sem, inc)

        with nc.Block() as block:
            @block.tensor
            def _(e):
                replay("pe", e)

            @block.scalar
            def _(e):
                replay("act", e)

            @block.vector
            def _(e):
                replay("dve", e)

            @block.gpsimd
            def _(e):
                replay("pool", e)

            @block.sync
            def _(e):
                replay("sp", e)


def build_nc(debug=None):
    debug = debug or {}
    nc = bass.Bass("TRN2", target_bir_lowering=False)
    es = contextlib.ExitStack()
    with es:
        _build(nc, es, debug)
    return nc


def _dram_in(nc, name, shape, dt=F32):
    return nc.dram_tensor(name, list(shape), dt, kind="ExternalInput").ap()


class Ctx:
    pass


def _build(nc, es, debug):
    S = Sched(nc, es)
    C = Ctx()
    C.nc, C.S, C.debug = nc, S, debug
    stop_after = debug.get("stop_after", [[None], None])[0][0]

    C.x = _dram_in(nc, "x", [T, D])
    C.norm1_g = _dram_in(nc, "norm1_g", [1, D])
    C.w_in = _dram_in(nc, "w_in", [D, N_IN])
    C.ident_bf_d = _dram_in(nc, "ident_bf", [128, 128], BF16)
    C.cst_d = _dram_in(nc, "cst", [128, NCST])
    C.gla_dw = [_dram_in(nc, "gla_decay_w_f", [16, 1024]), _dram_in(nc, "gla_decay_w_b", [16, 1024])]
    C.gla_db = [_dram_in(nc, "gla_decay_b_f", [1, 1024]), _dram_in(nc, "gla_decay_b_b", [1, 1024])]
    C.gla_norm_g = _dram_in(nc, "gla_norm_g", [1, 128])
    C.gdn_norm_g = _dram_in(nc, "gdn_norm_g", [1, 128])
    C.w_branch_gla = _dram_in(nc, "w_branch_gla", [1024, D])
    C.w_branch_gdn = _dram_in(nc, "w_branch_gdn", [1024, D])
    C.w_out = _dram_in(nc, "w_out", [D, D])
    C.norm2_g = _dram_in(nc, "norm2_g", [1, D])
    C.w_up = _dram_in(nc, "w_up", [D, 2 * D_FF])
    C.w_down = _dram_in(nc, "w_down", [D_FF, D])
    C.ffn_cw_d = _dram_in(nc, "ffn_cw", [128, 88 * 4])
    C.final_norm_g = _dram_in(nc, "final_norm_g", [1, D])
    C.cst2_d = _dram_in(nc, "cst2", [128, NC2])
    C.cstb_d = _dram_in(nc, "cstb", [128, NCB], BF16)
    C.gdn_cw_d = _dram_in(nc, "gdn_cw", [128, 72])
    C.gdn_hp_d = _dram_in(nc, "gdn_hp", [16, 2])
    C.out = nc.dram_tensor("out", [T, D], F32, kind="ExternalOutput").ap()
    C.dbg_out = {}
    for name, (shape, dt) in debug.items():
        if dt is None:
            continue
        C.dbg_out[name] = nc.dram_tensor("dbg_" + name, list(shape), dt, kind="ExternalOutput").ap()
    C.scr = nc.dram_tensor("scr_proj", [N_IN, T], F32).ap()
    C.y_scr = nc.dram_tensor("scr_y", [2048, T], BF16).ap()
    C.h2_scr = nc.dram_tensor("scr_h2", [D, T], BF16).ap()
    C.x1_scr = nc.dram_tensor("scr_x1", [T, D], F32).ap()
    C.x2_scr = nc.dram_tensor("scr_x2", [T, D], F32).ap()

    C.psum = [es.enter_context(nc.psum_tensor("ps%d" % i, [128, 512], F32)) for i in range(8)]
    C.psum_b = [[Buf("ps%d" % i, excl=True)] * 4 for i in range(8)]
    C.ident_bf = es.enter_context(nc.sbuf_tensor("ident_bf_sb", [128, 128], BF16))
    C.ident_bf_b = Buf("ident_bf")
    C.cst = es.enter_context(nc.sbuf_tensor("cst_sb", [128, NCST], F32))
    C.cst_b = Buf("cst")
    S.dma("sp", C.ident_bf[:, :], C.ident_bf_d[:, :], writes=[C.ident_bf_b])
    S.dma("sp", C.cst[:, :], C.cst_d[:, :], writes=[C.cst_b])

    phase_proj(C)
    S.barrier()
    if stop_after != "proj":
        if debug.get("gla_heads", [[8], None])[0][0] > 0:
            phase_gla(C)
            S.barrier()
        if stop_after != "gla":
            if debug.get("gdn_heads", [[8], None])[0][0] > 0:
                phase_gdn(C)
                S.barrier()
            if stop_after != "gdn":
                phase_branch(C)
                S.barrier()
                if stop_after != "branch":
                    phase_ffn(C)
                    S.barrier()
                    phase_final(C)
                    S.barrier()

    if "proj" in C.dbg_out:
        S.dma("sp", C.dbg_out["proj"][:, :], C.scr[0:C.dbg_out["proj"].shape[0], :])
    if "y" in C.dbg_out:
        S.dma("sp", C.dbg_out["y"][:, :], C.y_scr[0:C.dbg_out["y"].shape[0], :])
    if "y2" in C.dbg_out:
        S.dma("sp", C.dbg_out["y2"][:, :], C.y_scr[1024:1024 + C.dbg_out["y2"].shape[0], :])
    for nm, ap_ in (("x1", C.x1_scr), ("x2", C.x2_scr)):
        if nm in C.dbg_out:
            S.dma("sp", C.dbg_out[nm][:, :], ap_[:, :])
    if "h2" in C.dbg_out:
        S.dma("sp", C.dbg_out["h2"][:, :], C.h2_scr[:, :])
    if stop_after is not None:
        S.dma("sp", C.out[0:128, :], C.x[0:128, :])
    S.finish()
    S.emit()


def phase_proj(C):
    nc, S, debug = C.nc, C.S, C.debug
    psum, psum_b = C.psum, C.psum_b
    with contextlib.ExitStack() as pes:
        def sb(name, shape, dt):
            return pes.enter_context(nc.sbuf_tensor(name, list(shape), dt))

        hT = sb("hT", [128, KC, T], BF16)
        hT_b = [Buf("hT%d" % t) for t in range(NT)]
        g1b = sb("g1b", [128, D], F32)
        g1b_b = Buf("g1b")
        S.dma("sp", g1b[:, :], C.norm1_g.partition_broadcast(128), writes=[g1b_b])

        xt = [sb("xt%d" % i, [128, D], F32) for i in range(2)]
        xt_b = [Buf("xt%d" % i) for i in range(2)]
        junk = sb("junk", [128, D], BF16)
        junk_b = Buf("junk")
        hb = [sb("hb%d" % i, [128, D], BF16) for i in range(2)]
        hb_b = [Buf("hb%d" % i) for i in range(2)]
        stat = sb("stat", [128, 4 * NT], F32)
        stat_b = [Buf("stat%d" % i) for i in range(NT)]
        ident_bf, ident_bf_b = C.ident_bf, C.ident_bf_b
        pcnt = 0
        pend0 = []
        for t in range(NT):
            i = t % 2
            S.dma("sp", xt[i][:, :], C.x[t * 128:(t + 1) * 128, :], writes=[xt_b[i]])
            ss = stat[:, 4 * t:4 * t + 1]
            lnv = stat[:, 4 * t + 1:4 * t + 2]
            rstd = stat[:, 4 * t + 2:4 * t + 3]
            S.op("act", lambda e, i=i, ss=ss: e.activation(out=junk[:, :], in_=xt[i][:, :], func=AF.Square, accum_out=ss),
                 reads=[xt_b[i]], writes=[junk_b, stat_b[t]])
            S.op("act", lambda e, ss=ss, lnv=lnv: e.activation(out=lnv, in_=ss, func=AF.Ln, scale=1.0 / D, bias=EPS),
                 reads=[stat_b[t]], writes=[stat_b[t]])
            S.op("act", lambda e, rstd=rstd, lnv=lnv: e.activation(out=rstd, in_=lnv, func=AF.Exp, scale=-0.5),
                 reads=[stat_b[t]], writes=[stat_b[t]])
            S.op("dve", lambda e, i=i, rstd=rstd: e.scalar_tensor_tensor(
                out=hb[i][:, :], in0=xt[i][:, :], scalar=rstd, in1=g1b[:, :], op0=ALU.mult, op1=ALU.mult),
                reads=[xt_b[i], stat_b[t], g1b_b], writes=[hb_b[i]])
            S.play(pend0)
            S.rec_begin()
            for g in range(4):
                pi = 4 + (pcnt % 4)
                pcnt += 1
                pt = psum[pi].bitcast(BF16)
                for q in range(4):
                    kc = g * 4 + q
                    S.op("pe", lambda e, pt=pt, q=q, i=i, kc=kc: e.transpose(
                        out=pt[:, q * 128:(q + 1) * 128], in_=hb[i][:, kc * 128:(kc + 1) * 128], identity=ident_bf[:, :]),
                        reads=[hb_b[i], ident_bf_b], writes=[psum_b[pi][q]], skip_same=True)
                if g % 2 == 0:
                    S.op("act", lambda e, pt=pt, g=g, t=t: e.activation(
                        out=hT[:, g * 4:(g + 1) * 4, t * 128:(t + 1) * 128],
                        in_=pt[:, 0:512].rearrange("p (a b) -> p a b", a=4), func=AF.Copy),
                        reads=psum_b[pi], writes=[hT_b[t]])
                else:
                    S.op("dve", lambda e, pt=pt, g=g, t=t: e.tensor_copy(
                        out=hT[:, g * 4:(g + 1) * 4, t * 128:(t + 1) * 128],
                        in_=pt[:, 0:512].rearrange("p (a b) -> p a b", a=4)),
                        reads=psum_b[pi], writes=[hT_b[t]])
            pend0 = S.rec_end()
        S.play(pend0)

        scr = C.scr
        w_in_r = C.w_in.rearrange("(kc p) n -> p kc n", p=128)
        units = []
        for c0 in range(0, 4096, 128):
            units.append((c0, 128))
        units.append((C_LR, 32))
        for c0 in range(C_DQ, C_AB, 128):
            units.append((c0, 128))
        units.append((C_AB, 32))
        for c0 in range(C_BG, N_IN, 128):
            units.append((c0, 128))
        if "nunits" in debug:
            units = units[:debug["nunits"][0][0]]
        NW = 8
        wring = [sb("wr%d" % i, [128, KC, 128], BF16) for i in range(NW)]
        wring_b = [Buf("wr%d" % i) for i in range(NW)]
        NSTG = 3
        stg = [sb("stg%d" % i, [128, T], F32) for i in range(NSTG)]
        stg_b = [Buf("stg%d" % i) for i in range(NSTG)]
        ecnt = 0
        for u, (c0, ncol) in enumerate(units):
            wt, wb = wring[u % NW], wring_b[u % NW]
            S.dma("pool", wt[:, :, 0:ncol], w_in_r[:, :, c0:c0 + ncol], writes=[wb])
            st, stb = stg[u % NSTG], stg_b[u % NSTG]
            for blk in range(4):
                pi = (4 * u + blk) % 8
                for kc in range(KC):
                    S.op("pe", lambda e, pi=pi, wt=wt, kc=kc, blk=blk, ncol=ncol: e.matmul(
                        out=psum[pi][0:ncol, :], lhsT=wt[:, kc, 0:ncol], rhs=hT[:, kc, blk * 512:(blk + 1) * 512],
                        start=(kc == 0), stop=(kc == KC - 1)),
                        reads=[wb] + hT_b[4 * blk:4 * blk + 4], writes=psum_b[pi], skip_same=True)
                if ecnt % 2 == 0:
                    S.op("act", lambda e, pi=pi, st=st, blk=blk, ncol=ncol: e.activation(
                        out=st[0:ncol, blk * 512:(blk + 1) * 512], in_=psum[pi][0:ncol, :], func=AF.Copy),
                        reads=psum_b[pi], writes=[stb])
                else:
                    S.op("dve", lambda e, pi=pi, st=st, blk=blk, ncol=ncol: e.tensor_copy(
                        out=st[0:ncol, blk * 512:(blk + 1) * 512], in_=psum[pi][0:ncol, :]),
                        reads=psum_b[pi], writes=[stb])
                ecnt += 1
            S.dma("sp", scr[c0:c0 + ncol, :], st[0:ncol, :], reads=[stb])


def phase_gla(C):
    nc, S, debug = C.nc, C.S, C.debug
    psum, psum_b = C.psum, C.psum_b
    cst, cst_b = C.cst, C.cst_b
    scr = C.scr
    nheads = debug.get("gla_heads", [[8], None])[0][0]
    with contextlib.ExitStack() as pes:
        def sb(name, shape, dt):
            return pes.enter_context(nc.sbuf_tensor(name, list(shape), dt))

        def B(name):
            return Buf(name)

        lr = sb("lr", [49, T], F32)
        lr_b = B("lr")
        dw = sb("dw", [49, 1024], F32)
        dw_b = B("dw")
        gn = sb("gn", [128, 1], F32)
        gn_b = B("gn")
        S.op("pool", lambda e: e.memset(lr[:, :], 1.0), writes=[lr_b])
        for d in range(2):
            S.dma("sp", lr[32 * d:32 * d + 16, :], scr[C_LR + 16 * d:C_LR + 16 * d + 16, :], writes=[lr_b])
            S.dma("sp", dw[32 * d:32 * d + 16, :], C.gla_dw[d][:, :], writes=[dw_b])
            S.dma("sp", dw[32 * d + 16:32 * d + 17, :], C.gla_db[d][:, :], writes=[dw_b])
        S.dma("sp", gn[:, :], C.gla_norm_g.rearrange("o d -> d o"), writes=[gn_b])

        NB = 2
        qbf = [sb("qbf%d" % i, [128, T], BF16) for i in range(NB)]
        kbf = [sb("kbf%d" % i, [128, T], BF16) for i in range(NB)]
        vbf = [sb("vbf%d" % i, [128, T], BF16) for i in range(NB)]
        g32 = [sb("g32%d" % i, [128, T], F32) for i in range(NB)]
        qbf_b = [B("qbf") for i in range(NB)]
        kbf_b = [B("kbf") for i in range(NB)]
        vbf_b = [B("vbf") for i in range(NB)]
        g32_b = [B("g32") for i in range(NB)]
        vtm = sb("vtm", [128, NT, 128], BF16)
        vtm_b = B("vtm")
        sg = sb("sg", [128, T], BF16)
        sg_b = B("sg")
        Ls = [sb("L%d" % d, [128, NT, 128], F32) for d in range(2)]
        Ls_b = [B("L") for d in range(2)]
        EGs = [sb("EG%d" % d, [128, T], BF16) for d in range(2)]
        EGs_b = [B("EG") for d in range(2)]
        EGis = [sb("EGi%d" % d, [128, T], BF16) for d in range(2)]
        EGis_b = [B("EGi") for d in range(2)]
        EKTs = [sb("EKT%d" % d, [128, NT, 128], BF16) for d in range(2)]
        EKTs_b = [B("EKT") for d in range(2)]
        dcl = sb("dcl", [128, 2, NT], F32)
        dcl_b = [B("dcl0"), B("dcl1")]
        qd = [sb("qd%d" % d, [128, T], BF16) for d in range(2)]
        qd_b = [B("qd") for d in range(2)]
        ki = [sb("ki%d" % d, [128, T], BF16) for d in range(2)]
        ki_b = [B("ki") for d in range(2)]
        kt = [sb("kt%d" % d, [128, NT, 128], BF16) for d in range(2)]
        kt_b = [B("kt") for d in range(2)]
        CSs = [sb("CS%d" % d, [128, NT, 128], F32) for d in range(2)]
        CSs_b = [B("CS") for d in range(2)]
        Sst = [sb("Sst%d" % d, [128, NT, 128], F32) for d in range(2)]
        Sst_b = [B("Sst") for d in range(2)]
        Sbf = [sb("Sbf%d" % d, [128, NT, 128], BF16) for d in range(2)]
        Sbf_b = [B("Sbf") for d in range(2)]
        tE = [sb("tE%d" % i, [128, 512], F32) for i in range(2)]
        tE_b = [B("tE") for i in range(2)]
        sc1s = [sb("sc1_%d" % i, [128, 512], BF16) for i in range(2)]
        sc1s_b = [B("sc1") for i in range(2)]
        sc2s = [sb("sc2_%d" % i, [128, 512], BF16) for i in range(2)]
        sc2s_b = [B("sc2") for i in range(2)]
        sqs = [sb("sq_%d" % i, [128, 512], F32) for i in range(2)]
        sqs_b = [B("sq") for i in range(2)]
        rss = [sb("rs_%d" % i, [128, 512], F32) for i in range(2)]
        rss_b = [B("rs") for i in range(2)]
        tts = [sb("tt_%d" % i, [128, 512], F32) for i in range(2)]
        tts_b = [B("tt") for i in range(2)]
        yb = [sb("yb%d" % i, [128, T], BF16) for i in range(2)]
        yb_b = [B("yb") for i in range(2)]

        ident_bf, ident_bf_b = C.ident_bf, C.ident_bf_b
        TRI = [cst[:, 0:128], cst[:, 128:256]]
        STRI = [cst[:, 256:384], cst[:, 384:512]]
        ONES = cst[:, 512:640]
        MASK = [cst[:, 640:1152], cst[:, 1152:1664]]
        pc = [0]
        tec = [0]

        def bank():
            pi = 4 + (pc[0] % 4)
            pc[0] += 1
            return pi

        def load_head(h):
            i = h % NB
            S.dma("pool", qbf[i][:, :], scr[C_GQ + h * 128:C_GQ + (h + 1) * 128, :], writes=[qbf_b[i]], max_dma_last_dim=4096)
            S.dma("pool", kbf[i][:, :], scr[C_GK + h * 128:C_GK + (h + 1) * 128, :], writes=[kbf_b[i]], max_dma_last_dim=4096)
            S.dma("pool", vbf[i][:, :], scr[C_GV + h * 128:C_GV + (h + 1) * 128, :], writes=[vbf_b[i]], max_dma_last_dim=4096)
            S.dma("sp", g32[i][:, :], scr[C_GG + h * 128:C_GG + (h + 1) * 128, :], writes=[g32_b[i]])

        load_head(0)
        for h in range(nheads):
            i = h % NB
            if h + 1 < nheads:
                load_head(h + 1)
            for g in range(4):
                pi = bank()
                pt = psum[pi].bitcast(BF16)
                for q in range(4):
                    c = 4 * g + q
                    S.op("pe", lambda e, pt=pt, q=q, c=c, i=i: e.transpose(
                        out=pt[:, q * 128:(q + 1) * 128], in_=vbf[i][:, c * 128:(c + 1) * 128], identity=ident_bf[:, :]),
                        reads=[vbf_b[i], ident_bf_b], writes=[psum_b[pi][q]], skip_same=True)
                S.op("act", lambda e, pt=pt, g=g: e.activation(
                    out=vtm[:, 4 * g:4 * g + 4, :], in_=pt[:, 0:512].rearrange("p (a b) -> p a b", a=4), func=AF.Copy),
                    reads=psum_b[pi], writes=[vtm_b])
            for blk in range(4):
                sl = slice(blk * 512, (blk + 1) * 512)
                j = tec[0] % 2
                tec[0] += 1
                S.op("act", lambda e, j=j, sl=sl, i=i: e.activation(out=tE[j][:, :], in_=g32[i][:, sl], func=AF.Exp, scale=-1.0),
                     reads=[g32_b[i]], writes=[tE_b[j]])
                S.op("act", lambda e, j=j: e.activation(out=tE[j][:, :], in_=tE[j][:, :], func=AF.Ln, bias=1.0),
                     reads=[tE_b[j]], writes=[tE_b[j]])
                S.op("act", lambda e, j=j: e.activation(out=tE[j][:, :], in_=tE[j][:, :], func=AF.Exp, scale=-1.0),
                     reads=[tE_b[j]], writes=[tE_b[j]])
                S.op("dve", lambda e, j=j, sl=sl, i=i: e.tensor_tensor(out=sg[:, sl], in0=g32[i][:, sl], in1=tE[j][:, :], op=ALU.mult),
                     reads=[g32_b[i], tE_b[j]], writes=[sg_b])
            for d in range(2):
                p0 = 32 * d
                for g in range(4):
                    pi = bank()
                    for q in range(4):
                        c = 4 * g + q
                        S.op("pe", lambda e, d=d, pi=pi, q=q, c=c, p0=p0, h=h: e.matmul(
                            out=psum[pi][:, q * 128:(q + 1) * 128], lhsT=lr[p0:p0 + 17, c * 128:(c + 1) * 128],
                            rhs=dw[p0:p0 + 17, h * 128:(h + 1) * 128], start=True, stop=True),
                            reads=[lr_b, dw_b], writes=[psum_b[pi][q]], skip_same=True)
                    j = tec[0] % 2
                    tec[0] += 1
                    S.op("act", lambda e, d=d, j=j, pi=pi: e.activation(out=tE[j][:, :], in_=psum[pi][:, :], func=AF.Exp, scale=-1.0),
                         reads=psum_b[pi], writes=[tE_b[j]])
                    S.op("act", lambda e, d=d, j=j, g=g: e.activation(
                        out=Ls[d][:, 4 * g:4 * g + 4, :], in_=tE[j][:, :].rearrange("p (a b) -> p a b", a=4), func=AF.Ln, bias=1.0),
                        reads=[tE_b[j]], writes=[Ls_b[d]])
            for d in range(2):
                p0 = 32 * d
                for g in range(4):
                    pi = bank()
                    sl = slice(g * 512, (g + 1) * 512)
                    for q in range(4):
                        c = 4 * g + q
                        S.op("pe", lambda e, pi=pi, q=q, c=c, d=d: e.matmul(
                            out=psum[pi][:, q * 128:(q + 1) * 128], lhsT=Ls[d][:, c, :], rhs=TRI[d], start=True, stop=True),
                            reads=[Ls_b[d], cst_b], writes=[psum_b[pi][q]], skip_same=True)
                    S.op("act", lambda e, d=d, pi=pi, sl=sl: e.activation(out=EGs[d][:, sl], in_=psum[pi][:, :], func=AF.Exp),
                         reads=psum_b[pi], writes=[EGs_b[d]])
                    S.op("act", lambda e, d=d, pi=pi, sl=sl: e.activation(out=EGis[d][:, sl], in_=psum[pi][:, :], func=AF.Exp, scale=-1.0),
                         reads=psum_b[pi], writes=[EGis_b[d]])
                    col = 127 if d == 0 else 0
                    S.op("act", lambda e, pi=pi, g=g, d=d, col=col: e.activation(
                        out=dcl[:, d, 4 * g:4 * g + 4], in_=psum[pi][:, :].rearrange("p (a b) -> p a b", a=4)[:, :, col], func=AF.Exp),
                        reads=psum_b[pi], writes=[dcl_b[d]])
            for d in range(2):
                p0 = 32 * d
                for g in range(4):
                    pi = bank()
                    for q in range(4):
                        c = 4 * g + q
                        S.op("pe", lambda e, pi=pi, q=q, c=c, d=d: e.matmul(
                            out=psum[pi][:, q * 128:(q + 1) * 128], lhsT=STRI[d], rhs=Ls[d][:, c, :], start=True, stop=True),
                            reads=[Ls_b[d], cst_b], writes=[psum_b[pi][q]], skip_same=True)
                    S.op("act", lambda e, d=d, pi=pi, g=g: e.activation(
                        out=EKTs[d][:, 4 * g:4 * g + 4, :], in_=psum[pi][:, :].rearrange("p (a b) -> p a b", a=4), func=AF.Exp),
                        reads=psum_b[pi], writes=[EKTs_b[d]])
            for d in range(2):
                p0 = 32 * d
                S.op("dve", lambda e, d=d, i=i: e.scalar_tensor_tensor(
                    out=qd[d][:, :], in0=qbf[i][:, :], scalar=float(128 ** -0.5), in1=EGs[d][:, :], op0=ALU.mult, op1=ALU.mult),
                    reads=[qbf_b[i], EGs_b[d]], writes=[qd_b[d]])
                S.op("dve", lambda e, d=d, i=i: e.tensor_tensor(out=ki[d][:, :], in0=kbf[i][:, :], in1=EGis[d][:, :], op=ALU.mult),
                     reads=[kbf_b[i], EGis_b[d]], writes=[ki_b[d]])
            for d in range(2):
                p0 = 32 * d
                for g in range(4):
                    pi = bank()
                    pt = psum[pi].bitcast(BF16)
                    for q in range(4):
                        c = 4 * g + q
                        S.op("pe", lambda e, d=d, pt=pt, q=q, c=c, i=i: e.transpose(
                            out=pt[:, q * 128:(q + 1) * 128], in_=kbf[i][:, c * 128:(c + 1) * 128], identity=ident_bf[:, :]),
                            reads=[kbf_b[i], ident_bf_b], writes=[psum_b[pi][q]], skip_same=True)
                    S.op("dve", lambda e, pt=pt, g=g, d=d: e.tensor_tensor(
                        out=kt[d][:, 4 * g:4 * g + 4, :], in0=pt[:, 0:512].rearrange("p (a b) -> p a b", a=4),
                        in1=EKTs[d][:, 4 * g:4 * g + 4, :], op=ALU.mult),
                        reads=psum_b[pi] + [EKTs_b[d]], writes=[kt_b[d]])
            for d in range(2):
                p0 = 32 * d
                for g in range(4):
                    pi = bank()
                    for q in range(4):
                        c = 4 * g + q
                        S.op("pe", lambda e, pi=pi, q=q, c=c, d=d: e.matmul(
                            out=psum[pi][:, q * 128:(q + 1) * 128], lhsT=kt[d][:, c, :], rhs=vtm[:, c, :], start=True, stop=True),
                            reads=[kt_b[d], vtm_b], writes=[psum_b[pi][q]], skip_same=True)
                    S.op("act", lambda e, d=d, pi=pi, g=g: e.activation(
                        out=CSs[d][:, 4 * g:4 * g + 4, :], in_=psum[pi][:, :].rearrange("p (a b) -> p a b", a=4), func=AF.Copy),
                        reads=psum_b[pi], writes=[CSs_b[d]])
            S.op("pool", lambda e: e.memset(Sst[0][:, 0, :], 0.0), writes=[Sst_b[0]])
            S.op("pool", lambda e: e.memset(Sst[1][:, NT - 1, :], 0.0), writes=[Sst_b[1]])
            for s in range(1, NT):
                c = s
                S.op("dve", lambda e, c=c: e.scalar_tensor_tensor(
                    out=Sst[0][:, c, :], in0=Sst[0][:, c - 1, :], scalar=dcl[:, 0, c - 1:c], in1=CSs[0][:, c - 1, :],
                    op0=ALU.mult, op1=ALU.add),
                    reads=[Sst_b[0], dcl_b[0], CSs_b[0]], writes=[Sst_b[0]])
                c = NT - 1 - s
                S.op("dve", lambda e, c=c: e.scalar_tensor_tensor(
                    out=Sst[1][:, c, :], in0=Sst[1][:, c + 1, :], scalar=dcl[:, 1, c + 1:c + 2], in1=CSs[1][:, c + 1, :],
                    op0=ALU.mult, op1=ALU.add),
                    reads=[Sst_b[1], dcl_b[1], CSs_b[1]], writes=[Sst_b[1]])
            S.op("act", lambda e: e.activation(out=Sbf[0][:, :, :], in_=Sst[0][:, :, :], func=AF.Copy),
                 reads=[Sst_b[0]], writes=[Sbf_b[0]])
            S.op("dve", lambda e: e.tensor_copy(out=Sbf[1][:, :, :], in_=Sst[1][:, :, :]),
                 reads=[Sst_b[1]], writes=[Sbf_b[1]])
            yi = h % 2
            for gp in range(2):
                GG = [2 * gp, 2 * gp + 1]
                BK = {g: (4 * (g % 2), 4 * (g % 2) + 1, 4 * (g % 2) + 2, 4 * (g % 2) + 3) for g in GG}
                for g in GG:
                    pA, pB, pO, pN = BK[g]
                    for q in range(4):
                        c = 4 * g + q
                        cs_ = slice(c * 128, (c + 1) * 128)
                        S.op("pe", lambda e, q=q, cs_=cs_, pA=pA: e.matmul(
                            out=psum[pA][:, q * 128:(q + 1) * 128], lhsT=ki[0][:, cs_], rhs=qd[0][:, cs_], start=True, stop=True),
                            reads=[ki_b[0], qd_b[0]], writes=psum_b[pA], skip_same=True)
                        S.op("pe", lambda e, q=q, cs_=cs_, pB=pB: e.matmul(
                            out=psum[pB][:, q * 128:(q + 1) * 128], lhsT=ki[1][:, cs_], rhs=qd[1][:, cs_], start=True, stop=True),
                            reads=[ki_b[1], qd_b[1]], writes=psum_b[pB], skip_same=True)
                for g in GG:
                    pA, pB, pO, pN = BK[g]
                    k2 = g % 2
                    S.op("dve", lambda e, pA=pA, k2=k2: e.tensor_tensor(out=sc1s[k2][:, :], in0=psum[pA][:, :], in1=MASK[0], op=ALU.mult),
                         reads=psum_b[pA] + [cst_b], writes=[sc1s_b[k2]])
                    S.op("dve", lambda e, pB=pB, k2=k2: e.tensor_tensor(out=sc2s[k2][:, :], in0=psum[pB][:, :], in1=MASK[1], op=ALU.mult),
                         reads=psum_b[pB] + [cst_b], writes=[sc2s_b[k2]])
                for g in GG:
                    pA, pB, pO, pN = BK[g]
                    k2 = g % 2
                    for q in range(4):
                        c = 4 * g + q
                        cs_ = slice(c * 128, (c + 1) * 128)
                        osl = slice(q * 128, (q + 1) * 128)
                        has_f = c >= 1
                        has_b = c <= NT - 2
                        S.op("pe", lambda e, osl=osl, c=c, pO=pO, k2=k2: e.matmul(
                            out=psum[pO][:, osl], lhsT=vtm[:, c, :], rhs=sc1s[k2][:, osl], start=True, stop=False),
                            reads=[vtm_b, sc1s_b[k2]], writes=psum_b[pO], skip_same=True)
                        S.op("pe", lambda e, osl=osl, c=c, pO=pO, k2=k2, has_f=has_f, has_b=has_b: e.matmul(
                            out=psum[pO][:, osl], lhsT=vtm[:, c, :], rhs=sc2s[k2][:, osl], start=False, stop=not (has_f or has_b)),
                            reads=[vtm_b, sc2s_b[k2]], writes=psum_b[pO], skip_same=True)
                        if has_f:
                            S.op("pe", lambda e, osl=osl, c=c, cs_=cs_, has_b=has_b, pO=pO: e.matmul(
                                out=psum[pO][:, osl], lhsT=Sbf[0][:, c, :], rhs=qd[0][:, cs_], start=False, stop=not has_b),
                                reads=[Sbf_b[0], qd_b[0]], writes=psum_b[pO], skip_same=True)
                        if has_b:
                            S.op("pe", lambda e, osl=osl, c=c, cs_=cs_, pO=pO: e.matmul(
                                out=psum[pO][:, osl], lhsT=Sbf[1][:, c, :], rhs=qd[1][:, cs_], start=False, stop=True),
                                reads=[Sbf_b[1], qd_b[1]], writes=psum_b[pO], skip_same=True)
                for g in GG:
                    pA, pB, pO, pN = BK[g]
                    k2 = g % 2
                    S.op("act", lambda e, pO=pO, k2=k2: e.activation(out=sqs[k2][:, :], in_=psum[pO][:, :], func=AF.Square),
                         reads=psum_b[pO], writes=[sqs_b[k2]])
                for g in GG:
                    pA, pB, pO, pN = BK[g]
                    k2 = g % 2
                    S.op("pe", lambda e, pN=pN, k2=k2: e.matmul(out=psum[pN][:, :], lhsT=ONES, rhs=sqs[k2][:, :], start=True, stop=True),
                         reads=[sqs_b[k2], cst_b], writes=psum_b[pN], skip_same=True)
                for g in GG:
                    pA, pB, pO, pN = BK[g]
                    k2 = g % 2
                    S.op("act", lambda e, pN=pN, k2=k2: e.activation(out=rss[k2][:, :], in_=psum[pN][:, :], func=AF.Ln, scale=1.0 / 128, bias=EPS),
                         reads=psum_b[pN], writes=[rss_b[k2]])
                for g in GG:
                    k2 = g % 2
                    S.op("act", lambda e, k2=k2: e.activation(out=rss[k2][:, :], in_=rss[k2][:, :], func=AF.Exp, scale=-0.5),
                         reads=[rss_b[k2]], writes=[rss_b[k2]])
                for g in GG:
                    pA, pB, pO, pN = BK[g]
                    k2 = g % 2
                    S.op("dve", lambda e, pO=pO, k2=k2: e.scalar_tensor_tensor(
                        out=tts[k2][:, :], in0=psum[pO][:, :], scalar=gn[:, 0:1], in1=rss[k2][:, :], op0=ALU.mult, op1=ALU.mult),
                        reads=psum_b[pO] + [gn_b, rss_b[k2]], writes=[tts_b[k2]])
                for g in GG:
                    k2 = g % 2
                    sl = slice(g * 512, (g + 1) * 512)
                    S.op("dve", lambda e, sl=sl, yi=yi, k2=k2: e.tensor_tensor(out=yb[yi][:, sl], in0=tts[k2][:, :], in1=sg[:, sl], op=ALU.mult),
                         reads=[tts_b[k2], sg_b], writes=[yb_b[yi]])
            S.dma("sp", C.y_scr[h * 128:(h + 1) * 128, :], yb[yi][:, :], reads=[yb_b[yi]])


NCST = 1664

def phase_gdn(C):
    nc, S, debug = C.nc, C.S, C.debug
    psum, psum_b = C.psum, C.psum_b
    cst, cst_b = C.cst, C.cst_b
    scr = C.scr
    nheads = debug.get("gdn_heads", [[8], None])[0][0]
    with contextlib.ExitStack() as pes:
        def sb(name, shape, dt):
            return pes.enter_context(nc.sbuf_tensor(name, list(shape), dt))

        def B(name):
            return Buf(name)

        ident_bf, ident_bf_b = C.ident_bf, C.ident_bf_b
        ONES = cst[:, 512:640]
        c2 = sb("c2", [128, NC2], F32)
        c2_b = B("c2")
        S.dma("sp", c2[:, :], C.cst2_d[:, :], writes=[c2_b])
        U = [c2[:, 0:128], c2[:, 128:256]]
        SU = [c2[:, 256:384], c2[:, 384:512]]
        PEN = [c2[:, 512:640], c2[:, 640:768]]
        IDF = c2[:, 768:896]
        NEGONES = c2[:, 896:1024]
        cb = sb("cb", [128, NCB], BF16)
        cb_b = B("cb")
        S.dma("sp", cb[:, :], C.cstb_d[:, :], writes=[cb_b])

        def lmask(d, l):
            o = (d * 7 + (l - 1)) * 128
            return cb[:, o:o + 128]

        cw = sb("cw", [128, 24 * 3], F32)
        cw_b = B("cw")
        S.dma("sp", cw[:, :], C.gdn_cw_d[:, :], writes=[cw_b])
        gn = sb("gn2", [128, 1], F32)
        gn_b = B("gn2")
        S.dma("sp", gn[:, :], C.gdn_norm_g.rearrange("o d -> d o"), writes=[gn_b])
        hp = sb("hp", [16, 4], F32)
        hp_b = B("hp")
        S.dma("sp", hp[:, 0:2], C.gdn_hp_d[:, :], writes=[hp_b])

        c1 = sb("c1", [128, T], F32)
        c1_b = B("c1")
        gb48 = c1
        gb48_b = c1_b
        S.op("pool", lambda e: e.memset(gb48[:, :], 0.0), writes=[gb48_b])
        S.dma("sp", gb48[0:16, :], scr[C_AB:C_AB + 16, :], writes=[gb48_b])
        S.dma("sp", gb48[32:48, :], scr[C_AB + 16:C_AB + 32, :], writes=[gb48_b])
        S.op("act", lambda e: e.activation(out=hp[:, 2:3], in_=hp[:, 0:1], func=AF.Exp), reads=[hp_b], writes=[hp_b])
        S.op("dve", lambda e: e.tensor_scalar(out=hp[:, 2:3], in0=hp[:, 2:3], scalar1=-1.0, scalar2=None, op0=ALU.mult),
             reads=[hp_b], writes=[hp_b])
        S.op("act", lambda e: e.activation(out=gb48[0:16, :], in_=gb48[0:16, :], func=AF.Exp, bias=hp[:, 1:2]),
             reads=[gb48_b, hp_b], writes=[gb48_b])
        S.op("act", lambda e: e.activation(out=gb48[0:16, :], in_=gb48[0:16, :], func=AF.Ln, bias=1.0),
             reads=[gb48_b], writes=[gb48_b])
        S.op("dve", lambda e: e.tensor_scalar(out=gb48[0:16, :], in0=gb48[0:16, :], scalar1=hp[:, 2:3], scalar2=None, op0=ALU.mult),
             reads=[gb48_b, hp_b], writes=[gb48_b])
        S.op("act", lambda e: e.activation(out=gb48[32:48, :], in_=gb48[32:48, :], func=AF.Exp, scale=-1.0),
             reads=[gb48_b], writes=[gb48_b])
        S.op("act", lambda e: e.activation(out=gb48[32:48, :], in_=gb48[32:48, :], func=AF.Ln, bias=1.0),
             reads=[gb48_b], writes=[gb48_b])
        S.op("act", lambda e: e.activation(out=gb48[32:48, :], in_=gb48[32:48, :], func=AF.Exp, scale=-1.0),
             reads=[gb48_b], writes=[gb48_b])
        gtm = sb("gtm", [128, NT, 48], F32)
        gtm_b = B("gtm")
        for g in range(4):
            pi = 2 + g
            for q in range(4):
                t = 4 * g + q
                S.op("pe", lambda e, pi=pi, q=q, t=t: e.transpose(
                    out=psum[pi][:, q * 48:(q + 1) * 48], in_=gb48[0:48, t * 128:(t + 1) * 128], identity=IDF[0:48, 0:48]),
                    reads=[gb48_b, c2_b], writes=[psum_b[pi][0]], skip_same=True)
            S.op("act", lambda e, pi=pi, g=g: e.activation(
                out=gtm[:, 4 * g:4 * g + 4, :], in_=psum[pi][:, 0:192].rearrange("p (a b) -> p a b", a=4), func=AF.Copy),
                reads=[psum_b[pi][0]], writes=[gtm_b])
        egtm = sb("egtm", [128, NT, 16], F32)
        egtm_b = B("egtm")
        ektm = sb("ektm", [128, NT, 16], F32)
        ektm_b = B("ektm")
        for (mats, dst, dst_b, pi) in ((U, egtm, egtm_b, 6), (SU, ektm, ektm_b, 7)):
            for t in range(NT):
                for d in range(2):
                    S.op("pe", lambda e, pi=pi, t=t, d=d, mats=mats: e.matmul(
                        out=psum[pi][:, t * 16 + d * 8:t * 16 + d * 8 + 8], lhsT=mats[d], rhs=gtm[:, t, d * 8:d * 8 + 8],
                        start=True, stop=True),
                        reads=[gtm_b, c2_b], writes=[psum_b[pi][0]], skip_same=True)
            S.op("act", lambda e, pi=pi, dst=dst: e.activation(
                out=dst[:, :, :], in_=psum[pi][:, 0:256].rearrange("p (a b) -> p a b", a=NT), func=AF.Exp),
                reads=[psum_b[pi][0]], writes=[dst_b])

        pin = [sb("pin%d" % i, [128, T + 2], F32) for i in range(1)]
        pin_b = [B("pin") for i in range(1)]
        for i in range(1):
            S.op("pool", lambda e, i=i: e.memset(pin[i][:, :], 0.0), writes=[pin_b[i]])
        tB = sb("tB", [128, T], F32)
        tB_b = B("tB")
        qT = sb("qT", [128, T], BF16)
        qT_b = B("qT")
        kT = sb("kT", [128, T], BF16)
        kT_b = B("kT")
        vT = sb("vT", [128, T], BF16)
        vT_b = B("vT")
        szs = [sb("sz%d" % i, [128, T], BF16) for i in range(2)]
        szs_b = [B("sz") for i in range(2)]
        ktm = sb("ktm", [128, NT, 128], BF16)
        ktm_b = B("ktm")
        vtms = [sb("vtm2_%d" % i, [128, NT, 128], BF16) for i in range(2)]
        vtms_b = [B("vtm2") for i in range(2)]
        rsi = sb("rsi", [128, 512], F32)
        rsi_b = B("rsi")
        kg = [sb("kg%d" % d, [128, NT, 128], BF16) for d in range(2)]
        kg_b = [B("kg") for d in range(2)]
        ktl = [sb("ktl%d" % d, [128, NT, 128], BF16) for d in range(2)]
        ktl_b = [B("ktl") for d in range(2)]
        qdT = [sb("qdT%d" % d, [128, T], BF16) for d in range(2)]
        qdT_b = [B("qdT") for d in range(2)]
        VT = [sb("VT%d" % d, [128, T], BF16) for d in range(2)]
        VT_b = [B("VT") for d in range(2)]
        atT = [sb("atT%d" % d, [128, T], BF16) for d in range(2)]
        atT_b = [B("atT") for d in range(2)]
        wpn = [sb("wpn%d" % d, [128, T], BF16) for d in range(2)]
        wpn_b = [B("wpn") for d in range(2)]
        dcl = sb("dcl2", [128, 2, NT], F32)
        dcl_b = [B("dcl20"), B("dcl21")]
        gU = sb("gU", [128, NT, 128], F32)
        gU_b = B("gU")
        DTs = [sb("DT%d" % k, [128, 512], F32) for k in range(4)]
        DTs_b = [B("DT") for k in range(4)]
        ebs = [sb("eb%d" % k, [128, 512], F32) for k in range(4)]
        ebs_b = [B("eb") for k in range(4)]
        NTs = [sb("NT%d" % k, [128, 512], BF16) for k in range(4)]
        NTs_b = [B("NT") for k in range(4)]
        Xs = [sb("Xs%d" % k, [128, 512], BF16) for k in range(4)]
        Xs_b = [B("Xs") for k in range(4)]
        Ys = [sb("Ys%d" % k, [128, 512], BF16) for k in range(4)]
        Ys_b = [B("Ys") for k in range(4)]
        Ps = [sb("Ps%d" % k, [128, 512], BF16) for k in range(4)]
        Ps_b = [B("Ps") for k in range(4)]
        S32 = [sb("S32_%d" % d, [128, 128], F32) for d in range(2)]
        S32_b = [B("S32") for d in range(2)]
        Sbf = [sb("Sbf2_%d" % d, [128, 128], BF16) for d in range(2)]
        Sbf_b = [B("Sbf2") for d in range(2)]
        vnb = [sb("vnb%d" % d, [128, 128], BF16) for d in range(2)]
        vnb_b = [B("vnb") for d in range(2)]
        oacc = [sb("oacc%d" % d, [128, T], F32) for d in range(2)]
        oacc_b = [B("oacc") for d in range(2)]
        sqs2 = [sb("sq2_%d" % i, [128, 512], F32) for i in range(2)]
        sqs2_b = [B("sq2") for i in range(2)]
        rss2 = [sb("rs2_%d" % i, [128, 512], F32) for i in range(2)]
        rss2_b = [B("rs2") for i in range(2)]
        tts2 = [sb("tt2_%d" % i, [128, 512], F32) for i in range(2)]
        tts2_b = [B("tt2") for i in range(2)]
        yb = [sb("yb2_%d" % i, [128, T], BF16) for i in range(1)] * 2
        yb_b = [B("yb2")] * 2

        pc = [0]

        def bank():
            pi = pc[0] % 8
            pc[0] += 1
            return pi

        pinc = [0]
        ipc = [0]

        def ibank():
            pi = 6 + (ipc[0] % 2)
            ipc[0] += 1
            return pi

        def silu_inplace(x, x_b, n=T):
            for hs in (slice(0, n // 2), slice(n // 2, n)):
                S.op("act", lambda e, hs=hs: e.activation(out=tB[:, hs], in_=x[:, hs], func=AF.Exp, scale=-1.0), reads=[x_b], writes=[tB_b])
                S.op("act", lambda e, hs=hs: e.activation(out=tB[:, hs], in_=tB[:, hs], func=AF.Ln, bias=1.0), reads=[tB_b], writes=[tB_b])
                S.op("act", lambda e, hs=hs: e.activation(out=tB[:, hs], in_=tB[:, hs], func=AF.Exp, scale=-1.0), reads=[tB_b], writes=[tB_b])
                S.op("dve", lambda e, hs=hs: e.tensor_tensor(out=x[:, hs], in0=x[:, hs], in1=tB[:, hs], op=ALU.mult),
                     reads=[x_b, tB_b], writes=[x_b])

        def conv_silu(row0, blk):
            i = 0
            S.dma("sp", pin[i][:, 1:T + 1], scr[row0:row0 + 128, :], writes=[pin_b[i]])
            w0 = cw[:, blk * 3:blk * 3 + 1]
            w1 = cw[:, blk * 3 + 1:blk * 3 + 2]
            w2 = cw[:, blk * 3 + 2:blk * 3 + 3]
            S.op("act", lambda e, i=i, w0=w0: e.activation(out=c1[:, :], in_=pin[i][:, 0:T], func=AF.Copy, scale=w0),
                 reads=[pin_b[i], cw_b], writes=[c1_b])
            S.op("dve", lambda e, i=i, w1=w1: e.scalar_tensor_tensor(
                out=c1[:, :], in0=pin[i][:, 1:T + 1], scalar=w1, in1=c1[:, :], op0=ALU.mult, op1=ALU.add),
                reads=[pin_b[i], cw_b, c1_b], writes=[c1_b])
            S.op("dve", lambda e, i=i, w2=w2: e.scalar_tensor_tensor(
                out=c1[:, :], in0=pin[i][:, 2:T + 2], scalar=w2, in1=c1[:, :], op0=ALU.mult, op1=ALU.add),
                reads=[pin_b[i], cw_b, c1_b], writes=[c1_b])
            silu_inplace(c1, c1_b)

        def l2norm_to(dst, dst_b, scale):
            for hs in (slice(0, T // 2), slice(T // 2, T)):
                S.op("act", lambda e, hs=hs: e.activation(out=tB[:, hs], in_=c1[:, hs], func=AF.Square), reads=[c1_b], writes=[tB_b])
            for blk in range(4):
                sl = slice(blk * 512, (blk + 1) * 512)
                pi = ibank()
                S.op("pe", lambda e, pi=pi, sl=sl: e.matmul(out=psum[pi][:, :], lhsT=ONES, rhs=tB[:, sl], start=True, stop=True),
                     reads=[tB_b, cst_b], writes=psum_b[pi], skip_same=True)
                S.op("act", lambda e, pi=pi: e.activation(out=rsi[:, :], in_=psum[pi][:, :], func=AF.Ln, bias=EPS),
                     reads=psum_b[pi], writes=[rsi_b])
                S.op("act", lambda e: e.activation(out=rsi[:, :], in_=rsi[:, :], func=AF.Exp, scale=-0.5), reads=[rsi_b], writes=[rsi_b])
                S.op("dve", lambda e, sl=sl: e.scalar_tensor_tensor(
                    out=dst[:, sl], in0=c1[:, sl], scalar=float(scale), in1=rsi[:, :], op0=ALU.mult, op1=ALU.mult),
                    reads=[c1_b, rsi_b], writes=[dst_b])

        def to_token_major(src, src_b, dst, dst_b):
            for g in range(4):
                pi = ibank()
                pt = psum[pi].bitcast(BF16)
                for q in range(4):
                    c = 4 * g + q
                    S.op("pe", lambda e, pt=pt, q=q, c=c: e.transpose(
                        out=pt[:, q * 128:(q + 1) * 128], in_=src[:, c * 128:(c + 1) * 128], identity=ident_bf[:, :]),
                        reads=[src_b, ident_bf_b], writes=[psum_b[pi][q]], skip_same=True)
                S.op("act", lambda e, pt=pt, g=g: e.activation(
                    out=dst[:, 4 * g:4 * g + 4, :], in_=pt[:, 0:512].rearrange("p (a b) -> p a b", a=4), func=AF.Copy),
                    reads=psum_b[pi], writes=[dst_b])

        def bc4(ap2d):
            return ap2d.unsqueeze(1).broadcast_to([128, 4, 128])

        def r4(ap):
            return ap.rearrange("p (a b) -> p a b", a=4)

        def emit_inputs(h):
            sz, sz_b = szs[h % 2], szs_b[h % 2]
            vtm, vtm_b = vtms[h % 2], vtms_b[h % 2]
            conv_silu(C_DQ + h * 128, h)
            l2norm_to(qT, qT_b, 128 ** -0.5)
            conv_silu(C_DK + h * 128, 8 + h)
            l2norm_to(kT, kT_b, 1.0)
            conv_silu(C_DV + h * 128, 16 + h)
            for hs in (slice(0, T // 2), slice(T // 2, T)):
                S.op("act", lambda e, hs=hs: e.activation(out=vT[:, hs], in_=c1[:, hs], func=AF.Copy), reads=[c1_b], writes=[vT_b])
            to_token_major(kT, kT_b, ktm, ktm_b)
            to_token_major(vT, vT_b, vtm, vtm_b)
            S.dma("sp", c1[:, :], scr[C_DZ + h * 128:C_DZ + (h + 1) * 128, :], writes=[c1_b])
            silu_inplace(c1, c1_b)
            for hs in (slice(0, T // 2), slice(T // 2, T)):
                S.op("act", lambda e, hs=hs: e.activation(out=sz[:, hs], in_=c1[:, hs], func=AF.Copy), reads=[c1_b], writes=[sz_b])
            if "gdn_qkv" in C.dbg_out and h == 0:
                S.dma("sp", C.dbg_out["gdn_qkv"][0:128, :], qT[:, :], reads=[qT_b])
                S.dma("sp", C.dbg_out["gdn_qkv"][128:256, :], kT[:, :], reads=[kT_b])
                S.dma("sp", C.dbg_out["gdn_qkv"][256:384, :], vT[:, :], reads=[vT_b])

        emit_inputs(0)
        for h in range(nheads):
            sz, sz_b = szs[h % 2], szs_b[h % 2]
            vtm, vtm_b = vtms[h % 2], vtms_b[h % 2]

            for d in range(2):
                col = d * 8 + h
                gcol = gtm[:, :, col:col + 1]
                bcol = gtm[:, :, 32 + col:32 + col + 1]
                S.op("dve", lambda e, d=d, col=col: e.tensor_tensor(
                    out=kg[d][:, :, :], in0=ktm[:, :, :], in1=egtm[:, :, col:col + 1].broadcast_to([128, NT, 128]), op=ALU.mult),
                    reads=[ktm_b, egtm_b], writes=[kg_b[d]])
                S.op("dve", lambda e, d=d, col=col: e.tensor_tensor(
                    out=ktl[d][:, :, :], in0=ktm[:, :, :], in1=ektm[:, :, col:col + 1].broadcast_to([128, NT, 128]), op=ALU.mult),
                    reads=[ktm_b, ektm_b], writes=[ktl_b[d]])
                S.op("pool", lambda e, d=d, gcol=gcol: e.tensor_tensor(
                    out=gU[:, :, :], in0=U[d].unsqueeze(1).broadcast_to([128, NT, 128]), in1=gcol.broadcast_to([128, NT, 128]), op=ALU.mult),
                    reads=[c2_b, gtm_b], writes=[gU_b])
                last = 127 if d == 0 else 0
                KB = 4
                G = list(range(4))

                def qsl(q):
                    return slice(q * 128, (q + 1) * 128)

                pA = [bank() for k in G]
                for k in G:
                    for q in range(4):
                        c = 4 * k + q
                        S.op("pe", lambda e, p=pA[k], q=q, c=c: e.matmul(
                            out=psum[p][:, qsl(q)], lhsT=ONES, rhs=gU[:, c, :], start=True, stop=False),
                            reads=[gU_b, cst_b], writes=psum_b[pA[k]], skip_same=True)
                        S.op("pe", lambda e, p=pA[k], q=q, c=c: e.matmul(
                            out=psum[p][:, qsl(q)], lhsT=gU[:, c, :], rhs=NEGONES, start=False, stop=False),
                            reads=[gU_b, c2_b], writes=psum_b[pA[k]], skip_same=True)
                        S.op("pe", lambda e, p=pA[k], q=q, d=d: e.matmul(
                            out=psum[p][:, qsl(q)], lhsT=IDF, rhs=PEN[d], start=False, stop=True),
                            reads=[c2_b], writes=psum_b[pA[k]], skip_same=True)
                for k in G:
                    S.op("act", lambda e, p=pA[k], k=k: e.activation(out=DTs[k][:, :], in_=psum[p][:, :], func=AF.Exp),
                         reads=psum_b[pA[k]], writes=[DTs_b[k]])
                pB = [bank() for k in G]
                for k in G:
                    for q in range(4):
                        c = 4 * k + q
                        S.op("pe", lambda e, p=pB[k], q=q, c=c: e.matmul(
                            out=psum[p][:, qsl(q)], lhsT=NEGONES, rhs=gU[:, c, :], start=True, stop=True),
                            reads=[gU_b, c2_b], writes=psum_b[pB[k]], skip_same=True)
                for k in G:
                    S.op("act", lambda e, p=pB[k], k=k: e.activation(out=ebs[k][:, :], in_=psum[p][:, :], func=AF.Exp, scale=-1.0),
                         reads=psum_b[pB[k]], writes=[ebs_b[k]])
                    S.op("act", lambda e, p=pB[k], k=k, d=d, last=last: e.activation(
                        out=dcl[:, d, 4 * k:4 * k + 4], in_=r4(psum[p][:, :])[:, :, last], func=AF.Exp, scale=-1.0),
                        reads=psum_b[pB[k]], writes=[dcl_b[d]])
                for k in G:
                    sl = slice(k * 512, (k + 1) * 512)
                    S.op("dve", lambda e, d=d, sl=sl, k=k: e.tensor_tensor(out=qdT[d][:, sl], in0=qT[:, sl], in1=ebs[k][:, :], op=ALU.mult),
                         reads=[qT_b, ebs_b[k]], writes=[qdT_b[d]])
                pD = [bank() for k in G]
                for k in G:
                    for q in range(4):
                        cs_ = slice((4 * k + q) * 128, (4 * k + q + 1) * 128)
                        S.op("pe", lambda e, p=pD[k], q=q, cs_=cs_: e.matmul(
                            out=psum[p][:, qsl(q)], lhsT=kT[:, cs_], rhs=qT[:, cs_], start=True, stop=True),
                            reads=[kT_b, qT_b], writes=psum_b[pD[k]], skip_same=True)
                for k in G:
                    sl = slice(k * 512, (k + 1) * 512)
                    S.op("dve", lambda e, p=pD[k], d=d, sl=sl, k=k: e.tensor_tensor(out=atT[d][:, sl], in0=psum[p][:, :], in1=DTs[k][:, :], op=ALU.mult),
                         reads=psum_b[pD[k]] + [DTs_b[k]], writes=[atT_b[d]])
                for k in G:
                    S.op("dve", lambda e, k=k, bcol=bcol: e.tensor_tensor(
                        out=r4(DTs[k][:, :]), in0=r4(DTs[k][:, :]), in1=bcol[:, 4 * k:4 * k + 4, :].broadcast_to([128, 4, 128]), op=ALU.mult),
                        reads=[DTs_b[k], gtm_b], writes=[DTs_b[k]])
                pC = [bank() for k in G]
                for k in G:
                    for q in range(4):
                        cs_ = slice((4 * k + q) * 128, (4 * k + q + 1) * 128)
                        S.op("pe", lambda e, p=pC[k], q=q, cs_=cs_: e.matmul(
                            out=psum[p][:, qsl(q)], lhsT=kT[:, cs_], rhs=kT[:, cs_], start=True, stop=True),
                            reads=[kT_b], writes=psum_b[pC[k]], skip_same=True)
                for k in G:
                    S.op("dve", lambda e, p=pC[k], k=k: e.tensor_tensor(out=NTs[k][:, :], in0=psum[p][:, :], in1=DTs[k][:, :], op=ALU.mult),
                         reads=psum_b[pC[k]] + [DTs_b[k]], writes=[NTs_b[k]])
                for l in range(1, 8):
                    def Xop(k, q, l=l):
                        return ident_bf[:, :] if l == 1 else Xs[k][:, qsl(q)]

                    def Yop(k, q, l=l):
                        return ident_bf[:, :] if l == 1 else Ys[k][:, qsl(q)]

                    pP = [bank() for k in G]
                    for k in G:
                        for q in range(4):
                            S.op("pe", lambda e, xo=Xop(k, q), yo=Yop(k, q), p=pP[k], q=q, k=k: e.matmul(out=psum[p][:, qsl(q)], lhsT=NTs[k][:, qsl(q)], rhs=xo, start=True, stop=True),
                                 reads=[NTs_b[k], Xs_b[k]], writes=psum_b[pP[k]], skip_same=True)
                    for k in G:
                        S.op("dve", lambda e, p=pP[k], k=k, d=d, l=l: e.tensor_tensor(
                            out=r4(Ps[k][:, :]), in0=r4(psum[p][:, :]), in1=bc4(lmask(d, l)), op=ALU.mult),
                            reads=psum_b[pP[k]] + [cb_b], writes=[Ps_b[k]])
                    if l < 7:
                        pX = [bank() for k in G]
                        for k in G:
                            for q in range(4):
                                S.op("pe", lambda e, xo=Xop(k, q), yo=Yop(k, q), p=pX[k], q=q, k=k: e.matmul(out=psum[p][:, qsl(q)], lhsT=ident_bf[:, :], rhs=xo, start=True, stop=False),
                                     reads=[ident_bf_b, Xs_b[k]], writes=psum_b[pX[k]], skip_same=True)
                                S.op("pe", lambda e, xo=Xop(k, q), yo=Yop(k, q), p=pX[k], q=q, k=k: e.matmul(out=psum[p][:, qsl(q)], lhsT=yo, rhs=Ps[k][:, qsl(q)], start=False, stop=True),
                                     reads=[Ys_b[k], Ps_b[k]], writes=psum_b[pX[k]], skip_same=True)
                    pY = [bank() for k in G]
                    for k in G:
                        via_act = (k >= 2)
                        for q in range(4):
                            if via_act:
                                S.op("pe", lambda e, xo=Xop(k, q), yo=Yop(k, q), p=pY[k], q=q, k=k: e.matmul(out=psum[p][:, qsl(q)], lhsT=ident_bf[:, :], rhs=yo, start=True, stop=False),
                                     reads=[ident_bf_b, Ys_b[k]], writes=psum_b[pY[k]], skip_same=True)
                            S.op("pe", lambda e, xo=Xop(k, q), yo=Yop(k, q), p=pY[k], q=q, k=k, via_act=via_act: e.matmul(out=psum[p][:, qsl(q)], lhsT=Ps[k][:, qsl(q)], rhs=yo, start=not via_act, stop=True),
                                 reads=[Ps_b[k], Ys_b[k]], writes=psum_b[pY[k]], skip_same=True)
                    if l < 7:
                        for k in G:
                            S.op("act", lambda e, p=pX[k], k=k: e.activation(out=Xs[k][:, :], in_=psum[p][:, :], func=AF.Copy),
                                 reads=psum_b[pX[k]], writes=[Xs_b[k]])
                    for k in G:
                        sl = slice(k * 512, (k + 1) * 512)
                        dst = Ys[k][:, :] if l < 7 else VT[d][:, sl]
                        dst_b = Ys_b[k] if l < 7 else VT_b[d]
                        if k >= 2:
                            S.op("act", lambda e, p=pY[k], dst=dst: e.activation(out=dst, in_=psum[p][:, :], func=AF.Copy),
                                 reads=psum_b[pY[k]], writes=[dst_b])
                        else:
                            yin = bc4(ident_bf[:, :]) if l == 1 else r4(Ys[k][:, :])
                            S.op("dve", lambda e, p=pY[k], dst=dst, yin=yin: e.tensor_tensor(out=r4(dst), in0=r4(psum[p][:, :]), in1=yin, op=ALU.add),
                                 reads=psum_b[pY[k]] + [Ys_b[k], ident_bf_b], writes=[dst_b])
                pH = [bank() for k in G]
                for k in G:
                    for q in range(4):
                        c = 4 * k + q
                        cs_ = slice(c * 128, (c + 1) * 128)
                        S.op("pe", lambda e, p=pH[k], q=q, c=c, cs_=cs_, d=d: e.matmul(
                            out=psum[p][:, qsl(q)], lhsT=kg[d][:, c, :], rhs=VT[d][:, cs_], start=True, stop=True),
                            reads=[kg_b[d], VT_b[d]], writes=psum_b[pH[k]], skip_same=True)
                for k in G:
                    sl = slice(k * 512, (k + 1) * 512)
                    S.op("act", lambda e, p=pH[k], d=d, sl=sl: e.activation(out=wpn[d][:, sl], in_=psum[p][:, :], func=AF.Copy, scale=-1.0),
                         reads=psum_b[pH[k]], writes=[wpn_b[d]])

            for d in range(2):
                S.op("pool", lambda e, d=d: e.memset(S32[d][:, :], 0.0), writes=[S32_b[d]])
                S.op("pool", lambda e, d=d: e.memset(Sbf[d][:, :], 0.0), writes=[Sbf_b[d]])
            pend = []
            if h + 1 < nheads:
                S.rec_begin()
                emit_inputs(h + 1)
                pend = S.rec_end()
            per_step = (len(pend) + 2 * NT - 1) // (2 * NT) if pend else 0
            for s in range(NT):
                for d in range(2):
                    c = s if d == 0 else NT - 1 - s
                    cs_ = slice(c * 128, (c + 1) * 128)
                    col = d * 8 + h
                    pv, pS, pO = 3 * d, 3 * d + 1, 3 * d + 2
                    S.op("pe", lambda e, pv=pv, d=d, c=c, cs_=cs_, vtm=vtm: e.matmul(
                        out=psum[pv][:, 0:128], lhsT=VT[d][:, cs_], rhs=vtm[:, c, :], start=True, stop=False),
                        reads=[VT_b[d], vtm_b], writes=[psum_b[pv][0]], skip_same=True)
                    S.op("pe", lambda e, pv=pv, d=d, cs_=cs_: e.matmul(
                        out=psum[pv][:, 0:128], lhsT=wpn[d][:, cs_], rhs=Sbf[d][:, :], start=False, stop=True),
                        reads=[wpn_b[d], Sbf_b[d]], writes=[psum_b[pv][0]], skip_same=True)
                    S.op("act", lambda e, pv=pv, d=d, c=c, col=col: e.activation(
                        out=vnb[d][:, :], in_=psum[pv][:, 0:128], func=AF.Copy, scale=gtm[:, c, 32 + col:32 + col + 1]),
                        reads=[psum_b[pv][0], gtm_b], writes=[vnb_b[d]])
                    S.op("pe", lambda e, pS=pS, d=d, c=c: e.matmul(
                        out=psum[pS][:, 0:128], lhsT=ktl[d][:, c, :], rhs=vnb[d][:, :], start=True, stop=True),
                        reads=[ktl_b[d], vnb_b[d]], writes=[psum_b[pS][1]], skip_same=True)
                    S.op("pe", lambda e, pO=pO, d=d, cs_=cs_: e.matmul(
                        out=psum[pO][:, 0:128], lhsT=Sbf[d][:, :], rhs=qdT[d][:, cs_], start=True, stop=False),
                        reads=[Sbf_b[d], qdT_b[d]], writes=[psum_b[pO][2]], skip_same=True)
                    S.op("pe", lambda e, pO=pO, d=d, cs_=cs_: e.matmul(
                        out=psum[pO][:, 0:128], lhsT=vnb[d][:, :], rhs=atT[d][:, cs_], start=False, stop=True),
                        reads=[vnb_b[d], atT_b[d]], writes=[psum_b[pO][2]], skip_same=True)
                    S.op("dve", lambda e, pS=pS, d=d, c=c: e.scalar_tensor_tensor(
                        out=S32[d][:, :], in0=S32[d][:, :], scalar=dcl[:, d, c:c + 1], in1=psum[pS][:, 0:128],
                        op0=ALU.mult, op1=ALU.add),
                        reads=[S32_b[d], dcl_b[d], psum_b[pS][1]], writes=[S32_b[d]])
                    S.op("pool", lambda e, d=d: e.tensor_copy(out=Sbf[d][:, :], in_=S32[d][:, :]),
                         reads=[S32_b[d]], writes=[Sbf_b[d]])
                    S.op("dve", lambda e, pO=pO, d=d, cs_=cs_: e.tensor_copy(out=oacc[d][:, cs_], in_=psum[pO][:, 0:128]),
                         reads=[psum_b[pO][2]], writes=[oacc_b[d]])
                    S.play(pend, per_step)
            S.play(pend)

            yi = h % 2
            S.op("dve", lambda e: e.tensor_tensor(out=oacc[0][:, :], in0=oacc[0][:, :], in1=oacc[1][:, :], op=ALU.add),
                 reads=[oacc_b[0], oacc_b[1]], writes=[oacc_b[0]])
            if "gdn_o" in C.dbg_out and h == 0:
                S.dma("sp", C.dbg_out["gdn_o"][:, :], oacc[0][:, :], reads=[oacc_b[0]])
            for bp in range(2):
                BL = [2 * bp, 2 * bp + 1]
                pNs = {blk: bank() for blk in BL}
                for blk in BL:
                    sl = slice(blk * 512, (blk + 1) * 512)
                    k2 = blk % 2
                    S.op("act", lambda e, sl=sl, k2=k2: e.activation(out=sqs2[k2][:, :], in_=oacc[0][:, sl], func=AF.Square),
                         reads=[oacc_b[0]], writes=[sqs2_b[k2]])
                for blk in BL:
                    k2 = blk % 2
                    S.op("pe", lambda e, pN=pNs[blk], k2=k2: e.matmul(out=psum[pN][:, :], lhsT=ONES, rhs=sqs2[k2][:, :], start=True, stop=True),
                         reads=[sqs2_b[k2], cst_b], writes=psum_b[pNs[blk]], skip_same=True)
                for blk in BL:
                    k2 = blk % 2
                    S.op("act", lambda e, pN=pNs[blk], k2=k2: e.activation(out=rss2[k2][:, :], in_=psum[pN][:, :], func=AF.Ln, scale=1.0 / 128, bias=EPS),
                         reads=psum_b[pNs[blk]], writes=[rss2_b[k2]])
                for blk in BL:
                    k2 = blk % 2
                    S.op("act", lambda e, k2=k2: e.activation(out=rss2[k2][:, :], in_=rss2[k2][:, :], func=AF.Exp, scale=-0.5),
                         reads=[rss2_b[k2]], writes=[rss2_b[k2]])
                for blk in BL:
                    sl = slice(blk * 512, (blk + 1) * 512)
                    k2 = blk % 2
                    S.op("dve", lambda e, sl=sl, k2=k2: e.scalar_tensor_tensor(
                        out=tts2[k2][:, :], in0=oacc[0][:, sl], scalar=gn[:, 0:1], in1=rss2[k2][:, :], op0=ALU.mult, op1=ALU.mult),
                        reads=[oacc_b[0], gn_b, rss2_b[k2]], writes=[tts2_b[k2]])
                for blk in BL:
                    sl = slice(blk * 512, (blk + 1) * 512)
                    k2 = blk % 2
                    S.op("dve", lambda e, sl=sl, yi=yi, sz=sz, k2=k2: e.tensor_tensor(out=yb[yi][:, sl], in0=tts2[k2][:, :], in1=sz[:, sl], op=ALU.mult),
                         reads=[tts2_b[k2], sz_b], writes=[yb_b[yi]])
            S.dma("sp", C.y_scr[1024 + h * 128:1024 + (h + 1) * 128, :], yb[yi][:, :], reads=[yb_b[yi]])


def phase_branch(C):
    nc, S, debug = C.nc, C.S, C.debug
    psum, psum_b = C.psum, C.psum_b
    scr = C.scr
    ident_bf, ident_bf_b = C.ident_bf, C.ident_bf_b
    with contextlib.ExitStack() as oes:
        mergedT = oes.enter_context(nc.sbuf_tensor("mergedT", [128, KC, T], BF16))
        mg_b = [Buf("mg%d" % t) for t in range(NT)]
        with contextlib.ExitStack() as pes:
            def sb(name, shape, dt):
                return pes.enter_context(nc.sbuf_tensor(name, list(shape), dt))
            yT = sb("yT", [128, KC, T], BF16)
            yT_b = [Buf("yT%d" % k) for k in range(KC)]
            y_r = C.y_scr.rearrange("(kc p) t -> p kc t", p=128)
            for k in range(KC):
                S.dma("sp", yT[:, k, :], y_r[:, k, :], writes=[yT_b[k]])
            NWB = 3
            wg = [sb("wbg%d" % i, [128, 8, 128], BF16) for i in range(NWB)]
            wd = [sb("wbd%d" % i, [128, 8, 128], BF16) for i in range(NWB)]
            wg_b = [Buf("wbg") for i in range(NWB)]
            wd_b = [Buf("wbd") for i in range(NWB)]
            gg = [sb("gg%d" % i, [128, T], F32) for i in range(2)]
            gd = [sb("gd%d" % i, [128, T], F32) for i in range(2)]
            gg_b = [Buf("gg") for i in range(2)]
            gd_b = [Buf("gd") for i in range(2)]
            sgg = [sb("sgg%d" % i, [128, 512], F32) for i in range(2)]
            sgd = [sb("sgd%d" % i, [128, 512], F32) for i in range(2)]
            sgg_b = [Buf("sgg") for i in range(2)]
            sgd_b = [Buf("sgd") for i in range(2)]
            t1 = [sb("t1_%d" % i, [128, 512], F32) for i in range(2)]
            t2 = [sb("t2_%d" % i, [128, 512], F32) for i in range(2)]
            t1_b = [Buf("t1") for i in range(2)]
            t2_b = [Buf("t2") for i in range(2)]
            wbg_r = C.w_branch_gla.rearrange("(kc p) n -> p kc n", p=128)
            wbd_r = C.w_branch_gdn.rearrange("(kc p) n -> p kc n", p=128)
            cnt = 0
            for db in range(KC):
                wi = db % NWB
                gi = db % 2
                S.dma("pool", wg[wi][:, :, :], wbg_r[:, :, db * 128:(db + 1) * 128], writes=[wg_b[wi]])
                S.dma("pool", wd[wi][:, :, :], wbd_r[:, :, db * 128:(db + 1) * 128], writes=[wd_b[wi]])
                S.dma("sp", gg[gi][:, :], scr[C_BG + db * 128:C_BG + (db + 1) * 128, :], writes=[gg_b[gi]])
                S.dma("sp", gd[gi][:, :], scr[C_BD + db * 128:C_BD + (db + 1) * 128, :], writes=[gd_b[gi]])
                for blk in range(4):
                    sl = slice(blk * 512, (blk + 1) * 512)
                    pG = (2 * cnt) % 8
                    pD = (2 * cnt + 1) % 8
                    j = cnt % 2
                    cnt += 1
                    for kc in range(8):
                        S.op("pe", lambda e, pG=pG, wi=wi, kc=kc, sl=sl: e.matmul(
                            out=psum[pG][:, :], lhsT=wg[wi][:, kc, :], rhs=yT[:, kc, sl], start=(kc == 0), stop=(kc == 7)),
                            reads=[wg_b[wi], yT_b[kc]], writes=psum_b[pG], skip_same=True)
                    for kc in range(8):
                        S.op("pe", lambda e, pD=pD, wi=wi, kc=kc, sl=sl: e.matmul(
                            out=psum[pD][:, :], lhsT=wd[wi][:, kc, :], rhs=yT[:, 8 + kc, sl], start=(kc == 0), stop=(kc == 7)),
                            reads=[wd_b[wi], yT_b[8 + kc]], writes=psum_b[pD], skip_same=True)
                    S.op("act", lambda e, j=j, gi=gi, sl=sl: e.activation(out=sgg[j][:, :], in_=gg[gi][:, sl], func=AF.Sigmoid),
                         reads=[gg_b[gi]], writes=[sgg_b[j]])
                    S.op("act", lambda e, j=j, gi=gi, sl=sl: e.activation(out=sgd[j][:, :], in_=gd[gi][:, sl], func=AF.Sigmoid),
                         reads=[gd_b[gi]], writes=[sgd_b[j]])
                    S.op("dve", lambda e, j=j, pG=pG: e.tensor_tensor(out=t1[j][:, :], in0=psum[pG][:, :], in1=sgg[j][:, :], op=ALU.mult),
                         reads=psum_b[pG] + [sgg_b[j]], writes=[t1_b[j]])
                    S.op("dve", lambda e, j=j, pD=pD: e.tensor_tensor(out=t2[j][:, :], in0=psum[pD][:, :], in1=sgd[j][:, :], op=ALU.mult),
                         reads=psum_b[pD] + [sgd_b[j]], writes=[t2_b[j]])
                    S.op("dve", lambda e, j=j, db=db, sl=sl: e.tensor_tensor(out=mergedT[:, db, sl], in0=t1[j][:, :], in1=t2[j][:, :], op=ALU.add),
                         reads=[t1_b[j], t2_b[j]], writes=mg_b[4 * blk:4 * blk + 4])
        S.barrier()
        if "merged" in C.dbg_out:
            S.dma("sp", C.dbg_out["merged"].rearrange("(kc p) t -> p kc t", p=128), mergedT[:, :, :], reads=mg_b)
        with contextlib.ExitStack() as pes:
            def sb(name, shape, dt):
                return pes.enter_context(nc.sbuf_tensor(name, list(shape), dt))
            Wout = sb("Wout", [128, KC, D], BF16)
            Wout_b = [Buf("Wout%d" % k) for k in range(KC)]
            wo_r = C.w_out.rearrange("(kc p) n -> p kc n", p=128)
            for k in range(KC):
                S.dma("pool", Wout[:, k, :], wo_r[:, k, :], writes=[Wout_b[k]], max_dma_last_dim=4096)
            g2b = sb("g2b", [128, D], F32)
            g2b_b = Buf("g2b")
            S.dma("sp", g2b[:, :], C.norm2_g.partition_broadcast(128), writes=[g2b_b])
            xt = [sb("xt2_%d" % i, [128, D], F32) for i in range(2)]
            xt_b = [Buf("xt2") for i in range(2)]
            x1t = [sb("x1t%d" % i, [128, D], F32) for i in range(2)]
            x1t_b = [Buf("x1t") for i in range(2)]
            junk = sb("junk2", [128, D], BF16)
            junk_b = Buf("junk2")
            hb = [sb("hb2_%d" % i, [128, D], BF16) for i in range(2)]
            hb_b = [Buf("hb2") for i in range(2)]
            hst = [sb("hst%d" % i, [128, KC, 128], BF16) for i in range(2)]
            hst_b = [Buf("hst") for i in range(2)]
            stat = sb("stat2", [128, 4 * NT], F32)
            stat_b = [Buf("stat2") for i in range(NT)]
            h2_r = C.h2_scr.rearrange("(kc p) t -> p kc t", p=128)
            pcnt = 0
            pend_t = []
            for t in range(NT):
                i = t % 2
                ts_ = slice(t * 128, (t + 1) * 128)
                S.dma("sp", xt[i][:, :], C.x[ts_, :], writes=[xt_b[i]])
                for cb_ in range(4):
                    pi = cb_ + 4 * (t % 2)
                    csl = slice(cb_ * 512, (cb_ + 1) * 512)
                    for kc in range(KC):
                        S.op("pe", lambda e, pi=pi, kc=kc, ts_=ts_, csl=csl: e.matmul(
                            out=psum[pi][:, :], lhsT=mergedT[:, kc, ts_], rhs=Wout[:, kc, csl], start=(kc == 0), stop=(kc == KC - 1)),
                            reads=[mg_b[t], Wout_b[kc]], writes=psum_b[pi], skip_same=True)
                    S.op("dve", lambda e, pi=pi, i=i, csl=csl: e.tensor_tensor(out=x1t[i][:, csl], in0=psum[pi][:, :], in1=xt[i][:, csl], op=ALU.add),
                         reads=psum_b[pi] + [xt_b[i]], writes=[x1t_b[i]])
                S.dma("pool", C.x1_scr[ts_, :], x1t[i][:, :], reads=[x1t_b[i]])
                ss = stat[:, 4 * t:4 * t + 1]
                lnv = stat[:, 4 * t + 1:4 * t + 2]
                rstd = stat[:, 4 * t + 2:4 * t + 3]
                S.op("act", lambda e, i=i, ss=ss: e.activation(out=junk[:, :], in_=x1t[i][:, :], func=AF.Square, accum_out=ss),
                     reads=[x1t_b[i]], writes=[junk_b, stat_b[t]])
                S.op("act", lambda e, ss=ss, lnv=lnv: e.activation(out=lnv, in_=ss, func=AF.Ln, scale=1.0 / D, bias=EPS),
                     reads=[stat_b[t]], writes=[stat_b[t]])
                S.op("act", lambda e, rstd=rstd, lnv=lnv: e.activation(out=rstd, in_=lnv, func=AF.Exp, scale=-0.5),
                     reads=[stat_b[t]], writes=[stat_b[t]])
                S.op("dve", lambda e, i=i, rstd=rstd: e.scalar_tensor_tensor(
                    out=hb[i][:, :], in0=x1t[i][:, :], scalar=rstd, in1=g2b[:, :], op0=ALU.mult, op1=ALU.mult),
                    reads=[x1t_b[i], stat_b[t], g2b_b], writes=[hb_b[i]])
                S.play(pend_t)
                S.rec_begin()
                for g in range(4):
                    pi = ((pcnt + 1) % 2) * 4 + g
                    pt = psum[pi].bitcast(BF16)
                    for q in range(4):
                        kc = g * 4 + q
                        S.op("pe", lambda e, pt=pt, q=q, i=i, kc=kc: e.transpose(
                            out=pt[:, q * 128:(q + 1) * 128], in_=hb[i][:, kc * 128:(kc + 1) * 128], identity=ident_bf[:, :]),
                            reads=[hb_b[i], ident_bf_b], writes=psum_b[pi], skip_same=True)
                    S.op("act", lambda e, pt=pt, g=g, i=i: e.activation(
                        out=hst[i][:, g * 4:(g + 1) * 4, :], in_=pt[:, 0:512].rearrange("p (a b) -> p a b", a=4), func=AF.Copy),
                        reads=psum_b[pi], writes=[hst_b[i]])
                pcnt += 1
                S.dma("pool", h2_r[:, :, ts_], hst[i][:, :, :], reads=[hst_b[i]])
                pend_t = S.rec_end()
            S.play(pend_t)


def phase_ffn(C):
    nc, S, debug = C.nc, C.S, C.debug
    psum, psum_b = C.psum, C.psum_b
    TH = 1024
    NB3 = 342
    with contextlib.ExitStack() as pes:
        def sb(name, shape, dt):
            return pes.enter_context(nc.sbuf_tensor(name, list(shape), dt))
        h2T = sb("h2T", [128, KC, TH + 2], BF16)
        h2T_b = Buf("h2T")
        aT = sb("aT", [128, 44, TH], BF16)
        aT_b = [Buf("aT%d" % j) for j in range(44)]
        NW = 5
        wu = [sb("wu%d" % i, [128, KC, 128], BF16) for i in range(NW)]
        wu_b = [Buf("wu") for i in range(NW)]
        pg = [sb("pg%d" % i, [128, TH + 2], F32) for i in range(2)]
        pg_b = [Buf("pg") for i in range(2)]
        cg = [sb("cg%d" % i, [128, TH], F32) for i in range(2)]
        cg_b = [Buf("cg") for i in range(2)]
        fw = sb("fw", [128, 88 * 4], F32)
        fw_b = Buf("fw")
        S.dma("sp", fw[:, :], C.ffn_cw_d[:, :], writes=[fw_b])
        NWD = 2
        wdn = [sb("wdn%d" % i, [128, 44, 128], BF16) for i in range(NWD)]
        wdn_b = [Buf("wdn") for i in range(NWD)]
        x1s = [sb("x1s%d" % i, [128, 512], F32) for i in range(2)]
        x1s_b = [Buf("x1s") for i in range(2)]
        xo = [sb("xo%d" % i, [128, 512], F32) for i in range(2)]
        xo_b = [Buf("xo") for i in range(2)]
        wup_r = C.w_up.rearrange("(kc p) n -> p kc n", p=128)
        wdn_r = C.w_down.rearrange("(fc p) n -> p fc n", p=128)
        h2_r = C.h2_scr.rearrange("(kc p) t -> p kc t", p=128)
        ucnt = 0
        pcnt = 0
        dcnt = 0
        for half in range(2):
            t0 = half * TH
            S.op("pool", lambda e: e.memset(h2T[:, :, :], 0.0), writes=[h2T_b])
            lo = max(t0 - 1, 0)
            hi = min(t0 + TH + 1, T)
            S.dma("sp", h2T[:, :, lo - (t0 - 1):hi - (t0 - 1)], h2_r[:, :, lo:hi], writes=[h2T_b])
            ulist = [(j, which) for j in range(44) for which in range(2)]
            PF = NW - 1

            def issue_unit(k):
                j_, which_ = ulist[k]
                c0_ = which_ * D_FF + j_ * 128
                wi_ = (ucnt + k) % NW
                S.dma("pool", wu[wi_][:, :, :], wup_r[:, :, c0_:c0_ + 128], writes=[wu_b[wi_]])

            for k in range(min(PF, len(ulist))):
                issue_unit(k)
            for k, (j, which) in enumerate(ulist):
                if k + PF < len(ulist):
                    issue_unit(k + PF)
                blk = which * 44 + j
                wi = (ucnt + k) % NW
                pgi = k % 2
                for b3 in range(3):
                    pi = pcnt % 8
                    pcnt += 1
                    s3 = slice(b3 * NB3, (b3 + 1) * NB3)
                    for kc in range(KC):
                        S.op("pe", lambda e, pi=pi, wi=wi, kc=kc, s3=s3: e.matmul(
                            out=psum[pi][:, 0:NB3], lhsT=wu[wi][:, kc, :], rhs=h2T[:, kc, s3], start=(kc == 0), stop=(kc == KC - 1)),
                            reads=[wu_b[wi], h2T_b], writes=psum_b[pi], skip_same=True)
                    if pcnt % 2 == 0:
                        S.op("act", lambda e, pi=pi, pgi=pgi, s3=s3: e.activation(out=pg[pgi][:, s3], in_=psum[pi][:, 0:NB3], func=AF.Copy),
                             reads=psum_b[pi], writes=[pg_b[pgi]])
                    else:
                        S.op("dve", lambda e, pi=pi, pgi=pgi, s3=s3: e.tensor_copy(out=pg[pgi][:, s3], in_=psum[pi][:, 0:NB3]),
                             reads=psum_b[pi], writes=[pg_b[pgi]])
                w0 = fw[:, blk * 4:blk * 4 + 1]
                w1 = fw[:, blk * 4 + 1:blk * 4 + 2]
                w2 = fw[:, blk * 4 + 2:blk * 4 + 3]
                bb = fw[:, blk * 4 + 3:blk * 4 + 4]
                S.op("act", lambda e, pgi=pgi, which=which, w1=w1, bb=bb: e.activation(
                    out=cg[which][:, :], in_=pg[pgi][:, 1:TH + 1], func=AF.Identity, scale=w1, bias=bb),
                    reads=[pg_b[pgi], fw_b], writes=[cg_b[which]])
                S.op("dve", lambda e, pgi=pgi, which=which, w0=w0: e.scalar_tensor_tensor(
                    out=cg[which][:, :], in0=pg[pgi][:, 0:TH], scalar=w0, in1=cg[which][:, :], op0=ALU.mult, op1=ALU.add),
                    reads=[pg_b[pgi], fw_b, cg_b[which]], writes=[cg_b[which]])
                S.op("dve", lambda e, pgi=pgi, which=which, w2=w2: e.scalar_tensor_tensor(
                    out=cg[which][:, :], in0=pg[pgi][:, 2:TH + 2], scalar=w2, in1=cg[which][:, :], op0=ALU.mult, op1=ALU.add),
                    reads=[pg_b[pgi], fw_b, cg_b[which]], writes=[cg_b[which]])
                if which == 0:
                    S.op("act", lambda e: e.activation(out=cg[0][:, :], in_=cg[0][:, :], func=AF.Silu),
                         reads=[cg_b[0]], writes=[cg_b[0]])
                else:
                    S.op("dve", lambda e, j=j: e.tensor_tensor(out=aT[:, j, :], in0=cg[0][:, :], in1=cg[1][:, :], op=ALU.mult),
                         reads=[cg_b[0], cg_b[1]], writes=[aT_b[j]])
            ucnt += len(ulist)
            for cgp in range(4):
                units = []
                for u in range(4):
                    c0 = cgp * 512 + u * 128
                    wi = dcnt % NWD
                    dcnt += 1
                    S.dma("pool", wdn[wi][:, :, :], wdn_r[:, :, c0:c0 + 128], writes=[wdn_b[wi]])
                    units.append(wi)
                    for tt in range(TH // 128):
                        pi = tt % 8
                        ts_ = slice(tt * 128, (tt + 1) * 128)
                        for fc in range(44):
                            S.op("pe", lambda e, pi=pi, u=u, fc=fc, ts_=ts_, wi=wi: e.matmul(
                                out=psum[pi][:, u * 128:(u + 1) * 128], lhsT=aT[:, fc, ts_], rhs=wdn[wi][:, fc, :],
                                start=(fc == 0), stop=(fc == 43)),
                                reads=[aT_b[fc], wdn_b[wi]], writes=psum_b[pi], skip_same=True)
                for tt in range(TH // 128):
                    pi = tt % 8
                    i = tt % 2
                    rows = slice(t0 + tt * 128, t0 + (tt + 1) * 128)
                    csl = slice(cgp * 512, (cgp + 1) * 512)
                    S.dma("sp", x1s[i][:, :], C.x1_scr[rows, csl], writes=[x1s_b[i]])
                    S.op("dve", lambda e, pi=pi, i=i: e.tensor_tensor(out=xo[i][:, :], in0=psum[pi][:, :], in1=x1s[i][:, :], op=ALU.add),
                         reads=psum_b[pi] + [x1s_b[i]], writes=[xo_b[i]])
                    S.dma("act", C.x2_scr[rows, csl], xo[i][:, :], reads=[xo_b[i]])


def phase_final(C):
    nc, S, debug = C.nc, C.S, C.debug
    with contextlib.ExitStack() as pes:
        def sb(name, shape, dt):
            return pes.enter_context(nc.sbuf_tensor(name, list(shape), dt))
        gfb = sb("gfb", [128, D], F32)
        gfb_b = Buf("gfb")
        S.dma("sp", gfb[:, :], C.final_norm_g.partition_broadcast(128), writes=[gfb_b])
        NBF = 3
        xt = [sb("xf%d" % i, [128, D], F32) for i in range(NBF)]
        xt_b = [Buf("xf") for i in range(NBF)]
        ot = [sb("of%d" % i, [128, D], F32) for i in range(NBF)]
        ot_b = [Buf("of") for i in range(NBF)]
        junk = sb("junk3", [128, D], BF16)
        junk_b = Buf("junk3")
        stat = sb("stat3", [128, 4 * NT], F32)
        stat_b = [Buf("stat3") for i in range(NT)]
        for t in range(NT):
            i = t % NBF
            ts_ = slice(t * 128, (t + 1) * 128)
            S.dma("sp", xt[i][:, :], C.x2_scr[ts_, :], writes=[xt_b[i]])
            ss = stat[:, 4 * t:4 * t + 1]
            lnv = stat[:, 4 * t + 1:4 * t + 2]
            rstd = stat[:, 4 * t + 2:4 * t + 3]
            S.op("act", lambda e, i=i, ss=ss: e.activation(out=junk[:, :], in_=xt[i][:, :], func=AF.Square, accum_out=ss),
                 reads=[xt_b[i]], writes=[junk_b, stat_b[t]])
            S.op("act", lambda e, ss=ss, lnv=lnv: e.activation(out=lnv, in_=ss, func=AF.Ln, scale=1.0 / D, bias=EPS),
                 reads=[stat_b[t]], writes=[stat_b[t]])
            S.op("act", lambda e, rstd=rstd, lnv=lnv: e.activation(out=rstd, in_=lnv, func=AF.Exp, scale=-0.5),
                 reads=[stat_b[t]], writes=[stat_b[t]])
            S.op("dve", lambda e, i=i, rstd=rstd: e.scalar_tensor_tensor(
                out=ot[i][:, :], in0=xt[i][:, :], scalar=rstd, in1=gfb[:, :], op0=ALU.mult, op1=ALU.mult),
                reads=[xt_b[i], stat_b[t], gfb_b], writes=[ot_b[i]])
            S.dma("pool", C.out[ts_, :], ot[i][:, :], reads=[ot_b[i]])

NC2 = 1024
NCB = 14 * 128


_NC_CACHE = {}


def _consts():
    j = np.arange(128)[:, None]
    i = np.arange(128)[None, :]
    cst = np.zeros((128, NCST), np.float32)
    cst[:, 0:128] = np.where(j <= i, -1.0 / 16, 0.0)
    cst[:, 128:256] = np.where(j >= i, -1.0 / 16, 0.0)
    cst[:, 256:384] = np.where(j > i, -1.0 / 16, 0.0)
    cst[:, 384:512] = np.where(j < i, -1.0 / 16, 0.0)
    cst[:, 512:640] = 1.0
    cst[:, 640:1152] = np.tile(np.where(j <= i, 1.0, 0.0), (1, 4))
    cst[:, 1152:1664] = np.tile(np.where(j >= i, 1.0, 0.0), (1, 4))
    c2 = np.zeros((128, NC2), np.float32)
    c2[:, 0:128] = np.where(j <= i, 1.0, 0.0)
    c2[:, 128:256] = np.where(j >= i, 1.0, 0.0)
    c2[:, 256:384] = np.where(j > i, 1.0, 0.0)
    c2[:, 384:512] = np.where(j < i, 1.0, 0.0)
    c2[:, 512:640] = np.where(j <= i, 0.0, -30000.0)
    c2[:, 640:768] = np.where(j >= i, 0.0, -30000.0)
    c2[:, 768:896] = np.eye(128)
    c2[:, 896:1024] = -1.0
    cb = np.zeros((128, NCB), np.float32)
    for d in range(2):
        for l in range(1, 8):
            b = 1 << (l - 1)
            same = (i // (2 * b)) == (j // (2 * b))
            if d == 0:
                m = same & ((j % (2 * b)) >= b) & ((i % (2 * b)) < b)
            else:
                m = same & ((j % (2 * b)) < b) & ((i % (2 * b)) >= b)
            o = (d * 7 + (l - 1)) * 128
            cb[:, o:o + 128] = -m.astype(np.float32)
    return {
        "ident_bf": np.eye(128, dtype=np.float32).astype(ml_dtypes.bfloat16),
        "cst": cst,
        "cst2": c2,
        "cstb": cb.astype(ml_dtypes.bfloat16),
    }


def make_in_maps(inputs, n_cores=8):
    c = _consts()
    maps = []
    xs = np.ascontiguousarray(inputs["x"])
    for b in range(n_cores):
        m = {
            "x": xs[b],
            "norm1_g": np.ascontiguousarray(inputs["norm1_g"]).reshape(1, D),
            "w_in": np.ascontiguousarray(inputs["w_in"]).reshape(D, N_IN),
            "gla_decay_w_f": np.ascontiguousarray(inputs["gla_decay_w_f"]).reshape(16, 1024),
            "gla_decay_w_b": np.ascontiguousarray(inputs["gla_decay_w_b"]).reshape(16, 1024),
            "gla_decay_b_f": np.ascontiguousarray(inputs["gla_decay_b_f"]).reshape(1, 1024),
            "gla_decay_b_b": np.ascontiguousarray(inputs["gla_decay_b_b"]).reshape(1, 1024),
            "gla_norm_g": np.ascontiguousarray(inputs["gla_norm_g"]).reshape(1, 128),
            "gdn_norm_g": np.ascontiguousarray(inputs["gdn_norm_g"]).reshape(1, 128),
            "w_branch_gla": np.ascontiguousarray(inputs["w_branch_gla"]).reshape(1024, D),
            "w_branch_gdn": np.ascontiguousarray(inputs["w_branch_gdn"]).reshape(1024, D),
            "w_out": np.ascontiguousarray(inputs["w_out"]).reshape(D, D),
            "norm2_g": np.ascontiguousarray(inputs["norm2_g"]).reshape(1, D),
            "w_up": np.ascontiguousarray(inputs["w_up"]).reshape(D, 2 * D_FF),
            "w_down": np.ascontiguousarray(inputs["w_down"]).reshape(D_FF, D),
            "ffn_cw": np.ascontiguousarray(np.concatenate([
                np.asarray(inputs["ffn_conv_w"]).reshape(3, 88, 128), np.asarray(inputs["ffn_conv_b"]).reshape(1, 88, 128)],
                axis=0).transpose(2, 1, 0).reshape(128, 88 * 4)),
            "final_norm_g": np.ascontiguousarray(inputs["final_norm_g"]).reshape(1, D),
            "gdn_cw": np.ascontiguousarray(
                np.asarray(inputs["gdn_conv_w"]).reshape(3, 24, 128).transpose(2, 1, 0).reshape(128, 72)),
            "gdn_hp": np.ascontiguousarray(np.stack([
                np.concatenate([np.asarray(inputs["gdn_a_log_f"]).reshape(8), np.asarray(inputs["gdn_a_log_b"]).reshape(8)]),
                np.concatenate([np.asarray(inputs["gdn_dt_bias_f"]).reshape(8), np.asarray(inputs["gdn_dt_bias_b"]).reshape(8)]),
            ], axis=1).astype(np.float32)),
        }
        m.update(c)
        maps.append(m)
    return maps


def kernel(**inputs):
    nc = build_nc()
    in_maps = make_in_maps(inputs, 8)
    res = run_bass_kernel_spmd(nc, in_maps, core_ids=list(range(8)))
    return np.stack([np.asarray(r["out"]) for r in res.results], axis=0)
```
